# Optimizing a Trainium2 kernel written in Bass

```python
import math
import jax, jax.numpy as jnp
from jax import lax
import numpy as np

D_MODEL = 1024
BATCH = 1
SEQ = 16384
DEPTH = 1

HEAD_DIM = 128
MIX_WIDTH = D_MODEL
GDN_WIDTH = MIX_WIDTH // 2
RET_WIDTH = MIX_WIDTH - GDN_WIDTH
GDN_HEADS = GDN_WIDTH // HEAD_DIM
RET_HEADS = RET_WIDTH // HEAD_DIM
SHORT_CONV = 4
FFN_CONV = 3
D_FF = ((8 * D_MODEL // 3 + 255) // 256) * 256
CHUNK = 64
RET_ROPE_BASE = 10000.0
EPS = 1e-6
IN_SPLITS = (GDN_WIDTH, GDN_WIDTH, GDN_WIDTH, GDN_WIDTH, GDN_HEADS, GDN_HEADS,
             RET_WIDTH, RET_WIDTH, RET_WIDTH, RET_WIDTH)
IN_COLS = 4 * GDN_WIDTH + 2 * GDN_HEADS + 4 * RET_WIDTH

kernel_name = "hybrid_gdn_retention_convffn"


def _rms_norm(x, w):
    xf = x.astype(jnp.float32)
    y = xf * lax.rsqrt(jnp.mean(xf * xf, axis=-1, keepdims=True) + EPS)
    return (y * w.astype(jnp.float32)).astype(x.dtype)


def _rms_f32(xf):
    return xf * lax.rsqrt(jnp.mean(xf * xf, axis=-1, keepdims=True) + EPS)


def _l2norm(xf):
    return xf * lax.rsqrt(jnp.sum(xf * xf, axis=-1, keepdims=True) + EPS)


def _causal_dwconv(x, w):
    k, c = w.shape
    return lax.conv_general_dilated(
        x, w[:, None, :].astype(x.dtype), window_strides=(1,), padding=[(k - 1, 0)],
        dimension_numbers=("NWC", "WIO", "NWC"), feature_group_count=c)


def _to_chunks(t):
    b, s, h, d = t.shape
    return t.reshape(b, s // CHUNK, CHUNK, h, d).transpose(1, 0, 3, 2, 4)


def _from_chunks(t):
    nc, b, h, c, d = t.shape
    return t.transpose(1, 0, 3, 2, 4).reshape(b, nc * c, h, d)


def _scalar_chunks(t):
    b, s, h = t.shape
    return t.reshape(b, s // CHUNK, CHUNK, h).transpose(1, 0, 3, 2)


def _gated_deltanet(q, k, v, gate, a, b, conv_w, a_log, dt_bias, norm_w):
    bsz, s, _ = q.shape
    h, d = GDN_HEADS, HEAD_DIM
    qkv = jax.nn.silu(_causal_dwconv(jnp.concatenate([q, k, v], axis=-1), conv_w))
    qkv = qkv.astype(jnp.float32)
    q, k, v = jnp.split(qkv, 3, axis=-1)
    q = _l2norm(q.reshape(bsz, s, h, d)) * (d ** -0.5)
    k = _l2norm(k.reshape(bsz, s, h, d))
    v = v.reshape(bsz, s, h, d)
    beta = jax.nn.sigmoid(b.astype(jnp.float32))
    g = -jnp.exp(a_log.astype(jnp.float32)) * jax.nn.softplus(
        a.astype(jnp.float32) + dt_bias.astype(jnp.float32))

    qc, kc, vc = _to_chunks(q), _to_chunks(k), _to_chunks(v)
    gcum = jnp.cumsum(_scalar_chunks(g), axis=-1)
    betac = _scalar_chunks(beta)
    causal = jnp.tril(jnp.ones((CHUNK, CHUNK), dtype=bool))
    strict = jnp.tril(jnp.ones((CHUNK, CHUNK), dtype=bool), -1)
    diff = gcum[..., :, None] - gcum[..., None, :]
    decay = jnp.exp(jnp.where(causal, diff, -jnp.inf))
    kb = kc * betac[..., None]
    vb = vc * betac[..., None]
    m = jnp.where(strict, jnp.einsum('nbhik,nbhjk->nbhij', kb, kc) * decay, 0.0)
    eye = jnp.eye(CHUNK, dtype=jnp.float32)
    t_mat = lax.linalg.triangular_solve(eye + m, jnp.broadcast_to(eye, m.shape),
                                        left_side=True, lower=True, unit_diagonal=True)
    u_base = jnp.einsum('nbhij,nbhjv->nbhiv', t_mat, vb)
    w_mat = jnp.einsum('nbhij,nbhjk->nbhik', t_mat, kb * jnp.exp(gcum)[..., None])
    qk = jnp.einsum('nbhik,nbhjk->nbhij', qc, kc) * decay
    q_dec = qc * jnp.exp(gcum)[..., None]
    k_dec = kc * jnp.exp(gcum[..., -1:] - gcum)[..., None]
    chunk_decay = jnp.exp(gcum[..., -1])

    def step(state, inp):
        u_b, w_c, qk_c, qd_c, kd_c, cd_c = inp
        u = u_b - jnp.einsum('bhck,bhkv->bhcv', w_c, state)
        o = jnp.einsum('bhck,bhkv->bhcv', qd_c, state) + jnp.einsum('bhij,bhjv->bhiv', qk_c, u)
        state = state * cd_c[..., None, None] + jnp.einsum('bhck,bhcv->bhkv', kd_c, u)
        return state, o

    s0 = jnp.zeros((bsz, h, d, d), dtype=jnp.float32)
    _, o = lax.scan(step, s0, (u_base, w_mat, qk, q_dec, k_dec, chunk_decay))
    o = _from_chunks(o)
    gt = gate.astype(jnp.float32).reshape(bsz, s, h, d)
    o = _rms_f32(o) * norm_w.astype(jnp.float32) * jax.nn.silu(gt)
    return o.reshape(bsz, s, h * d).astype(gate.dtype)


def _rotate_every_two(x):
    x1 = x[..., ::2]
    x2 = x[..., 1::2]
    return jnp.stack((-x2, x1), axis=-1).reshape(x.shape)


def _xpos_rotary(x, pos):
    d = x.shape[-1]
    angle = 1.0 / (RET_ROPE_BASE ** jnp.linspace(0.0, 1.0, d // 2, dtype=jnp.float32))
    angle = jnp.repeat(angle, 2)
    phase = pos[:, None] * angle[None, :]
    return x * jnp.cos(phase)[:, None, :] + _rotate_every_two(x) * jnp.sin(phase)[:, None, :]


def _retention(q, k, v, gate):
    bsz, s, _ = q.shape
    h, d = RET_HEADS, HEAD_DIM
    pos = jnp.arange(s, dtype=jnp.float32)
    q = _xpos_rotary(q.astype(jnp.float32).reshape(bsz, s, h, d), pos)
    k = _xpos_rotary(k.astype(jnp.float32).reshape(bsz, s, h, d), pos) * (d ** -0.5)
    v = v.astype(jnp.float32).reshape(bsz, s, h, d)
    log_gamma = jnp.log(1.0 - jnp.exp2(-5.0 - jnp.arange(h, dtype=jnp.float32)))
    idx = jnp.arange(CHUNK, dtype=jnp.float32)
    causal = jnp.tril(jnp.ones((CHUNK, CHUNK), dtype=bool))
    rel = idx[:, None] - idx[None, :]
    d_mat = jnp.exp(jnp.where(causal[None], rel[None] * log_gamma[:, None, None], -jnp.inf))
    xi = jnp.exp((idx[None, :] + 1.0) * log_gamma[:, None])[..., None]
    zeta = jnp.exp((CHUNK - 1.0 - idx[None, :]) * log_gamma[:, None])[..., None]
    chunk_gamma = jnp.exp(CHUNK * log_gamma)

    qc, kc, vc = _to_chunks(q), _to_chunks(k), _to_chunks(v)
    inner = jnp.einsum('nbhij,nbhjv->nbhiv',
                       jnp.einsum('nbhik,nbhjk->nbhij', qc, kc) * d_mat, vc)
    q_xi = qc * xi
    kv = jnp.einsum('nbhck,nbhcv->nbhkv', kc * zeta, vc)

    def step(r, inp):
        inner_c, qx_c, kv_c = inp
        o = inner_c + jnp.einsum('bhck,bhkv->bhcv', qx_c, r)
        r = r * chunk_gamma[:, None, None] + kv_c
        return r, o

    r0 = jnp.zeros((bsz, h, d, d), dtype=jnp.float32)
    _, o = lax.scan(step, r0, (inner, q_xi, kv))
    o = _rms_f32(_from_chunks(o))
    o = jax.nn.silu(gate.astype(jnp.float32).reshape(bsz, s, h, d)) * o
    return o.reshape(bsz, s, h * d).astype(gate.dtype)


def _conv_glu_mlp(x, w_up, conv_w, w_down):
    hdn = jnp.einsum('bsd,df->bsf', x, w_up)
    hdn = _causal_dwconv(hdn, conv_w)
    g, u = jnp.split(hdn, 2, axis=-1)
    return jnp.einsum('bsf,fd->bsd', jax.nn.silu(g) * u, w_down)


def setup_inputs(seed: int = 0) -> dict:
    key = jax.random.key(seed)
    ks = jax.random.split(key, 14)
    f32 = jnp.float32
    x = jax.random.normal(ks[0], (BATCH, SEQ, D_MODEL), f32)
    attn_norm_w = 1.0 + 0.02 * jax.random.normal(ks[1], (DEPTH, D_MODEL), f32)
    w_in = jax.random.normal(ks[2], (DEPTH, D_MODEL, IN_COLS), f32) * D_MODEL ** -0.5
    gdn_conv_w = jax.random.normal(ks[3], (DEPTH, SHORT_CONV, 3 * GDN_WIDTH), f32) * SHORT_CONV ** -0.5
    gdn_a_log = jnp.log(jax.random.uniform(ks[4], (DEPTH, GDN_HEADS), f32, minval=1.0, maxval=16.0))
    dt = jnp.exp(jax.random.uniform(ks[5], (DEPTH, GDN_HEADS), f32,
                                    minval=math.log(1e-3), maxval=math.log(1e-1)))
    gdn_dt_bias = dt + jnp.log(-jnp.expm1(-dt))
    gdn_norm_w = 1.0 + 0.02 * jax.random.normal(ks[6], (DEPTH, HEAD_DIM), f32)
    w_out = jax.random.normal(ks[7], (DEPTH, MIX_WIDTH, D_MODEL), f32) * MIX_WIDTH ** -0.5
    mlp_norm_w = 1.0 + 0.02 * jax.random.normal(ks[8], (DEPTH, D_MODEL), f32)
    w_up = jax.random.normal(ks[9], (DEPTH, D_MODEL, 2 * D_FF), f32) * D_MODEL ** -0.5
    mlp_conv_w = jax.random.normal(ks[10], (DEPTH, FFN_CONV, 2 * D_FF), f32) * FFN_CONV ** -0.5
    w_down = jax.random.normal(ks[11], (DEPTH, D_FF, D_MODEL), f32) * D_FF ** -0.5
    final_norm_w = 1.0 + 0.02 * jax.random.normal(ks[12], (D_MODEL,), f32)
    return {"x": x, "attn_norm_w": attn_norm_w, "w_in": w_in, "gdn_conv_w": gdn_conv_w,
            "gdn_a_log": gdn_a_log, "gdn_dt_bias": gdn_dt_bias, "gdn_norm_w": gdn_norm_w,
            "w_out": w_out, "mlp_norm_w": mlp_norm_w, "w_up": w_up, "mlp_conv_w": mlp_conv_w,
            "w_down": w_down, "final_norm_w": final_norm_w}


def reference(x, attn_norm_w, w_in, gdn_conv_w, gdn_a_log, gdn_dt_bias, gdn_norm_w,
              w_out, mlp_norm_w, w_up, mlp_conv_w, w_down, final_norm_w):
    split_idx = []
    acc = 0
    for sz in IN_SPLITS[:-1]:
        acc += sz
        split_idx.append(acc)
    h = x
    for l in range(DEPTH):
        n = _rms_norm(h, attn_norm_w[l])
        proj = jnp.einsum('bsd,dc->bsc', n, w_in[l])
        (gq, gk, gv, gg, ga, gb, rq, rk, rv, rg) = jnp.split(proj, split_idx, axis=-1)
        o_gdn = _gated_deltanet(gq, gk, gv, gg, ga, gb, gdn_conv_w[l], gdn_a_log[l],
                                gdn_dt_bias[l], gdn_norm_w[l])
        o_ret = _retention(rq, rk, rv, rg)
        mix = jnp.concatenate([o_gdn, o_ret], axis=-1)
        h = h + jnp.einsum('bsm,md->bsd', mix, w_out[l])
        n2 = _rms_norm(h, mlp_norm_w[l])
        h = h + _conv_glu_mlp(n2, w_up[l], mlp_conv_w[l], w_down[l])
    return _rms_norm(h, final_norm_w)
```

```python
import math
from contextlib import ExitStack
import numpy as np
import concourse.bass as bass
import concourse.mybir as mybir
from concourse.bass_utils import run_bass_kernel_spmd

F32 = mybir.dt.float32
BF16 = mybir.dt.bfloat16
ALU = mybir.AluOpType
AF = mybir.ActivationFunctionType

NCORES = 8
S = 16384
D = 1024
NT = S // 128
NG = NT // 4
OWN = 2048
T0 = NT - OWN // 128 - 1
NSLOT = NT - T0
G_FULL = T0 // 4
DFF = 2816
EPS = 1e-6
HD = 128
IN_COLS = 4104


class Prog:
    ENG = ("pe", "act", "dve", "pool", "sp")

    def __init__(self, nc, n_dma_sems=8):
        self.nc = nc
        self.eng = {"pe": nc.tensor, "act": nc.scalar, "dve": nc.vector,
                    "pool": nc.gpsimd, "sp": nc.sync}
        self.sem = {e: nc.alloc_semaphore("c_" + e) for e in self.ENG}
        self.cnt = {e: 0 for e in self.ENG}
        self.dsem = {e: [nc.alloc_semaphore("d_%s%d" % (e, i)) for i in range(n_dma_sems)]
                     for e in ("sp", "pool")}
        self.dval = {e: [0] * n_dma_sems for e in ("sp", "pool")}
        self.drr = {e: 0 for e in ("sp", "pool")}
        self.seen = {e: {} for e in self.ENG}
        self.lastw = {}
        self.readers = {}
        self.n_ins = 0
        self.n_wait = 0

    def _semof(self, tok):
        return self.sem[tok] if isinstance(tok, str) else self.dsem[tok[0]][tok[1]]

    def _wait(self, e, tok, val):
        if tok == e and e == "pe":
            return
        if self.seen[e].get(tok, 0) >= val:
            return
        self.eng[e].wait_ge(self._semof(tok), val)
        self.seen[e][tok] = val
        self.n_wait += 1

    def _deps(self, e, reads, writes):
        for k in reads:
            w = self.lastw.get(k)
            if w is not None:
                self._wait(e, *w)
        for k in writes:
            w = self.lastw.get(k)
            if w is not None:
                self._wait(e, *w)
            for r in self.readers.get(k, ()):
                self._wait(e, *r)

    def _commit(self, tokval, reads, writes):
        for k in writes:
            self.lastw[k] = tokval
            self.readers[k] = []
        for k in reads:
            lst = self.readers.setdefault(k, [])
            lst[:] = [r for r in lst if r[0] != tokval[0]]
            lst.append(tokval)

    def op(self, e, fn, reads=(), writes=()):
        ex = [k for k in reads if k.startswith("ps")]
        if ex:
            reads = [k for k in reads if not k.startswith("ps")]
            writes = list(writes) + [k for k in ex if k not in writes]
        self._deps(e, reads, writes)
        ins = fn(self.eng[e])
        self.cnt[e] += 1
        ins.then_inc(self.sem[e], 1)
        self._commit((e, self.cnt[e]), reads, writes)
        self.n_ins += 1
        return ins

    def dma(self, e, out, in_, reads=(), writes=(), **kw):
        i = self.drr[e]
        self.drr[e] = (i + 1) % len(self.dsem[e])
        tok = (e, i)
        if self.dval[e][i] > 0:
            self._wait(e, tok, self.dval[e][i])
        self._deps(e, reads, writes)
        ins = self.eng[e].dma_start(out=out, in_=in_, **kw)
        self.dval[e][i] += 16
        ins.then_inc(self.dsem[e][i], 16)
        self._commit((tok, self.dval[e][i]), reads, writes)
        self.n_ins += 1
        return ins

    def barrier(self):
        for e in self.ENG:
            for e2 in self.ENG:
                if e2 != e and self.cnt[e2] > 0:
                    self._wait(e, e2, self.cnt[e2])
            for q in self.dsem:
                for i, v in enumerate(self.dval[q]):
                    if v > 0:
                        self._wait(e, (q, i), v)
        self.lastw.clear()
        self.readers.clear()

    def finish(self):
        for q in self.dsem:
            for i, v in enumerate(self.dval[q]):
                if v > 0:
                    self._wait("sp", (q, i), v)


def build_program(ng_run=NG, phase="all", parts="gr", dev_g0=None):
    nc = bass.Bass("TRN2", target_bir_lowering=False)
    dt_in = lambda name, shape: nc.dram_tensor(name, list(shape), F32, kind="ExternalInput").ap()
    xpad = dt_in("xpad", [S, D])
    w_in = dt_in("w_in", [D, IN_COLS])
    w_out = dt_in("w_out", [D, D])
    w_up = dt_in("w_up", [D, 2 * DFF])
    w_down = dt_in("w_down", [DFF, D])
    gcwT = dt_in("gcwT", [1536, 4])
    mcwT = dt_in("mcwT", [2 * DFF, 3])
    vecs = dt_in("vecs", [3, D])
    gsm = dt_in("gsm", [1, 8 + 128])
    cmat = dt_in("cmat", [10, 128, 128])
    rmat = dt_in("rmat", [4, 128, 128])
    rvec = dt_in("rvec", [128, 8])
    rxi = dt_in("rxi", [4, 128])
    cosF = dt_in("cosF", [S, 128])
    sinS = dt_in("sinS", [S, 128])
    out = nc.dram_tensor("out", [OWN, D], F32, kind="ExternalOutput").ap()

    p = Prog(nc)
    SKIP = ""
    op = p.op
    def dma(q, o_, i_, reads=(), writes=(), tag="", **kw):
        if tag and tag in SKIP:
            return p.op("pool", lambda e: e.memset(o_, 1.0), reads=reads, writes=writes)
        return p.dma(q, o_, i_, reads=reads, writes=writes, **kw)

    bank = [nc.alloc_psum_tensor("bank%d" % i, [128, 1024], BF16) for i in range(8)]
    bankF = [bk_[:, :].bitcast(F32) for bk_ in bank]
    rr = {"B": 0, "F": 0}
    POOLS = {"B": [6, 7], "F": [0, 1, 2, 3, 4, 5]}

    def nB():
        lst = POOLS["B"]; i = lst[rr["B"] % len(lst)]; rr["B"] += 1
        return bank[i], "ps%d" % i

    def nF():
        lst = POOLS["F"]; i = lst[rr["F"] % len(lst)]; rr["F"] += 1
        return bankF[i], "ps%d" % i

    es_r = ExitStack()
    def R(name, shape, dt):
        return es_r.enter_context(nc.sbuf_tensor(name, list(shape), dt, side="right"))
    cm = R("cm", [128, 10, 128], F32)
    identb = R("identb", [128, 128], BF16)
    rm = R("rm", [128, 4, 128], F32)
    rv = R("rv", [128, 8], F32)
    xibc = R("xibc", [128, 4, 128], F32)
    nwbc = R("nwbc", [128, 1, D], F32)
    gsmbc = R("gsmbc", [128, 136], F32)
    gconst = R("gconst", [128, 16], F32)
    IDENT, ONES, MSU, MU, TRI, SC0, SC1 = [cm[:, i, :] for i in range(7)]
    dma("sp", cm[:], cmat.rearrange("m p n -> p m n"), writes=["cm"], tag="m")
    dma("sp", rm[:], rmat.rearrange("m p n -> p m n"), writes=["rm"], tag="m")
    dma("sp", rv[:], rvec, writes=["rv"])
    for h in range(4):
        dma("sp", xibc[:, h, :], rxi[h:h + 1, :].partition_broadcast(128), writes=["xibc"], tag="b")
    dma("sp", nwbc[:, 0, :], vecs[0:1, :].partition_broadcast(128), writes=["nwbc"], tag="b")
    dma("sp", gsmbc[:], gsm.partition_broadcast(128), writes=["gsmbc"], tag="b")
    op("dve", lambda e: e.tensor_copy(out=identb[:], in_=IDENT), reads=["cm"], writes=["identb"])
    op("act", lambda e: e.activation(out=gconst[:, 0:4], in_=gsmbc[:, 0:4], func=AF.Exp), reads=["gsmbc"], writes=["gconst"])
    op("dve", lambda e: e.tensor_scalar(out=gconst[:, 0:4], in0=gconst[:, 0:4], scalar1=-1.0, scalar2=None, op0=ALU.mult),
       reads=["gconst"], writes=["gconst"])
    op("dve", lambda e: e.tensor_copy(out=gconst[:, 4:8], in_=gsmbc[:, 4:8]), reads=["gsmbc", "gconst"], writes=["gconst"])
    GNW = gsmbc[:, 8:136]

    es_mix = ExitStack()
    mix = es_mix.enter_context(nc.sbuf_tensor("mix", [128, NSLOT, D], BF16, side="left"))
    es_keep = ExitStack()
    es_ph = [ExitStack()]
    PHTAG = [""]
    def L(name, shape, dt, es=None):
        if es is None:
            name = name + PHTAG[0]
        return (es or es_ph[0]).enter_context(nc.sbuf_tensor(name, list(shape), dt, side="left"))
    K = lambda name, shape, dt: L(name, shape, dt, es_keep)

    w_in_v = w_in.rearrange("(k p) c -> p k c", p=128)
    Wg = {}
    for h in range(4):
        for X in (1, 2):
            Wg[(h, X)] = K("Wg%d%d" % (h, X), [128, 8, 128], BF16)
            c0 = X * 512 + h * 128
            dma("pool", Wg[(h, X)][:], w_in_v[:, :, c0:c0 + 128], writes=["Wg%d%d" % (h, X)], tag="w")
    Wab = K("Wab", [128, 8, 8], BF16)
    dma("pool", Wab[:], w_in_v[:, :, 2048:2056], writes=["Wab"], tag="w")
    Wrkv = [K("Wrkv%d" % h, [128, 8, 256], BF16) for h in range(4)]
    for h in range(4):
        for j, X in enumerate((1, 2)):
            c0 = 2056 + X * 512 + h * 128
            dma("pool", Wrkv[h][:, :, j * 128:(j + 1) * 128], w_in_v[:, :, c0:c0 + 128], writes=["Wrkv%d" % h], tag="w")
    gcw = K("gcw", [128, 12, 4], F32)
    dma("sp", gcw[:], gcwT.rearrange("(c p) i -> p c i", p=128), writes=["gcw"], tag="c")
    xt = [K("xt%d" % i, [128, D], F32) for i in range(2)]
    junk = K("junk", [128, D], BF16)
    nb = [K("nb%d" % i, [128, D], BF16) for i in range(2)]
    nTg = [K("nTg%d" % i, [128, 8, 512], BF16) for i in range(2)]
    st = K("st", [128, 16], F32)
    ab = K("ab", [128, 2, 4, 8], F32)
    CN = ["cG", "cLB", "cB", "cGAM", "cGL0", "cGL1", "cEG", "cEKD", "cCD0", "cCD1", "cGLB", "cGC", "cBG", "cTMP"]
    CT = {n: K(n, [128, 2, 4, 4], F32) for n in CN}
    halo = [[K("halo%d%d" % (h, X), [128, 4], F32) for X in range(3)] for h in range(4)]
    Sst = [K("S%d" % h, [128, 128], F32) for h in range(4)]
    Sb = [K("Sb%d" % h, [128, 128], BF16) for h in range(4)]
    Rst = [K("R%d" % h, [128, 128], F32) for h in range(4)]
    Rb = [K("Rb%d" % h, [128, 128], BF16) for h in range(4)]
    for h in range(4):
        op("pool", lambda e, h=h: e.memset(Sst[h][:], 0.0), writes=["S%d" % h])
        op("pool", lambda e, h=h: e.memset(Sb[h][:], 0.0), writes=["Sb%d" % h])
        op("pool", lambda e, h=h: e.memset(Rst[h][:], 0.0), writes=["R%d" % h])
        op("pool", lambda e, h=h: e.memset(Rb[h][:], 0.0), writes=["Rb%d" % h])
        for X in range(3):
            op("pool", lambda e, h=h, X=X: e.memset(halo[h][X][:], 0.0), writes=["halo%d%d" % (h, X)])

    CHN = ("kd", "ub", "wT", "ubf", "qkT", "qdT", "osb", "og", "sg")
    PH = {}

    def alloc_stream(sid, full, nchain, lb, HB, cbanks):
        nE = 2 if full else 1
        cx = {"id": sid, "lb": lb, "HB": HB, "cbanks": cbanks}
        Ld = {}
        for n, shp, dt in (("lr", [128, 4, 2], F32), ("rkb", [128, 8, 4], F32), ("Dab", [128, nE, 4, 128], F32),
                           ("E", [128, nE, 4, 128], BF16),
                           ("YP0", [128, 4, 256], BF16), ("YP1", [128, 4, 256], BF16),
                           ("XX0", [128, 4, 128], BF16), ("XX1", [128, 4, 128], BF16), ("kg", [128, 4, 128], BF16),
                           ("vtok", [128, 4, 128], BF16), ("TTb", [128, 4, 128], BF16)):
            Ld[n] = L("%s_%s" % (n, sid), shp, dt)
        if full:
            Ld["Eq"] = L("Eq_%s" % sid, [128, 4, 128], BF16)
        cx["L"] = Ld
        cx["XT"] = [(L("XT%s%d" % (sid, X), [128, 512], BF16) if (full or X != 0) else None) for X in range(3)]
        cx["ybuf"] = [L("ybuf%s%d" % (sid, i), [128, 512], F32) for i in range(2)]
        cx["xw"] = [L("xw%s%d" % (sid, i), [128, 515], F32) for i in range(2)]
        cx["sq"] = L("sq%s" % sid, [128, 512], F32)
        cx["ssc"] = L("ssc%s" % sid, [128, 4, 2], F32)
        cx["cs"] = []
        for i in range(nchain):
            d = {"id": "%s%d" % (sid, i)}
            lst = [("kd", [128, 4, 128], BF16), ("ub", [128, 4, 128], F32), ("wT", [128, 4, 128], BF16), ("ubf", [128, 128], BF16)]
            if full:
                lst += [("qkT", [128, 4, 128], BF16), ("qdT", [128, 4, 128], BF16),
                        ("osb", [128, 128], F32), ("og", [128, 128], F32), ("sg", [128, 128], BF16)]
            for n, shp, dt in lst:
                d[n] = L("%s_%s" % (n, d["id"]), shp, dt)
            op("pool", lambda e, d=d: e.memset(d["ubf"][:], 0.0), writes=["ubf_%s" % d["id"]])
            cx["cs"].append(d)
        return cx

    def alloc_phase(full):
        PH.clear()
        PHTAG[0] = "_F" if full else "_S"
        Rr = {}
        lst = [("rsb", [128, 256], F32), ("rt1", [128, 128], F32), ("rt2", [128, 128], F32), ("krot", [128, 128], BF16),
               ("kz", [128, 128], BF16), ("rvt", [128, 128], BF16)]
        if full:
            lst += [("qrot", [128, 128], BF16), ("rkT", [128, 128], BF16), ("rqT", [128, 128], BF16), ("rqx", [128, 128], BF16),
                    ("rPT", [128, 128], BF16), ("sgr", [128, 128], BF16), ("osb", [128, 128], F32), ("og", [128, 128], F32)]
        for n, shp, dt in lst:
            Rr[n] = L("%s_r" % n, shp, dt)
        PH["RS"] = Rr
        PH["nm"] = L("nm", [128, 2 if full else 1, 4, 128], F32)
        op("pool", lambda e: e.tensor_copy(out=PH["nm"][:, 0], in_=cm[:, 7, :].unsqueeze(1).broadcast_to([128, 4, 128])), reads=["cm"], writes=["nm"])
        if full:
            op("pool", lambda e: e.tensor_copy(out=PH["nm"][:, 1], in_=cm[:, 8, :].unsqueeze(1).broadcast_to([128, 4, 128])), reads=["cm", "nm"], writes=["nm"])
        PH["cs_t"] = [L("cs_t%d" % i, [128, 128], F32) for i in range(2)]
        PH["sn_t"] = [L("sn_t%d" % i, [128, 128], F32) for i in range(2)]
        if full:
            for h in range(4):
                Wg[(h, 0)] = L("Wg%d0" % h, [128, 8, 128], BF16)
                dma("pool", Wg[(h, 0)][:], w_in_v[:, :, h * 128:(h + 1) * 128], writes=["Wg%d0" % h], tag="w")
            PH["Wgg"] = L("Wgg", [128, 8, 512], BF16)
            dma("pool", PH["Wgg"][:], w_in_v[:, :, 1536:2048], writes=["Wgg"], tag="w")
            PH["Wrqg"] = [L("Wrqg%d" % h, [128, 8, 256], BF16) for h in range(4)]
            for h in range(4):
                for j, X in enumerate((0, 3)):
                    c0 = 2056 + X * 512 + h * 128
                    dma("pool", PH["Wrqg"][h][:, :, j * 128:(j + 1) * 128], w_in_v[:, :, c0:c0 + 128], writes=["Wrqg%d" % h], tag="w")
            PH["streams"] = [alloc_stream("A", True, 2, [0, 1, 2], [(0, 2, 0), (1, 2, 256)], [3, 4])]
            PH["rbank"] = 5; PH["rbank2"] = 6
        else:
            PH["streams"] = [alloc_stream("A", False, 2, [0, 1, 0], [(0, 1, 0), (0, 1, 256)], [4]),
                             alloc_stream("B", False, 2, [2, 3, 2], [(2, 3, 0), (2, 3, 256)], [5])]
            PH["rbank"] = 6; PH["rbank2"] = 6

    LN_DS = math.log(HD ** -0.5)

    def rms_rstd(src_ap, rkeys, n, col, ckey, jout=None, jkey="junk"):
        jo = junk[:, 0:n] if jout is None else jout
        op("act", lambda e: e.activation(out=jo, in_=src_ap, func=AF.Square, accum_out=col),
           reads=rkeys, writes=[jkey, ckey])
        op("act", lambda e: e.activation(out=col, in_=col, func=AF.Ln, bias=EPS, scale=1.0 / n),
           reads=[ckey], writes=[ckey])
        op("act", lambda e: e.activation(out=col, in_=col, func=AF.Exp, scale=-0.5), reads=[ckey], writes=[ckey])

    def norm_a(t):
        xb_ = xt[t % 2]; xk = "xt%d" % (t % 2); nb_ = nb[t % 2]; nbk = "nb%d" % (t % 2)
        dma("sp", xb_[:], xpad[t * 128:(t + 1) * 128, :], writes=[xk])
        rms_rstd(xb_[:], [xk], D, st[:, 0:1], "st0", jout=nb_[:], jkey=nbk)
        op("dve", lambda e: e.scalar_tensor_tensor(out=nb_[:], in0=xb_[:], scalar=st[:, 0:1], in1=nwbc[:, 0, :],
                                                   op0=ALU.mult, op1=ALU.mult), reads=[xk, "st0", "nwbc"], writes=[nbk])

    def norm_b(t, gi, tt):
        nb_ = nb[t % 2]; nbk = "nb%d" % (t % 2)
        pb, pk = bank[7], "ps7"
        for k in range(8):
            op("pe", lambda e, k=k: e.transpose(out=pb[:, k * 128:(k + 1) * 128], in_=nb_[:, k * 128:(k + 1) * 128],
                                               identity=identb[:]), reads=[nbk, "identb"], writes=[pk])
        nk = "nTg%d" % gi
        op("act", lambda e: e.copy(out=nTg[gi][:, :, tt * 128:(tt + 1) * 128],
                                   in_=pb[:, :].rearrange("p (k n) -> p k n", k=8)), reads=[pk], writes=[nk])

    def gdn_common(gi, full):
        nk = "nTg%d" % gi
        C = {n: CT[n][:, gi] for n in CN}
        ck = lambda n: "%s_%d" % (n, gi)
        abk = "ab%d" % gi
        for tt in range(4):
            pf, fk = bankF[7], "ps7"
            for k in range(8):
                op("pe", lambda e, k=k: e.matmul(out=pf[:, 0:8], lhsT=nTg[gi][:, k, tt * 128:(tt + 1) * 128],
                                                rhs=Wab[:, k, :], start=(k == 0), stop=(k == 7)),
                   reads=[nk, "Wab"], writes=[fk])
            op("dve", lambda e: e.tensor_copy(out=ab[:, gi, tt, :], in_=pf[:, 0:8]), reads=[fk], writes=[abk])
            yield
        for h in range(4):
            op("dve", lambda e, h=h: e.tensor_scalar(out=C["cTMP"][:, :, h], in0=ab[:, gi, :, h], scalar1=gconst[:, 4 + h:5 + h],
                                                    scalar2=None, op0=ALU.add), reads=[abk, "gconst"], writes=[ck("cTMP")])
        op("act", lambda e: e.activation(out=C["cTMP"], in_=C["cTMP"], func=AF.Exp), reads=[ck("cTMP")], writes=[ck("cTMP")])
        op("act", lambda e: e.activation(out=C["cTMP"], in_=C["cTMP"], func=AF.Ln, bias=1.0), reads=[ck("cTMP")], writes=[ck("cTMP")])
        for h in range(4):
            op("dve", lambda e, h=h: e.tensor_scalar(out=C["cG"][:, :, h], in0=C["cTMP"][:, :, h], scalar1=gconst[:, h:h + 1],
                                                    scalar2=None, op0=ALU.mult), reads=[ck("cTMP"), "gconst"], writes=[ck("cG")])
        op("act", lambda e: e.activation(out=C["cLB"], in_=ab[:, gi, :, 4:8], func=AF.Exp, scale=-1.0), reads=[abk], writes=[ck("cLB")])
        op("act", lambda e: e.activation(out=C["cLB"], in_=C["cLB"], func=AF.Ln, bias=1.0), reads=[ck("cLB")], writes=[ck("cLB")])
        op("act", lambda e: e.activation(out=C["cB"], in_=C["cLB"], func=AF.Exp, scale=-1.0), reads=[ck("cLB")], writes=[ck("cB")])
        op("dve", lambda e: e.tensor_scalar(out=C["cLB"], in0=C["cLB"], scalar1=-1.0, scalar2=None, op0=ALU.mult), reads=[ck("cLB"), ck("cB")], writes=[ck("cLB")])
        yield
        gflat = C["cG"].rearrange("p t h -> p (t h)")
        for lhs, dn in ((TRI, "cGAM"), (SC0, "cGL0"), (SC1, "cGL1")):
            pf, fk = bankF[7], "ps7"
            op("pe", lambda e, lhs=lhs: e.matmul(out=pf[:, 0:16], lhsT=lhs, rhs=gflat, start=True, stop=True),
               reads=["cm", ck("cG")], writes=[fk])
            op("dve", lambda e, dn=dn: e.tensor_copy(out=C[dn].rearrange("p t h -> p (t h)"), in_=pf[:, 0:16]),
               reads=[fk], writes=[ck(dn)])
        yield
        op("act", lambda e: e.activation(out=C["cEG"], in_=C["cGAM"], func=AF.Exp), reads=[ck("cGAM")], writes=[ck("cEG")])
        op("act", lambda e: e.activation(out=C["cCD0"], in_=C["cGL0"], func=AF.Exp), reads=[ck("cGL0")], writes=[ck("cCD0")])
        op("act", lambda e: e.activation(out=C["cCD1"], in_=C["cGL1"], func=AF.Exp), reads=[ck("cGL1")], writes=[ck("cCD1")])
        op("dve", lambda e: e.tensor_tensor(out=C["cEKD"][0:64], in0=C["cGL0"][0:64], in1=C["cGAM"][0:64], op=ALU.subtract),
           reads=[ck("cGL0"), ck("cGAM")], writes=[ck("cEKD")])
        op("dve", lambda e: e.tensor_tensor(out=C["cEKD"][64:128], in0=C["cGL1"][64:128], in1=C["cGAM"][64:128], op=ALU.subtract),
           reads=[ck("cGL1"), ck("cGAM"), ck("cEKD")], writes=[ck("cEKD")])
        op("act", lambda e: e.activation(out=C["cEKD"], in_=C["cEKD"], func=AF.Exp), reads=[ck("cEKD")], writes=[ck("cEKD")])
        op("dve", lambda e: e.tensor_tensor(out=C["cGLB"], in0=C["cGAM"], in1=C["cLB"], op=ALU.add), reads=[ck("cGAM"), ck("cLB")], writes=[ck("cGLB")])
        op("dve", lambda e: e.tensor_scalar(out=C["cGC"], in0=C["cGAM"], scalar1=LN_DS, scalar2=None, op0=ALU.add),
           reads=[ck("cGAM")], writes=[ck("cGC")])
        op("dve", lambda e: e.tensor_tensor(out=C["cBG"], in0=C["cB"], in1=C["cEG"], op=ALU.mult), reads=[ck("cB"), ck("cEG")], writes=[ck("cBG")])
        yield

    def gdn_proj(gi, h, full, cx):
        nk = "nTg%d" % gi
        sid = cx["id"]; lb = cx["lb"]
        sq_ = cx["sq"]; sqk = "sq%s" % sid
        Xs = (1, 2, 0) if full else (1, 2)

        def s1(i, X):
            xb_ = cx["xw"][i % 2]; xk = "xw%s%d" % (sid, i % 2)
            wk = "Wg%d%d" % (h, X); hk_ = "halo%d%d" % (h, X)
            op("pool", lambda e: e.tensor_copy(out=xb_[:, 0:3], in_=halo[h][X][:, 0:3]), reads=[hk_, xk], writes=[xk])
            pf, fk = bankF[lb[i % 3]], "ps%d" % lb[i % 3]
            for k in range(8):
                op("pe", lambda e, k=k: e.matmul(out=pf[:, :], lhsT=Wg[(h, X)][:, k, :], rhs=nTg[gi][:, k, :],
                                                start=(k == 0), stop=(k == 7)), reads=[wk, nk], writes=[fk])
            op("act", lambda e: e.copy(out=xb_[:, 3:515], in_=pf[:, :]), reads=[fk, xk], writes=[xk])

        def s2(i, X):
            xb_ = cx["xw"][i % 2]; xk = "xw%s%d" % (sid, i % 2)
            yb_ = cx["ybuf"][i % 2]; yk = "ybuf%s%d" % (sid, i % 2)
            tk = "XT%s%d" % (sid, X); hk_ = "halo%d%d" % (h, X)
            cw = lambda j: gcw[:, X * 4 + h, j:j + 1]
            op("dve", lambda e: e.tensor_scalar(out=yb_[:], in0=xb_[:, 3:515], scalar1=cw(3), scalar2=None, op0=ALU.mult),
               reads=[xk, "gcw"], writes=[yk])
            for j in (2, 1, 0):
                op("dve", lambda e, j=j: e.scalar_tensor_tensor(out=yb_[:], in0=xb_[:, j:j + 512], scalar=cw(j), in1=yb_[:],
                                                               op0=ALU.mult, op1=ALU.add), reads=[xk, "gcw", yk], writes=[yk])
            op("pool", lambda e: e.tensor_copy(out=halo[h][X][:, 0:3], in_=xb_[:, 512:515]), reads=[xk], writes=[hk_])
            op("act", lambda e: e.activation(out=cx["XT"][X][:], in_=yb_[:], func=AF.Silu), reads=[yk], writes=[tk])
            if X != 2:
                op("act", lambda e: e.activation(out=yb_[:], in_=yb_[:], func=AF.Silu), reads=[yk], writes=[yk])
                op("act", lambda e: e.activation(out=sq_[:], in_=yb_[:], func=AF.Square), reads=[yk], writes=[sqk])

        def s3(i, X):
            if X == 2:
                return
            j = 0 if X == 1 else 1
            pf2, fk2 = bankF[lb[(i + 1) % 3]], "ps%d" % lb[(i + 1) % 3]
            for tt in range(4):
                op("pe", lambda e, tt=tt: e.matmul(out=pf2[:, tt:tt + 1], lhsT=sq_[:, tt * 128:(tt + 1) * 128], rhs=cm[:, 1, 0:1], start=True, stop=True),
                   reads=[sqk, "cm"], writes=[fk2])
            op("dve", lambda e: e.tensor_copy(out=cx["ssc"][:, :, j], in_=pf2[:, 0:4]), reads=[fk2], writes=["ssc%s" % sid])

        n = len(Xs)
        for r in range(n + 3):
            if 0 <= r - 3 < n:
                s3(r - 3, Xs[r - 3])
            if 0 <= r - 1 < n:
                s2(r - 1, Xs[r - 1])
            if r < n:
                s1(r, Xs[r])
            yield

    def gdn_local(g, h, cx, cs):
        gi = g % 2; full = g >= G_FULL
        B = dict(cx["L"]); B.update(cs)
        sid = cx["id"]; lb = cx["lb"]
        key = lambda n: ("%s_%s" % (n, cs["id"])) if n in CHN else ("%s_%s" % (n, sid))
        C = {n: CT[n][:, gi] for n in CN}
        ck = lambda n: "%s_%d" % (n, gi)
        bF = lambda i: (bankF[lb[i]], "ps%d" % lb[i])
        bB = lambda i: (bank[lb[i]], "ps%d" % lb[i])
        kTa, vTa, qTa = cx["XT"][1], cx["XT"][2], cx["XT"][0]
        kTk, vTk, qTk = "XT%s1" % sid, "XT%s2" % sid, "XT%s0" % sid
        lr, rkb, Dab, E = B["lr"], B["rkb"], B["Dab"], B["E"]
        YP = [B["YP0"], B["YP1"]]; XX = [B["XX0"], B["XX1"]]
        kg, kd, vtok, TTb, ub, wT = B["kg"], B["kd"], B["vtok"], B["TTb"], B["ub"], B["wT"]
        Eq, qkT, qdT = B.get("Eq"), B.get("qkT"), B.get("qdT")
        hc = lambda n: C[n][:, :, h]
        bc4 = lambda ap2: ap2.unsqueeze(2).broadcast_to([128, 4, 128])
        rep4 = lambda ap2: ap2.unsqueeze(1).broadcast_to([128, 4, 128])
        v4 = lambda ap, w=128: ap.rearrange("p (a b) -> p a b", a=4)
        nc_ = 2 if full else 1
        sk = "ssc%s" % sid
        op("act", lambda e: e.activation(out=lr[:, :, 0:nc_], in_=cx["ssc"][:, :, 0:nc_], func=AF.Ln, bias=EPS), reads=[sk], writes=[key("lr")])
        op("act", lambda e: e.activation(out=rkb[:, 0, :], in_=lr[:, :, 0], func=AF.Exp, scale=-0.5), reads=[key("lr")], writes=[key("rk0")])
        op("dve", lambda e: e.scalar_tensor_tensor(out=rkb[:, 1, :], in0=lr[:, :, 0], scalar=-0.5, in1=hc("cGLB"), op0=ALU.mult, op1=ALU.add),
           reads=[key("lr"), ck("cGLB")], writes=[key("rk1")])
        op("dve", lambda e: e.tensor_tensor(out=rkb[:, 3, :], in0=rkb[:, 0, :], in1=hc("cBG"), op=ALU.mult), reads=[key("rk0"), ck("cBG")], writes=[key("rk3")])
        op("dve", lambda e: e.tensor_tensor(out=rkb[:, 4, :], in0=rkb[:, 0, :], in1=hc("cEKD"), op=ALU.mult), reads=[key("rk0"), ck("cEKD")], writes=[key("rk4")])
        op("dve", lambda e: e.scalar_tensor_tensor(out=rkb[:, 6, :], in0=lr[:, :, 0], scalar=-0.5, in1=hc("cGAM"), op0=ALU.mult, op1=ALU.subtract),
           reads=[key("lr"), ck("cGAM")], writes=[key("rk6")])
        if full:
            op("dve", lambda e: e.scalar_tensor_tensor(out=rkb[:, 2, :], in0=lr[:, :, 1], scalar=-0.5, in1=hc("cGC"), op0=ALU.mult, op1=ALU.add),
               reads=[key("lr"), ck("cGC")], writes=[key("rk2")])
        yield
        pb, pk = bB(0)
        for tt in range(4):
            op("pe", lambda e, tt=tt: e.transpose(out=pb[:, tt * 128:(tt + 1) * 128], in_=kTa[:, tt * 128:(tt + 1) * 128], identity=identb[:]),
               reads=[kTk, "identb"], writes=[pk])
        for tt in range(4):
            op("pe", lambda e, tt=tt: e.transpose(out=pb[:, 512 + tt * 128:512 + (tt + 1) * 128], in_=vTa[:, tt * 128:(tt + 1) * 128], identity=identb[:]),
               reads=[vTk, "identb"], writes=[pk])
        op("dve", lambda e: e.tensor_tensor(out=kg[:], in0=v4(pb[:, 0:512]), in1=bc4(rkb[:, 3, :]), op=ALU.mult), reads=[pk, key("rk3")], writes=[key("kg")])
        op("dve", lambda e: e.tensor_tensor(out=kd[:], in0=v4(pb[:, 0:512]), in1=bc4(rkb[:, 4, :]), op=ALU.mult), reads=[pk, key("rk4")], writes=[key("kd")])
        op("dve", lambda e: e.tensor_tensor(out=vtok[:], in0=v4(pb[:, 512:1024]), in1=bc4(hc("cB")), op=ALU.mult), reads=[pk, ck("cB")], writes=[key("vtok")])
        yield
        op("pool", lambda e: e.tensor_tensor(out=Dab[:, 0], in0=rep4(IDENT), in1=bc4(rkb[:, 1, :]), op=ALU.mult), reads=["cm", key("rk1")], writes=[key("Dab0")])
        pbc, bck = bF(1)
        op("pe", lambda e: e.matmul(out=pbc[:, :], lhsT=ONES, rhs=Dab[:, 0].rearrange("p a b -> p (a b)"), start=True, stop=False),
           reads=["cm", key("Dab0")], writes=[bck])
        op("pe", lambda e: e.matmul(out=pbc[:, :], lhsT=IDENT, rhs=PH["nm"][:, 0].rearrange("p a b -> p (a b)"), start=False, stop=True),
           reads=["cm", "nm", bck], writes=[bck])
        if full:
            op("pool", lambda e: e.tensor_tensor(out=Dab[:, 1], in0=rep4(IDENT), in1=bc4(rkb[:, 2, :]), op=ALU.mult), reads=["cm", key("rk2")], writes=[key("Dab1")])
            pbc2, bck2 = bF(2)
            op("pe", lambda e: e.matmul(out=pbc2[:, :], lhsT=ONES, rhs=Dab[:, 1].rearrange("p a b -> p (a b)"), start=True, stop=True),
               reads=["cm", key("Dab1")], writes=[bck2])
        for tt in range(4):
            op("act", lambda e, tt=tt: e.activation(out=E[:, 0, tt, :], in_=pbc[:, tt * 128:(tt + 1) * 128], func=AF.Exp, bias=rkb[:, 6, tt:tt + 1]),
               reads=[bck, key("rk6")], writes=[key("E0")])
        yield
        pkk, kkk = bF(0)
        for tt in range(4):
            sl = slice(tt * 128, (tt + 1) * 128)
            op("pe", lambda e, sl=sl: e.matmul(out=pkk[:, sl], lhsT=kTa[:, sl], rhs=kTa[:, sl], start=True, stop=True), reads=[kTk], writes=[kkk])
        Y0 = YP[0][:, :, 0:128]
        op("dve", lambda e: e.scalar_tensor_tensor(out=Y0, in0=v4(pkk[:, :]), scalar=-1.0, in1=E[:, 0], op0=ALU.mult, op1=ALU.mult),
           reads=[kkk, key("E0")], writes=[key("YP0a"), key("YP0a") + "h0", key("YP0a") + "h1"])
        if full:
            op("act", lambda e: e.activation(out=Eq[:], in_=v4(pbc2[:, :]), func=AF.Exp), reads=[bck2], writes=[key("Eq")])
            op("pe", lambda e: e.matmul(out=pbc2[:, :], lhsT=IDENT, rhs=PH["nm"][:, 1].rearrange("p a b -> p (a b)"), start=False, stop=True, skip_group_check=True),
               reads=["cm", "nm", bck2], writes=[bck2])
            for tt in range(4):
                op("act", lambda e, tt=tt: e.activation(out=E[:, 1, tt, :], in_=pbc2[:, tt * 128:(tt + 1) * 128], func=AF.Exp, bias=rkb[:, 6, tt:tt + 1]),
                   reads=[bck2, key("rk6")], writes=[key("E1")])
            pqk, qkk = bF(1)
            for tt in range(4):
                sl = slice(tt * 128, (tt + 1) * 128)
                op("pe", lambda e, sl=sl: e.matmul(out=pqk[:, sl], lhsT=kTa[:, sl], rhs=qTa[:, sl], start=True, stop=True), reads=[kTk, qTk], writes=[qkk])
            op("dve", lambda e: e.tensor_tensor(out=qkT[:], in0=v4(pqk[:, :]), in1=E[:, 1], op=ALU.mult), reads=[qkk, key("E1")], writes=[key("qkT")])
            op("pool", lambda e: e.tensor_tensor(out=qdT[:], in0=v4(qTa[:, :]), in1=Eq[:], op=ALU.mult), reads=[qTk, key("Eq")], writes=[key("qdT")])
        yield "FRONT_DONE"
        HB = cx["HB"]
        bR = lambda i: (bankF[i], "ps%d" % i)
        hk = lambda n, hh: key(n) + "h%d" % hh
        px, pxk = bB(2)
        for tt in range(4):
            op("pe", lambda e, tt=tt: e.transpose(out=px[:, tt * 128:(tt + 1) * 128], in_=YP[0][:, tt, 0:128], identity=identb[:]),
               reads=[key("YP0a"), "identb"], writes=[pxk])
        for hh in range(2):
            tsl = slice(2 * hh, 2 * hh + 2)
            op("act", lambda e, tsl=tsl, hh=hh: e.copy(out=XX[0][:, tsl, :], in_=v4(px[:, 0:512])[:, tsl, :]), reads=[pxk], writes=[hk("XX0", hh)])
            op("pool", lambda e, tsl=tsl: e.tensor_tensor(out=YP[1][:, tsl, 128:256], in0=YP[0][:, tsl, 0:128], in1=identb[:].unsqueeze(1).broadcast_to([128, 2, 128]), op=ALU.add),
               reads=[key("YP0a"), "identb"], writes=[hk("YP1b", hh)])
        yield
        for hh in range(2):
            yb_, xb2, xo = HB[hh]
            py, pyk = bR(yb_); pz, pzk = bR(xb2)
            for tt in (2 * hh, 2 * hh + 1):
                o0 = (tt % 2) * 128
                op("pe", lambda e, tt=tt, o0=o0, py=py: e.matmul(out=py[:, o0:o0 + 128], lhsT=XX[0][:, tt, :], rhs=YP[0][:, tt, 0:128], start=True, stop=True),
                   reads=[hk("XX0", hh), key("YP0a")], writes=[pyk])
                op("pe", lambda e, tt=tt, o0=o0, pz=pz, xo=xo: e.matmul(out=pz[:, xo + o0:xo + o0 + 128], lhsT=YP[0][:, tt, 0:128], rhs=XX[0][:, tt, :], start=True, stop=True),
                   reads=[hk("XX0", hh), key("YP0a")], writes=[pzk])
        for hh in range(2):
            yb_, xb2, xo = HB[hh]
            py, pyk = bR(yb_); pz, pzk = bR(xb2)
            tsl = slice(2 * hh, 2 * hh + 2)
            op("dve", lambda e, tsl=tsl, py=py: e.tensor_copy(out=YP[1][:, tsl, 0:128], in_=py[:, 0:256].rearrange("p (a b) -> p a b", a=2)),
               reads=[pyk], writes=[hk("YP1a", hh)])
            op("act", lambda e, tsl=tsl, pz=pz, xo=xo: e.copy(out=XX[1][:, tsl, :], in_=pz[:, xo:xo + 256].rearrange("p (a b) -> p a b", a=2)),
               reads=[pzk], writes=[hk("XX1", hh)])
        yield
        cur = 1
        for lvl in range(2, 6):
            c_, n_ = cur, 1 - cur
            last = (lvl == 5)
            for hh in range(2):
                yb_, xb2, xo = HB[hh]
                pv_, pvk = bR(yb_); pxx, pxxk = bR(xb2)
                for tt in (2 * hh, 2 * hh + 1):
                    o0 = (tt % 2) * 256
                    if not last:
                        op("pe", lambda e, tt=tt, c_=c_, pv_=pv_, o0=o0: e.matmul(out=pv_[:, o0:o0 + 256], lhsT=XX[c_][:, tt, :], rhs=YP[c_][:, tt, :], start=True, stop=True),
                           reads=[hk("XX%d" % c_, hh), hk("YP%da" % c_, hh), hk("YP%db" % c_, hh)], writes=[pvk])
                    else:
                        op("pe", lambda e, tt=tt, c_=c_, pv_=pv_, o0=o0: e.matmul(out=pv_[:, o0 + 128:o0 + 256], lhsT=XX[c_][:, tt, :], rhs=YP[c_][:, tt, 128:256], start=True, stop=True),
                           reads=[hk("XX%d" % c_, hh), hk("YP%db" % c_, hh)], writes=[pvk])
                for tt in (2 * hh, 2 * hh + 1):
                    o1 = (tt % 2) * 128
                    op("pe", lambda e, tt=tt, c_=c_, pxx=pxx, o1=o1, xo=xo: e.matmul(out=pxx[:, xo + o1:xo + o1 + 128], lhsT=YP[c_][:, tt, 0:128], rhs=XX[c_][:, tt, :], start=True, stop=True),
                       reads=[hk("XX%d" % c_, hh), hk("YP%da" % c_, hh)], writes=[pxxk])
            for hh in range(2):
                yb_, xb2, xo = HB[hh]
                pv_, pvk = bR(yb_); pxx, pxxk = bR(xb2)
                pv3 = pv_[:, :].rearrange("p (a b) -> p a b", a=2)
                tsl = slice(2 * hh, 2 * hh + 2)
                if not last:
                    op("act", lambda e, n_=n_, pv3=pv3, tsl=tsl: e.copy(out=YP[n_][:, tsl, 0:128], in_=pv3[:, :, 0:128]), reads=[pvk], writes=[hk("YP%da" % n_, hh)] + ([key("YP0a")] if n_ == 0 else []))
                op("dve", lambda e, c_=c_, n_=n_, pv3=pv3, tsl=tsl: e.tensor_tensor(out=YP[n_][:, tsl, 128:256], in0=YP[c_][:, tsl, 128:256], in1=pv3[:, :, 128:256], op=ALU.add),
                   reads=[pvk, hk("YP%db" % c_, hh)], writes=[hk("YP%db" % n_, hh)])
                op("act", lambda e, n_=n_, pxx=pxx, tsl=tsl, xo=xo: e.copy(out=XX[n_][:, tsl, :], in_=pxx[:, xo:xo + 256].rearrange("p (a b) -> p a b", a=2)),
                   reads=[pxxk], writes=[hk("XX%d" % n_, hh)])
            cur = n_
            yield
        c_ = cur
        for hh in range(2):
            yb_, xb2, xo = HB[hh]
            py, pyk = bR(yb_)
            tsl = slice(2 * hh, 2 * hh + 2)
            for tt in (2 * hh, 2 * hh + 1):
                o1 = (tt % 2) * 128
                op("pe", lambda e, tt=tt, py=py, o1=o1: e.matmul(out=py[:, o1:o1 + 128], lhsT=XX[c_][:, tt, :], rhs=YP[c_][:, tt, 128:256], start=True, stop=True),
                   reads=[hk("XX%d" % c_, hh), hk("YP%db" % c_, hh)], writes=[pyk])
            op("dve", lambda e, py=py, tsl=tsl: e.tensor_tensor(out=TTb[:, tsl, :], in0=YP[c_][:, tsl, 128:256], in1=py[:, 0:256].rearrange("p (a b) -> p a b", a=2), op=ALU.add),
               reads=[pyk, hk("YP%db" % c_, hh)], writes=[key("TTb") + "h%d" % hh])
        ttk = [key("TTb") + "h0", key("TTb") + "h1"]
        yield
        pu, puk = bF(1)
        pw_, pwk_ = bF(2)
        for tt in range(4):
            sl = slice(tt * 128, (tt + 1) * 128)
            op("pe", lambda e, tt=tt, sl=sl: e.matmul(out=pu[:, sl], lhsT=TTb[:, tt, :], rhs=vtok[:, tt, :], start=True, stop=True),
               reads=ttk + [key("vtok")], writes=[puk])
        for tt in range(4):
            sl = slice(tt * 128, (tt + 1) * 128)
            op("pe", lambda e, tt=tt, sl=sl: e.matmul(out=pw_[:, sl], lhsT=kg[:, tt, :], rhs=TTb[:, tt, :], start=True, stop=True),
               reads=[key("kg")] + ttk, writes=[pwk_])
        op("act", lambda e: e.copy(out=ub[:], in_=v4(pu[:, :])), reads=[puk], writes=[key("ub")])
        op("dve", lambda e: e.tensor_copy(out=wT[:], in_=v4(pw_[:, :])), reads=[pwk_], writes=[key("wT")])
        yield

    def gdn_chain(g, h, cx, cs):
        gi = g % 2; full = g >= G_FULL
        B = cs
        key = lambda n: "%s_%s" % (n, cs["id"])
        cbk = cx["cbanks"]
        C = {n: CT[n][:, gi] for n in CN}
        ck = lambda n: "%s_%d" % (n, gi)
        bF = lambda i: (bankF[i], "ps%d" % i)
        kd, ub, wT, ubf = B["kd"], B["ub"], B["wT"], B["ubf"]
        qkT, qdT, osb, og, sg = B.get("qkT"), B.get("qdT"), B.get("osb"), B.get("og"), B.get("sg")
        Sk, Sbk = "S%d" % h, "Sb%d" % h
        flip = 0
        for tt in range(4):
            t = 4 * g + tt
            store = full and t >= T0
            for hf in range(2):
                rows = slice(hf * 64, hf * 64 + 64)
                cdn = "cCD0" if hf == 0 else "cCD1"
                cd = C[cdn][:, tt, h:h + 1]
                pw, pwk = bF(cbk[flip % len(cbk)]); ps_, psk = bF(cbk[(flip + 1) % len(cbk)]); flip = 1 - flip
                op("pe", lambda e: e.matmul(out=pw[:, 0:128], lhsT=wT[:, tt, :], rhs=Sb[h][:], start=True, stop=True), reads=[key("wT"), Sbk], writes=[pwk])
                op("dve", lambda e: e.tensor_tensor(out=ubf[rows, :], in0=ub[rows, tt, :], in1=pw[rows, 0:128], op=ALU.subtract),
                   reads=[key("ub"), pwk], writes=[key("ubf")])
                if store:
                    op("pe", lambda e: e.matmul(out=pw[:, 128:256], lhsT=qdT[:, tt, :], rhs=Sb[h][:], start=True, stop=False), reads=[key("qdT"), Sbk, pwk], writes=[pwk])
                    op("pe", lambda e: e.matmul(out=pw[:, 128:256], lhsT=qkT[:, tt, :], rhs=ubf[:], start=False, stop=True),
                       reads=[key("qkT"), key("ubf"), pwk], writes=[pwk])
                    op("act", lambda e: e.copy(out=osb[rows, :], in_=pw[rows, 128:256]), reads=[pwk], writes=[key("osb")])
                yield
                op("pe", lambda e: e.matmul(out=ps_[:, 384:512], lhsT=kd[rows, tt, :], rhs=ubf[rows, :], start=True, stop=True), reads=[key("kd"), key("ubf")], writes=[psk])
                op("dve", lambda e: e.scalar_tensor_tensor(out=Sb[h][:], in0=Sst[h][:], scalar=cd, in1=ps_[:, 384:512], op0=ALU.mult, op1=ALU.add),
                   reads=[Sk, ck(cdn), psk], writes=[Sbk])
                op("dve", lambda e: e.scalar_tensor_tensor(out=Sst[h][:], in0=Sst[h][:], scalar=cd, in1=ps_[:, 384:512], op0=ALU.mult, op1=ALU.add),
                   reads=[Sk, ck(cdn), psk], writes=[Sk])
                yield
            if store:
                s_ = t - T0
                pg, pgk = bF(cbk[flip % len(cbk)])
                for k in range(8):
                    op("pe", lambda e, k=k: e.matmul(out=pg[:, 256:384], lhsT=nTg[gi][:, k, tt * 128:(tt + 1) * 128],
                                                    rhs=PH["Wgg"][:, k, h * 128:(h + 1) * 128], start=(k == 0), stop=(k == 7)),
                       reads=["nTg%d" % gi, "Wgg"], writes=[pgk])
                op("act", lambda e: e.activation(out=sg[:], in_=pg[:, 256:384], func=AF.Silu), reads=[pgk], writes=[key("sg")])
                sc = st[:, 4 + h:5 + h]; sck = "st%d" % (4 + h)
                rms_rstd(osb[:], [key("osb")], HD, sc, sck)
                op("dve", lambda e: e.scalar_tensor_tensor(out=og[:], in0=osb[:], scalar=sc, in1=GNW, op0=ALU.mult, op1=ALU.mult),
                   reads=[key("osb"), sck, "gsmbc"], writes=[key("og")])
                op("pool", lambda e: e.tensor_tensor(out=mix[:, s_, h * 128:(h + 1) * 128], in0=og[:], in1=sg[:], op=ALU.mult),
                   reads=[key("og"), key("sg")], writes=["mix%d" % h])
                yield

    def gdn_task(g, heads, cx, prefetch):
        gi = g % 2; full = g >= G_FULL
        prev = None
        if cx.get("prefetched") != g:
            for _ in gdn_proj(gi, heads[0], full, cx):
                yield
        for i, h in enumerate(heads):
            cs = cx["cs"][i % len(cx["cs"])]
            subs = [gdn_local(g, h, cx, cs)] + ([prev] if prev is not None else [])
            while subs:
                for sg_ in list(subs):
                    try:
                        r = next(sg_)
                        if r == "FRONT_DONE" and i + 1 < len(heads):
                            subs.append(gdn_proj(gi, heads[i + 1], full, cx))
                    except StopIteration:
                        subs.remove(sg_)
                yield
            prev = gdn_chain(g, h, cx, cs)
        subs = [prev]
        if prefetch:
            subs.append(gdn_proj((g + 1) % 2, heads[0], full, cx))
            cx["prefetched"] = g + 1
        while subs:
            for sg_ in list(subs):
                try:
                    next(sg_)
                except StopIteration:
                    subs.remove(sg_)
            yield

    def ret_tile(t, tt, gi, h, full, ci):
        nk = "nTg%d" % gi
        Rr = PH["RS"]; rbk = PH["rbank"]; rbk2 = PH["rbank2"]
        cs_t, sn_t = PH["cs_t"], PH["sn_t"]
        rk_ = lambda n: "%s_r" % n
        rsb, rt1, rt2, krot, kz, rvt = Rr["rsb"], Rr["rt1"], Rr["rt2"], Rr["krot"], Rr["kz"], Rr["rvt"]
        qrot, rkT, rqT, rqx, rPT, sgr, osb, og = [Rr.get(n) for n in ("qrot", "rkT", "rqT", "rqx", "rPT", "sgr", "osb", "og")]
        csk, snk = "cs_t%d" % ci, "sn_t%d" % ci
        store = full and t >= T0
        pf, fk = bankF[rbk], "ps%d" % rbk
        for k in range(8):
            op("pe", lambda e, k=k: e.matmul(out=pf[:, 0:256], lhsT=nTg[gi][:, k, tt * 128:(tt + 1) * 128], rhs=Wrkv[h][:, k, :],
                                            start=(k == 0), stop=(k == 7)), reads=[nk, "Wrkv%d" % h], writes=[fk])
        if store:
            for k in range(8):
                op("pe", lambda e, k=k: e.matmul(out=pf[:, 256:512], lhsT=nTg[gi][:, k, tt * 128:(tt + 1) * 128], rhs=PH["Wrqg"][h][:, k, :],
                                                start=(k == 0), stop=(k == 7)), reads=[nk, "Wrqg%d" % h, fk], writes=[fk])
        op("act", lambda e: e.copy(out=rsb[:, 0:128], in_=pf[:, 0:128]), reads=[fk], writes=[rk_("rsb")])
        op("act", lambda e: e.copy(out=rvt[:], in_=pf[:, 128:256]), reads=[fk], writes=[rk_("rvt")])
        if store:
            op("act", lambda e: e.copy(out=rsb[:, 128:256], in_=pf[:, 256:384]), reads=[fk, rk_("rsb")], writes=[rk_("rsb")])
        if store:
            op("act", lambda e: e.activation(out=sgr[:], in_=pf[:, 384:512], func=AF.Silu), reads=[fk], writes=[rk_("sgr")])
        yield

        def rotary(src, dst, dstk):
            op("dve", lambda e: e.tensor_tensor(out=rt1[:], in0=src, in1=cs_t[ci][:], op=ALU.mult), reads=[rk_("rsb"), csk], writes=[rk_("rt1")])
            op("dve", lambda e: e.tensor_tensor(out=rt2[:, 0:128:2], in0=src[:, 1:128:2], in1=sn_t[ci][:, 0:128:2], op=ALU.mult),
               reads=[rk_("rsb"), snk], writes=[rk_("rt2e")])
            op("dve", lambda e: e.tensor_tensor(out=rt2[:, 1:128:2], in0=src[:, 0:128:2], in1=sn_t[ci][:, 1:128:2], op=ALU.mult),
               reads=[rk_("rsb"), snk], writes=[rk_("rt2o")])
            op("pool", lambda e: e.tensor_tensor(out=dst[:], in0=rt1[:], in1=rt2[:], op=ALU.add),
               reads=[rk_("rt1"), rk_("rt2e"), rk_("rt2o")], writes=[dstk])
        rotary(rsb[:, 0:128], krot, rk_("krot"))
        op("act", lambda e: e.activation(out=kz[:], in_=krot[:], func=AF.Copy, scale=rv[:, h:h + 1]), reads=[rk_("krot"), "rv"], writes=[rk_("kz")])
        Rk, Rbk = "R%d" % h, "Rb%d" % h
        yield
        if store:
            rotary(rsb[:, 128:256], qrot, rk_("qrot"))
            pb, pk = bank[rbk2], "ps%d" % rbk2
            op("pe", lambda e: e.transpose(out=pb[:, 0:128], in_=krot[:], identity=identb[:]), reads=[rk_("krot"), "identb"], writes=[pk])
            op("pe", lambda e: e.transpose(out=pb[:, 128:256], in_=qrot[:], identity=identb[:]), reads=[rk_("qrot"), "identb"], writes=[pk])
            op("act", lambda e: e.copy(out=rkT[:], in_=pb[:, 0:128]), reads=[pk], writes=[rk_("rkT")])
            op("act", lambda e: e.copy(out=rqT[:], in_=pb[:, 128:256]), reads=[pk], writes=[rk_("rqT")])
            op("pool", lambda e: e.tensor_tensor(out=rqx[:], in0=rqT[:], in1=xibc[:, h, :], op=ALU.mult), reads=[rk_("rqT"), "xibc"], writes=[rk_("rqx")])
            yield
            psc, sck = bankF[rbk2], "ps%d" % rbk2
            op("pe", lambda e: e.matmul(out=psc[:, 0:128], lhsT=rkT[:], rhs=rqT[:], start=True, stop=True), reads=[rk_("rkT"), rk_("rqT")], writes=[sck])
            op("dve", lambda e: e.tensor_tensor(out=rPT[:], in0=psc[:, 0:128], in1=rm[:, h, :], op=ALU.mult), reads=[sck, "rm"], writes=[rk_("rPT")])
            po, pok = bankF[rbk], "ps%d" % rbk
            op("pe", lambda e: e.matmul(out=po[:, 0:128], lhsT=rPT[:], rhs=rvt[:], start=True, stop=False), reads=[rk_("rPT"), rk_("rvt")], writes=[pok])
            op("pe", lambda e: e.matmul(out=po[:, 0:128], lhsT=rqx[:], rhs=Rb[h][:], start=False, stop=True), reads=[rk_("rqx"), Rbk, pok], writes=[pok])
            s_ = t - T0
            op("act", lambda e: e.copy(out=osb[:], in_=po[:, 0:128]), reads=[pok], writes=[rk_("osb")])
            yield
            sc = st[:, 8 + h:9 + h]; sck2 = "st%d" % (8 + h)
            rms_rstd(osb[:], [rk_("osb")], HD, sc, sck2)
            op("dve", lambda e: e.tensor_scalar(out=og[:], in0=osb[:], scalar1=sc, scalar2=None, op0=ALU.mult), reads=[rk_("osb"), sck2], writes=[rk_("og")])
            op("pool", lambda e: e.tensor_tensor(out=mix[:, s_, 512 + h * 128:512 + (h + 1) * 128], in0=og[:], in1=sgr[:], op=ALU.mult),
               reads=[rk_("og"), rk_("sgr")], writes=["mix%d" % (4 + h)])
        if not store:
            yield
        pr, prk = bankF[rbk], "ps%d" % rbk
        op("pe", lambda e: e.matmul(out=pr[:, 0:128], lhsT=kz[:], rhs=rvt[:], start=True, stop=True), reads=[rk_("kz"), rk_("rvt")], writes=[prk])
        g128 = float(np.float64(1.0 - 2.0 ** (-5 - h)) ** 128)
        op("dve", lambda e: e.scalar_tensor_tensor(out=Rst[h][:], in0=Rst[h][:], scalar=g128, in1=pr[:, 0:128], op0=ALU.mult, op1=ALU.add),
           reads=[Rk, prk], writes=[Rk])
        op("act", lambda e: e.copy(out=Rb[h][:], in_=Rst[h][:]), reads=[Rk], writes=[Rbk])
        yield

    def prep_task(g):
        gi = g % 2
        for r in range(6):
            if 0 <= r - 2 < 4:
                norm_b(4 * g + r - 2, gi, r - 2)
            if r < 4:
                norm_a(4 * g + r)
            yield
        yield
        if "g" in parts:
            yield from gdn_common(gi, g >= G_FULL)

    def ret_task(g):
        gi = g % 2; full = g >= G_FULL
        for tt in range(4):
            t = 4 * g + tt
            ci = t % 2
            dma("sp", PH["cs_t"][ci][:], cosF[t * 128:(t + 1) * 128, :], writes=["cs_t%d" % ci])
            dma("sp", PH["sn_t"][ci][:], sinS[t * 128:(t + 1) * 128, :], writes=["sn_t%d" % ci])
            for h in range(4):
                yield from ret_tile(t, tt, gi, h, full, ci)


    def run_tasks(tasks, reps=None):
        tasks = list(tasks)
        reps = dict(reps or {})
        while tasks:
            for tk_ in list(tasks):
                for _ in range(reps.get(id(tk_), 1)):
                    try:
                        next(tk_)
                    except StopIteration:
                        tasks.remove(tk_)
                        break

    g0 = NG - ng_run
    g1 = NG
    if dev_g0 is not None:
        g0 = dev_g0; g1 = g0 + ng_run
    run_tasks([prep_task(g0)])
    cur_phase = [None]
    for g in range(g0, g1):
        full = g >= G_FULL
        if cur_phase[0] != full:
            if cur_phase[0] is not None:
                p.barrier()
                es_ph[0].close()
                es_ph[0] = ExitStack()
            alloc_phase(full)
            cur_phase[0] = full
        tasks = []
        if "g" in parts:
            pf_ok = (g + 1 < g1) and ((g + 1 >= G_FULL) == full)
            if full:
                tasks.append(gdn_task(g, [0, 1, 2, 3], PH["streams"][0], pf_ok))
            else:
                tasks.append(gdn_task(g, [0, 2], PH["streams"][0], pf_ok))
                tasks.append(gdn_task(g, [1, 3], PH["streams"][1], pf_ok))
        reps = {}
        if "r" in parts:
            rt_ = ret_task(g)
            tasks.append(rt_)
            reps[id(rt_)] = 1 if full else 2
        if g + 1 < g1:
            tasks.append(prep_task(g + 1))
        run_tasks(tasks, reps)

    def early(src):
        p.barrier()
        dma("sp", out[0:128, :], src, writes=["out0"])
        p.finish()
        return nc, p
    if phase == "pass":
        return early(xt[0][:])
    p.barrier()
    es_ph[0].close()
    es_keep.close()
    PHTAG[0] = "_X"
    hres = R("hres", [128, NSLOT, D], F32)
    n2T = R("n2T", [128, 8, 2 + OWN], BF16)
    es_b = ExitStack()
    Wout = L("Wout", [128, 8, D], BF16, es_b)
    mixT = L("mixT", [128, 8, 128], BF16, es_b)
    xt2 = [L("xt2%d" % i, [128, D], F32, es_b) for i in range(2)]
    junk = L("junk2", [128, D], F32, es_b)
    nb2 = L("nb2", [128, D], BF16, es_b)
    st = L("st2", [128, 4], F32, es_b)
    dma("pool", Wout[:], w_out.rearrange("(k p) c -> p k c", p=128), writes=["Wout"])
    nw2 = L("nw2", [128, D], F32, es_b)
    dma("sp", nw2[:], vecs[1:2, :].partition_broadcast(128), writes=["nw2"])
    for s_ in range(NSLOT):
        t = T0 + s_
        xb_ = xt2[s_ % 2]; xk = "xt2%d" % (s_ % 2)
        dma("sp", xb_[:], xpad[t * 128:(t + 1) * 128, :], writes=[xk])
        pb, pk = nB()
        for k in range(8):
            op("pe", lambda e, k=k: e.transpose(out=pb[:, k * 128:(k + 1) * 128], in_=mix[:, s_, k * 128:(k + 1) * 128], identity=identb[:]),
               reads=["mix%d" % k, "identb"], writes=[pk])
        op("act", lambda e: e.copy(out=mixT[:], in_=pb[:, :].rearrange("p (k n) -> p k n", k=8)), reads=[pk], writes=["mixT"])
        hk = "h%d" % s_
        for half in range(2):
            pf, fk = nF()
            for k in range(8):
                op("pe", lambda e, k=k: e.matmul(out=pf[:, :], lhsT=mixT[:, k, :], rhs=Wout[:, k, half * 512:(half + 1) * 512],
                                                start=(k == 0), stop=(k == 7)), reads=["mixT", "Wout"], writes=[fk])
            op("dve", lambda e: e.tensor_tensor(out=hres[:, s_, half * 512:(half + 1) * 512], in0=xb_[:, half * 512:(half + 1) * 512],
                                                in1=pf[:, :], op=ALU.add), reads=[xk, fk], writes=[hk + "_%d" % half])
        hks = [hk + "_0", hk + "_1"]
        rms_rstd(hres[:, s_, :], hks, D, st[:, 0:1], "st0")
        op("dve", lambda e: e.scalar_tensor_tensor(out=nb2[:], in0=hres[:, s_, :], scalar=st[:, 0:1], in1=nw2[:],
                                                   op0=ALU.mult, op1=ALU.mult), reads=hks + ["st0", "nw2"], writes=["nb2"])
        pb, pk = nB()
        for k in range(8):
            op("pe", lambda e, k=k: e.transpose(out=pb[:, k * 128:(k + 1) * 128], in_=nb2[:, k * 128:(k + 1) * 128], identity=identb[:]),
               reads=["nb2", "identb"], writes=[pk])
        pv = pb[:, :].rearrange("p (k n) -> p k n", k=8)
        if s_ == 0:
            op("act", lambda e: e.copy(out=n2T[:, :, 0:2], in_=pv[:, :, 126:128]), reads=[pk], writes=["n2T_h"])
        else:
            op("act", lambda e: e.copy(out=n2T[:, :, 2 + (s_ - 1) * 128:2 + s_ * 128], in_=pv), reads=[pk], writes=["n2T_%d" % ((s_ - 1) // 4)])

    if phase == "b3":
        return early(hres[:, 1, :])
    p.barrier()
    es_b.close()
    es_mix.close()
    es_c = ExitStack()
    NGRP = DFF // 256
    Wgu = [L("Wgu%d" % i, [128, 2, 8, 256], BF16, es_c) for i in range(2)]
    Wd = [L("Wd%d" % i, [128, 2, D], BF16, es_c) for i in range(2)]
    mcw = L("mcw", [128, 44, 3], F32, es_c)
    hb = [L("hb%d" % i, [128, 2 + OWN], F32, es_c) for i in range(2)]
    yb = [L("yb%d" % i, [128, OWN], F32, es_c) for i in range(2)]
    actT = [L("actT%d" % i, [128, 2, OWN], BF16, es_c) for i in range(2)]
    st = L("st3", [128, 4], F32, es_c)
    dma("sp", mcw[:], mcwT.rearrange("(c p) i -> p c i", p=128), writes=["mcw"])
    nw3 = L("nw3", [128, D], F32, es_c)
    dma("sp", nw3[:], vecs[2:3, :].partition_broadcast(128), writes=["nw3"])
    w_up_v = w_up.rearrange("(k p) c -> p k c", p=128)
    w_dn_v = w_down.rearrange("(c p) d -> p c d", p=128)

    def load_grp(gi_):
        b = gi_ % 2
        dma("pool", Wgu[b][:, 0, :, :], w_up_v[:, :, gi_ * 256:(gi_ + 1) * 256], writes=["Wgu%d" % b])
        dma("pool", Wgu[b][:, 1, :, :], w_up_v[:, :, DFF + gi_ * 256:DFF + (gi_ + 1) * 256], writes=["Wgu%d" % b])
        dma("pool", Wd[b][:], w_dn_v[:, gi_ * 2:gi_ * 2 + 2, :], writes=["Wd%d" % b])

    n2keys = ["n2T_h"] + ["n2T_%d" % i for i in range(4)]
    load_grp(0)
    for gi_ in range(NGRP):
        b = gi_ % 2
        if gi_ + 1 < NGRP:
            load_grp(gi_ + 1)
        ak = "actT%d" % b
        for fc in range(2):
            for which in range(2):
                hb_ = hb[which]; hbk = "hb%d" % which
                cidx = which * 22 + gi_ * 2 + fc
                lw = lambda k: Wgu[b][:, which, k, fc * 128:(fc + 1) * 128]
                pf, fk = nF()
                for k in range(8):
                    op("pe", lambda e, k=k: e.matmul(out=pf[:, 0:2], lhsT=lw(k), rhs=n2T[:, k, 0:2], start=(k == 0), stop=(k == 7)),
                       reads=["Wgu%d" % b, "n2T_h"], writes=[fk])
                op("act", lambda e: e.copy(out=hb_[:, 0:2], in_=pf[:, 0:2]), reads=[fk], writes=[hbk + "h"])
                for tb in range(4):
                    pf, fk = nF()
                    for k in range(8):
                        op("pe", lambda e, k=k: e.matmul(out=pf[:, :], lhsT=lw(k), rhs=n2T[:, k, 2 + tb * 512:2 + (tb + 1) * 512],
                                                        start=(k == 0), stop=(k == 7)), reads=["Wgu%d" % b, "n2T_%d" % tb], writes=[fk])
                    op("act", lambda e, tb=tb: e.copy(out=hb_[:, 2 + tb * 512:2 + (tb + 1) * 512], in_=pf[:, :]), reads=[fk], writes=[hbk + "_%d" % tb])
                hbks = [hbk + "h"] + [hbk + "_%d" % i for i in range(4)]
                ybk = "yb%d" % which
                op("act", lambda e: e.activation(out=yb[which][:], in_=hb_[:, 2:2 + OWN], func=AF.Copy, scale=mcw[:, cidx, 2:3]),
                   reads=hbks + ["mcw"], writes=[ybk])
                for i in (1, 0):
                    op("dve", lambda e, i=i: e.scalar_tensor_tensor(out=yb[which][:], in0=hb_[:, i:i + OWN], scalar=mcw[:, cidx, i:i + 1],
                                                                   in1=yb[which][:], op0=ALU.mult, op1=ALU.add), reads=hbks + ["mcw", ybk], writes=[ybk])
            op("act", lambda e: e.activation(out=yb[0][:], in_=yb[0][:], func=AF.Silu), reads=["yb0"], writes=["yb0"])
            op("pool", lambda e: e.tensor_tensor(out=actT[b][:, fc, :], in0=yb[0][:], in1=yb[1][:], op=ALU.mult),
               reads=["yb0", "yb1"], writes=[ak + "_%d" % fc])
        for tt in range(16):
            for half in range(2):
                pf, fk = nF()
                for fc in range(2):
                    op("pe", lambda e, fc=fc: e.matmul(out=pf[:, :], lhsT=actT[b][:, fc, tt * 128:(tt + 1) * 128],
                                                      rhs=Wd[b][:, fc, half * 512:(half + 1) * 512], start=(fc == 0), stop=(fc == 1)),
                       reads=[ak + "_0", ak + "_1", "Wd%d" % b], writes=[fk])
                hk = "h%d_%d" % (tt + 1, half)
                op("dve", lambda e: e.tensor_tensor(out=hres[:, tt + 1, half * 512:(half + 1) * 512],
                                                    in0=hres[:, tt + 1, half * 512:(half + 1) * 512], in1=pf[:, :], op=ALU.add),
                   reads=[fk, hk], writes=[hk])

    p.barrier()
    ob = [yb[i][:, 0:D] for i in range(2)]
    junk = hb[0]
    for tt in range(16):
        hks = ["h%d_0" % (tt + 1), "h%d_1" % (tt + 1)]
        rms_rstd(hres[:, tt + 1, :], hks, D, st[:, 0:1], "st0")
        o_ = ob[tt % 2]; ok = "ob%d" % (tt % 2)
        op("dve", lambda e: e.scalar_tensor_tensor(out=o_, in0=hres[:, tt + 1, :], scalar=st[:, 0:1], in1=nw3[:],
                                                   op0=ALU.mult, op1=ALU.mult), reads=hks + ["st0", "nw3"], writes=[ok])
        dma("sp", out[tt * 128:(tt + 1) * 128, :], o_, reads=[ok], writes=["out%d" % tt])
    p.finish()
    es_c.close()
    es_r.close()
    return nc, p


def _consts():
    idx = np.arange(128)
    same = (idx[:, None] // 64) == (idx[None, :] // 64)
    cm = np.zeros((10, 128, 128), np.float32)
    cm[0] = np.eye(128)
    cm[1] = 1.0
    cm[2] = (same & (idx[:, None] < idx[None, :]))
    cm[3] = (same & (idx[:, None] <= idx[None, :]))
    cm[4] = cm[3]
    cm[5] = (idx[:, None] < 64) * np.ones((1, 128))
    cm[6] = (idx[:, None] >= 64) * np.ones((1, 128))
    cm[7] = (1.0 - cm[2]) * -30000.0
    cm[8] = (1.0 - cm[3]) * -30000.0
    hh = np.arange(4, dtype=np.float64)
    gam = 1.0 - 2.0 ** (-5.0 - hh)
    lg = np.log(gam)
    rel = (idx[None, :] - idx[:, None]).astype(np.float64)
    rmat = np.where(rel[None] >= 0, np.exp(rel[None] * lg[:, None, None]), 0.0) * HD ** -0.5
    rvec = np.zeros((128, 8), np.float64)
    rvec[:, 0:4] = np.exp((127.0 - idx[:, None]) * lg[None, :]) * HD ** -0.5
    rxi = np.exp((idx[None, :] + 1.0) * lg[:, None])
    return cm, rmat.astype(np.float32), rvec.astype(np.float32), rxi.astype(np.float32)


_CACHE = {}


def kernel(x, attn_norm_w, w_in, gdn_conv_w, gdn_a_log, gdn_dt_bias, gdn_norm_w, w_out, mlp_norm_w,
           w_up, mlp_conv_w, w_down, final_norm_w):
    f = lambda a: np.ascontiguousarray(np.asarray(a, dtype=np.float32))
    x2 = f(x).reshape(S, D)
    if "nc" not in _CACHE:
        _CACHE["nc"] = build_program()[0]
    nc = _CACHE["nc"]
    cm, rmat, rvec, rxi = _consts()
    vecs = np.stack([f(attn_norm_w)[0], f(mlp_norm_w)[0], f(final_norm_w)], 0)
    gsm = np.concatenate([f(gdn_a_log)[0], f(gdn_dt_bias)[0], f(gdn_norm_w)[0]])[None, :]
    angle = (1.0 / (10000.0 ** np.linspace(0.0, 1.0, 64, dtype=np.float32))).astype(np.float32)
    angle = np.repeat(angle, 2)
    sign = np.tile(np.array([-1.0, 1.0], np.float32), 64)
    common = {
        "w_in": f(w_in)[0], "w_out": f(w_out)[0], "w_up": f(w_up)[0], "w_down": f(w_down)[0],
        "gcwT": np.ascontiguousarray(f(gdn_conv_w)[0].T), "mcwT": np.ascontiguousarray(f(mlp_conv_w)[0].T),
        "vecs": np.ascontiguousarray(vecs), "gsm": np.ascontiguousarray(gsm),
        "cmat": cm, "rmat": rmat, "rvec": rvec, "rxi": rxi,
    }
    in_maps = []
    for c in range(NCORES):
        n_real = OWN * (c + 1)
        xp = np.zeros((S, D), np.float32)
        xp[S - n_real:] = x2[:n_real]
        pos = (np.arange(S, dtype=np.int64) - (S - n_real)).astype(np.float32)
        phase = pos[:, None] * angle[None, :]
        m = dict(common)
        m["xpad"] = xp
        m["cosF"] = np.cos(phase).astype(np.float32)
        m["sinS"] = (np.sin(phase) * sign[None, :]).astype(np.float32)
        in_maps.append(m)
    res = run_bass_kernel_spmd(nc, in_maps, core_ids=list(range(NCORES)))
    outs = [np.asarray(res.results[c]["out"], dtype=np.float32) for c in range(NCORES)]
    return np.concatenate(outs, 0).reshape(1, S, D)
```

```python
import math
from contextlib import ExitStack
import numpy as np
import concourse.bass as bass
import concourse.mybir as mybir
from concourse.bass_utils import run_bass_kernel_spmd

F32 = mybir.dt.float32
BF16 = mybir.dt.bfloat16
ALU = mybir.AluOpType
AF = mybir.ActivationFunctionType

NCORES = 8
S = 16384
D = 1024
NT = S // 128
NG = NT // 4
OWN = 2048
T0 = NT - OWN // 128 - 1
NSLOT = NT - T0
G_FULL = T0 // 4
DFF = 2816
EPS = 1e-6
HD = 128
IN_COLS = 4104


class Prog:
    ENG = ("pe", "act", "dve", "pool", "sp")

    def __init__(self, nc, n_dma_sems=8):
        self.nc = nc
        self.eng = {"pe": nc.tensor, "act": nc.scalar, "dve": nc.vector,
                    "pool": nc.gpsimd, "sp": nc.sync}
        self.sem = {e: nc.alloc_semaphore("c_" + e) for e in self.ENG}
        self.cnt = {e: 0 for e in self.ENG}
        self.dsem = {e: [nc.alloc_semaphore("d_%s%d" % (e, i)) for i in range(n_dma_sems)]
                     for e in ("sp", "pool")}
        self.dval = {e: [0] * n_dma_sems for e in ("sp", "pool")}
        self.drr = {e: 0 for e in ("sp", "pool")}
        self.seen = {e: {} for e in self.ENG}
        self.lastw = {}
        self.readers = {}
        self.n_ins = 0
        self.n_wait = 0

    def _semof(self, tok):
        return self.sem[tok] if isinstance(tok, str) else self.dsem[tok[0]][tok[1]]

    def _wait(self, e, tok, val):
        if tok == e and e == "pe":
            return
        if self.seen[e].get(tok, 0) >= val:
            return
        self.eng[e].wait_ge(self._semof(tok), val)
        self.seen[e][tok] = val
        self.n_wait += 1

    def _deps(self, e, reads, writes):
        for k in reads:
            w = self.lastw.get(k)
            if w is not None:
                self._wait(e, *w)
        for k in writes:
            w = self.lastw.get(k)
            if w is not None:
                self._wait(e, *w)
            for r in self.readers.get(k, ()):
                self._wait(e, *r)

    def _commit(self, tokval, reads, writes):
        for k in writes:
            self.lastw[k] = tokval
            self.readers[k] = []
        for k in reads:
            lst = self.readers.setdefault(k, [])
            lst[:] = [r for r in lst if r[0] != tokval[0]]
            lst.append(tokval)

    def op(self, e, fn, reads=(), writes=()):
        ex = [k for k in reads if k.startswith("ps")]
        if ex:
            reads = [k for k in reads if not k.startswith("ps")]
            writes = list(writes) + [k for k in ex if k not in writes]
        self._deps(e, reads, writes)
        ins = fn(self.eng[e])
        self.cnt[e] += 1
        ins.then_inc(self.sem[e], 1)
        self._commit((e, self.cnt[e]), reads, writes)
        self.n_ins += 1
        return ins

    def dma(self, e, out, in_, reads=(), writes=(), **kw):
        i = self.drr[e]
        self.drr[e] = (i + 1) % len(self.dsem[e])
        tok = (e, i)
        if self.dval[e][i] > 0:
            self._wait(e, tok, self.dval[e][i])
        self._deps(e, reads, writes)
        ins = self.eng[e].dma_start(out=out, in_=in_, **kw)
        self.dval[e][i] += 16
        ins.then_inc(self.dsem[e][i], 16)
        self._commit((tok, self.dval[e][i]), reads, writes)
        self.n_ins += 1
        return ins

    def barrier(self):
        for e in self.ENG:
            for e2 in self.ENG:
                if e2 != e and self.cnt[e2] > 0:
                    self._wait(e, e2, self.cnt[e2])
            for q in self.dsem:
                for i, v in enumerate(self.dval[q]):
                    if v > 0:
                        self._wait(e, (q, i), v)
        self.lastw.clear()
        self.readers.clear()

    def finish(self):
        for q in self.dsem:
            for i, v in enumerate(self.dval[q]):
                if v > 0:
                    self._wait("sp", (q, i), v)


def build_program(ng_run=NG, phase="all", parts="gr", dev_g0=None):
    nc = bass.Bass("TRN2", target_bir_lowering=False)
    dt_in = lambda name, shape: nc.dram_tensor(name, list(shape), F32, kind="ExternalInput").ap()
    xpad = dt_in("xpad", [S, D])
    w_in = dt_in("w_in", [D, IN_COLS])
    w_out = dt_in("w_out", [D, D])
    w_up = dt_in("w_up", [D, 2 * DFF])
    w_down = dt_in("w_down", [DFF, D])
    gcwT = dt_in("gcwT", [1536, 4])
    mcwT = dt_in("mcwT", [2 * DFF, 3])
    vecs = dt_in("vecs", [3, D])
    gsm = dt_in("gsm", [1, 8 + 128])
    cmat = dt_in("cmat", [10, 128, 128])
    rmat = dt_in("rmat", [4, 128, 128])
    rvec = dt_in("rvec", [128, 8])
    rxi = dt_in("rxi", [4, 128])
    cosF = dt_in("cosF", [S, 128])
    sinS = dt_in("sinS", [S, 128])
    out = nc.dram_tensor("out", [OWN, D], F32, kind="ExternalOutput").ap()

    p = Prog(nc)
    SKIP = ""
    op = p.op
    def dma(q, o_, i_, reads=(), writes=(), tag="", **kw):
        if tag and tag in SKIP:
            return p.op("pool", lambda e: e.memset(o_, 1.0), reads=reads, writes=writes)
        return p.dma(q, o_, i_, reads=reads, writes=writes, **kw)

    bank = [nc.alloc_psum_tensor("bank%d" % i, [128, 1024], BF16) for i in range(8)]
    bankF = [bk_[:, :].bitcast(F32) for bk_ in bank]
    rr = {"B": 0, "F": 0}
    POOLS = {"B": [6, 7], "F": [0, 1, 2, 3, 4, 5]}

    def nB():
        lst = POOLS["B"]; i = lst[rr["B"] % len(lst)]; rr["B"] += 1
        return bank[i], "ps%d" % i

    def nF():
        lst = POOLS["F"]; i = lst[rr["F"] % len(lst)]; rr["F"] += 1
        return bankF[i], "ps%d" % i

    es_r = ExitStack()
    def R(name, shape, dt):
        return es_r.enter_context(nc.sbuf_tensor(name, list(shape), dt, side="right"))
    cm = R("cm", [128, 10, 128], F32)
    identb = R("identb", [128, 128], BF16)
    rm = R("rm", [128, 4, 128], F32)
    rv = R("rv", [128, 8], F32)
    xibc = R("xibc", [128, 4, 128], F32)
    nwbc = R("nwbc", [128, 1, D], F32)
    gsmbc = R("gsmbc", [128, 136], F32)
    gconst = R("gconst", [128, 16], F32)
    IDENT, ONES, MSU, MU, TRI, SC0, SC1 = [cm[:, i, :] for i in range(7)]
    dma("sp", cm[:], cmat.rearrange("m p n -> p m n"), writes=["cm"], tag="m")
    dma("sp", rm[:], rmat.rearrange("m p n -> p m n"), writes=["rm"], tag="m")
    dma("sp", rv[:], rvec, writes=["rv"])
    for h in range(4):
        dma("sp", xibc[:, h, :], rxi[h:h + 1, :].partition_broadcast(128), writes=["xibc"], tag="b")
    dma("sp", nwbc[:, 0, :], vecs[0:1, :].partition_broadcast(128), writes=["nwbc"], tag="b")
    dma("sp", gsmbc[:], gsm.partition_broadcast(128), writes=["gsmbc"], tag="b")
    op("dve", lambda e: e.tensor_copy(out=identb[:], in_=IDENT), reads=["cm"], writes=["identb"])
    op("act", lambda e: e.activation(out=gconst[:, 0:4], in_=gsmbc[:, 0:4], func=AF.Exp), reads=["gsmbc"], writes=["gconst"])
    op("dve", lambda e: e.tensor_scalar(out=gconst[:, 0:4], in0=gconst[:, 0:4], scalar1=-1.0, scalar2=None, op0=ALU.mult),
       reads=["gconst"], writes=["gconst"])
    op("dve", lambda e: e.tensor_copy(out=gconst[:, 4:8], in_=gsmbc[:, 4:8]), reads=["gsmbc", "gconst"], writes=["gconst"])
    GNW = gsmbc[:, 8:136]

    es_mix = ExitStack()
    mix = es_mix.enter_context(nc.sbuf_tensor("mix", [128, NSLOT, D], BF16, side="left"))
    es_keep = ExitStack()
    es_ph = [ExitStack()]
    PHTAG = [""]
    def L(name, shape, dt, es=None):
        if es is None:
            name = name + PHTAG[0]
        return (es or es_ph[0]).enter_context(nc.sbuf_tensor(name, list(shape), dt, side="left"))
    K = lambda name, shape, dt: L(name, shape, dt, es_keep)

    w_in_v = w_in.rearrange("(k p) c -> p k c", p=128)
    Wg = {}
    for h in range(4):
        for X in (1, 2):
            Wg[(h, X)] = K("Wg%d%d" % (h, X), [128, 8, 128], BF16)
            c0 = X * 512 + h * 128
            dma("pool", Wg[(h, X)][:], w_in_v[:, :, c0:c0 + 128], writes=["Wg%d%d" % (h, X)], tag="w")
    Wab = K("Wab", [128, 8, 8], BF16)
    dma("pool", Wab[:], w_in_v[:, :, 2048:2056], writes=["Wab"], tag="w")
    Wrkv = [K("Wrkv%d" % h, [128, 8, 256], BF16) for h in range(4)]
    for h in range(4):
        for j, X in enumerate((1, 2)):
            c0 = 2056 + X * 512 + h * 128
            dma("pool", Wrkv[h][:, :, j * 128:(j + 1) * 128], w_in_v[:, :, c0:c0 + 128], writes=["Wrkv%d" % h], tag="w")
    gcw = K("gcw", [128, 12, 4], F32)
    dma("sp", gcw[:], gcwT.rearrange("(c p) i -> p c i", p=128), writes=["gcw"], tag="c")
    xt = [K("xt%d" % i, [128, D], F32) for i in range(2)]
    junk = K("junk", [128, D], BF16)
    nb = [K("nb%d" % i, [128, D], BF16) for i in range(2)]
    nTg = [K("nTg%d" % i, [128, 8, 512], BF16) for i in range(2)]
    st = K("st", [128, 16], F32)
    ab = K("ab", [128, 2, 4, 8], F32)
    CN = ["cG", "cLB", "cB", "cGAM", "cGL0", "cGL1", "cEG", "cEKD", "cCD0", "cCD1", "cGLB", "cGC", "cBG", "cTMP"]
    CT = {n: K(n, [128, 2, 4, 4], F32) for n in CN}
    halo = [[K("halo%d%d" % (h, X), [128, 4], F32) for X in range(3)] for h in range(4)]
    Sst = [K("S%d" % h, [128, 128], F32) for h in range(4)]
    Sb = [K("Sb%d" % h, [128, 128], BF16) for h in range(4)]
    Rst = [K("R%d" % h, [128, 128], F32) for h in range(4)]
    Rb = [K("Rb%d" % h, [128, 128], BF16) for h in range(4)]
    for h in range(4):
        op("pool", lambda e, h=h: e.memset(Sst[h][:], 0.0), writes=["S%d" % h])
        op("pool", lambda e, h=h: e.memset(Sb[h][:], 0.0), writes=["Sb%d" % h])
        op("pool", lambda e, h=h: e.memset(Rst[h][:], 0.0), writes=["R%d" % h])
        op("pool", lambda e, h=h: e.memset(Rb[h][:], 0.0), writes=["Rb%d" % h])
        for X in range(3):
            op("pool", lambda e, h=h, X=X: e.memset(halo[h][X][:], 0.0), writes=["halo%d%d" % (h, X)])

    CHN = ("kd", "ub", "wT", "ubf", "qkT", "qdT", "osb", "og", "sg")
    PH = {}

    def alloc_stream(sid, full, nchain, lb, HB, cbanks):
        nE = 2 if full else 1
        cx = {"id": sid, "lb": lb, "HB": HB, "cbanks": cbanks}
        Ld = {}
        for n, shp, dt in (("lr", [128, 4, 2], F32), ("rkb", [128, 8, 4], F32), ("Dab", [128, nE, 4, 128], F32),
                           ("E", [128, nE, 4, 128], BF16), ("tmpm", [128, 4, 128], F32),
                           ("YP0", [128, 4, 256], BF16), ("YP1", [128, 4, 256], BF16),
                           ("XX0", [128, 4, 128], BF16), ("XX1", [128, 4, 128], BF16), ("kg", [128, 4, 128], BF16),
                           ("vtok", [128, 4, 128], BF16), ("TTb", [128, 4, 128], BF16), ("TbT", [128, 4, 128], BF16)):
            Ld[n] = L("%s_%s" % (n, sid), shp, dt)
        if full:
            Ld["Eq"] = L("Eq_%s" % sid, [128, 4, 128], BF16)
        cx["L"] = Ld
        cx["XT"] = [(L("XT%s%d" % (sid, X), [128, 512], BF16) if (full or X != 0) else None) for X in range(3)]
        cx["ybuf"] = [L("ybuf%s%d" % (sid, i), [128, 512], F32) for i in range(2)]
        cx["xw"] = [L("xw%s%d" % (sid, i), [128, 515], F32) for i in range(2)]
        cx["sq"] = L("sq%s" % sid, [128, 512], F32)
        cx["ssc"] = L("ssc%s" % sid, [128, 4, 2], F32)
        cx["cs"] = []
        for i in range(nchain):
            d = {"id": "%s%d" % (sid, i)}
            lst = [("kd", [128, 4, 128], BF16), ("ub", [128, 4, 128], F32), ("wT", [128, 4, 128], BF16), ("ubf", [128, 128], BF16)]
            if full:
                lst += [("qkT", [128, 4, 128], BF16), ("qdT", [128, 4, 128], BF16),
                        ("osb", [128, 128], F32), ("og", [128, 128], F32), ("sg", [128, 128], BF16)]
            for n, shp, dt in lst:
                d[n] = L("%s_%s" % (n, d["id"]), shp, dt)
            op("pool", lambda e, d=d: e.memset(d["ubf"][:], 0.0), writes=["ubf_%s" % d["id"]])
            cx["cs"].append(d)
        return cx

    def alloc_phase(full):
        PH.clear()
        PHTAG[0] = "_F" if full else "_S"
        Rr = {}
        lst = [("rsb", [128, 256], F32), ("rt1", [128, 128], F32), ("rt2", [128, 128], F32), ("krot", [128, 128], BF16),
               ("kz", [128, 128], BF16), ("rvt", [128, 128], BF16)]
        if full:
            lst += [("qrot", [128, 128], BF16), ("rkT", [128, 128], BF16), ("rqT", [128, 128], BF16), ("rqx", [128, 128], BF16),
                    ("rPT", [128, 128], BF16), ("sgr", [128, 128], BF16), ("osb", [128, 128], F32), ("og", [128, 128], F32)]
        for n, shp, dt in lst:
            Rr[n] = L("%s_r" % n, shp, dt)
        PH["RS"] = Rr
        PH["cs_t"] = [L("cs_t%d" % i, [128, 128], F32) for i in range(2)]
        PH["sn_t"] = [L("sn_t%d" % i, [128, 128], F32) for i in range(2)]
        if full:
            for h in range(4):
                Wg[(h, 0)] = L("Wg%d0" % h, [128, 8, 128], BF16)
                dma("pool", Wg[(h, 0)][:], w_in_v[:, :, h * 128:(h + 1) * 128], writes=["Wg%d0" % h], tag="w")
            PH["Wgg"] = L("Wgg", [128, 8, 512], BF16)
            dma("pool", PH["Wgg"][:], w_in_v[:, :, 1536:2048], writes=["Wgg"], tag="w")
            PH["Wrqg"] = [L("Wrqg%d" % h, [128, 8, 256], BF16) for h in range(4)]
            for h in range(4):
                for j, X in enumerate((0, 3)):
                    c0 = 2056 + X * 512 + h * 128
                    dma("pool", PH["Wrqg"][h][:, :, j * 128:(j + 1) * 128], w_in_v[:, :, c0:c0 + 128], writes=["Wrqg%d" % h], tag="w")
            PH["streams"] = [alloc_stream("A", True, 2, [0, 1, 2], [(0, 2, 0), (1, 2, 256)], [3, 4])]
            PH["rbank"] = 5; PH["rbank2"] = 6
        else:
            PH["streams"] = [alloc_stream("A", False, 2, [0, 1, 0], [(0, 1, 0), (0, 1, 256)], [4]),
                             alloc_stream("B", False, 2, [2, 3, 2], [(2, 3, 0), (2, 3, 256)], [5])]
            PH["rbank"] = 6; PH["rbank2"] = 6

    LN_DS = math.log(HD ** -0.5)

    def rms_rstd(src_ap, rkeys, n, col, ckey, jout=None, jkey="junk"):
        jo = junk[:, 0:n] if jout is None else jout
        op("act", lambda e: e.activation(out=jo, in_=src_ap, func=AF.Square, accum_out=col),
           reads=rkeys, writes=[jkey, ckey])
        op("act", lambda e: e.activation(out=col, in_=col, func=AF.Ln, bias=EPS, scale=1.0 / n),
           reads=[ckey], writes=[ckey])
        op("act", lambda e: e.activation(out=col, in_=col, func=AF.Exp, scale=-0.5), reads=[ckey], writes=[ckey])

    def norm_a(t):
        xb_ = xt[t % 2]; xk = "xt%d" % (t % 2); nb_ = nb[t % 2]; nbk = "nb%d" % (t % 2)
        dma("sp", xb_[:], xpad[t * 128:(t + 1) * 128, :], writes=[xk])
        rms_rstd(xb_[:], [xk], D, st[:, 0:1], "st0", jout=nb_[:], jkey=nbk)
        op("dve", lambda e: e.scalar_tensor_tensor(out=nb_[:], in0=xb_[:], scalar=st[:, 0:1], in1=nwbc[:, 0, :],
                                                   op0=ALU.mult, op1=ALU.mult), reads=[xk, "st0", "nwbc"], writes=[nbk])

    def norm_b(t, gi, tt):
        nb_ = nb[t % 2]; nbk = "nb%d" % (t % 2)
        pb, pk = bank[7], "ps7"
        for k in range(8):
            op("pe", lambda e, k=k: e.transpose(out=pb[:, k * 128:(k + 1) * 128], in_=nb_[:, k * 128:(k + 1) * 128],
                                               identity=identb[:]), reads=[nbk, "identb"], writes=[pk])
        nk = "nTg%d" % gi
        op("act", lambda e: e.copy(out=nTg[gi][:, :, tt * 128:(tt + 1) * 128],
                                   in_=pb[:, :].rearrange("p (k n) -> p k n", k=8)), reads=[pk], writes=[nk])

    def gdn_common(gi, full):
        nk = "nTg%d" % gi
        C = {n: CT[n][:, gi] for n in CN}
        ck = lambda n: "%s_%d" % (n, gi)
        abk = "ab%d" % gi
        for tt in range(4):
            pf, fk = bankF[7], "ps7"
            for k in range(8):
                op("pe", lambda e, k=k: e.matmul(out=pf[:, 0:8], lhsT=nTg[gi][:, k, tt * 128:(tt + 1) * 128],
                                                rhs=Wab[:, k, :], start=(k == 0), stop=(k == 7)),
                   reads=[nk, "Wab"], writes=[fk])
            op("dve", lambda e: e.tensor_copy(out=ab[:, gi, tt, :], in_=pf[:, 0:8]), reads=[fk], writes=[abk])
            yield
        for h in range(4):
            op("dve", lambda e, h=h: e.tensor_scalar(out=C["cTMP"][:, :, h], in0=ab[:, gi, :, h], scalar1=gconst[:, 4 + h:5 + h],
                                                    scalar2=None, op0=ALU.add), reads=[abk, "gconst"], writes=[ck("cTMP")])
        op("act", lambda e: e.activation(out=C["cTMP"], in_=C["cTMP"], func=AF.Exp), reads=[ck("cTMP")], writes=[ck("cTMP")])
        op("act", lambda e: e.activation(out=C["cTMP"], in_=C["cTMP"], func=AF.Ln, bias=1.0), reads=[ck("cTMP")], writes=[ck("cTMP")])
        for h in range(4):
            op("dve", lambda e, h=h: e.tensor_scalar(out=C["cG"][:, :, h], in0=C["cTMP"][:, :, h], scalar1=gconst[:, h:h + 1],
                                                    scalar2=None, op0=ALU.mult), reads=[ck("cTMP"), "gconst"], writes=[ck("cG")])
        op("act", lambda e: e.activation(out=C["cLB"], in_=ab[:, gi, :, 4:8], func=AF.Exp, scale=-1.0), reads=[abk], writes=[ck("cLB")])
        op("act", lambda e: e.activation(out=C["cLB"], in_=C["cLB"], func=AF.Ln, bias=1.0), reads=[ck("cLB")], writes=[ck("cLB")])
        op("act", lambda e: e.activation(out=C["cB"], in_=C["cLB"], func=AF.Exp, scale=-1.0), reads=[ck("cLB")], writes=[ck("cB")])
        op("dve", lambda e: e.tensor_scalar(out=C["cLB"], in0=C["cLB"], scalar1=-1.0, scalar2=None, op0=ALU.mult), reads=[ck("cLB"), ck("cB")], writes=[ck("cLB")])
        yield
        gflat = C["cG"].rearrange("p t h -> p (t h)")
        for lhs, dn in ((TRI, "cGAM"), (SC0, "cGL0"), (SC1, "cGL1")):
            pf, fk = bankF[7], "ps7"
            op("pe", lambda e, lhs=lhs: e.matmul(out=pf[:, 0:16], lhsT=lhs, rhs=gflat, start=True, stop=True),
               reads=["cm", ck("cG")], writes=[fk])
            op("dve", lambda e, dn=dn: e.tensor_copy(out=C[dn].rearrange("p t h -> p (t h)"), in_=pf[:, 0:16]),
               reads=[fk], writes=[ck(dn)])
        yield
        op("act", lambda e: e.activation(out=C["cEG"], in_=C["cGAM"], func=AF.Exp), reads=[ck("cGAM")], writes=[ck("cEG")])
        op("act", lambda e: e.activation(out=C["cCD0"], in_=C["cGL0"], func=AF.Exp), reads=[ck("cGL0")], writes=[ck("cCD0")])
        op("act", lambda e: e.activation(out=C["cCD1"], in_=C["cGL1"], func=AF.Exp), reads=[ck("cGL1")], writes=[ck("cCD1")])
        op("dve", lambda e: e.tensor_tensor(out=C["cEKD"][0:64], in0=C["cGL0"][0:64], in1=C["cGAM"][0:64], op=ALU.subtract),
           reads=[ck("cGL0"), ck("cGAM")], writes=[ck("cEKD")])
        op("dve", lambda e: e.tensor_tensor(out=C["cEKD"][64:128], in0=C["cGL1"][64:128], in1=C["cGAM"][64:128], op=ALU.subtract),
           reads=[ck("cGL1"), ck("cGAM"), ck("cEKD")], writes=[ck("cEKD")])
        op("act", lambda e: e.activation(out=C["cEKD"], in_=C["cEKD"], func=AF.Exp), reads=[ck("cEKD")], writes=[ck("cEKD")])
        op("dve", lambda e: e.tensor_tensor(out=C["cGLB"], in0=C["cGAM"], in1=C["cLB"], op=ALU.add), reads=[ck("cGAM"), ck("cLB")], writes=[ck("cGLB")])
        op("dve", lambda e: e.tensor_scalar(out=C["cGC"], in0=C["cGAM"], scalar1=LN_DS, scalar2=None, op0=ALU.add),
           reads=[ck("cGAM")], writes=[ck("cGC")])
        op("dve", lambda e: e.tensor_tensor(out=C["cBG"], in0=C["cB"], in1=C["cEG"], op=ALU.mult), reads=[ck("cB"), ck("cEG")], writes=[ck("cBG")])
        yield

    def gdn_proj(gi, h, full, cx):
        nk = "nTg%d" % gi
        sid = cx["id"]; lb = cx["lb"]
        sq_ = cx["sq"]; sqk = "sq%s" % sid
        Xs = (1, 2, 0) if full else (1, 2)

        def s1(i, X):
            xb_ = cx["xw"][i % 2]; xk = "xw%s%d" % (sid, i % 2)
            wk = "Wg%d%d" % (h, X); hk_ = "halo%d%d" % (h, X)
            op("pool", lambda e: e.tensor_copy(out=xb_[:, 0:3], in_=halo[h][X][:, 0:3]), reads=[hk_, xk], writes=[xk])
            pf, fk = bankF[lb[i % 3]], "ps%d" % lb[i % 3]
            for k in range(8):
                op("pe", lambda e, k=k: e.matmul(out=pf[:, :], lhsT=Wg[(h, X)][:, k, :], rhs=nTg[gi][:, k, :],
                                                start=(k == 0), stop=(k == 7)), reads=[wk, nk], writes=[fk])
            op("act", lambda e: e.copy(out=xb_[:, 3:515], in_=pf[:, :]), reads=[fk, xk], writes=[xk])

        def s2(i, X):
            xb_ = cx["xw"][i % 2]; xk = "xw%s%d" % (sid, i % 2)
            yb_ = cx["ybuf"][i % 2]; yk = "ybuf%s%d" % (sid, i % 2)
            tk = "XT%s%d" % (sid, X); hk_ = "halo%d%d" % (h, X)
            cw = lambda j: gcw[:, X * 4 + h, j:j + 1]
            op("dve", lambda e: e.tensor_scalar(out=yb_[:], in0=xb_[:, 3:515], scalar1=cw(3), scalar2=None, op0=ALU.mult),
               reads=[xk, "gcw"], writes=[yk])
            for j in (2, 1, 0):
                op("dve", lambda e, j=j: e.scalar_tensor_tensor(out=yb_[:], in0=xb_[:, j:j + 512], scalar=cw(j), in1=yb_[:],
                                                               op0=ALU.mult, op1=ALU.add), reads=[xk, "gcw", yk], writes=[yk])
            op("pool", lambda e: e.tensor_copy(out=halo[h][X][:, 0:3], in_=xb_[:, 512:515]), reads=[xk], writes=[hk_])
            op("act", lambda e: e.activation(out=cx["XT"][X][:], in_=yb_[:], func=AF.Silu), reads=[yk], writes=[tk])
            if X != 2:
                op("act", lambda e: e.activation(out=yb_[:], in_=yb_[:], func=AF.Silu), reads=[yk], writes=[yk])
                op("act", lambda e: e.activation(out=sq_[:], in_=yb_[:], func=AF.Square), reads=[yk], writes=[sqk])

        def s3(i, X):
            if X == 2:
                return
            j = 0 if X == 1 else 1
            pf2, fk2 = bankF[lb[(i + 1) % 3]], "ps%d" % lb[(i + 1) % 3]
            for tt in range(4):
                op("pe", lambda e, tt=tt: e.matmul(out=pf2[:, tt:tt + 1], lhsT=sq_[:, tt * 128:(tt + 1) * 128], rhs=cm[:, 1, 0:1], start=True, stop=True),
                   reads=[sqk, "cm"], writes=[fk2])
            op("dve", lambda e: e.tensor_copy(out=cx["ssc"][:, :, j], in_=pf2[:, 0:4]), reads=[fk2], writes=["ssc%s" % sid])

        n = len(Xs)
        for r in range(n + 3):
            if 0 <= r - 3 < n:
                s3(r - 3, Xs[r - 3])
            if 0 <= r - 1 < n:
                s2(r - 1, Xs[r - 1])
            if r < n:
                s1(r, Xs[r])
            yield

    def gdn_local(g, h, cx, cs):
        gi = g % 2; full = g >= G_FULL
        B = dict(cx["L"]); B.update(cs)
        sid = cx["id"]; lb = cx["lb"]
        key = lambda n: ("%s_%s" % (n, cs["id"])) if n in CHN else ("%s_%s" % (n, sid))
        C = {n: CT[n][:, gi] for n in CN}
        ck = lambda n: "%s_%d" % (n, gi)
        bF = lambda i: (bankF[lb[i]], "ps%d" % lb[i])
        bB = lambda i: (bank[lb[i]], "ps%d" % lb[i])
        kTa, vTa, qTa = cx["XT"][1], cx["XT"][2], cx["XT"][0]
        kTk, vTk, qTk = "XT%s1" % sid, "XT%s2" % sid, "XT%s0" % sid
        lr, rkb, Dab, E, tmpm = B["lr"], B["rkb"], B["Dab"], B["E"], B["tmpm"]
        YP = [B["YP0"], B["YP1"]]; XX = [B["XX0"], B["XX1"]]
        kg, kd, vtok, TTb, TbT, ub, wT = B["kg"], B["kd"], B["vtok"], B["TTb"], B["TbT"], B["ub"], B["wT"]
        Eq, qkT, qdT = B.get("Eq"), B.get("qkT"), B.get("qdT")
        hc = lambda n: C[n][:, :, h]
        bc4 = lambda ap2: ap2.unsqueeze(2).broadcast_to([128, 4, 128])
        rep4 = lambda ap2: ap2.unsqueeze(1).broadcast_to([128, 4, 128])
        v4 = lambda ap, w=128: ap.rearrange("p (a b) -> p a b", a=4)
        nc_ = 2 if full else 1
        sk = "ssc%s" % sid
        op("act", lambda e: e.activation(out=lr[:, :, 0:nc_], in_=cx["ssc"][:, :, 0:nc_], func=AF.Ln, bias=EPS), reads=[sk], writes=[key("lr")])
        op("act", lambda e: e.activation(out=rkb[:, 0, :], in_=lr[:, :, 0], func=AF.Exp, scale=-0.5), reads=[key("lr")], writes=[key("rk0")])
        op("dve", lambda e: e.scalar_tensor_tensor(out=rkb[:, 1, :], in0=lr[:, :, 0], scalar=-0.5, in1=hc("cGLB"), op0=ALU.mult, op1=ALU.add),
           reads=[key("lr"), ck("cGLB")], writes=[key("rk1")])
        op("dve", lambda e: e.tensor_tensor(out=rkb[:, 3, :], in0=rkb[:, 0, :], in1=hc("cBG"), op=ALU.mult), reads=[key("rk0"), ck("cBG")], writes=[key("rk3")])
        op("dve", lambda e: e.tensor_tensor(out=rkb[:, 4, :], in0=rkb[:, 0, :], in1=hc("cEKD"), op=ALU.mult), reads=[key("rk0"), ck("cEKD")], writes=[key("rk4")])
        op("dve", lambda e: e.scalar_tensor_tensor(out=rkb[:, 6, :], in0=lr[:, :, 0], scalar=-0.5, in1=hc("cGAM"), op0=ALU.mult, op1=ALU.subtract),
           reads=[key("lr"), ck("cGAM")], writes=[key("rk6")])
        if full:
            op("dve", lambda e: e.scalar_tensor_tensor(out=rkb[:, 2, :], in0=lr[:, :, 1], scalar=-0.5, in1=hc("cGC"), op0=ALU.mult, op1=ALU.add),
               reads=[key("lr"), ck("cGC")], writes=[key("rk2")])
        yield
        pb, pk = bB(0)
        for tt in range(4):
            op("pe", lambda e, tt=tt: e.transpose(out=pb[:, tt * 128:(tt + 1) * 128], in_=kTa[:, tt * 128:(tt + 1) * 128], identity=identb[:]),
               reads=[kTk, "identb"], writes=[pk])
        for tt in range(4):
            op("pe", lambda e, tt=tt: e.transpose(out=pb[:, 512 + tt * 128:512 + (tt + 1) * 128], in_=vTa[:, tt * 128:(tt + 1) * 128], identity=identb[:]),
               reads=[vTk, "identb"], writes=[pk])
        op("dve", lambda e: e.tensor_tensor(out=kg[:], in0=v4(pb[:, 0:512]), in1=bc4(rkb[:, 3, :]), op=ALU.mult), reads=[pk, key("rk3")], writes=[key("kg")])
        op("dve", lambda e: e.tensor_tensor(out=kd[:], in0=v4(pb[:, 0:512]), in1=bc4(rkb[:, 4, :]), op=ALU.mult), reads=[pk, key("rk4")], writes=[key("kd")])
        op("act", lambda e: e.copy(out=vtok[:], in_=v4(pb[:, 512:1024])), reads=[pk], writes=[key("vtok")])
        yield
        op("pool", lambda e: e.tensor_tensor(out=Dab[:, 0], in0=rep4(IDENT), in1=bc4(rkb[:, 1, :]), op=ALU.mult), reads=["cm", key("rk1")], writes=[key("Dab0")])
        pbc, bck = bF(1)
        op("pe", lambda e: e.matmul(out=pbc[:, :], lhsT=ONES, rhs=Dab[:, 0].rearrange("p a b -> p (a b)"), start=True, stop=True),
           reads=["cm", key("Dab0")], writes=[bck])
        if full:
            op("pool", lambda e: e.tensor_tensor(out=Dab[:, 1], in0=rep4(IDENT), in1=bc4(rkb[:, 2, :]), op=ALU.mult), reads=["cm", key("rk2")], writes=[key("Dab1")])
            pbc2, bck2 = bF(2)
            op("pe", lambda e: e.matmul(out=pbc2[:, :], lhsT=ONES, rhs=Dab[:, 1].rearrange("p a b -> p (a b)"), start=True, stop=True),
               reads=["cm", key("Dab1")], writes=[bck2])
        op("dve", lambda e: e.tensor_tensor(out=tmpm[:], in0=v4(pbc[:, :]), in1=rep4(cm[:, 7, :]), op=ALU.add), reads=[bck, "cm"],
           writes=[key("tmpm"), key("tmpm") + "h0", key("tmpm") + "h1"])
        for tt in range(4):
            op("act", lambda e, tt=tt: e.activation(out=E[:, 0, tt, :], in_=tmpm[:, tt, :], func=AF.Exp, bias=rkb[:, 6, tt:tt + 1]),
               reads=[key("tmpm"), key("rk6")], writes=[key("E0")])
        yield
        pkk, kkk = bF(0)
        for tt in range(4):
            sl = slice(tt * 128, (tt + 1) * 128)
            op("pe", lambda e, sl=sl: e.matmul(out=pkk[:, sl], lhsT=kTa[:, sl], rhs=kTa[:, sl], start=True, stop=True), reads=[kTk], writes=[kkk])
        Y0 = YP[0][:, :, 0:128]
        op("dve", lambda e: e.scalar_tensor_tensor(out=Y0, in0=v4(pkk[:, :]), scalar=-1.0, in1=E[:, 0], op0=ALU.mult, op1=ALU.mult),
           reads=[kkk, key("E0")], writes=[key("YP0a"), key("YP0a") + "h0", key("YP0a") + "h1"])
        if full:
            op("dve", lambda e: e.tensor_tensor(out=tmpm[:], in0=v4(pbc2[:, :]), in1=rep4(cm[:, 8, :]), op=ALU.add), reads=[bck2, "cm", key("tmpm")], writes=[key("tmpm")])
            op("act", lambda e: e.activation(out=Eq[:], in_=v4(pbc2[:, :]), func=AF.Exp), reads=[bck2], writes=[key("Eq")])
            for tt in range(4):
                op("act", lambda e, tt=tt: e.activation(out=E[:, 1, tt, :], in_=tmpm[:, tt, :], func=AF.Exp, bias=rkb[:, 6, tt:tt + 1]),
                   reads=[key("tmpm"), key("rk6")], writes=[key("E1")])
            pqk, qkk = bF(1)
            for tt in range(4):
                sl = slice(tt * 128, (tt + 1) * 128)
                op("pe", lambda e, sl=sl: e.matmul(out=pqk[:, sl], lhsT=kTa[:, sl], rhs=qTa[:, sl], start=True, stop=True), reads=[kTk, qTk], writes=[qkk])
            op("dve", lambda e: e.tensor_tensor(out=qkT[:], in0=v4(pqk[:, :]), in1=E[:, 1], op=ALU.mult), reads=[qkk, key("E1")], writes=[key("qkT")])
            op("pool", lambda e: e.tensor_tensor(out=qdT[:], in0=v4(qTa[:, :]), in1=Eq[:], op=ALU.mult), reads=[qTk, key("Eq")], writes=[key("qdT")])
        yield "FRONT_DONE"
        HB = cx["HB"]
        bR = lambda i: (bankF[i], "ps%d" % i)
        hk = lambda n, hh: key(n) + "h%d" % hh
        px, pxk = bB(2)
        for tt in range(4):
            op("pe", lambda e, tt=tt: e.transpose(out=px[:, tt * 128:(tt + 1) * 128], in_=YP[0][:, tt, 0:128], identity=identb[:]),
               reads=[key("YP0a"), "identb"], writes=[pxk])
        for hh in range(2):
            tsl = slice(2 * hh, 2 * hh + 2)
            op("act", lambda e, tsl=tsl, hh=hh: e.copy(out=XX[0][:, tsl, :], in_=v4(px[:, 0:512])[:, tsl, :]), reads=[pxk], writes=[hk("XX0", hh)])
            op("pool", lambda e, tsl=tsl: e.tensor_tensor(out=YP[1][:, tsl, 128:256], in0=YP[0][:, tsl, 0:128], in1=identb[:].unsqueeze(1).broadcast_to([128, 2, 128]), op=ALU.add),
               reads=[key("YP0a"), "identb"], writes=[hk("YP1b", hh)])
        yield
        for hh in range(2):
            yb_, xb2, xo = HB[hh]
            py, pyk = bR(yb_); pz, pzk = bR(xb2)
            for tt in (2 * hh, 2 * hh + 1):
                o0 = (tt % 2) * 128
                op("pe", lambda e, tt=tt, o0=o0, py=py: e.matmul(out=py[:, o0:o0 + 128], lhsT=XX[0][:, tt, :], rhs=YP[0][:, tt, 0:128], start=True, stop=True),
                   reads=[hk("XX0", hh), key("YP0a")], writes=[pyk])
                op("pe", lambda e, tt=tt, o0=o0, pz=pz, xo=xo: e.matmul(out=pz[:, xo + o0:xo + o0 + 128], lhsT=YP[0][:, tt, 0:128], rhs=XX[0][:, tt, :], start=True, stop=True),
                   reads=[hk("XX0", hh), key("YP0a")], writes=[pzk])
        for hh in range(2):
            yb_, xb2, xo = HB[hh]
            py, pyk = bR(yb_); pz, pzk = bR(xb2)
            tsl = slice(2 * hh, 2 * hh + 2)
            op("dve", lambda e, tsl=tsl, py=py: e.tensor_copy(out=YP[1][:, tsl, 0:128], in_=py[:, 0:256].rearrange("p (a b) -> p a b", a=2)),
               reads=[pyk], writes=[hk("YP1a", hh)])
            op("act", lambda e, tsl=tsl, pz=pz, xo=xo: e.copy(out=XX[1][:, tsl, :], in_=pz[:, xo:xo + 256].rearrange("p (a b) -> p a b", a=2)),
               reads=[pzk], writes=[hk("XX1", hh)])
        yield
        cur = 1
        for lvl in range(2, 6):
            c_, n_ = cur, 1 - cur
            last = (lvl == 5)
            for hh in range(2):
                yb_, xb2, xo = HB[hh]
                pv_, pvk = bR(yb_); pxx, pxxk = bR(xb2)
                for tt in (2 * hh, 2 * hh + 1):
                    o0 = (tt % 2) * 256
                    if not last:
                        op("pe", lambda e, tt=tt, c_=c_, pv_=pv_, o0=o0: e.matmul(out=pv_[:, o0:o0 + 256], lhsT=XX[c_][:, tt, :], rhs=YP[c_][:, tt, :], start=True, stop=True),
                           reads=[hk("XX%d" % c_, hh), hk("YP%da" % c_, hh), hk("YP%db" % c_, hh)], writes=[pvk])
                    else:
                        op("pe", lambda e, tt=tt, c_=c_, pv_=pv_, o0=o0: e.matmul(out=pv_[:, o0 + 128:o0 + 256], lhsT=XX[c_][:, tt, :], rhs=YP[c_][:, tt, 128:256], start=True, stop=True),
                           reads=[hk("XX%d" % c_, hh), hk("YP%db" % c_, hh)], writes=[pvk])
                for tt in (2 * hh, 2 * hh + 1):
                    o1 = (tt % 2) * 128
                    op("pe", lambda e, tt=tt, c_=c_, pxx=pxx, o1=o1, xo=xo: e.matmul(out=pxx[:, xo + o1:xo + o1 + 128], lhsT=YP[c_][:, tt, 0:128], rhs=XX[c_][:, tt, :], start=True, stop=True),
                       reads=[hk("XX%d" % c_, hh), hk("YP%da" % c_, hh)], writes=[pxxk])
            for hh in range(2):
                yb_, xb2, xo = HB[hh]
                pv_, pvk = bR(yb_); pxx, pxxk = bR(xb2)
                pv3 = pv_[:, :].rearrange("p (a b) -> p a b", a=2)
                tsl = slice(2 * hh, 2 * hh + 2)
                if not last:
                    op("act", lambda e, n_=n_, pv3=pv3, tsl=tsl: e.copy(out=YP[n_][:, tsl, 0:128], in_=pv3[:, :, 0:128]), reads=[pvk], writes=[hk("YP%da" % n_, hh)] + ([key("YP0a")] if n_ == 0 else []))
                op("dve", lambda e, c_=c_, n_=n_, pv3=pv3, tsl=tsl: e.tensor_tensor(out=YP[n_][:, tsl, 128:256], in0=YP[c_][:, tsl, 128:256], in1=pv3[:, :, 128:256], op=ALU.add),
                   reads=[pvk, hk("YP%db" % c_, hh)], writes=[hk("YP%db" % n_, hh)])
                op("act", lambda e, n_=n_, pxx=pxx, tsl=tsl, xo=xo: e.copy(out=XX[n_][:, tsl, :], in_=pxx[:, xo:xo + 256].rearrange("p (a b) -> p a b", a=2)),
                   reads=[pxxk], writes=[hk("XX%d" % n_, hh)])
            cur = n_
            yield
        c_ = cur
        for hh in range(2):
            yb_, xb2, xo = HB[hh]
            py, pyk = bR(yb_)
            tsl = slice(2 * hh, 2 * hh + 2)
            for tt in (2 * hh, 2 * hh + 1):
                o1 = (tt % 2) * 128
                op("pe", lambda e, tt=tt, py=py, o1=o1: e.matmul(out=py[:, o1:o1 + 128], lhsT=XX[c_][:, tt, :], rhs=YP[c_][:, tt, 128:256], start=True, stop=True),
                   reads=[hk("XX%d" % c_, hh), hk("YP%db" % c_, hh)], writes=[pyk])
            op("dve", lambda e, py=py, tsl=tsl: e.tensor_tensor(out=tmpm[:, tsl, :], in0=YP[c_][:, tsl, 128:256], in1=py[:, 0:256].rearrange("p (a b) -> p a b", a=2), op=ALU.add),
               reads=[pyk, hk("YP%db" % c_, hh), key("tmpm")], writes=[key("tmpm") + "h%d" % hh])
        tmk = [key("tmpm") + "h0", key("tmpm") + "h1"]
        op("act", lambda e: e.copy(out=TTb[:], in_=tmpm[:]), reads=tmk, writes=[key("TTb")])
        op("pool", lambda e: e.tensor_tensor(out=TbT[:], in0=tmpm[:], in1=bc4(hc("cB")), op=ALU.mult), reads=tmk + [ck("cB")], writes=[key("TbT")])
        yield
        pu, puk = bF(1)
        pw_, pwk_ = bF(2)
        for tt in range(4):
            sl = slice(tt * 128, (tt + 1) * 128)
            op("pe", lambda e, tt=tt, sl=sl: e.matmul(out=pu[:, sl], lhsT=TbT[:, tt, :], rhs=vtok[:, tt, :], start=True, stop=True),
               reads=[key("TbT"), key("vtok")], writes=[puk])
        for tt in range(4):
            sl = slice(tt * 128, (tt + 1) * 128)
            op("pe", lambda e, tt=tt, sl=sl: e.matmul(out=pw_[:, sl], lhsT=kg[:, tt, :], rhs=TTb[:, tt, :], start=True, stop=True),
               reads=[key("kg"), key("TTb")], writes=[pwk_])
        op("act", lambda e: e.copy(out=ub[:], in_=v4(pu[:, :])), reads=[puk], writes=[key("ub")])
        op("dve", lambda e: e.tensor_copy(out=wT[:], in_=v4(pw_[:, :])), reads=[pwk_], writes=[key("wT")])
        yield

    def gdn_chain(g, h, cx, cs):
        gi = g % 2; full = g >= G_FULL
        B = cs
        key = lambda n: "%s_%s" % (n, cs["id"])
        cbk = cx["cbanks"]
        C = {n: CT[n][:, gi] for n in CN}
        ck = lambda n: "%s_%d" % (n, gi)
        bF = lambda i: (bankF[i], "ps%d" % i)
        kd, ub, wT, ubf = B["kd"], B["ub"], B["wT"], B["ubf"]
        qkT, qdT, osb, og, sg = B.get("qkT"), B.get("qdT"), B.get("osb"), B.get("og"), B.get("sg")
        Sk, Sbk = "S%d" % h, "Sb%d" % h
        flip = 0
        for tt in range(4):
            t = 4 * g + tt
            store = full and t >= T0
            for hf in range(2):
                rows = slice(hf * 64, hf * 64 + 64)
                cdn = "cCD0" if hf == 0 else "cCD1"
                cd = C[cdn][:, tt, h:h + 1]
                pw, pwk = bF(cbk[flip % len(cbk)]); ps_, psk = bF(cbk[(flip + 1) % len(cbk)]); flip = 1 - flip
                op("pe", lambda e: e.matmul(out=pw[:, 0:128], lhsT=wT[:, tt, :], rhs=Sb[h][:], start=True, stop=True), reads=[key("wT"), Sbk], writes=[pwk])
                op("dve", lambda e: e.tensor_tensor(out=ubf[rows, :], in0=ub[rows, tt, :], in1=pw[rows, 0:128], op=ALU.subtract),
                   reads=[key("ub"), pwk], writes=[key("ubf")])
                if store:
                    op("pe", lambda e: e.matmul(out=pw[:, 128:256], lhsT=qdT[:, tt, :], rhs=Sb[h][:], start=True, stop=False), reads=[key("qdT"), Sbk, pwk], writes=[pwk])
                    op("pe", lambda e: e.matmul(out=pw[:, 128:256], lhsT=qkT[:, tt, :], rhs=ubf[:], start=False, stop=True),
                       reads=[key("qkT"), key("ubf"), pwk], writes=[pwk])
                    op("act", lambda e: e.copy(out=osb[rows, :], in_=pw[rows, 128:256]), reads=[pwk], writes=[key("osb")])
                yield
                op("pe", lambda e: e.matmul(out=ps_[:, 384:512], lhsT=kd[rows, tt, :], rhs=ubf[rows, :], start=True, stop=True), reads=[key("kd"), key("ubf")], writes=[psk])
                op("dve", lambda e: e.scalar_tensor_tensor(out=Sb[h][:], in0=Sst[h][:], scalar=cd, in1=ps_[:, 384:512], op0=ALU.mult, op1=ALU.add),
                   reads=[Sk, ck(cdn), psk], writes=[Sbk])
                op("dve", lambda e: e.scalar_tensor_tensor(out=Sst[h][:], in0=Sst[h][:], scalar=cd, in1=ps_[:, 384:512], op0=ALU.mult, op1=ALU.add),
                   reads=[Sk, ck(cdn), psk], writes=[Sk])
                yield
            if store:
                s_ = t - T0
                pg, pgk = bF(cbk[flip % len(cbk)])
                for k in range(8):
                    op("pe", lambda e, k=k: e.matmul(out=pg[:, 256:384], lhsT=nTg[gi][:, k, tt * 128:(tt + 1) * 128],
                                                    rhs=PH["Wgg"][:, k, h * 128:(h + 1) * 128], start=(k == 0), stop=(k == 7)),
                       reads=["nTg%d" % gi, "Wgg"], writes=[pgk])
                op("act", lambda e: e.activation(out=sg[:], in_=pg[:, 256:384], func=AF.Silu), reads=[pgk], writes=[key("sg")])
                sc = st[:, 4 + h:5 + h]; sck = "st%d" % (4 + h)
                rms_rstd(osb[:], [key("osb")], HD, sc, sck)
                op("dve", lambda e: e.scalar_tensor_tensor(out=og[:], in0=osb[:], scalar=sc, in1=GNW, op0=ALU.mult, op1=ALU.mult),
                   reads=[key("osb"), sck, "gsmbc"], writes=[key("og")])
                op("pool", lambda e: e.tensor_tensor(out=mix[:, s_, h * 128:(h + 1) * 128], in0=og[:], in1=sg[:], op=ALU.mult),
                   reads=[key("og"), key("sg")], writes=["mix%d" % h])
                yield

    def gdn_task(g, heads, cx, prefetch):
        gi = g % 2; full = g >= G_FULL
        prev = None
        if cx.get("prefetched") != g:
            for _ in gdn_proj(gi, heads[0], full, cx):
                yield
        for i, h in enumerate(heads):
            cs = cx["cs"][i % len(cx["cs"])]
            subs = [gdn_local(g, h, cx, cs)] + ([prev] if prev is not None else [])
            while subs:
                for sg_ in list(subs):
                    try:
                        r = next(sg_)
                        if r == "FRONT_DONE" and i + 1 < len(heads):
                            subs.append(gdn_proj(gi, heads[i + 1], full, cx))
                    except StopIteration:
                        subs.remove(sg_)
                yield
            prev = gdn_chain(g, h, cx, cs)
        subs = [prev]
        if prefetch:
            subs.append(gdn_proj((g + 1) % 2, heads[0], full, cx))
            cx["prefetched"] = g + 1
        while subs:
            for sg_ in list(subs):
                try:
                    next(sg_)
                except StopIteration:
                    subs.remove(sg_)
            yield

    def ret_tile(t, tt, gi, h, full, ci):
        nk = "nTg%d" % gi
        Rr = PH["RS"]; rbk = PH["rbank"]; rbk2 = PH["rbank2"]
        cs_t, sn_t = PH["cs_t"], PH["sn_t"]
        rk_ = lambda n: "%s_r" % n
        rsb, rt1, rt2, krot, kz, rvt = Rr["rsb"], Rr["rt1"], Rr["rt2"], Rr["krot"], Rr["kz"], Rr["rvt"]
        qrot, rkT, rqT, rqx, rPT, sgr, osb, og = [Rr.get(n) for n in ("qrot", "rkT", "rqT", "rqx", "rPT", "sgr", "osb", "og")]
        csk, snk = "cs_t%d" % ci, "sn_t%d" % ci
        store = full and t >= T0
        pf, fk = bankF[rbk], "ps%d" % rbk
        for k in range(8):
            op("pe", lambda e, k=k: e.matmul(out=pf[:, 0:256], lhsT=nTg[gi][:, k, tt * 128:(tt + 1) * 128], rhs=Wrkv[h][:, k, :],
                                            start=(k == 0), stop=(k == 7)), reads=[nk, "Wrkv%d" % h], writes=[fk])
        if store:
            for k in range(8):
                op("pe", lambda e, k=k: e.matmul(out=pf[:, 256:512], lhsT=nTg[gi][:, k, tt * 128:(tt + 1) * 128], rhs=PH["Wrqg"][h][:, k, :],
                                                start=(k == 0), stop=(k == 7)), reads=[nk, "Wrqg%d" % h, fk], writes=[fk])
        op("act", lambda e: e.copy(out=rsb[:, 0:128], in_=pf[:, 0:128]), reads=[fk], writes=[rk_("rsb")])
        op("act", lambda e: e.copy(out=rvt[:], in_=pf[:, 128:256]), reads=[fk], writes=[rk_("rvt")])
        if store:
            op("act", lambda e: e.copy(out=rsb[:, 128:256], in_=pf[:, 256:384]), reads=[fk, rk_("rsb")], writes=[rk_("rsb")])
        if store:
            op("act", lambda e: e.activation(out=sgr[:], in_=pf[:, 384:512], func=AF.Silu), reads=[fk], writes=[rk_("sgr")])
        yield

        def rotary(src, dst, dstk):
            op("dve", lambda e: e.tensor_tensor(out=rt1[:], in0=src, in1=cs_t[ci][:], op=ALU.mult), reads=[rk_("rsb"), csk], writes=[rk_("rt1")])
            op("dve", lambda e: e.tensor_tensor(out=rt2[:, 0:128:2], in0=src[:, 1:128:2], in1=sn_t[ci][:, 0:128:2], op=ALU.mult),
               reads=[rk_("rsb"), snk], writes=[rk_("rt2e")])
            op("dve", lambda e: e.tensor_tensor(out=rt2[:, 1:128:2], in0=src[:, 0:128:2], in1=sn_t[ci][:, 1:128:2], op=ALU.mult),
               reads=[rk_("rsb"), snk], writes=[rk_("rt2o")])
            op("pool", lambda e: e.tensor_tensor(out=dst[:], in0=rt1[:], in1=rt2[:], op=ALU.add),
               reads=[rk_("rt1"), rk_("rt2e"), rk_("rt2o")], writes=[dstk])
        rotary(rsb[:, 0:128], krot, rk_("krot"))
        op("act", lambda e: e.activation(out=kz[:], in_=krot[:], func=AF.Copy, scale=rv[:, h:h + 1]), reads=[rk_("krot"), "rv"], writes=[rk_("kz")])
        Rk, Rbk = "R%d" % h, "Rb%d" % h
        yield
        if store:
            rotary(rsb[:, 128:256], qrot, rk_("qrot"))
            pb, pk = bank[rbk2], "ps%d" % rbk2
            op("pe", lambda e: e.transpose(out=pb[:, 0:128], in_=krot[:], identity=identb[:]), reads=[rk_("krot"), "identb"], writes=[pk])
            op("pe", lambda e: e.transpose(out=pb[:, 128:256], in_=qrot[:], identity=identb[:]), reads=[rk_("qrot"), "identb"], writes=[pk])
            op("act", lambda e: e.copy(out=rkT[:], in_=pb[:, 0:128]), reads=[pk], writes=[rk_("rkT")])
            op("act", lambda e: e.copy(out=rqT[:], in_=pb[:, 128:256]), reads=[pk], writes=[rk_("rqT")])
            op("pool", lambda e: e.tensor_tensor(out=rqx[:], in0=rqT[:], in1=xibc[:, h, :], op=ALU.mult), reads=[rk_("rqT"), "xibc"], writes=[rk_("rqx")])
            yield
            psc, sck = bankF[rbk2], "ps%d" % rbk2
            op("pe", lambda e: e.matmul(out=psc[:, 0:128], lhsT=rkT[:], rhs=rqT[:], start=True, stop=True), reads=[rk_("rkT"), rk_("rqT")], writes=[sck])
            op("dve", lambda e: e.tensor_tensor(out=rPT[:], in0=psc[:, 0:128], in1=rm[:, h, :], op=ALU.mult), reads=[sck, "rm"], writes=[rk_("rPT")])
            po, pok = bankF[rbk], "ps%d" % rbk
            op("pe", lambda e: e.matmul(out=po[:, 0:128], lhsT=rPT[:], rhs=rvt[:], start=True, stop=False), reads=[rk_("rPT"), rk_("rvt")], writes=[pok])
            op("pe", lambda e: e.matmul(out=po[:, 0:128], lhsT=rqx[:], rhs=Rb[h][:], start=False, stop=True), reads=[rk_("rqx"), Rbk, pok], writes=[pok])
            s_ = t - T0
            op("act", lambda e: e.copy(out=osb[:], in_=po[:, 0:128]), reads=[pok], writes=[rk_("osb")])
            yield
            sc = st[:, 8 + h:9 + h]; sck2 = "st%d" % (8 + h)
            rms_rstd(osb[:], [rk_("osb")], HD, sc, sck2)
            op("dve", lambda e: e.tensor_scalar(out=og[:], in0=osb[:], scalar1=sc, scalar2=None, op0=ALU.mult), reads=[rk_("osb"), sck2], writes=[rk_("og")])
            op("pool", lambda e: e.tensor_tensor(out=mix[:, s_, 512 + h * 128:512 + (h + 1) * 128], in0=og[:], in1=sgr[:], op=ALU.mult),
               reads=[rk_("og"), rk_("sgr")], writes=["mix%d" % (4 + h)])
        if not store:
            yield
        pr, prk = bankF[rbk], "ps%d" % rbk
        op("pe", lambda e: e.matmul(out=pr[:, 0:128], lhsT=kz[:], rhs=rvt[:], start=True, stop=True), reads=[rk_("kz"), rk_("rvt")], writes=[prk])
        g128 = float(np.float64(1.0 - 2.0 ** (-5 - h)) ** 128)
        op("dve", lambda e: e.scalar_tensor_tensor(out=Rst[h][:], in0=Rst[h][:], scalar=g128, in1=pr[:, 0:128], op0=ALU.mult, op1=ALU.add),
           reads=[Rk, prk], writes=[Rk])
        op("act", lambda e: e.copy(out=Rb[h][:], in_=Rst[h][:]), reads=[Rk], writes=[Rbk])
        yield

    def prep_task(g):
        gi = g % 2
        for r in range(6):
            if 0 <= r - 2 < 4:
                norm_b(4 * g + r - 2, gi, r - 2)
            if r < 4:
                norm_a(4 * g + r)
            yield
        yield
        if "g" in parts:
            yield from gdn_common(gi, g >= G_FULL)

    def ret_task(g):
        gi = g % 2; full = g >= G_FULL
        for tt in range(4):
            t = 4 * g + tt
            ci = t % 2
            dma("sp", PH["cs_t"][ci][:], cosF[t * 128:(t + 1) * 128, :], writes=["cs_t%d" % ci])
            dma("sp", PH["sn_t"][ci][:], sinS[t * 128:(t + 1) * 128, :], writes=["sn_t%d" % ci])
            for h in range(4):
                yield from ret_tile(t, tt, gi, h, full, ci)


    def run_tasks(tasks, reps=None):
        tasks = list(tasks)
        reps = dict(reps or {})
        while tasks:
            for tk_ in list(tasks):
                for _ in range(reps.get(id(tk_), 1)):
                    try:
                        next(tk_)
                    except StopIteration:
                        tasks.remove(tk_)
                        break

    g0 = NG - ng_run
    g1 = NG
    if dev_g0 is not None:
        g0 = dev_g0; g1 = g0 + ng_run
    run_tasks([prep_task(g0)])
    cur_phase = [None]
    for g in range(g0, g1):
        full = g >= G_FULL
        if cur_phase[0] != full:
            if cur_phase[0] is not None:
                p.barrier()
                es_ph[0].close()
                es_ph[0] = ExitStack()
            alloc_phase(full)
            cur_phase[0] = full
        tasks = []
        if "g" in parts:
            pf_ok = (g + 1 < g1) and ((g + 1 >= G_FULL) == full)
            if full:
                tasks.append(gdn_task(g, [0, 1, 2, 3], PH["streams"][0], pf_ok))
            else:
                tasks.append(gdn_task(g, [0, 2], PH["streams"][0], pf_ok))
                tasks.append(gdn_task(g, [1, 3], PH["streams"][1], pf_ok))
        reps = {}
        if "r" in parts:
            rt_ = ret_task(g)
            tasks.append(rt_)
            reps[id(rt_)] = 1 if full else 2
        if g + 1 < g1:
            tasks.append(prep_task(g + 1))
        run_tasks(tasks, reps)

    def early(src):
        p.barrier()
        dma("sp", out[0:128, :], src, writes=["out0"])
        p.finish()
        return nc, p
    if phase == "pass":
        return early(xt[0][:])
    p.barrier()
    es_ph[0].close()
    es_keep.close()
    PHTAG[0] = "_X"
    hres = R("hres", [128, NSLOT, D], F32)
    n2T = R("n2T", [128, 8, 2 + OWN], BF16)
    es_b = ExitStack()
    Wout = L("Wout", [128, 8, D], BF16, es_b)
    mixT = L("mixT", [128, 8, 128], BF16, es_b)
    xt2 = [L("xt2%d" % i, [128, D], F32, es_b) for i in range(2)]
    junk = L("junk2", [128, D], F32, es_b)
    nb2 = L("nb2", [128, D], BF16, es_b)
    st = L("st2", [128, 4], F32, es_b)
    dma("pool", Wout[:], w_out.rearrange("(k p) c -> p k c", p=128), writes=["Wout"])
    nw2 = L("nw2", [128, D], F32, es_b)
    dma("sp", nw2[:], vecs[1:2, :].partition_broadcast(128), writes=["nw2"])
    for s_ in range(NSLOT):
        t = T0 + s_
        xb_ = xt2[s_ % 2]; xk = "xt2%d" % (s_ % 2)
        dma("sp", xb_[:], xpad[t * 128:(t + 1) * 128, :], writes=[xk])
        pb, pk = nB()
        for k in range(8):
            op("pe", lambda e, k=k: e.transpose(out=pb[:, k * 128:(k + 1) * 128], in_=mix[:, s_, k * 128:(k + 1) * 128], identity=identb[:]),
               reads=["mix%d" % k, "identb"], writes=[pk])
        op("act", lambda e: e.copy(out=mixT[:], in_=pb[:, :].rearrange("p (k n) -> p k n", k=8)), reads=[pk], writes=["mixT"])
        hk = "h%d" % s_
        for half in range(2):
            pf, fk = nF()
            for k in range(8):
                op("pe", lambda e, k=k: e.matmul(out=pf[:, :], lhsT=mixT[:, k, :], rhs=Wout[:, k, half * 512:(half + 1) * 512],
                                                start=(k == 0), stop=(k == 7)), reads=["mixT", "Wout"], writes=[fk])
            op("dve", lambda e: e.tensor_tensor(out=hres[:, s_, half * 512:(half + 1) * 512], in0=xb_[:, half * 512:(half + 1) * 512],
                                                in1=pf[:, :], op=ALU.add), reads=[xk, fk], writes=[hk + "_%d" % half])
        hks = [hk + "_0", hk + "_1"]
        rms_rstd(hres[:, s_, :], hks, D, st[:, 0:1], "st0")
        op("dve", lambda e: e.scalar_tensor_tensor(out=nb2[:], in0=hres[:, s_, :], scalar=st[:, 0:1], in1=nw2[:],
                                                   op0=ALU.mult, op1=ALU.mult), reads=hks + ["st0", "nw2"], writes=["nb2"])
        pb, pk = nB()
        for k in range(8):
            op("pe", lambda e, k=k: e.transpose(out=pb[:, k * 128:(k + 1) * 128], in_=nb2[:, k * 128:(k + 1) * 128], identity=identb[:]),
               reads=["nb2", "identb"], writes=[pk])
        pv = pb[:, :].rearrange("p (k n) -> p k n", k=8)
        if s_ == 0:
            op("act", lambda e: e.copy(out=n2T[:, :, 0:2], in_=pv[:, :, 126:128]), reads=[pk], writes=["n2T_h"])
        else:
            op("act", lambda e: e.copy(out=n2T[:, :, 2 + (s_ - 1) * 128:2 + s_ * 128], in_=pv), reads=[pk], writes=["n2T_%d" % ((s_ - 1) // 4)])

    if phase == "b3":
        return early(hres[:, 1, :])
    p.barrier()
    es_b.close()
    es_mix.close()
    es_c = ExitStack()
    NGRP = DFF // 256
    Wgu = [L("Wgu%d" % i, [128, 2, 8, 256], BF16, es_c) for i in range(2)]
    Wd = [L("Wd%d" % i, [128, 2, D], BF16, es_c) for i in range(2)]
    mcw = L("mcw", [128, 44, 3], F32, es_c)
    hb = [L("hb%d" % i, [128, 2 + OWN], F32, es_c) for i in range(2)]
    yb = [L("yb%d" % i, [128, OWN], F32, es_c) for i in range(2)]
    actT = [L("actT%d" % i, [128, 2, OWN], BF16, es_c) for i in range(2)]
    st = L("st3", [128, 4], F32, es_c)
    dma("sp", mcw[:], mcwT.rearrange("(c p) i -> p c i", p=128), writes=["mcw"])
    nw3 = L("nw3", [128, D], F32, es_c)
    dma("sp", nw3[:], vecs[2:3, :].partition_broadcast(128), writes=["nw3"])
    w_up_v = w_up.rearrange("(k p) c -> p k c", p=128)
    w_dn_v = w_down.rearrange("(c p) d -> p c d", p=128)

    def load_up(gi_):
        b = gi_ % 2
        dma("pool", Wgu[b][:, 0, :, :], w_up_v[:, :, gi_ * 256:(gi_ + 1) * 256], writes=["Wgu%d" % b])
        dma("pool", Wgu[b][:, 1, :, :], w_up_v[:, :, DFF + gi_ * 256:DFF + (gi_ + 1) * 256], writes=["Wgu%d" % b])

    def load_dn(gi_):
        b = gi_ % 2
        dma("pool", Wd[b][:], w_dn_v[:, gi_ * 2:gi_ * 2 + 2, :], writes=["Wd%d" % b])

    def ffn_up(gi_):
        b = gi_ % 2
        ak = "actT%d" % b
        for fc in range(2):
            for which in range(2):
                hb_ = hb[which]; hbk = "hb%d" % which
                cidx = which * 22 + gi_ * 2 + fc
                lw = lambda k: Wgu[b][:, which, k, fc * 128:(fc + 1) * 128]
                pf, fk = nF()
                for k in range(8):
                    op("pe", lambda e, k=k: e.matmul(out=pf[:, 0:2], lhsT=lw(k), rhs=n2T[:, k, 0:2], start=(k == 0), stop=(k == 7)),
                       reads=["Wgu%d" % b, "n2T_h"], writes=[fk])
                op("act", lambda e: e.copy(out=hb_[:, 0:2], in_=pf[:, 0:2]), reads=[fk], writes=[hbk + "h"])
                for tb in range(4):
                    pf, fk = nF()
                    for k in range(8):
                        op("pe", lambda e, k=k: e.matmul(out=pf[:, :], lhsT=lw(k), rhs=n2T[:, k, 2 + tb * 512:2 + (tb + 1) * 512],
                                                        start=(k == 0), stop=(k == 7)), reads=["Wgu%d" % b, "n2T_%d" % tb], writes=[fk])
                    op("act", lambda e, tb=tb: e.copy(out=hb_[:, 2 + tb * 512:2 + (tb + 1) * 512], in_=pf[:, :]), reads=[fk], writes=[hbk + "_%d" % tb])
                hbks = [hbk + "h"] + [hbk + "_%d" % i for i in range(4)]
                ybk = "yb%d" % which
                op("act", lambda e: e.activation(out=yb[which][:], in_=hb_[:, 2:2 + OWN], func=AF.Copy, scale=mcw[:, cidx, 2:3]),
                   reads=hbks + ["mcw"], writes=[ybk])
                for i in (1, 0):
                    op("dve", lambda e, i=i: e.scalar_tensor_tensor(out=yb[which][:], in0=hb_[:, i:i + OWN], scalar=mcw[:, cidx, i:i + 1],
                                                                   in1=yb[which][:], op0=ALU.mult, op1=ALU.add), reads=hbks + ["mcw", ybk], writes=[ybk])
            op("act", lambda e: e.activation(out=yb[0][:], in_=yb[0][:], func=AF.Silu), reads=["yb0"], writes=["yb0"])
            op("pool", lambda e: e.tensor_tensor(out=actT[b][:, fc, :], in0=yb[0][:], in1=yb[1][:], op=ALU.mult),
               reads=["yb0", "yb1"], writes=[ak + "_%d" % fc])

    def ffn_down(gi_):
        b = gi_ % 2
        ak = "actT%d" % b
        for tt in range(16):
            for half in range(2):
                pf, fk = nF()
                for fc in range(2):
                    op("pe", lambda e, fc=fc: e.matmul(out=pf[:, :], lhsT=actT[b][:, fc, tt * 128:(tt + 1) * 128],
                                                      rhs=Wd[b][:, fc, half * 512:(half + 1) * 512], start=(fc == 0), stop=(fc == 1)),
                       reads=[ak + "_0", ak + "_1", "Wd%d" % b], writes=[fk])
                hk = "h%d_%d" % (tt + 1, half)
                op("dve", lambda e: e.tensor_tensor(out=hres[:, tt + 1, half * 512:(half + 1) * 512],
                                                    in0=hres[:, tt + 1, half * 512:(half + 1) * 512], in1=pf[:, :], op=ALU.add),
                   reads=[fk, hk], writes=[hk])

    load_up(0); load_dn(0)
    for gi_ in range(NGRP + 1):
        if gi_ + 1 < NGRP:
            load_up(gi_ + 1)
        if gi_ < NGRP:
            ffn_up(gi_)
        if gi_ >= 1:
            ffn_down(gi_ - 1)
        if gi_ + 1 < NGRP:
            load_dn(gi_ + 1)

    p.barrier()
    ob = [yb[i][:, 0:D] for i in range(2)]
    junk = hb[0]
    for tt in range(16):
        hks = ["h%d_0" % (tt + 1), "h%d_1" % (tt + 1)]
        rms_rstd(hres[:, tt + 1, :], hks, D, st[:, 0:1], "st0")
        o_ = ob[tt % 2]; ok = "ob%d" % (tt % 2)
        op("dve", lambda e: e.scalar_tensor_tensor(out=o_, in0=hres[:, tt + 1, :], scalar=st[:, 0:1], in1=nw3[:],
                                                   op0=ALU.mult, op1=ALU.mult), reads=hks + ["st0", "nw3"], writes=[ok])
        dma("sp", out[tt * 128:(tt + 1) * 128, :], o_, reads=[ok], writes=["out%d" % tt])
    p.finish()
    es_c.close()
    es_r.close()
    return nc, p


def _consts():
    idx = np.arange(128)
    same = (idx[:, None] // 64) == (idx[None, :] // 64)
    cm = np.zeros((10, 128, 128), np.float32)
    cm[0] = np.eye(128)
    cm[1] = 1.0
    cm[2] = (same & (idx[:, None] < idx[None, :]))
    cm[3] = (same & (idx[:, None] <= idx[None, :]))
    cm[4] = cm[3]
    cm[5] = (idx[:, None] < 64) * np.ones((1, 128))
    cm[6] = (idx[:, None] >= 64) * np.ones((1, 128))
    cm[7] = (1.0 - cm[2]) * -30000.0
    cm[8] = (1.0 - cm[3]) * -30000.0
    hh = np.arange(4, dtype=np.float64)
    gam = 1.0 - 2.0 ** (-5.0 - hh)
    lg = np.log(gam)
    rel = (idx[None, :] - idx[:, None]).astype(np.float64)
    rmat = np.where(rel[None] >= 0, np.exp(rel[None] * lg[:, None, None]), 0.0) * HD ** -0.5
    rvec = np.zeros((128, 8), np.float64)
    rvec[:, 0:4] = np.exp((127.0 - idx[:, None]) * lg[None, :]) * HD ** -0.5
    rxi = np.exp((idx[None, :] + 1.0) * lg[:, None])
    return cm, rmat.astype(np.float32), rvec.astype(np.float32), rxi.astype(np.float32)


_CACHE = {}


def kernel(x, attn_norm_w, w_in, gdn_conv_w, gdn_a_log, gdn_dt_bias, gdn_norm_w, w_out, mlp_norm_w,
           w_up, mlp_conv_w, w_down, final_norm_w):
    f = lambda a: np.ascontiguousarray(np.asarray(a, dtype=np.float32))
    x2 = f(x).reshape(S, D)
    if "nc" not in _CACHE:
        _CACHE["nc"] = build_program()[0]
    nc = _CACHE["nc"]
    cm, rmat, rvec, rxi = _consts()
    vecs = np.stack([f(attn_norm_w)[0], f(mlp_norm_w)[0], f(final_norm_w)], 0)
    gsm = np.concatenate([f(gdn_a_log)[0], f(gdn_dt_bias)[0], f(gdn_norm_w)[0]])[None, :]
    angle = (1.0 / (10000.0 ** np.linspace(0.0, 1.0, 64, dtype=np.float32))).astype(np.float32)
    angle = np.repeat(angle, 2)
    sign = np.tile(np.array([-1.0, 1.0], np.float32), 64)
    common = {
        "w_in": f(w_in)[0], "w_out": f(w_out)[0], "w_up": f(w_up)[0], "w_down": f(w_down)[0],
        "gcwT": np.ascontiguousarray(f(gdn_conv_w)[0].T), "mcwT": np.ascontiguousarray(f(mlp_conv_w)[0].T),
        "vecs": np.ascontiguousarray(vecs), "gsm": np.ascontiguousarray(gsm),
        "cmat": cm, "rmat": rmat, "rvec": rvec, "rxi": rxi,
    }
    in_maps = []
    for c in range(NCORES):
        n_real = OWN * (c + 1)
        xp = np.zeros((S, D), np.float32)
        xp[S - n_real:] = x2[:n_real]
        pos = (np.arange(S, dtype=np.int64) - (S - n_real)).astype(np.float32)
        phase = pos[:, None] * angle[None, :]
        m = dict(common)
        m["xpad"] = xp
        m["cosF"] = np.cos(phase).astype(np.float32)
        m["sinS"] = (np.sin(phase) * sign[None, :]).astype(np.float32)
        in_maps.append(m)
    res = run_bass_kernel_spmd(nc, in_maps, core_ids=list(range(NCORES)))
    outs = [np.asarray(res.results[c]["out"], dtype=np.float32) for c in range(NCORES)]
    return np.concatenate(outs, 0).reshape(1, S, D)
```

```python
import math
from contextlib import ExitStack
import numpy as np
import concourse.bass as bass
import concourse.mybir as mybir
from concourse.bass_utils import run_bass_kernel_spmd

F32 = mybir.dt.float32
BF16 = mybir.dt.bfloat16
ALU = mybir.AluOpType
AF = mybir.ActivationFunctionType

NCORES = 8
S = 16384
D = 1024
NT = S // 128
NG = NT // 4
OWN = 2048
T0 = NT - OWN // 128 - 1
NSLOT = NT - T0
G_FULL = T0 // 4
DFF = 2816
EPS = 1e-6
HD = 128
IN_COLS = 4104


class Prog:
    ENG = ("pe", "act", "dve", "pool", "sp")

    def __init__(self, nc, n_dma_sems=8):
        self.nc = nc
        self.eng = {"pe": nc.tensor, "act": nc.scalar, "dve": nc.vector,
                    "pool": nc.gpsimd, "sp": nc.sync}
        self.sem = {e: nc.alloc_semaphore("c_" + e) for e in self.ENG}
        self.cnt = {e: 0 for e in self.ENG}
        self.dsem = {e: [nc.alloc_semaphore("d_%s%d" % (e, i)) for i in range(n_dma_sems)]
                     for e in ("sp", "pool")}
        self.dval = {e: [0] * n_dma_sems for e in ("sp", "pool")}
        self.drr = {e: 0 for e in ("sp", "pool")}
        self.seen = {e: {} for e in self.ENG}
        self.lastw = {}
        self.readers = {}
        self.n_ins = 0
        self.n_wait = 0

    def _semof(self, tok):
        return self.sem[tok] if isinstance(tok, str) else self.dsem[tok[0]][tok[1]]

    def _wait(self, e, tok, val):
        if tok == e and e == "pe":
            return
        if self.seen[e].get(tok, 0) >= val:
            return
        self.eng[e].wait_ge(self._semof(tok), val)
        self.seen[e][tok] = val
        self.n_wait += 1

    def _deps(self, e, reads, writes):
        for k in reads:
            w = self.lastw.get(k)
            if w is not None:
                self._wait(e, *w)
        for k in writes:
            w = self.lastw.get(k)
            if w is not None:
                self._wait(e, *w)
            for r in self.readers.get(k, ()):
                self._wait(e, *r)

    def _commit(self, tokval, reads, writes):
        for k in writes:
            self.lastw[k] = tokval
            self.readers[k] = []
        for k in reads:
            lst = self.readers.setdefault(k, [])
            lst[:] = [r for r in lst if r[0] != tokval[0]]
            lst.append(tokval)

    def op(self, e, fn, reads=(), writes=()):
        ex = [k for k in reads if k.startswith("ps")]
        if ex:
            reads = [k for k in reads if not k.startswith("ps")]
            writes = list(writes) + [k for k in ex if k not in writes]
        self._deps(e, reads, writes)
        ins = fn(self.eng[e])
        self.cnt[e] += 1
        ins.then_inc(self.sem[e], 1)
        self._commit((e, self.cnt[e]), reads, writes)
        self.n_ins += 1
        return ins

    def dma(self, e, out, in_, reads=(), writes=(), **kw):
        i = self.drr[e]
        self.drr[e] = (i + 1) % len(self.dsem[e])
        tok = (e, i)
        if self.dval[e][i] > 0:
            self._wait(e, tok, self.dval[e][i])
        self._deps(e, reads, writes)
        ins = self.eng[e].dma_start(out=out, in_=in_, **kw)
        self.dval[e][i] += 16
        ins.then_inc(self.dsem[e][i], 16)
        self._commit((tok, self.dval[e][i]), reads, writes)
        self.n_ins += 1
        return ins

    def barrier(self):
        for e in self.ENG:
            for e2 in self.ENG:
                if e2 != e and self.cnt[e2] > 0:
                    self._wait(e, e2, self.cnt[e2])
            for q in self.dsem:
                for i, v in enumerate(self.dval[q]):
                    if v > 0:
                        self._wait(e, (q, i), v)
        self.lastw.clear()
        self.readers.clear()

    def finish(self):
        for q in self.dsem:
            for i, v in enumerate(self.dval[q]):
                if v > 0:
                    self._wait("sp", (q, i), v)


def build_program(ng_run=NG, phase="all", parts="gr", dev_g0=None):
    nc = bass.Bass("TRN2", target_bir_lowering=False)
    dt_in = lambda name, shape: nc.dram_tensor(name, list(shape), F32, kind="ExternalInput").ap()
    xpad = dt_in("xpad", [S, D])
    w_in = dt_in("w_in", [D, IN_COLS])
    w_out = dt_in("w_out", [D, D])
    w_up = dt_in("w_up", [D, 2 * DFF])
    w_down = dt_in("w_down", [DFF, D])
    gcwT = dt_in("gcwT", [1536, 4])
    mcwT = dt_in("mcwT", [2 * DFF, 3])
    vecs = dt_in("vecs", [3, D])
    gsm = dt_in("gsm", [1, 8 + 128])
    cmat = dt_in("cmat", [10, 128, 128])
    rmat = dt_in("rmat", [4, 128, 128])
    rvec = dt_in("rvec", [128, 8])
    rxi = dt_in("rxi", [4, 128])
    cosF = dt_in("cosF", [S, 128])
    sinS = dt_in("sinS", [S, 128])
    out = nc.dram_tensor("out", [OWN, D], F32, kind="ExternalOutput").ap()

    p = Prog(nc)
    SKIP = ""
    op = p.op
    def dma(q, o_, i_, reads=(), writes=(), tag="", **kw):
        if tag and tag in SKIP:
            return p.op("pool", lambda e: e.memset(o_, 1.0), reads=reads, writes=writes)
        return p.dma(q, o_, i_, reads=reads, writes=writes, **kw)

    bank = [nc.alloc_psum_tensor("bank%d" % i, [128, 1024], BF16) for i in range(8)]
    bankF = [bk_[:, :].bitcast(F32) for bk_ in bank]
    rr = {"B": 0, "F": 0}
    POOLS = {"B": [6, 7], "F": [0, 1, 2, 3, 4, 5]}

    def nB():
        lst = POOLS["B"]; i = lst[rr["B"] % len(lst)]; rr["B"] += 1
        return bank[i], "ps%d" % i

    def nF():
        lst = POOLS["F"]; i = lst[rr["F"] % len(lst)]; rr["F"] += 1
        return bankF[i], "ps%d" % i

    es_r = ExitStack()
    def R(name, shape, dt):
        return es_r.enter_context(nc.sbuf_tensor(name, list(shape), dt, side="right"))
    cm = R("cm", [128, 10, 128], F32)
    identb = R("identb", [128, 128], BF16)
    rm = R("rm", [128, 4, 128], F32)
    rv = R("rv", [128, 8], F32)
    xibc = R("xibc", [128, 4, 128], F32)
    nwbc = R("nwbc", [128, 1, D], F32)
    gsmbc = R("gsmbc", [128, 136], F32)
    gconst = R("gconst", [128, 16], F32)
    IDENT, ONES, MSU, MU, TRI, SC0, SC1 = [cm[:, i, :] for i in range(7)]
    dma("sp", cm[:], cmat.rearrange("m p n -> p m n"), writes=["cm"], tag="m")
    dma("sp", rm[:], rmat.rearrange("m p n -> p m n"), writes=["rm"], tag="m")
    dma("sp", rv[:], rvec, writes=["rv"])
    for h in range(4):
        dma("sp", xibc[:, h, :], rxi[h:h + 1, :].partition_broadcast(128), writes=["xibc"], tag="b")
    dma("sp", nwbc[:, 0, :], vecs[0:1, :].partition_broadcast(128), writes=["nwbc"], tag="b")
    dma("sp", gsmbc[:], gsm.partition_broadcast(128), writes=["gsmbc"], tag="b")
    op("dve", lambda e: e.tensor_copy(out=identb[:], in_=IDENT), reads=["cm"], writes=["identb"])
    op("act", lambda e: e.activation(out=gconst[:, 0:4], in_=gsmbc[:, 0:4], func=AF.Exp), reads=["gsmbc"], writes=["gconst"])
    op("dve", lambda e: e.tensor_scalar(out=gconst[:, 0:4], in0=gconst[:, 0:4], scalar1=-1.0, scalar2=None, op0=ALU.mult),
       reads=["gconst"], writes=["gconst"])
    op("dve", lambda e: e.tensor_copy(out=gconst[:, 4:8], in_=gsmbc[:, 4:8]), reads=["gsmbc", "gconst"], writes=["gconst"])
    GNW = gsmbc[:, 8:136]

    es_mix = ExitStack()
    mix = es_mix.enter_context(nc.sbuf_tensor("mix", [128, NSLOT, D], BF16, side="left"))
    es_keep = ExitStack()
    es_ph = [ExitStack()]
    PHTAG = [""]
    def L(name, shape, dt, es=None):
        if es is None:
            name = name + PHTAG[0]
        return (es or es_ph[0]).enter_context(nc.sbuf_tensor(name, list(shape), dt, side="left"))
    K = lambda name, shape, dt: L(name, shape, dt, es_keep)

    w_in_v = w_in.rearrange("(k p) c -> p k c", p=128)
    Wg = {}
    for h in range(4):
        for X in (1, 2):
            Wg[(h, X)] = K("Wg%d%d" % (h, X), [128, 8, 128], BF16)
            c0 = X * 512 + h * 128
            dma("pool", Wg[(h, X)][:], w_in_v[:, :, c0:c0 + 128], writes=["Wg%d%d" % (h, X)], tag="w")
    Wab = K("Wab", [128, 8, 8], BF16)
    dma("pool", Wab[:], w_in_v[:, :, 2048:2056], writes=["Wab"], tag="w")
    Wrkv = [K("Wrkv%d" % h, [128, 8, 256], BF16) for h in range(4)]
    for h in range(4):
        for j, X in enumerate((1, 2)):
            c0 = 2056 + X * 512 + h * 128
            dma("pool", Wrkv[h][:, :, j * 128:(j + 1) * 128], w_in_v[:, :, c0:c0 + 128], writes=["Wrkv%d" % h], tag="w")
    gcw = K("gcw", [128, 12, 4], F32)
    dma("sp", gcw[:], gcwT.rearrange("(c p) i -> p c i", p=128), writes=["gcw"], tag="c")
    xt = [K("xt%d" % i, [128, D], F32) for i in range(2)]
    junk = K("junk", [128, D], BF16)
    nb = [K("nb%d" % i, [128, D], BF16) for i in range(2)]
    nTg = [K("nTg%d" % i, [128, 8, 512], BF16) for i in range(2)]
    st = K("st", [128, 16], F32)
    ab = K("ab", [128, 2, 4, 8], F32)
    CN = ["cG", "cLB", "cB", "cGAM", "cGL0", "cGL1", "cEG", "cEKD", "cCD0", "cCD1", "cGLB", "cGC", "cBG", "cTMP"]
    CT = {n: K(n, [128, 2, 4, 4], F32) for n in CN}
    halo = [[K("halo%d%d" % (h, X), [128, 4], F32) for X in range(3)] for h in range(4)]
    Sst = [K("S%d" % h, [128, 128], F32) for h in range(4)]
    Sb = [K("Sb%d" % h, [128, 128], BF16) for h in range(4)]
    Rst = [K("R%d" % h, [128, 128], F32) for h in range(4)]
    Rb = [K("Rb%d" % h, [128, 128], BF16) for h in range(4)]
    for h in range(4):
        op("pool", lambda e, h=h: e.memset(Sst[h][:], 0.0), writes=["S%d" % h])
        op("pool", lambda e, h=h: e.memset(Sb[h][:], 0.0), writes=["Sb%d" % h])
        op("pool", lambda e, h=h: e.memset(Rst[h][:], 0.0), writes=["R%d" % h])
        op("pool", lambda e, h=h: e.memset(Rb[h][:], 0.0), writes=["Rb%d" % h])
        for X in range(3):
            op("pool", lambda e, h=h, X=X: e.memset(halo[h][X][:], 0.0), writes=["halo%d%d" % (h, X)])

    CHN = ("kd", "ub", "wT", "ubf", "qkT", "qdT", "osb", "og", "sg")
    PH = {}

    def alloc_stream(sid, full, nchain, lb, HB, cbanks):
        nE = 2 if full else 1
        cx = {"id": sid, "lb": lb, "HB": HB, "cbanks": cbanks}
        Ld = {}
        for n, shp, dt in (("lr", [128, 4, 2], F32), ("rkb", [128, 8, 4], F32), ("Dab", [128, nE, 4, 128], F32),
                           ("E", [128, nE, 4, 128], BF16), ("tmpm", [128, 4, 128], F32),
                           ("YP0", [128, 4, 256], BF16), ("YP1", [128, 4, 256], BF16),
                           ("XX0", [128, 4, 128], BF16), ("XX1", [128, 4, 128], BF16), ("kg", [128, 4, 128], BF16),
                           ("vtok", [128, 4, 128], BF16), ("TTb", [128, 4, 128], BF16), ("TbT", [128, 4, 128], BF16)):
            Ld[n] = L("%s_%s" % (n, sid), shp, dt)
        if full:
            Ld["Eq"] = L("Eq_%s" % sid, [128, 4, 128], BF16)
        cx["L"] = Ld
        cx["XT"] = [(L("XT%s%d" % (sid, X), [128, 512], BF16) if (full or X != 0) else None) for X in range(3)]
        cx["ybuf"] = [L("ybuf%s%d" % (sid, i), [128, 512], F32) for i in range(2)]
        cx["xw"] = [L("xw%s%d" % (sid, i), [128, 515], F32) for i in range(2)]
        cx["sq"] = L("sq%s" % sid, [128, 512], F32)
        cx["ssc"] = L("ssc%s" % sid, [128, 4, 2], F32)
        cx["cs"] = []
        for i in range(nchain):
            d = {"id": "%s%d" % (sid, i)}
            lst = [("kd", [128, 4, 128], BF16), ("ub", [128, 4, 128], F32), ("wT", [128, 4, 128], BF16), ("ubf", [128, 128], BF16)]
            if full:
                lst += [("qkT", [128, 4, 128], BF16), ("qdT", [128, 4, 128], BF16),
                        ("osb", [128, 128], F32), ("og", [128, 128], F32), ("sg", [128, 128], BF16)]
            for n, shp, dt in lst:
                d[n] = L("%s_%s" % (n, d["id"]), shp, dt)
            op("pool", lambda e, d=d: e.memset(d["ubf"][:], 0.0), writes=["ubf_%s" % d["id"]])
            cx["cs"].append(d)
        return cx

    def alloc_phase(full):
        PH.clear()
        PHTAG[0] = "_F" if full else "_S"
        Rr = {}
        lst = [("rsb", [128, 256], F32), ("rt1", [128, 128], F32), ("rt2", [128, 128], F32), ("krot", [128, 128], BF16),
               ("kz", [128, 128], BF16), ("rvt", [128, 128], BF16)]
        if full:
            lst += [("qrot", [128, 128], BF16), ("rkT", [128, 128], BF16), ("rqT", [128, 128], BF16), ("rqx", [128, 128], BF16),
                    ("rPT", [128, 128], BF16), ("sgr", [128, 128], BF16), ("osb", [128, 128], F32), ("og", [128, 128], F32)]
        for n, shp, dt in lst:
            Rr[n] = L("%s_r" % n, shp, dt)
        PH["RS"] = Rr
        PH["cs_t"] = [L("cs_t%d" % i, [128, 128], F32) for i in range(2)]
        PH["sn_t"] = [L("sn_t%d" % i, [128, 128], F32) for i in range(2)]
        if full:
            for h in range(4):
                Wg[(h, 0)] = L("Wg%d0" % h, [128, 8, 128], BF16)
                dma("pool", Wg[(h, 0)][:], w_in_v[:, :, h * 128:(h + 1) * 128], writes=["Wg%d0" % h], tag="w")
            PH["Wgg"] = L("Wgg", [128, 8, 512], BF16)
            dma("pool", PH["Wgg"][:], w_in_v[:, :, 1536:2048], writes=["Wgg"], tag="w")
            PH["Wrqg"] = [L("Wrqg%d" % h, [128, 8, 256], BF16) for h in range(4)]
            for h in range(4):
                for j, X in enumerate((0, 3)):
                    c0 = 2056 + X * 512 + h * 128
                    dma("pool", PH["Wrqg"][h][:, :, j * 128:(j + 1) * 128], w_in_v[:, :, c0:c0 + 128], writes=["Wrqg%d" % h], tag="w")
            PH["streams"] = [alloc_stream("A", True, 2, [0, 1, 2], [(0, 2, 0), (1, 2, 256)], [3, 4])]
            PH["rbank"] = 5; PH["rbank2"] = 6
        else:
            PH["streams"] = [alloc_stream("A", False, 2, [0, 1, 0], [(0, 1, 0), (0, 1, 256)], [4]),
                             alloc_stream("B", False, 2, [2, 3, 2], [(2, 3, 0), (2, 3, 256)], [5])]
            PH["rbank"] = 6; PH["rbank2"] = 6

    LN_DS = math.log(HD ** -0.5)

    def rms_rstd(src_ap, rkeys, n, col, ckey, jout=None, jkey="junk"):
        jo = junk[:, 0:n] if jout is None else jout
        op("act", lambda e: e.activation(out=jo, in_=src_ap, func=AF.Square, accum_out=col),
           reads=rkeys, writes=[jkey, ckey])
        op("act", lambda e: e.activation(out=col, in_=col, func=AF.Ln, bias=EPS, scale=1.0 / n),
           reads=[ckey], writes=[ckey])
        op("act", lambda e: e.activation(out=col, in_=col, func=AF.Exp, scale=-0.5), reads=[ckey], writes=[ckey])

    def norm_a(t):
        xb_ = xt[t % 2]; xk = "xt%d" % (t % 2); nb_ = nb[t % 2]; nbk = "nb%d" % (t % 2)
        dma("sp", xb_[:], xpad[t * 128:(t + 1) * 128, :], writes=[xk])
        rms_rstd(xb_[:], [xk], D, st[:, 0:1], "st0", jout=nb_[:], jkey=nbk)
        op("dve", lambda e: e.scalar_tensor_tensor(out=nb_[:], in0=xb_[:], scalar=st[:, 0:1], in1=nwbc[:, 0, :],
                                                   op0=ALU.mult, op1=ALU.mult), reads=[xk, "st0", "nwbc"], writes=[nbk])

    def norm_b(t, gi, tt):
        nb_ = nb[t % 2]; nbk = "nb%d" % (t % 2)
        pb, pk = bank[7], "ps7"
        for k in range(8):
            op("pe", lambda e, k=k: e.transpose(out=pb[:, k * 128:(k + 1) * 128], in_=nb_[:, k * 128:(k + 1) * 128],
                                               identity=identb[:]), reads=[nbk, "identb"], writes=[pk])
        nk = "nTg%d" % gi
        op("act", lambda e: e.copy(out=nTg[gi][:, :, tt * 128:(tt + 1) * 128],
                                   in_=pb[:, :].rearrange("p (k n) -> p k n", k=8)), reads=[pk], writes=[nk])

    def gdn_common(gi, full):
        nk = "nTg%d" % gi
        C = {n: CT[n][:, gi] for n in CN}
        ck = lambda n: "%s_%d" % (n, gi)
        abk = "ab%d" % gi
        for tt in range(4):
            pf, fk = bankF[7], "ps7"
            for k in range(8):
                op("pe", lambda e, k=k: e.matmul(out=pf[:, 0:8], lhsT=nTg[gi][:, k, tt * 128:(tt + 1) * 128],
                                                rhs=Wab[:, k, :], start=(k == 0), stop=(k == 7)),
                   reads=[nk, "Wab"], writes=[fk])
            op("dve", lambda e: e.tensor_copy(out=ab[:, gi, tt, :], in_=pf[:, 0:8]), reads=[fk], writes=[abk])
            yield
        for h in range(4):
            op("dve", lambda e, h=h: e.tensor_scalar(out=C["cTMP"][:, :, h], in0=ab[:, gi, :, h], scalar1=gconst[:, 4 + h:5 + h],
                                                    scalar2=None, op0=ALU.add), reads=[abk, "gconst"], writes=[ck("cTMP")])
        op("act", lambda e: e.activation(out=C["cTMP"], in_=C["cTMP"], func=AF.Exp), reads=[ck("cTMP")], writes=[ck("cTMP")])
        op("act", lambda e: e.activation(out=C["cTMP"], in_=C["cTMP"], func=AF.Ln, bias=1.0), reads=[ck("cTMP")], writes=[ck("cTMP")])
        for h in range(4):
            op("dve", lambda e, h=h: e.tensor_scalar(out=C["cG"][:, :, h], in0=C["cTMP"][:, :, h], scalar1=gconst[:, h:h + 1],
                                                    scalar2=None, op0=ALU.mult), reads=[ck("cTMP"), "gconst"], writes=[ck("cG")])
        op("act", lambda e: e.activation(out=C["cLB"], in_=ab[:, gi, :, 4:8], func=AF.Exp, scale=-1.0), reads=[abk], writes=[ck("cLB")])
        op("act", lambda e: e.activation(out=C["cLB"], in_=C["cLB"], func=AF.Ln, bias=1.0), reads=[ck("cLB")], writes=[ck("cLB")])
        op("act", lambda e: e.activation(out=C["cB"], in_=C["cLB"], func=AF.Exp, scale=-1.0), reads=[ck("cLB")], writes=[ck("cB")])
        op("dve", lambda e: e.tensor_scalar(out=C["cLB"], in0=C["cLB"], scalar1=-1.0, scalar2=None, op0=ALU.mult), reads=[ck("cLB"), ck("cB")], writes=[ck("cLB")])
        yield
        gflat = C["cG"].rearrange("p t h -> p (t h)")
        for lhs, dn in ((TRI, "cGAM"), (SC0, "cGL0"), (SC1, "cGL1")):
            pf, fk = bankF[7], "ps7"
            op("pe", lambda e, lhs=lhs: e.matmul(out=pf[:, 0:16], lhsT=lhs, rhs=gflat, start=True, stop=True),
               reads=["cm", ck("cG")], writes=[fk])
            op("dve", lambda e, dn=dn: e.tensor_copy(out=C[dn].rearrange("p t h -> p (t h)"), in_=pf[:, 0:16]),
               reads=[fk], writes=[ck(dn)])
        yield
        op("act", lambda e: e.activation(out=C["cEG"], in_=C["cGAM"], func=AF.Exp), reads=[ck("cGAM")], writes=[ck("cEG")])
        op("act", lambda e: e.activation(out=C["cCD0"], in_=C["cGL0"], func=AF.Exp), reads=[ck("cGL0")], writes=[ck("cCD0")])
        op("act", lambda e: e.activation(out=C["cCD1"], in_=C["cGL1"], func=AF.Exp), reads=[ck("cGL1")], writes=[ck("cCD1")])
        op("dve", lambda e: e.tensor_tensor(out=C["cEKD"][0:64], in0=C["cGL0"][0:64], in1=C["cGAM"][0:64], op=ALU.subtract),
           reads=[ck("cGL0"), ck("cGAM")], writes=[ck("cEKD")])
        op("dve", lambda e: e.tensor_tensor(out=C["cEKD"][64:128], in0=C["cGL1"][64:128], in1=C["cGAM"][64:128], op=ALU.subtract),
           reads=[ck("cGL1"), ck("cGAM"), ck("cEKD")], writes=[ck("cEKD")])
        op("act", lambda e: e.activation(out=C["cEKD"], in_=C["cEKD"], func=AF.Exp), reads=[ck("cEKD")], writes=[ck("cEKD")])
        op("dve", lambda e: e.tensor_tensor(out=C["cGLB"], in0=C["cGAM"], in1=C["cLB"], op=ALU.add), reads=[ck("cGAM"), ck("cLB")], writes=[ck("cGLB")])
        op("dve", lambda e: e.tensor_scalar(out=C["cGC"], in0=C["cGAM"], scalar1=LN_DS, scalar2=None, op0=ALU.add),
           reads=[ck("cGAM")], writes=[ck("cGC")])
        op("dve", lambda e: e.tensor_tensor(out=C["cBG"], in0=C["cB"], in1=C["cEG"], op=ALU.mult), reads=[ck("cB"), ck("cEG")], writes=[ck("cBG")])
        yield

    def gdn_proj(gi, h, full, cx):
        nk = "nTg%d" % gi
        sid = cx["id"]; lb = cx["lb"]
        sq_ = cx["sq"]; sqk = "sq%s" % sid
        Xs = (1, 2, 0) if full else (1, 2)

        def s1(i, X):
            xb_ = cx["xw"][i % 2]; xk = "xw%s%d" % (sid, i % 2)
            wk = "Wg%d%d" % (h, X); hk_ = "halo%d%d" % (h, X)
            op("pool", lambda e: e.tensor_copy(out=xb_[:, 0:3], in_=halo[h][X][:, 0:3]), reads=[hk_, xk], writes=[xk])
            pf, fk = bankF[lb[i % 3]], "ps%d" % lb[i % 3]
            for k in range(8):
                op("pe", lambda e, k=k: e.matmul(out=pf[:, :], lhsT=Wg[(h, X)][:, k, :], rhs=nTg[gi][:, k, :],
                                                start=(k == 0), stop=(k == 7)), reads=[wk, nk], writes=[fk])
            op("act", lambda e: e.copy(out=xb_[:, 3:515], in_=pf[:, :]), reads=[fk, xk], writes=[xk])

        def s2(i, X):
            xb_ = cx["xw"][i % 2]; xk = "xw%s%d" % (sid, i % 2)
            yb_ = cx["ybuf"][i % 2]; yk = "ybuf%s%d" % (sid, i % 2)
            tk = "XT%s%d" % (sid, X); hk_ = "halo%d%d" % (h, X)
            cw = lambda j: gcw[:, X * 4 + h, j:j + 1]
            op("dve", lambda e: e.tensor_scalar(out=yb_[:], in0=xb_[:, 3:515], scalar1=cw(3), scalar2=None, op0=ALU.mult),
               reads=[xk, "gcw"], writes=[yk])
            for j in (2, 1, 0):
                op("dve", lambda e, j=j: e.scalar_tensor_tensor(out=yb_[:], in0=xb_[:, j:j + 512], scalar=cw(j), in1=yb_[:],
                                                               op0=ALU.mult, op1=ALU.add), reads=[xk, "gcw", yk], writes=[yk])
            op("pool", lambda e: e.tensor_copy(out=halo[h][X][:, 0:3], in_=xb_[:, 512:515]), reads=[xk], writes=[hk_])
            op("act", lambda e: e.activation(out=cx["XT"][X][:], in_=yb_[:], func=AF.Silu), reads=[yk], writes=[tk])
            if X != 2:
                op("act", lambda e: e.activation(out=yb_[:], in_=yb_[:], func=AF.Silu), reads=[yk], writes=[yk])
                op("act", lambda e: e.activation(out=sq_[:], in_=yb_[:], func=AF.Square), reads=[yk], writes=[sqk])

        def s3(i, X):
            if X == 2:
                return
            j = 0 if X == 1 else 1
            pf2, fk2 = bankF[lb[(i + 1) % 3]], "ps%d" % lb[(i + 1) % 3]
            for tt in range(4):
                op("pe", lambda e, tt=tt: e.matmul(out=pf2[:, tt:tt + 1], lhsT=sq_[:, tt * 128:(tt + 1) * 128], rhs=cm[:, 1, 0:1], start=True, stop=True),
                   reads=[sqk, "cm"], writes=[fk2])
            op("dve", lambda e: e.tensor_copy(out=cx["ssc"][:, :, j], in_=pf2[:, 0:4]), reads=[fk2], writes=["ssc%s" % sid])

        n = len(Xs)
        for r in range(n + 3):
            if 0 <= r - 3 < n:
                s3(r - 3, Xs[r - 3])
            if 0 <= r - 1 < n:
                s2(r - 1, Xs[r - 1])
            if r < n:
                s1(r, Xs[r])
            yield

    def gdn_local(g, h, cx, cs):
        gi = g % 2; full = g >= G_FULL
        B = dict(cx["L"]); B.update(cs)
        sid = cx["id"]; lb = cx["lb"]
        key = lambda n: ("%s_%s" % (n, cs["id"])) if n in CHN else ("%s_%s" % (n, sid))
        C = {n: CT[n][:, gi] for n in CN}
        ck = lambda n: "%s_%d" % (n, gi)
        bF = lambda i: (bankF[lb[i]], "ps%d" % lb[i])
        bB = lambda i: (bank[lb[i]], "ps%d" % lb[i])
        kTa, vTa, qTa = cx["XT"][1], cx["XT"][2], cx["XT"][0]
        kTk, vTk, qTk = "XT%s1" % sid, "XT%s2" % sid, "XT%s0" % sid
        lr, rkb, Dab, E, tmpm = B["lr"], B["rkb"], B["Dab"], B["E"], B["tmpm"]
        YP = [B["YP0"], B["YP1"]]; XX = [B["XX0"], B["XX1"]]
        kg, kd, vtok, TTb, TbT, ub, wT = B["kg"], B["kd"], B["vtok"], B["TTb"], B["TbT"], B["ub"], B["wT"]
        Eq, qkT, qdT = B.get("Eq"), B.get("qkT"), B.get("qdT")
        hc = lambda n: C[n][:, :, h]
        bc4 = lambda ap2: ap2.unsqueeze(2).broadcast_to([128, 4, 128])
        rep4 = lambda ap2: ap2.unsqueeze(1).broadcast_to([128, 4, 128])
        v4 = lambda ap, w=128: ap.rearrange("p (a b) -> p a b", a=4)
        nc_ = 2 if full else 1
        sk = "ssc%s" % sid
        op("act", lambda e: e.activation(out=lr[:, :, 0:nc_], in_=cx["ssc"][:, :, 0:nc_], func=AF.Ln, bias=EPS), reads=[sk], writes=[key("lr")])
        op("act", lambda e: e.activation(out=rkb[:, 0, :], in_=lr[:, :, 0], func=AF.Exp, scale=-0.5), reads=[key("lr")], writes=[key("rk0")])
        op("dve", lambda e: e.scalar_tensor_tensor(out=rkb[:, 1, :], in0=lr[:, :, 0], scalar=-0.5, in1=hc("cGLB"), op0=ALU.mult, op1=ALU.add),
           reads=[key("lr"), ck("cGLB")], writes=[key("rk1")])
        op("dve", lambda e: e.tensor_tensor(out=rkb[:, 3, :], in0=rkb[:, 0, :], in1=hc("cBG"), op=ALU.mult), reads=[key("rk0"), ck("cBG")], writes=[key("rk3")])
        op("dve", lambda e: e.tensor_tensor(out=rkb[:, 4, :], in0=rkb[:, 0, :], in1=hc("cEKD"), op=ALU.mult), reads=[key("rk0"), ck("cEKD")], writes=[key("rk4")])
        op("dve", lambda e: e.scalar_tensor_tensor(out=rkb[:, 6, :], in0=lr[:, :, 0], scalar=-0.5, in1=hc("cGAM"), op0=ALU.mult, op1=ALU.subtract),
           reads=[key("lr"), ck("cGAM")], writes=[key("rk6")])
        if full:
            op("dve", lambda e: e.scalar_tensor_tensor(out=rkb[:, 2, :], in0=lr[:, :, 1], scalar=-0.5, in1=hc("cGC"), op0=ALU.mult, op1=ALU.add),
               reads=[key("lr"), ck("cGC")], writes=[key("rk2")])
        yield
        pb, pk = bB(0)
        for tt in range(4):
            op("pe", lambda e, tt=tt: e.transpose(out=pb[:, tt * 128:(tt + 1) * 128], in_=kTa[:, tt * 128:(tt + 1) * 128], identity=identb[:]),
               reads=[kTk, "identb"], writes=[pk])
        for tt in range(4):
            op("pe", lambda e, tt=tt: e.transpose(out=pb[:, 512 + tt * 128:512 + (tt + 1) * 128], in_=vTa[:, tt * 128:(tt + 1) * 128], identity=identb[:]),
               reads=[vTk, "identb"], writes=[pk])
        op("dve", lambda e: e.tensor_tensor(out=kg[:], in0=v4(pb[:, 0:512]), in1=bc4(rkb[:, 3, :]), op=ALU.mult), reads=[pk, key("rk3")], writes=[key("kg")])
        op("dve", lambda e: e.tensor_tensor(out=kd[:], in0=v4(pb[:, 0:512]), in1=bc4(rkb[:, 4, :]), op=ALU.mult), reads=[pk, key("rk4")], writes=[key("kd")])
        op("act", lambda e: e.copy(out=vtok[:], in_=v4(pb[:, 512:1024])), reads=[pk], writes=[key("vtok")])
        yield
        op("pool", lambda e: e.tensor_tensor(out=Dab[:, 0], in0=rep4(IDENT), in1=bc4(rkb[:, 1, :]), op=ALU.mult), reads=["cm", key("rk1")], writes=[key("Dab0")])
        pbc, bck = bF(1)
        op("pe", lambda e: e.matmul(out=pbc[:, :], lhsT=ONES, rhs=Dab[:, 0].rearrange("p a b -> p (a b)"), start=True, stop=True),
           reads=["cm", key("Dab0")], writes=[bck])
        if full:
            op("pool", lambda e: e.tensor_tensor(out=Dab[:, 1], in0=rep4(IDENT), in1=bc4(rkb[:, 2, :]), op=ALU.mult), reads=["cm", key("rk2")], writes=[key("Dab1")])
            pbc2, bck2 = bF(2)
            op("pe", lambda e: e.matmul(out=pbc2[:, :], lhsT=ONES, rhs=Dab[:, 1].rearrange("p a b -> p (a b)"), start=True, stop=True),
               reads=["cm", key("Dab1")], writes=[bck2])
        op("dve", lambda e: e.tensor_tensor(out=tmpm[:], in0=v4(pbc[:, :]), in1=rep4(cm[:, 7, :]), op=ALU.add), reads=[bck, "cm"],
           writes=[key("tmpm"), key("tmpm") + "h0", key("tmpm") + "h1"])
        for tt in range(4):
            op("act", lambda e, tt=tt: e.activation(out=E[:, 0, tt, :], in_=tmpm[:, tt, :], func=AF.Exp, bias=rkb[:, 6, tt:tt + 1]),
               reads=[key("tmpm"), key("rk6")], writes=[key("E0")])
        yield
        pkk, kkk = bF(0)
        for tt in range(4):
            sl = slice(tt * 128, (tt + 1) * 128)
            op("pe", lambda e, sl=sl: e.matmul(out=pkk[:, sl], lhsT=kTa[:, sl], rhs=kTa[:, sl], start=True, stop=True), reads=[kTk], writes=[kkk])
        Y0 = YP[0][:, :, 0:128]
        op("dve", lambda e: e.scalar_tensor_tensor(out=Y0, in0=v4(pkk[:, :]), scalar=-1.0, in1=E[:, 0], op0=ALU.mult, op1=ALU.mult),
           reads=[kkk, key("E0")], writes=[key("YP0a"), key("YP0a") + "h0", key("YP0a") + "h1"])
        if full:
            op("dve", lambda e: e.tensor_tensor(out=tmpm[:], in0=v4(pbc2[:, :]), in1=rep4(cm[:, 8, :]), op=ALU.add), reads=[bck2, "cm", key("tmpm")], writes=[key("tmpm")])
            op("act", lambda e: e.activation(out=Eq[:], in_=v4(pbc2[:, :]), func=AF.Exp), reads=[bck2], writes=[key("Eq")])
            for tt in range(4):
                op("act", lambda e, tt=tt: e.activation(out=E[:, 1, tt, :], in_=tmpm[:, tt, :], func=AF.Exp, bias=rkb[:, 6, tt:tt + 1]),
                   reads=[key("tmpm"), key("rk6")], writes=[key("E1")])
            pqk, qkk = bF(1)
            for tt in range(4):
                sl = slice(tt * 128, (tt + 1) * 128)
                op("pe", lambda e, sl=sl: e.matmul(out=pqk[:, sl], lhsT=kTa[:, sl], rhs=qTa[:, sl], start=True, stop=True), reads=[kTk, qTk], writes=[qkk])
            op("dve", lambda e: e.tensor_tensor(out=qkT[:], in0=v4(pqk[:, :]), in1=E[:, 1], op=ALU.mult), reads=[qkk, key("E1")], writes=[key("qkT")])
            op("pool", lambda e: e.tensor_tensor(out=qdT[:], in0=v4(qTa[:, :]), in1=Eq[:], op=ALU.mult), reads=[qTk, key("Eq")], writes=[key("qdT")])
        yield "FRONT_DONE"
        HB = cx["HB"]
        bR = lambda i: (bankF[i], "ps%d" % i)
        hk = lambda n, hh: key(n) + "h%d" % hh
        px, pxk = bB(2)
        for tt in range(4):
            op("pe", lambda e, tt=tt: e.transpose(out=px[:, tt * 128:(tt + 1) * 128], in_=YP[0][:, tt, 0:128], identity=identb[:]),
               reads=[key("YP0a"), "identb"], writes=[pxk])
        for hh in range(2):
            tsl = slice(2 * hh, 2 * hh + 2)
            op("act", lambda e, tsl=tsl, hh=hh: e.copy(out=XX[0][:, tsl, :], in_=v4(px[:, 0:512])[:, tsl, :]), reads=[pxk], writes=[hk("XX0", hh)])
            op("pool", lambda e, tsl=tsl: e.tensor_tensor(out=YP[1][:, tsl, 128:256], in0=YP[0][:, tsl, 0:128], in1=identb[:].unsqueeze(1).broadcast_to([128, 2, 128]), op=ALU.add),
               reads=[key("YP0a"), "identb"], writes=[hk("YP1b", hh)])
        yield
        for hh in range(2):
            yb_, xb2, xo = HB[hh]
            py, pyk = bR(yb_); pz, pzk = bR(xb2)
            for tt in (2 * hh, 2 * hh + 1):
                o0 = (tt % 2) * 128
                op("pe", lambda e, tt=tt, o0=o0, py=py: e.matmul(out=py[:, o0:o0 + 128], lhsT=XX[0][:, tt, :], rhs=YP[0][:, tt, 0:128], start=True, stop=True),
                   reads=[hk("XX0", hh), key("YP0a")], writes=[pyk])
                op("pe", lambda e, tt=tt, o0=o0, pz=pz, xo=xo: e.matmul(out=pz[:, xo + o0:xo + o0 + 128], lhsT=YP[0][:, tt, 0:128], rhs=XX[0][:, tt, :], start=True, stop=True),
                   reads=[hk("XX0", hh), key("YP0a")], writes=[pzk])
        for hh in range(2):
            yb_, xb2, xo = HB[hh]
            py, pyk = bR(yb_); pz, pzk = bR(xb2)
            tsl = slice(2 * hh, 2 * hh + 2)
            op("dve", lambda e, tsl=tsl, py=py: e.tensor_copy(out=YP[1][:, tsl, 0:128], in_=py[:, 0:256].rearrange("p (a b) -> p a b", a=2)),
               reads=[pyk], writes=[hk("YP1a", hh)])
            op("act", lambda e, tsl=tsl, pz=pz, xo=xo: e.copy(out=XX[1][:, tsl, :], in_=pz[:, xo:xo + 256].rearrange("p (a b) -> p a b", a=2)),
               reads=[pzk], writes=[hk("XX1", hh)])
        yield
        cur = 1
        for lvl in range(2, 6):
            c_, n_ = cur, 1 - cur
            last = (lvl == 5)
            for hh in range(2):
                yb_, xb2, xo = HB[hh]
                pv_, pvk = bR(yb_); pxx, pxxk = bR(xb2)
                for tt in (2 * hh, 2 * hh + 1):
                    o0 = (tt % 2) * 256
                    if not last:
                        op("pe", lambda e, tt=tt, c_=c_, pv_=pv_, o0=o0: e.matmul(out=pv_[:, o0:o0 + 256], lhsT=XX[c_][:, tt, :], rhs=YP[c_][:, tt, :], start=True, stop=True),
                           reads=[hk("XX%d" % c_, hh), hk("YP%da" % c_, hh), hk("YP%db" % c_, hh)], writes=[pvk])
                    else:
                        op("pe", lambda e, tt=tt, c_=c_, pv_=pv_, o0=o0: e.matmul(out=pv_[:, o0 + 128:o0 + 256], lhsT=XX[c_][:, tt, :], rhs=YP[c_][:, tt, 128:256], start=True, stop=True),
                           reads=[hk("XX%d" % c_, hh), hk("YP%db" % c_, hh)], writes=[pvk])
                for tt in (2 * hh, 2 * hh + 1):
                    o1 = (tt % 2) * 128
                    op("pe", lambda e, tt=tt, c_=c_, pxx=pxx, o1=o1, xo=xo: e.matmul(out=pxx[:, xo + o1:xo + o1 + 128], lhsT=YP[c_][:, tt, 0:128], rhs=XX[c_][:, tt, :], start=True, stop=True),
                       reads=[hk("XX%d" % c_, hh), hk("YP%da" % c_, hh)], writes=[pxxk])
            for hh in range(2):
                yb_, xb2, xo = HB[hh]
                pv_, pvk = bR(yb_); pxx, pxxk = bR(xb2)
                pv3 = pv_[:, :].rearrange("p (a b) -> p a b", a=2)
                tsl = slice(2 * hh, 2 * hh + 2)
                if not last:
                    op("act", lambda e, n_=n_, pv3=pv3, tsl=tsl: e.copy(out=YP[n_][:, tsl, 0:128], in_=pv3[:, :, 0:128]), reads=[pvk], writes=[hk("YP%da" % n_, hh)] + ([key("YP0a")] if n_ == 0 else []))
                op("dve", lambda e, c_=c_, n_=n_, pv3=pv3, tsl=tsl: e.tensor_tensor(out=YP[n_][:, tsl, 128:256], in0=YP[c_][:, tsl, 128:256], in1=pv3[:, :, 128:256], op=ALU.add),
                   reads=[pvk, hk("YP%db" % c_, hh)], writes=[hk("YP%db" % n_, hh)])
                op("act", lambda e, n_=n_, pxx=pxx, tsl=tsl, xo=xo: e.copy(out=XX[n_][:, tsl, :], in_=pxx[:, xo:xo + 256].rearrange("p (a b) -> p a b", a=2)),
                   reads=[pxxk], writes=[hk("XX%d" % n_, hh)])
            cur = n_
            yield
        c_ = cur
        for hh in range(2):
            yb_, xb2, xo = HB[hh]
            py, pyk = bR(yb_)
            tsl = slice(2 * hh, 2 * hh + 2)
            for tt in (2 * hh, 2 * hh + 1):
                o1 = (tt % 2) * 128
                op("pe", lambda e, tt=tt, py=py, o1=o1: e.matmul(out=py[:, o1:o1 + 128], lhsT=XX[c_][:, tt, :], rhs=YP[c_][:, tt, 128:256], start=True, stop=True),
                   reads=[hk("XX%d" % c_, hh), hk("YP%db" % c_, hh)], writes=[pyk])
            op("dve", lambda e, py=py, tsl=tsl: e.tensor_tensor(out=tmpm[:, tsl, :], in0=YP[c_][:, tsl, 128:256], in1=py[:, 0:256].rearrange("p (a b) -> p a b", a=2), op=ALU.add),
               reads=[pyk, hk("YP%db" % c_, hh), key("tmpm")], writes=[key("tmpm") + "h%d" % hh])
        tmk = [key("tmpm") + "h0", key("tmpm") + "h1"]
        op("act", lambda e: e.copy(out=TTb[:], in_=tmpm[:]), reads=tmk, writes=[key("TTb")])
        op("pool", lambda e: e.tensor_tensor(out=TbT[:], in0=tmpm[:], in1=bc4(hc("cB")), op=ALU.mult), reads=tmk + [ck("cB")], writes=[key("TbT")])
        yield
        pu, puk = bF(1)
        pw_, pwk_ = bF(2)
        for tt in range(4):
            sl = slice(tt * 128, (tt + 1) * 128)
            op("pe", lambda e, tt=tt, sl=sl: e.matmul(out=pu[:, sl], lhsT=TbT[:, tt, :], rhs=vtok[:, tt, :], start=True, stop=True),
               reads=[key("TbT"), key("vtok")], writes=[puk])
        for tt in range(4):
            sl = slice(tt * 128, (tt + 1) * 128)
            op("pe", lambda e, tt=tt, sl=sl: e.matmul(out=pw_[:, sl], lhsT=kg[:, tt, :], rhs=TTb[:, tt, :], start=True, stop=True),
               reads=[key("kg"), key("TTb")], writes=[pwk_])
        op("act", lambda e: e.copy(out=ub[:], in_=v4(pu[:, :])), reads=[puk], writes=[key("ub")])
        op("dve", lambda e: e.tensor_copy(out=wT[:], in_=v4(pw_[:, :])), reads=[pwk_], writes=[key("wT")])
        yield

    def gdn_chain(g, h, cx, cs):
        gi = g % 2; full = g >= G_FULL
        B = cs
        key = lambda n: "%s_%s" % (n, cs["id"])
        cbk = cx["cbanks"]
        C = {n: CT[n][:, gi] for n in CN}
        ck = lambda n: "%s_%d" % (n, gi)
        bF = lambda i: (bankF[i], "ps%d" % i)
        kd, ub, wT, ubf = B["kd"], B["ub"], B["wT"], B["ubf"]
        qkT, qdT, osb, og, sg = B.get("qkT"), B.get("qdT"), B.get("osb"), B.get("og"), B.get("sg")
        Sk, Sbk = "S%d" % h, "Sb%d" % h
        flip = 0
        for tt in range(4):
            t = 4 * g + tt
            store = full and t >= T0
            for hf in range(2):
                rows = slice(hf * 64, hf * 64 + 64)
                cdn = "cCD0" if hf == 0 else "cCD1"
                cd = C[cdn][:, tt, h:h + 1]
                pw, pwk = bF(cbk[flip % len(cbk)]); ps_, psk = bF(cbk[(flip + 1) % len(cbk)]); flip = 1 - flip
                op("pe", lambda e: e.matmul(out=pw[:, 0:128], lhsT=wT[:, tt, :], rhs=Sb[h][:], start=True, stop=True), reads=[key("wT"), Sbk], writes=[pwk])
                op("dve", lambda e: e.tensor_tensor(out=ubf[rows, :], in0=ub[rows, tt, :], in1=pw[rows, 0:128], op=ALU.subtract),
                   reads=[key("ub"), pwk], writes=[key("ubf")])
                if store:
                    op("pe", lambda e: e.matmul(out=pw[:, 128:256], lhsT=qdT[:, tt, :], rhs=Sb[h][:], start=True, stop=False), reads=[key("qdT"), Sbk, pwk], writes=[pwk])
                    op("pe", lambda e: e.matmul(out=pw[:, 128:256], lhsT=qkT[:, tt, :], rhs=ubf[:], start=False, stop=True),
                       reads=[key("qkT"), key("ubf"), pwk], writes=[pwk])
                    op("act", lambda e: e.copy(out=osb[rows, :], in_=pw[rows, 128:256]), reads=[pwk], writes=[key("osb")])
                yield
                op("pe", lambda e: e.matmul(out=ps_[:, 384:512], lhsT=kd[rows, tt, :], rhs=ubf[rows, :], start=True, stop=True), reads=[key("kd"), key("ubf")], writes=[psk])
                op("dve", lambda e: e.scalar_tensor_tensor(out=Sb[h][:], in0=Sst[h][:], scalar=cd, in1=ps_[:, 384:512], op0=ALU.mult, op1=ALU.add),
                   reads=[Sk, ck(cdn), psk], writes=[Sbk])
                op("dve", lambda e: e.scalar_tensor_tensor(out=Sst[h][:], in0=Sst[h][:], scalar=cd, in1=ps_[:, 384:512], op0=ALU.mult, op1=ALU.add),
                   reads=[Sk, ck(cdn), psk], writes=[Sk])
                yield
            if store:
                s_ = t - T0
                pg, pgk = bF(cbk[flip % len(cbk)])
                for k in range(8):
                    op("pe", lambda e, k=k: e.matmul(out=pg[:, 256:384], lhsT=nTg[gi][:, k, tt * 128:(tt + 1) * 128],
                                                    rhs=PH["Wgg"][:, k, h * 128:(h + 1) * 128], start=(k == 0), stop=(k == 7)),
                       reads=["nTg%d" % gi, "Wgg"], writes=[pgk])
                op("act", lambda e: e.activation(out=sg[:], in_=pg[:, 256:384], func=AF.Silu), reads=[pgk], writes=[key("sg")])
                sc = st[:, 4 + h:5 + h]; sck = "st%d" % (4 + h)
                rms_rstd(osb[:], [key("osb")], HD, sc, sck)
                op("dve", lambda e: e.scalar_tensor_tensor(out=og[:], in0=osb[:], scalar=sc, in1=GNW, op0=ALU.mult, op1=ALU.mult),
                   reads=[key("osb"), sck, "gsmbc"], writes=[key("og")])
                op("pool", lambda e: e.tensor_tensor(out=mix[:, s_, h * 128:(h + 1) * 128], in0=og[:], in1=sg[:], op=ALU.mult),
                   reads=[key("og"), key("sg")], writes=["mix%d" % h])
                yield

    def gdn_task(g, heads, cx, prefetch):
        gi = g % 2; full = g >= G_FULL
        prev = None
        if cx.get("prefetched") != g:
            for _ in gdn_proj(gi, heads[0], full, cx):
                yield
        for i, h in enumerate(heads):
            cs = cx["cs"][i % len(cx["cs"])]
            subs = [gdn_local(g, h, cx, cs)] + ([prev] if prev is not None else [])
            while subs:
                for sg_ in list(subs):
                    try:
                        r = next(sg_)
                        if r == "FRONT_DONE" and i + 1 < len(heads):
                            subs.append(gdn_proj(gi, heads[i + 1], full, cx))
                    except StopIteration:
                        subs.remove(sg_)
                yield
            prev = gdn_chain(g, h, cx, cs)
        subs = [prev]
        if prefetch:
            subs.append(gdn_proj((g + 1) % 2, heads[0], full, cx))
            cx["prefetched"] = g + 1
        while subs:
            for sg_ in list(subs):
                try:
                    next(sg_)
                except StopIteration:
                    subs.remove(sg_)
            yield

    def ret_tile(t, tt, gi, h, full, ci):
        nk = "nTg%d" % gi
        Rr = PH["RS"]; rbk = PH["rbank"]; rbk2 = PH["rbank2"]
        cs_t, sn_t = PH["cs_t"], PH["sn_t"]
        rk_ = lambda n: "%s_r" % n
        rsb, rt1, rt2, krot, kz, rvt = Rr["rsb"], Rr["rt1"], Rr["rt2"], Rr["krot"], Rr["kz"], Rr["rvt"]
        qrot, rkT, rqT, rqx, rPT, sgr, osb, og = [Rr.get(n) for n in ("qrot", "rkT", "rqT", "rqx", "rPT", "sgr", "osb", "og")]
        csk, snk = "cs_t%d" % ci, "sn_t%d" % ci
        store = full and t >= T0
        pf, fk = bankF[rbk], "ps%d" % rbk
        for k in range(8):
            op("pe", lambda e, k=k: e.matmul(out=pf[:, 0:256], lhsT=nTg[gi][:, k, tt * 128:(tt + 1) * 128], rhs=Wrkv[h][:, k, :],
                                            start=(k == 0), stop=(k == 7)), reads=[nk, "Wrkv%d" % h], writes=[fk])
        if store:
            for k in range(8):
                op("pe", lambda e, k=k: e.matmul(out=pf[:, 256:512], lhsT=nTg[gi][:, k, tt * 128:(tt + 1) * 128], rhs=PH["Wrqg"][h][:, k, :],
                                                start=(k == 0), stop=(k == 7)), reads=[nk, "Wrqg%d" % h, fk], writes=[fk])
        op("act", lambda e: e.copy(out=rsb[:, 0:128], in_=pf[:, 0:128]), reads=[fk], writes=[rk_("rsb")])
        op("act", lambda e: e.copy(out=rvt[:], in_=pf[:, 128:256]), reads=[fk], writes=[rk_("rvt")])
        if store:
            op("act", lambda e: e.copy(out=rsb[:, 128:256], in_=pf[:, 256:384]), reads=[fk, rk_("rsb")], writes=[rk_("rsb")])
        if store:
            op("act", lambda e: e.activation(out=sgr[:], in_=pf[:, 384:512], func=AF.Silu), reads=[fk], writes=[rk_("sgr")])
        yield

        def rotary(src, dst, dstk):
            op("dve", lambda e: e.tensor_tensor(out=rt1[:], in0=src, in1=cs_t[ci][:], op=ALU.mult), reads=[rk_("rsb"), csk], writes=[rk_("rt1")])
            op("dve", lambda e: e.tensor_tensor(out=rt2[:, 0:128:2], in0=src[:, 1:128:2], in1=sn_t[ci][:, 0:128:2], op=ALU.mult),
               reads=[rk_("rsb"), snk], writes=[rk_("rt2e")])
            op("dve", lambda e: e.tensor_tensor(out=rt2[:, 1:128:2], in0=src[:, 0:128:2], in1=sn_t[ci][:, 1:128:2], op=ALU.mult),
               reads=[rk_("rsb"), snk], writes=[rk_("rt2o")])
            op("pool", lambda e: e.tensor_tensor(out=dst[:], in0=rt1[:], in1=rt2[:], op=ALU.add),
               reads=[rk_("rt1"), rk_("rt2e"), rk_("rt2o")], writes=[dstk])
        rotary(rsb[:, 0:128], krot, rk_("krot"))
        op("act", lambda e: e.activation(out=kz[:], in_=krot[:], func=AF.Copy, scale=rv[:, h:h + 1]), reads=[rk_("krot"), "rv"], writes=[rk_("kz")])
        Rk, Rbk = "R%d" % h, "Rb%d" % h
        yield
        if store:
            rotary(rsb[:, 128:256], qrot, rk_("qrot"))
            pb, pk = bank[rbk2], "ps%d" % rbk2
            op("pe", lambda e: e.transpose(out=pb[:, 0:128], in_=krot[:], identity=identb[:]), reads=[rk_("krot"), "identb"], writes=[pk])
            op("pe", lambda e: e.transpose(out=pb[:, 128:256], in_=qrot[:], identity=identb[:]), reads=[rk_("qrot"), "identb"], writes=[pk])
            op("act", lambda e: e.copy(out=rkT[:], in_=pb[:, 0:128]), reads=[pk], writes=[rk_("rkT")])
            op("act", lambda e: e.copy(out=rqT[:], in_=pb[:, 128:256]), reads=[pk], writes=[rk_("rqT")])
            op("pool", lambda e: e.tensor_tensor(out=rqx[:], in0=rqT[:], in1=xibc[:, h, :], op=ALU.mult), reads=[rk_("rqT"), "xibc"], writes=[rk_("rqx")])
            yield
            psc, sck = bankF[rbk2], "ps%d" % rbk2
            op("pe", lambda e: e.matmul(out=psc[:, 0:128], lhsT=rkT[:], rhs=rqT[:], start=True, stop=True), reads=[rk_("rkT"), rk_("rqT")], writes=[sck])
            op("dve", lambda e: e.tensor_tensor(out=rPT[:], in0=psc[:, 0:128], in1=rm[:, h, :], op=ALU.mult), reads=[sck, "rm"], writes=[rk_("rPT")])
            po, pok = bankF[rbk], "ps%d" % rbk
            op("pe", lambda e: e.matmul(out=po[:, 0:128], lhsT=rPT[:], rhs=rvt[:], start=True, stop=False), reads=[rk_("rPT"), rk_("rvt")], writes=[pok])
            op("pe", lambda e: e.matmul(out=po[:, 0:128], lhsT=rqx[:], rhs=Rb[h][:], start=False, stop=True), reads=[rk_("rqx"), Rbk, pok], writes=[pok])
            s_ = t - T0
            op("act", lambda e: e.copy(out=osb[:], in_=po[:, 0:128]), reads=[pok], writes=[rk_("osb")])
            yield
            sc = st[:, 8 + h:9 + h]; sck2 = "st%d" % (8 + h)
            rms_rstd(osb[:], [rk_("osb")], HD, sc, sck2)
            op("dve", lambda e: e.tensor_scalar(out=og[:], in0=osb[:], scalar1=sc, scalar2=None, op0=ALU.mult), reads=[rk_("osb"), sck2], writes=[rk_("og")])
            op("pool", lambda e: e.tensor_tensor(out=mix[:, s_, 512 + h * 128:512 + (h + 1) * 128], in0=og[:], in1=sgr[:], op=ALU.mult),
               reads=[rk_("og"), rk_("sgr")], writes=["mix%d" % (4 + h)])
        if not store:
            yield
        pr, prk = bankF[rbk], "ps%d" % rbk
        op("pe", lambda e: e.matmul(out=pr[:, 0:128], lhsT=kz[:], rhs=rvt[:], start=True, stop=True), reads=[rk_("kz"), rk_("rvt")], writes=[prk])
        g128 = float(np.float64(1.0 - 2.0 ** (-5 - h)) ** 128)
        op("dve", lambda e: e.scalar_tensor_tensor(out=Rst[h][:], in0=Rst[h][:], scalar=g128, in1=pr[:, 0:128], op0=ALU.mult, op1=ALU.add),
           reads=[Rk, prk], writes=[Rk])
        op("act", lambda e: e.copy(out=Rb[h][:], in_=Rst[h][:]), reads=[Rk], writes=[Rbk])
        yield

    def prep_task(g):
        gi = g % 2
        for r in range(6):
            if 0 <= r - 2 < 4:
                norm_b(4 * g + r - 2, gi, r - 2)
            if r < 4:
                norm_a(4 * g + r)
            yield
        yield
        if "g" in parts:
            yield from gdn_common(gi, g >= G_FULL)

    def ret_task(g):
        gi = g % 2; full = g >= G_FULL
        for tt in range(4):
            t = 4 * g + tt
            ci = t % 2
            dma("sp", PH["cs_t"][ci][:], cosF[t * 128:(t + 1) * 128, :], writes=["cs_t%d" % ci])
            dma("sp", PH["sn_t"][ci][:], sinS[t * 128:(t + 1) * 128, :], writes=["sn_t%d" % ci])
            for h in range(4):
                yield from ret_tile(t, tt, gi, h, full, ci)


    def run_tasks(tasks, reps=None):
        tasks = list(tasks)
        reps = dict(reps or {})
        while tasks:
            for tk_ in list(tasks):
                for _ in range(reps.get(id(tk_), 1)):
                    try:
                        next(tk_)
                    except StopIteration:
                        tasks.remove(tk_)
                        break

    g0 = NG - ng_run
    g1 = NG
    if dev_g0 is not None:
        g0 = dev_g0; g1 = g0 + ng_run
    run_tasks([prep_task(g0)])
    cur_phase = [None]
    for g in range(g0, g1):
        full = g >= G_FULL
        if cur_phase[0] != full:
            if cur_phase[0] is not None:
                p.barrier()
                es_ph[0].close()
                es_ph[0] = ExitStack()
            alloc_phase(full)
            cur_phase[0] = full
        tasks = []
        if "g" in parts:
            pf_ok = (g + 1 < g1) and ((g + 1 >= G_FULL) == full)
            if full:
                tasks.append(gdn_task(g, [0, 1, 2, 3], PH["streams"][0], pf_ok))
            else:
                tasks.append(gdn_task(g, [0, 2], PH["streams"][0], pf_ok))
                tasks.append(gdn_task(g, [1, 3], PH["streams"][1], pf_ok))
        reps = {}
        if "r" in parts:
            rt_ = ret_task(g)
            tasks.append(rt_)
            reps[id(rt_)] = 1 if full else 2
        if g + 1 < g1:
            tasks.append(prep_task(g + 1))
        run_tasks(tasks, reps)

    def early(src):
        p.barrier()
        dma("sp", out[0:128, :], src, writes=["out0"])
        p.finish()
        return nc, p
    if phase == "pass":
        return early(xt[0][:])
    p.barrier()
    es_ph[0].close()
    es_keep.close()
    PHTAG[0] = "_X"
    hres = R("hres", [128, NSLOT, D], F32)
    n2T = R("n2T", [128, 8, 2 + OWN], BF16)
    es_b = ExitStack()
    Wout = L("Wout", [128, 8, D], BF16, es_b)
    mixT = [L("mixT%d" % i, [128, 8, 128], BF16, es_b) for i in range(2)]
    xt2 = [L("xt2%d" % i, [128, D], F32, es_b) for i in range(2)]
    junk = L("junk2", [128, D], F32, es_b)
    nb2 = [L("nb2%d" % i, [128, D], BF16, es_b) for i in range(2)]
    st = L("st2", [128, 4], F32, es_b)
    dma("pool", Wout[:], w_out.rearrange("(k p) c -> p k c", p=128), writes=["Wout"])
    nw2 = L("nw2", [128, D], F32, es_b)
    dma("sp", nw2[:], vecs[1:2, :].partition_broadcast(128), writes=["nw2"])

    def b3_a(s_):
        t = T0 + s_
        dma("sp", xt2[s_ % 2][:], xpad[t * 128:(t + 1) * 128, :], writes=["xt2%d" % (s_ % 2)])
        pb, pk = nB()
        for k in range(8):
            op("pe", lambda e, k=k: e.transpose(out=pb[:, k * 128:(k + 1) * 128], in_=mix[:, s_, k * 128:(k + 1) * 128], identity=identb[:]),
               reads=["mix%d" % k, "identb"], writes=[pk])
        op("act", lambda e: e.copy(out=mixT[s_ % 2][:], in_=pb[:, :].rearrange("p (k n) -> p k n", k=8)), reads=[pk], writes=["mixT%d" % (s_ % 2)])

    def b3_b(s_):
        xb_ = xt2[s_ % 2]; xk = "xt2%d" % (s_ % 2); mT = mixT[s_ % 2]; mk = "mixT%d" % (s_ % 2)
        for half in range(2):
            pf, fk = nF()
            for k in range(8):
                op("pe", lambda e, k=k: e.matmul(out=pf[:, :], lhsT=mT[:, k, :], rhs=Wout[:, k, half * 512:(half + 1) * 512],
                                                start=(k == 0), stop=(k == 7)), reads=[mk, "Wout"], writes=[fk])
            op("dve", lambda e: e.tensor_tensor(out=hres[:, s_, half * 512:(half + 1) * 512], in0=xb_[:, half * 512:(half + 1) * 512],
                                                in1=pf[:, :], op=ALU.add), reads=[xk, fk], writes=["h%d_%d" % (s_, half)])

    def b3_c(s_):
        hks = ["h%d_0" % s_, "h%d_1" % s_]
        sc = st[:, (s_ % 2):(s_ % 2) + 1]; sck = "stb%d" % (s_ % 2)
        rms_rstd(hres[:, s_, :], hks, D, sc, sck)
        op("dve", lambda e: e.scalar_tensor_tensor(out=nb2[s_ % 2][:], in0=hres[:, s_, :], scalar=sc, in1=nw2[:],
                                                   op0=ALU.mult, op1=ALU.mult), reads=hks + [sck, "nw2"], writes=["nb2%d" % (s_ % 2)])

    def b3_d(s_):
        nb_ = nb2[s_ % 2]; nbk = "nb2%d" % (s_ % 2)
        pb, pk = nB()
        for k in range(8):
            op("pe", lambda e, k=k: e.transpose(out=pb[:, k * 128:(k + 1) * 128], in_=nb_[:, k * 128:(k + 1) * 128], identity=identb[:]),
               reads=[nbk, "identb"], writes=[pk])
        pv = pb[:, :].rearrange("p (k n) -> p k n", k=8)
        if s_ == 0:
            op("act", lambda e: e.copy(out=n2T[:, :, 0:2], in_=pv[:, :, 126:128]), reads=[pk], writes=["n2T_h"])
        else:
            op("act", lambda e: e.copy(out=n2T[:, :, 2 + (s_ - 1) * 128:2 + s_ * 128], in_=pv), reads=[pk], writes=["n2T_%d" % ((s_ - 1) // 4)])

    for r in range(NSLOT + 3):
        if 0 <= r - 3 < NSLOT:
            b3_d(r - 3)
        if 0 <= r - 2 < NSLOT:
            b3_c(r - 2)
        if 0 <= r - 1 < NSLOT:
            b3_b(r - 1)
        if r < NSLOT:
            b3_a(r)

    if phase == "b3":
        return early(hres[:, 1, :])
    p.barrier()
    es_b.close()
    es_mix.close()
    es_c = ExitStack()
    NGRP = DFF // 256
    Wgu = [L("Wgu%d" % i, [128, 2, 8, 256], BF16, es_c) for i in range(2)]
    Wd = [L("Wd%d" % i, [128, 2, D], BF16, es_c) for i in range(2)]
    mcw = L("mcw", [128, 44, 3], F32, es_c)
    hb = [L("hb%d" % i, [128, 2 + OWN], F32, es_c) for i in range(2)]
    yb = [L("yb%d" % i, [128, OWN], F32, es_c) for i in range(2)]
    actT = [L("actT%d" % i, [128, 2, OWN], BF16, es_c) for i in range(2)]
    st = L("st3", [128, 4], F32, es_c)
    dma("sp", mcw[:], mcwT.rearrange("(c p) i -> p c i", p=128), writes=["mcw"])
    nw3 = L("nw3", [128, D], F32, es_c)
    dma("sp", nw3[:], vecs[2:3, :].partition_broadcast(128), writes=["nw3"])
    w_up_v = w_up.rearrange("(k p) c -> p k c", p=128)
    w_dn_v = w_down.rearrange("(c p) d -> p c d", p=128)

    def load_up(gi_):
        b = gi_ % 2
        dma("pool", Wgu[b][:, 0, :, :], w_up_v[:, :, gi_ * 256:(gi_ + 1) * 256], writes=["Wgu%d" % b])
        dma("pool", Wgu[b][:, 1, :, :], w_up_v[:, :, DFF + gi_ * 256:DFF + (gi_ + 1) * 256], writes=["Wgu%d" % b])

    def load_dn(gi_):
        b = gi_ % 2
        dma("pool", Wd[b][:], w_dn_v[:, gi_ * 2:gi_ * 2 + 2, :], writes=["Wd%d" % b])

    def ffn_up(gi_):
        b = gi_ % 2
        ak = "actT%d" % b
        for fc in range(2):
            for which in range(2):
                hb_ = hb[which]; hbk = "hb%d" % which
                cidx = which * 22 + gi_ * 2 + fc
                lw = lambda k: Wgu[b][:, which, k, fc * 128:(fc + 1) * 128]
                pf, fk = nF()
                for k in range(8):
                    op("pe", lambda e, k=k: e.matmul(out=pf[:, 0:2], lhsT=lw(k), rhs=n2T[:, k, 0:2], start=(k == 0), stop=(k == 7)),
                       reads=["Wgu%d" % b, "n2T_h"], writes=[fk])
                op("act", lambda e: e.copy(out=hb_[:, 0:2], in_=pf[:, 0:2]), reads=[fk], writes=[hbk + "h"])
                for tb in range(4):
                    pf, fk = nF()
                    for k in range(8):
                        op("pe", lambda e, k=k: e.matmul(out=pf[:, :], lhsT=lw(k), rhs=n2T[:, k, 2 + tb * 512:2 + (tb + 1) * 512],
                                                        start=(k == 0), stop=(k == 7)), reads=["Wgu%d" % b, "n2T_%d" % tb], writes=[fk])
                    op("act", lambda e, tb=tb: e.copy(out=hb_[:, 2 + tb * 512:2 + (tb + 1) * 512], in_=pf[:, :]), reads=[fk], writes=[hbk + "_%d" % tb])
                hbks = [hbk + "h"] + [hbk + "_%d" % i for i in range(4)]
                ybk = "yb%d" % which
                op("act", lambda e: e.activation(out=yb[which][:], in_=hb_[:, 2:2 + OWN], func=AF.Copy, scale=mcw[:, cidx, 2:3]),
                   reads=hbks + ["mcw"], writes=[ybk])
                for i in (1, 0):
                    op("dve", lambda e, i=i: e.scalar_tensor_tensor(out=yb[which][:], in0=hb_[:, i:i + OWN], scalar=mcw[:, cidx, i:i + 1],
                                                                   in1=yb[which][:], op0=ALU.mult, op1=ALU.add), reads=hbks + ["mcw", ybk], writes=[ybk])
            op("act", lambda e: e.activation(out=yb[0][:], in_=yb[0][:], func=AF.Silu), reads=["yb0"], writes=["yb0"])
            op("pool", lambda e: e.tensor_tensor(out=actT[b][:, fc, :], in0=yb[0][:], in1=yb[1][:], op=ALU.mult),
               reads=["yb0", "yb1"], writes=[ak + "_%d" % fc])

    def ffn_down(gi_):
        b = gi_ % 2
        ak = "actT%d" % b
        for tt in range(16):
            for half in range(2):
                pf, fk = nF()
                for fc in range(2):
                    op("pe", lambda e, fc=fc: e.matmul(out=pf[:, :], lhsT=actT[b][:, fc, tt * 128:(tt + 1) * 128],
                                                      rhs=Wd[b][:, fc, half * 512:(half + 1) * 512], start=(fc == 0), stop=(fc == 1)),
                       reads=[ak + "_0", ak + "_1", "Wd%d" % b], writes=[fk])
                hk = "h%d_%d" % (tt + 1, half)
                op("dve", lambda e: e.tensor_tensor(out=hres[:, tt + 1, half * 512:(half + 1) * 512],
                                                    in0=hres[:, tt + 1, half * 512:(half + 1) * 512], in1=pf[:, :], op=ALU.add),
                   reads=[fk, hk], writes=[hk])

    load_up(0); load_dn(0)
    for gi_ in range(NGRP + 1):
        if gi_ + 1 < NGRP:
            load_up(gi_ + 1)
        if gi_ < NGRP:
            ffn_up(gi_)
        if gi_ >= 1:
            ffn_down(gi_ - 1)
        if gi_ + 1 < NGRP:
            load_dn(gi_ + 1)

    p.barrier()
    ob = [yb[i][:, 0:D] for i in range(2)]
    junk = hb[0]
    for tt in range(16):
        hks = ["h%d_0" % (tt + 1), "h%d_1" % (tt + 1)]
        rms_rstd(hres[:, tt + 1, :], hks, D, st[:, 0:1], "st0")
        o_ = ob[tt % 2]; ok = "ob%d" % (tt % 2)
        op("dve", lambda e: e.scalar_tensor_tensor(out=o_, in0=hres[:, tt + 1, :], scalar=st[:, 0:1], in1=nw3[:],
                                                   op0=ALU.mult, op1=ALU.mult), reads=hks + ["st0", "nw3"], writes=[ok])
        dma("sp", out[tt * 128:(tt + 1) * 128, :], o_, reads=[ok], writes=["out%d" % tt])
    p.finish()
    es_c.close()
    es_r.close()
    return nc, p


def _consts():
    idx = np.arange(128)
    same = (idx[:, None] // 64) == (idx[None, :] // 64)
    cm = np.zeros((10, 128, 128), np.float32)
    cm[0] = np.eye(128)
    cm[1] = 1.0
    cm[2] = (same & (idx[:, None] < idx[None, :]))
    cm[3] = (same & (idx[:, None] <= idx[None, :]))
    cm[4] = cm[3]
    cm[5] = (idx[:, None] < 64) * np.ones((1, 128))
    cm[6] = (idx[:, None] >= 64) * np.ones((1, 128))
    cm[7] = (1.0 - cm[2]) * -30000.0
    cm[8] = (1.0 - cm[3]) * -30000.0
    hh = np.arange(4, dtype=np.float64)
    gam = 1.0 - 2.0 ** (-5.0 - hh)
    lg = np.log(gam)
    rel = (idx[None, :] - idx[:, None]).astype(np.float64)
    rmat = np.where(rel[None] >= 0, np.exp(rel[None] * lg[:, None, None]), 0.0) * HD ** -0.5
    rvec = np.zeros((128, 8), np.float64)
    rvec[:, 0:4] = np.exp((127.0 - idx[:, None]) * lg[None, :]) * HD ** -0.5
    rxi = np.exp((idx[None, :] + 1.0) * lg[:, None])
    return cm, rmat.astype(np.float32), rvec.astype(np.float32), rxi.astype(np.float32)


_CACHE = {}


def kernel(x, attn_norm_w, w_in, gdn_conv_w, gdn_a_log, gdn_dt_bias, gdn_norm_w, w_out, mlp_norm_w,
           w_up, mlp_conv_w, w_down, final_norm_w):
    f = lambda a: np.ascontiguousarray(np.asarray(a, dtype=np.float32))
    x2 = f(x).reshape(S, D)
    if "nc" not in _CACHE:
        _CACHE["nc"] = build_program()[0]
    nc = _CACHE["nc"]
    cm, rmat, rvec, rxi = _consts()
    vecs = np.stack([f(attn_norm_w)[0], f(mlp_norm_w)[0], f(final_norm_w)], 0)
    gsm = np.concatenate([f(gdn_a_log)[0], f(gdn_dt_bias)[0], f(gdn_norm_w)[0]])[None, :]
    angle = (1.0 / (10000.0 ** np.linspace(0.0, 1.0, 64, dtype=np.float32))).astype(np.float32)
    angle = np.repeat(angle, 2)
    sign = np.tile(np.array([-1.0, 1.0], np.float32), 64)
    common = {
        "w_in": f(w_in)[0], "w_out": f(w_out)[0], "w_up": f(w_up)[0], "w_down": f(w_down)[0],
        "gcwT": np.ascontiguousarray(f(gdn_conv_w)[0].T), "mcwT": np.ascontiguousarray(f(mlp_conv_w)[0].T),
        "vecs": np.ascontiguousarray(vecs), "gsm": np.ascontiguousarray(gsm),
        "cmat": cm, "rmat": rmat, "rvec": rvec, "rxi": rxi,
    }
    in_maps = []
    for c in range(NCORES):
        n_real = OWN * (c + 1)
        xp = np.zeros((S, D), np.float32)
        xp[S - n_real:] = x2[:n_real]
        pos = (np.arange(S, dtype=np.int64) - (S - n_real)).astype(np.float32)
        phase = pos[:, None] * angle[None, :]
        m = dict(common)
        m["xpad"] = xp
        m["cosF"] = np.cos(phase).astype(np.float32)
        m["sinS"] = (np.sin(phase) * sign[None, :]).astype(np.float32)
        in_maps.append(m)
    res = run_bass_kernel_spmd(nc, in_maps, core_ids=list(range(NCORES)))
    outs = [np.asarray(res.results[c]["out"], dtype=np.float32) for c in range(NCORES)]
    return np.concatenate(outs, 0).reshape(1, S, D)
```

```python
import math
from contextlib import ExitStack
import numpy as np
import concourse.bass as bass
import concourse.mybir as mybir
from concourse.bass_utils import run_bass_kernel_spmd

F32 = mybir.dt.float32
BF16 = mybir.dt.bfloat16
ALU = mybir.AluOpType
AF = mybir.ActivationFunctionType

NCORES = 8
S = 16384
D = 1024
NT = S // 128
NG = NT // 4
OWN = 2048
T0 = NT - OWN // 128 - 1
NSLOT = NT - T0
G_FULL = T0 // 4
DFF = 2816
EPS = 1e-6
HD = 128
IN_COLS = 4104


class Prog:
    ENG = ("pe", "act", "dve", "pool", "sp")

    def __init__(self, nc, n_dma_sems=8):
        self.nc = nc
        self.eng = {"pe": nc.tensor, "act": nc.scalar, "dve": nc.vector,
                    "pool": nc.gpsimd, "sp": nc.sync}
        self.sem = {e: nc.alloc_semaphore("c_" + e) for e in self.ENG}
        self.cnt = {e: 0 for e in self.ENG}
        self.dsem = {e: [nc.alloc_semaphore("d_%s%d" % (e, i)) for i in range(n_dma_sems)]
                     for e in ("sp", "pool")}
        self.dval = {e: [0] * n_dma_sems for e in ("sp", "pool")}
        self.drr = {e: 0 for e in ("sp", "pool")}
        self.seen = {e: {} for e in self.ENG}
        self.lastw = {}
        self.readers = {}
        self.n_ins = 0
        self.n_wait = 0

    def _semof(self, tok):
        return self.sem[tok] if isinstance(tok, str) else self.dsem[tok[0]][tok[1]]

    def _wait(self, e, tok, val):
        if tok == e and e == "pe":
            return
        if self.seen[e].get(tok, 0) >= val:
            return
        self.eng[e].wait_ge(self._semof(tok), val)
        self.seen[e][tok] = val
        self.n_wait += 1

    def _deps(self, e, reads, writes):
        for k in reads:
            w = self.lastw.get(k)
            if w is not None:
                self._wait(e, *w)
        for k in writes:
            w = self.lastw.get(k)
            if w is not None:
                self._wait(e, *w)
            for r in self.readers.get(k, ()):
                self._wait(e, *r)

    def _commit(self, tokval, reads, writes):
        for k in writes:
            self.lastw[k] = tokval
            self.readers[k] = []
        for k in reads:
            lst = self.readers.setdefault(k, [])
            lst[:] = [r for r in lst if r[0] != tokval[0]]
            lst.append(tokval)

    def op(self, e, fn, reads=(), writes=()):
        ex = [k for k in reads if k.startswith("ps")]
        if ex:
            reads = [k for k in reads if not k.startswith("ps")]
            writes = list(writes) + [k for k in ex if k not in writes]
        self._deps(e, reads, writes)
        ins = fn(self.eng[e])
        self.cnt[e] += 1
        ins.then_inc(self.sem[e], 1)
        self._commit((e, self.cnt[e]), reads, writes)
        self.n_ins += 1
        return ins

    def dma(self, e, out, in_, reads=(), writes=(), **kw):
        i = self.drr[e]
        self.drr[e] = (i + 1) % len(self.dsem[e])
        tok = (e, i)
        if self.dval[e][i] > 0:
            self._wait(e, tok, self.dval[e][i])
        self._deps(e, reads, writes)
        ins = self.eng[e].dma_start(out=out, in_=in_, **kw)
        self.dval[e][i] += 16
        ins.then_inc(self.dsem[e][i], 16)
        self._commit((tok, self.dval[e][i]), reads, writes)
        self.n_ins += 1
        return ins

    def barrier(self):
        for e in self.ENG:
            for e2 in self.ENG:
                if e2 != e and self.cnt[e2] > 0:
                    self._wait(e, e2, self.cnt[e2])
            for q in self.dsem:
                for i, v in enumerate(self.dval[q]):
                    if v > 0:
                        self._wait(e, (q, i), v)
        self.lastw.clear()
        self.readers.clear()

    def finish(self):
        for q in self.dsem:
            for i, v in enumerate(self.dval[q]):
                if v > 0:
                    self._wait("sp", (q, i), v)


def build_program(ng_run=NG, phase="all", parts="gr", dev_g0=None):
    nc = bass.Bass("TRN2", target_bir_lowering=False)
    dt_in = lambda name, shape: nc.dram_tensor(name, list(shape), F32, kind="ExternalInput").ap()
    xpad = dt_in("xpad", [S, D])
    w_in = dt_in("w_in", [D, IN_COLS])
    w_out = dt_in("w_out", [D, D])
    w_up = dt_in("w_up", [D, 2 * DFF])
    w_down = dt_in("w_down", [DFF, D])
    gcwT = dt_in("gcwT", [1536, 4])
    mcwT = dt_in("mcwT", [2 * DFF, 3])
    vecs = dt_in("vecs", [3, D])
    gsm = dt_in("gsm", [1, 8 + 128])
    cmat = dt_in("cmat", [10, 128, 128])
    rmat = dt_in("rmat", [4, 128, 128])
    rvec = dt_in("rvec", [128, 8])
    rxi = dt_in("rxi", [4, 128])
    cosF = dt_in("cosF", [S, 128])
    sinS = dt_in("sinS", [S, 128])
    out = nc.dram_tensor("out", [OWN, D], F32, kind="ExternalOutput").ap()

    p = Prog(nc)
    SKIP = ""
    op = p.op
    def dma(q, o_, i_, reads=(), writes=(), tag="", **kw):
        if tag and tag in SKIP:
            return p.op("pool", lambda e: e.memset(o_, 1.0), reads=reads, writes=writes)
        return p.dma(q, o_, i_, reads=reads, writes=writes, **kw)

    bank = [nc.alloc_psum_tensor("bank%d" % i, [128, 1024], BF16) for i in range(8)]
    bankF = [bk_[:, :].bitcast(F32) for bk_ in bank]
    rr = {"B": 0, "F": 0}
    POOLS = {"B": [6, 7], "F": [0, 1, 2, 3, 4, 5]}

    def nB():
        lst = POOLS["B"]; i = lst[rr["B"] % len(lst)]; rr["B"] += 1
        return bank[i], "ps%d" % i

    def nF():
        lst = POOLS["F"]; i = lst[rr["F"] % len(lst)]; rr["F"] += 1
        return bankF[i], "ps%d" % i

    es_r = ExitStack()
    def R(name, shape, dt):
        return es_r.enter_context(nc.sbuf_tensor(name, list(shape), dt, side="right"))
    cm = R("cm", [128, 10, 128], F32)
    identb = R("identb", [128, 128], BF16)
    rm = R("rm", [128, 4, 128], F32)
    rv = R("rv", [128, 8], F32)
    xibc = R("xibc", [128, 4, 128], F32)
    nwbc = R("nwbc", [128, 1, D], F32)
    gsmbc = R("gsmbc", [128, 136], F32)
    gconst = R("gconst", [128, 16], F32)
    IDENT, ONES, MSU, MU, TRI, SC0, SC1 = [cm[:, i, :] for i in range(7)]
    dma("sp", cm[:], cmat.rearrange("m p n -> p m n"), writes=["cm"], tag="m")
    dma("sp", rm[:], rmat.rearrange("m p n -> p m n"), writes=["rm"], tag="m")
    dma("sp", rv[:], rvec, writes=["rv"])
    for h in range(4):
        dma("sp", xibc[:, h, :], rxi[h:h + 1, :].partition_broadcast(128), writes=["xibc"], tag="b")
    dma("sp", nwbc[:, 0, :], vecs[0:1, :].partition_broadcast(128), writes=["nwbc"], tag="b")
    dma("sp", gsmbc[:], gsm.partition_broadcast(128), writes=["gsmbc"], tag="b")
    op("dve", lambda e: e.tensor_copy(out=identb[:], in_=IDENT), reads=["cm"], writes=["identb"])
    op("act", lambda e: e.activation(out=gconst[:, 0:4], in_=gsmbc[:, 0:4], func=AF.Exp), reads=["gsmbc"], writes=["gconst"])
    op("dve", lambda e: e.tensor_scalar(out=gconst[:, 0:4], in0=gconst[:, 0:4], scalar1=-1.0, scalar2=None, op0=ALU.mult),
       reads=["gconst"], writes=["gconst"])
    op("dve", lambda e: e.tensor_copy(out=gconst[:, 4:8], in_=gsmbc[:, 4:8]), reads=["gsmbc", "gconst"], writes=["gconst"])
    GNW = gsmbc[:, 8:136]

    es_mix = ExitStack()
    mix = es_mix.enter_context(nc.sbuf_tensor("mix", [128, NSLOT, D], BF16, side="left"))
    es_keep = ExitStack()
    es_ph = [ExitStack()]
    PHTAG = [""]
    def L(name, shape, dt, es=None):
        if es is None:
            name = name + PHTAG[0]
        return (es or es_ph[0]).enter_context(nc.sbuf_tensor(name, list(shape), dt, side="left"))
    K = lambda name, shape, dt: L(name, shape, dt, es_keep)

    w_in_v = w_in.rearrange("(k p) c -> p k c", p=128)
    Wg = {}
    for h in range(4):
        for X in (1, 2):
            Wg[(h, X)] = K("Wg%d%d" % (h, X), [128, 8, 128], BF16)
            c0 = X * 512 + h * 128
            dma("pool", Wg[(h, X)][:], w_in_v[:, :, c0:c0 + 128], writes=["Wg%d%d" % (h, X)], tag="w")
    Wab = K("Wab", [128, 8, 8], BF16)
    dma("pool", Wab[:], w_in_v[:, :, 2048:2056], writes=["Wab"], tag="w")
    Wrkv = [K("Wrkv%d" % h, [128, 8, 256], BF16) for h in range(4)]
    for h in range(4):
        for j, X in enumerate((1, 2)):
            c0 = 2056 + X * 512 + h * 128
            dma("pool", Wrkv[h][:, :, j * 128:(j + 1) * 128], w_in_v[:, :, c0:c0 + 128], writes=["Wrkv%d" % h], tag="w")
    gcw = K("gcw", [128, 12, 4], F32)
    dma("sp", gcw[:], gcwT.rearrange("(c p) i -> p c i", p=128), writes=["gcw"], tag="c")
    xt = [K("xt%d" % i, [128, D], F32) for i in range(2)]
    junk = K("junk", [128, D], BF16)
    nb = [K("nb%d" % i, [128, D], BF16) for i in range(2)]
    nTg = [K("nTg%d" % i, [128, 8, 512], BF16) for i in range(2)]
    st = K("st", [128, 16], F32)
    ab = K("ab", [128, 2, 4, 8], F32)
    CN = ["cG", "cLB", "cB", "cGAM", "cGL0", "cGL1", "cEG", "cEKD", "cCD0", "cCD1", "cGLB", "cGC", "cBG", "cTMP"]
    CT = {n: K(n, [128, 2, 4, 4], F32) for n in CN}
    halo = [[K("halo%d%d" % (h, X), [128, 4], F32) for X in range(3)] for h in range(4)]
    Sst = [K("S%d" % h, [128, 128], F32) for h in range(4)]
    Sb = [K("Sb%d" % h, [128, 128], BF16) for h in range(4)]
    Rst = [K("R%d" % h, [128, 128], F32) for h in range(4)]
    Rb = [K("Rb%d" % h, [128, 128], BF16) for h in range(4)]
    for h in range(4):
        op("pool", lambda e, h=h: e.memset(Sst[h][:], 0.0), writes=["S%d" % h])
        op("pool", lambda e, h=h: e.memset(Sb[h][:], 0.0), writes=["Sb%d" % h])
        op("pool", lambda e, h=h: e.memset(Rst[h][:], 0.0), writes=["R%d" % h])
        op("pool", lambda e, h=h: e.memset(Rb[h][:], 0.0), writes=["Rb%d" % h])
        for X in range(3):
            op("pool", lambda e, h=h, X=X: e.memset(halo[h][X][:], 0.0), writes=["halo%d%d" % (h, X)])

    CHN = ("kd", "ub", "wT", "ubf", "qkT", "qdT", "osb", "og", "sg")
    PH = {}

    def alloc_stream(sid, full, nchain, lb, HB, cbanks):
        nE = 2 if full else 1
        cx = {"id": sid, "lb": lb, "HB": HB, "cbanks": cbanks}
        Ld = {}
        for n, shp, dt in (("lr", [128, 4, 2], F32), ("rkb", [128, 8, 4], F32), ("Dab", [128, nE, 4, 128], F32),
                           ("E", [128, nE, 4, 128], BF16), ("tmpm", [128, 4, 128], F32),
                           ("YP0", [128, 4, 256], BF16), ("YP1", [128, 4, 256], BF16),
                           ("XX0", [128, 4, 128], BF16), ("XX1", [128, 4, 128], BF16), ("kg", [128, 4, 128], BF16),
                           ("vtok", [128, 4, 128], BF16), ("TTb", [128, 4, 128], BF16), ("TbT", [128, 4, 128], BF16)):
            Ld[n] = L("%s_%s" % (n, sid), shp, dt)
        if full:
            Ld["Eq"] = L("Eq_%s" % sid, [128, 4, 128], BF16)
        cx["L"] = Ld
        cx["XT"] = [(L("XT%s%d" % (sid, X), [128, 512], BF16) if (full or X != 0) else None) for X in range(3)]
        cx["ybuf"] = [L("ybuf%s%d" % (sid, i), [128, 512], F32) for i in range(2)]
        cx["xw"] = [L("xw%s%d" % (sid, i), [128, 515], F32) for i in range(2)]
        cx["sq"] = L("sq%s" % sid, [128, 512], F32)
        cx["ssc"] = L("ssc%s" % sid, [128, 4, 2], F32)
        cx["cs"] = []
        for i in range(nchain):
            d = {"id": "%s%d" % (sid, i)}
            lst = [("kd", [128, 4, 128], BF16), ("ub", [128, 4, 128], F32), ("wT", [128, 4, 128], BF16), ("ubf", [128, 128], BF16)]
            if full:
                lst += [("qkT", [128, 4, 128], BF16), ("qdT", [128, 4, 128], BF16),
                        ("osb", [128, 128], F32), ("og", [128, 128], F32), ("sg", [128, 128], BF16)]
            for n, shp, dt in lst:
                d[n] = L("%s_%s" % (n, d["id"]), shp, dt)
            op("pool", lambda e, d=d: e.memset(d["ubf"][:], 0.0), writes=["ubf_%s" % d["id"]])
            cx["cs"].append(d)
        return cx

    def alloc_phase(full):
        PH.clear()
        PHTAG[0] = "_F" if full else "_S"
        Rr = {}
        lst = [("rsb", [128, 256], F32), ("rt1", [128, 128], F32), ("rt2", [128, 128], F32), ("krot", [128, 128], BF16),
               ("kz", [128, 128], BF16), ("rvt", [128, 128], BF16)]
        if full:
            lst += [("qrot", [128, 128], BF16), ("rkT", [128, 128], BF16), ("rqT", [128, 128], BF16), ("rqx", [128, 128], BF16),
                    ("rPT", [128, 128], BF16), ("sgr", [128, 128], BF16), ("osb", [128, 128], F32), ("og", [128, 128], F32)]
        for n, shp, dt in lst:
            Rr[n] = L("%s_r" % n, shp, dt)
        PH["RS"] = Rr
        PH["cs_t"] = [L("cs_t%d" % i, [128, 128], F32) for i in range(2)]
        PH["sn_t"] = [L("sn_t%d" % i, [128, 128], F32) for i in range(2)]
        if full:
            for h in range(4):
                Wg[(h, 0)] = L("Wg%d0" % h, [128, 8, 128], BF16)
                dma("pool", Wg[(h, 0)][:], w_in_v[:, :, h * 128:(h + 1) * 128], writes=["Wg%d0" % h], tag="w")
            PH["Wgg"] = L("Wgg", [128, 8, 512], BF16)
            dma("pool", PH["Wgg"][:], w_in_v[:, :, 1536:2048], writes=["Wgg"], tag="w")
            PH["Wrqg"] = [L("Wrqg%d" % h, [128, 8, 256], BF16) for h in range(4)]
            for h in range(4):
                for j, X in enumerate((0, 3)):
                    c0 = 2056 + X * 512 + h * 128
                    dma("pool", PH["Wrqg"][h][:, :, j * 128:(j + 1) * 128], w_in_v[:, :, c0:c0 + 128], writes=["Wrqg%d" % h], tag="w")
            PH["streams"] = [alloc_stream("A", True, 2, [0, 1, 2], [(0, 2, 0), (1, 2, 256)], [3, 4])]
            PH["rbank"] = 5; PH["rbank2"] = 6
        else:
            PH["streams"] = [alloc_stream("A", False, 2, [0, 1, 0], [(0, 1, 0), (0, 1, 256)], [4]),
                             alloc_stream("B", False, 2, [2, 3, 2], [(2, 3, 0), (2, 3, 256)], [5])]
            PH["rbank"] = 6; PH["rbank2"] = 6

    LN_DS = math.log(HD ** -0.5)

    def rms_rstd(src_ap, rkeys, n, col, ckey, jout=None, jkey="junk"):
        jo = junk[:, 0:n] if jout is None else jout
        op("act", lambda e: e.activation(out=jo, in_=src_ap, func=AF.Square, accum_out=col),
           reads=rkeys, writes=[jkey, ckey])
        op("act", lambda e: e.activation(out=col, in_=col, func=AF.Ln, bias=EPS, scale=1.0 / n),
           reads=[ckey], writes=[ckey])
        op("act", lambda e: e.activation(out=col, in_=col, func=AF.Exp, scale=-0.5), reads=[ckey], writes=[ckey])

    def norm_a(t):
        xb_ = xt[t % 2]; xk = "xt%d" % (t % 2); nb_ = nb[t % 2]; nbk = "nb%d" % (t % 2)
        dma("sp", xb_[:], xpad[t * 128:(t + 1) * 128, :], writes=[xk])
        rms_rstd(xb_[:], [xk], D, st[:, 0:1], "st0", jout=nb_[:], jkey=nbk)
        op("dve", lambda e: e.scalar_tensor_tensor(out=nb_[:], in0=xb_[:], scalar=st[:, 0:1], in1=nwbc[:, 0, :],
                                                   op0=ALU.mult, op1=ALU.mult), reads=[xk, "st0", "nwbc"], writes=[nbk])

    def norm_b(t, gi, tt):
        nb_ = nb[t % 2]; nbk = "nb%d" % (t % 2)
        pb, pk = bank[7], "ps7"
        for k in range(8):
            op("pe", lambda e, k=k: e.transpose(out=pb[:, k * 128:(k + 1) * 128], in_=nb_[:, k * 128:(k + 1) * 128],
                                               identity=identb[:]), reads=[nbk, "identb"], writes=[pk])
        nk = "nTg%d" % gi
        op("act", lambda e: e.copy(out=nTg[gi][:, :, tt * 128:(tt + 1) * 128],
                                   in_=pb[:, :].rearrange("p (k n) -> p k n", k=8)), reads=[pk], writes=[nk])

    def gdn_common(gi, full):
        nk = "nTg%d" % gi
        C = {n: CT[n][:, gi] for n in CN}
        ck = lambda n: "%s_%d" % (n, gi)
        abk = "ab%d" % gi
        for tt in range(4):
            pf, fk = bankF[7], "ps7"
            for k in range(8):
                op("pe", lambda e, k=k: e.matmul(out=pf[:, 0:8], lhsT=nTg[gi][:, k, tt * 128:(tt + 1) * 128],
                                                rhs=Wab[:, k, :], start=(k == 0), stop=(k == 7)),
                   reads=[nk, "Wab"], writes=[fk])
            op("dve", lambda e: e.tensor_copy(out=ab[:, gi, tt, :], in_=pf[:, 0:8]), reads=[fk], writes=[abk])
            yield
        for h in range(4):
            op("dve", lambda e, h=h: e.tensor_scalar(out=C["cTMP"][:, :, h], in0=ab[:, gi, :, h], scalar1=gconst[:, 4 + h:5 + h],
                                                    scalar2=None, op0=ALU.add), reads=[abk, "gconst"], writes=[ck("cTMP")])
        op("act", lambda e: e.activation(out=C["cTMP"], in_=C["cTMP"], func=AF.Exp), reads=[ck("cTMP")], writes=[ck("cTMP")])
        op("act", lambda e: e.activation(out=C["cTMP"], in_=C["cTMP"], func=AF.Ln, bias=1.0), reads=[ck("cTMP")], writes=[ck("cTMP")])
        for h in range(4):
            op("dve", lambda e, h=h: e.tensor_scalar(out=C["cG"][:, :, h], in0=C["cTMP"][:, :, h], scalar1=gconst[:, h:h + 1],
                                                    scalar2=None, op0=ALU.mult), reads=[ck("cTMP"), "gconst"], writes=[ck("cG")])
        op("act", lambda e: e.activation(out=C["cLB"], in_=ab[:, gi, :, 4:8], func=AF.Exp, scale=-1.0), reads=[abk], writes=[ck("cLB")])
        op("act", lambda e: e.activation(out=C["cLB"], in_=C["cLB"], func=AF.Ln, bias=1.0), reads=[ck("cLB")], writes=[ck("cLB")])
        op("act", lambda e: e.activation(out=C["cB"], in_=C["cLB"], func=AF.Exp, scale=-1.0), reads=[ck("cLB")], writes=[ck("cB")])
        op("dve", lambda e: e.tensor_scalar(out=C["cLB"], in0=C["cLB"], scalar1=-1.0, scalar2=None, op0=ALU.mult), reads=[ck("cLB"), ck("cB")], writes=[ck("cLB")])
        yield
        gflat = C["cG"].rearrange("p t h -> p (t h)")
        for lhs, dn in ((TRI, "cGAM"), (SC0, "cGL0"), (SC1, "cGL1")):
            pf, fk = bankF[7], "ps7"
            op("pe", lambda e, lhs=lhs: e.matmul(out=pf[:, 0:16], lhsT=lhs, rhs=gflat, start=True, stop=True),
               reads=["cm", ck("cG")], writes=[fk])
            op("dve", lambda e, dn=dn: e.tensor_copy(out=C[dn].rearrange("p t h -> p (t h)"), in_=pf[:, 0:16]),
               reads=[fk], writes=[ck(dn)])
        yield
        op("act", lambda e: e.activation(out=C["cEG"], in_=C["cGAM"], func=AF.Exp), reads=[ck("cGAM")], writes=[ck("cEG")])
        op("act", lambda e: e.activation(out=C["cCD0"], in_=C["cGL0"], func=AF.Exp), reads=[ck("cGL0")], writes=[ck("cCD0")])
        op("act", lambda e: e.activation(out=C["cCD1"], in_=C["cGL1"], func=AF.Exp), reads=[ck("cGL1")], writes=[ck("cCD1")])
        op("dve", lambda e: e.tensor_tensor(out=C["cEKD"][0:64], in0=C["cGL0"][0:64], in1=C["cGAM"][0:64], op=ALU.subtract),
           reads=[ck("cGL0"), ck("cGAM")], writes=[ck("cEKD")])
        op("dve", lambda e: e.tensor_tensor(out=C["cEKD"][64:128], in0=C["cGL1"][64:128], in1=C["cGAM"][64:128], op=ALU.subtract),
           reads=[ck("cGL1"), ck("cGAM"), ck("cEKD")], writes=[ck("cEKD")])
        op("act", lambda e: e.activation(out=C["cEKD"], in_=C["cEKD"], func=AF.Exp), reads=[ck("cEKD")], writes=[ck("cEKD")])
        op("dve", lambda e: e.tensor_tensor(out=C["cGLB"], in0=C["cGAM"], in1=C["cLB"], op=ALU.add), reads=[ck("cGAM"), ck("cLB")], writes=[ck("cGLB")])
        op("dve", lambda e: e.tensor_scalar(out=C["cGC"], in0=C["cGAM"], scalar1=LN_DS, scalar2=None, op0=ALU.add),
           reads=[ck("cGAM")], writes=[ck("cGC")])
        op("dve", lambda e: e.tensor_tensor(out=C["cBG"], in0=C["cB"], in1=C["cEG"], op=ALU.mult), reads=[ck("cB"), ck("cEG")], writes=[ck("cBG")])
        yield

    def gdn_proj(gi, h, full, cx):
        nk = "nTg%d" % gi
        sid = cx["id"]; lb = cx["lb"]
        sq_ = cx["sq"]; sqk = "sq%s" % sid
        Xs = (1, 2, 0) if full else (1, 2)

        def s1(i, X):
            xb_ = cx["xw"][i % 2]; xk = "xw%s%d" % (sid, i % 2)
            wk = "Wg%d%d" % (h, X); hk_ = "halo%d%d" % (h, X)
            op("pool", lambda e: e.tensor_copy(out=xb_[:, 0:3], in_=halo[h][X][:, 0:3]), reads=[hk_, xk], writes=[xk])
            pf, fk = bankF[lb[i % 3]], "ps%d" % lb[i % 3]
            for k in range(8):
                op("pe", lambda e, k=k: e.matmul(out=pf[:, :], lhsT=Wg[(h, X)][:, k, :], rhs=nTg[gi][:, k, :],
                                                start=(k == 0), stop=(k == 7)), reads=[wk, nk], writes=[fk])
            op("act", lambda e: e.copy(out=xb_[:, 3:515], in_=pf[:, :]), reads=[fk, xk], writes=[xk])

        def s2(i, X):
            xb_ = cx["xw"][i % 2]; xk = "xw%s%d" % (sid, i % 2)
            yb_ = cx["ybuf"][i % 2]; yk = "ybuf%s%d" % (sid, i % 2)
            tk = "XT%s%d" % (sid, X); hk_ = "halo%d%d" % (h, X)
            cw = lambda j: gcw[:, X * 4 + h, j:j + 1]
            op("dve", lambda e: e.tensor_scalar(out=yb_[:], in0=xb_[:, 3:515], scalar1=cw(3), scalar2=None, op0=ALU.mult),
               reads=[xk, "gcw"], writes=[yk])
            for j in (2, 1, 0):
                op("dve", lambda e, j=j: e.scalar_tensor_tensor(out=yb_[:], in0=xb_[:, j:j + 512], scalar=cw(j), in1=yb_[:],
                                                               op0=ALU.mult, op1=ALU.add), reads=[xk, "gcw", yk], writes=[yk])
            op("pool", lambda e: e.tensor_copy(out=halo[h][X][:, 0:3], in_=xb_[:, 512:515]), reads=[xk], writes=[hk_])
            op("act", lambda e: e.activation(out=cx["XT"][X][:], in_=yb_[:], func=AF.Silu), reads=[yk], writes=[tk])
            if X != 2:
                op("act", lambda e: e.activation(out=yb_[:], in_=yb_[:], func=AF.Silu), reads=[yk], writes=[yk])
                op("act", lambda e: e.activation(out=sq_[:], in_=yb_[:], func=AF.Square), reads=[yk], writes=[sqk])

        def s3(i, X):
            if X == 2:
                return
            j = 0 if X == 1 else 1
            pf2, fk2 = bankF[lb[(i + 1) % 3]], "ps%d" % lb[(i + 1) % 3]
            for tt in range(4):
                op("pe", lambda e, tt=tt: e.matmul(out=pf2[:, tt:tt + 1], lhsT=sq_[:, tt * 128:(tt + 1) * 128], rhs=cm[:, 1, 0:1], start=True, stop=True),
                   reads=[sqk, "cm"], writes=[fk2])
            op("dve", lambda e: e.tensor_copy(out=cx["ssc"][:, :, j], in_=pf2[:, 0:4]), reads=[fk2], writes=["ssc%s" % sid])

        n = len(Xs)
        for r in range(n + 3):
            if 0 <= r - 3 < n:
                s3(r - 3, Xs[r - 3])
            if 0 <= r - 1 < n:
                s2(r - 1, Xs[r - 1])
            if r < n:
                s1(r, Xs[r])
            yield

    def gdn_local(g, h, cx, cs):
        gi = g % 2; full = g >= G_FULL
        B = dict(cx["L"]); B.update(cs)
        sid = cx["id"]; lb = cx["lb"]
        key = lambda n: ("%s_%s" % (n, cs["id"])) if n in CHN else ("%s_%s" % (n, sid))
        C = {n: CT[n][:, gi] for n in CN}
        ck = lambda n: "%s_%d" % (n, gi)
        bF = lambda i: (bankF[lb[i]], "ps%d" % lb[i])
        bB = lambda i: (bank[lb[i]], "ps%d" % lb[i])
        kTa, vTa, qTa = cx["XT"][1], cx["XT"][2], cx["XT"][0]
        kTk, vTk, qTk = "XT%s1" % sid, "XT%s2" % sid, "XT%s0" % sid
        lr, rkb, Dab, E, tmpm = B["lr"], B["rkb"], B["Dab"], B["E"], B["tmpm"]
        YP = [B["YP0"], B["YP1"]]; XX = [B["XX0"], B["XX1"]]
        kg, kd, vtok, TTb, TbT, ub, wT = B["kg"], B["kd"], B["vtok"], B["TTb"], B["TbT"], B["ub"], B["wT"]
        Eq, qkT, qdT = B.get("Eq"), B.get("qkT"), B.get("qdT")
        hc = lambda n: C[n][:, :, h]
        bc4 = lambda ap2: ap2.unsqueeze(2).broadcast_to([128, 4, 128])
        rep4 = lambda ap2: ap2.unsqueeze(1).broadcast_to([128, 4, 128])
        v4 = lambda ap, w=128: ap.rearrange("p (a b) -> p a b", a=4)
        nc_ = 2 if full else 1
        sk = "ssc%s" % sid
        op("act", lambda e: e.activation(out=lr[:, :, 0:nc_], in_=cx["ssc"][:, :, 0:nc_], func=AF.Ln, bias=EPS), reads=[sk], writes=[key("lr")])
        op("act", lambda e: e.activation(out=rkb[:, 0, :], in_=lr[:, :, 0], func=AF.Exp, scale=-0.5), reads=[key("lr")], writes=[key("rk0")])
        op("dve", lambda e: e.scalar_tensor_tensor(out=rkb[:, 1, :], in0=lr[:, :, 0], scalar=-0.5, in1=hc("cGLB"), op0=ALU.mult, op1=ALU.add),
           reads=[key("lr"), ck("cGLB")], writes=[key("rk1")])
        op("dve", lambda e: e.tensor_tensor(out=rkb[:, 3, :], in0=rkb[:, 0, :], in1=hc("cBG"), op=ALU.mult), reads=[key("rk0"), ck("cBG")], writes=[key("rk3")])
        op("dve", lambda e: e.tensor_tensor(out=rkb[:, 4, :], in0=rkb[:, 0, :], in1=hc("cEKD"), op=ALU.mult), reads=[key("rk0"), ck("cEKD")], writes=[key("rk4")])
        op("dve", lambda e: e.scalar_tensor_tensor(out=rkb[:, 6, :], in0=lr[:, :, 0], scalar=-0.5, in1=hc("cGAM"), op0=ALU.mult, op1=ALU.subtract),
           reads=[key("lr"), ck("cGAM")], writes=[key("rk6")])
        if full:
            op("dve", lambda e: e.scalar_tensor_tensor(out=rkb[:, 2, :], in0=lr[:, :, 1], scalar=-0.5, in1=hc("cGC"), op0=ALU.mult, op1=ALU.add),
               reads=[key("lr"), ck("cGC")], writes=[key("rk2")])
        yield
        pb, pk = bB(0)
        for tt in range(4):
            op("pe", lambda e, tt=tt: e.transpose(out=pb[:, tt * 128:(tt + 1) * 128], in_=kTa[:, tt * 128:(tt + 1) * 128], identity=identb[:]),
               reads=[kTk, "identb"], writes=[pk])
        for tt in range(4):
            op("pe", lambda e, tt=tt: e.transpose(out=pb[:, 512 + tt * 128:512 + (tt + 1) * 128], in_=vTa[:, tt * 128:(tt + 1) * 128], identity=identb[:]),
               reads=[vTk, "identb"], writes=[pk])
        op("dve", lambda e: e.tensor_tensor(out=kg[:], in0=v4(pb[:, 0:512]), in1=bc4(rkb[:, 3, :]), op=ALU.mult), reads=[pk, key("rk3")], writes=[key("kg")])
        op("dve", lambda e: e.tensor_tensor(out=kd[:], in0=v4(pb[:, 0:512]), in1=bc4(rkb[:, 4, :]), op=ALU.mult), reads=[pk, key("rk4")], writes=[key("kd")])
        op("act", lambda e: e.copy(out=vtok[:], in_=v4(pb[:, 512:1024])), reads=[pk], writes=[key("vtok")])
        yield
        op("pool", lambda e: e.tensor_tensor(out=Dab[:, 0], in0=rep4(IDENT), in1=bc4(rkb[:, 1, :]), op=ALU.mult), reads=["cm", key("rk1")], writes=[key("Dab0")])
        pbc, bck = bF(1)
        op("pe", lambda e: e.matmul(out=pbc[:, :], lhsT=ONES, rhs=Dab[:, 0].rearrange("p a b -> p (a b)"), start=True, stop=True),
           reads=["cm", key("Dab0")], writes=[bck])
        if full:
            op("pool", lambda e: e.tensor_tensor(out=Dab[:, 1], in0=rep4(IDENT), in1=bc4(rkb[:, 2, :]), op=ALU.mult), reads=["cm", key("rk2")], writes=[key("Dab1")])
            pbc2, bck2 = bF(2)
            op("pe", lambda e: e.matmul(out=pbc2[:, :], lhsT=ONES, rhs=Dab[:, 1].rearrange("p a b -> p (a b)"), start=True, stop=True),
               reads=["cm", key("Dab1")], writes=[bck2])
        op("dve", lambda e: e.tensor_tensor(out=tmpm[:], in0=v4(pbc[:, :]), in1=rep4(cm[:, 7, :]), op=ALU.add), reads=[bck, "cm"],
           writes=[key("tmpm"), key("tmpm") + "h0", key("tmpm") + "h1"])
        for tt in range(4):
            op("act", lambda e, tt=tt: e.activation(out=E[:, 0, tt, :], in_=tmpm[:, tt, :], func=AF.Exp, bias=rkb[:, 6, tt:tt + 1]),
               reads=[key("tmpm"), key("rk6")], writes=[key("E0")])
        yield
        pkk, kkk = bF(0)
        for tt in range(4):
            sl = slice(tt * 128, (tt + 1) * 128)
            op("pe", lambda e, sl=sl: e.matmul(out=pkk[:, sl], lhsT=kTa[:, sl], rhs=kTa[:, sl], start=True, stop=True), reads=[kTk], writes=[kkk])
        Y0 = YP[0][:, :, 0:128]
        op("dve", lambda e: e.scalar_tensor_tensor(out=Y0, in0=v4(pkk[:, :]), scalar=-1.0, in1=E[:, 0], op0=ALU.mult, op1=ALU.mult),
           reads=[kkk, key("E0")], writes=[key("YP0a"), key("YP0a") + "h0", key("YP0a") + "h1"])
        if full:
            op("dve", lambda e: e.tensor_tensor(out=tmpm[:], in0=v4(pbc2[:, :]), in1=rep4(cm[:, 8, :]), op=ALU.add), reads=[bck2, "cm", key("tmpm")], writes=[key("tmpm")])
            op("act", lambda e: e.activation(out=Eq[:], in_=v4(pbc2[:, :]), func=AF.Exp), reads=[bck2], writes=[key("Eq")])
            for tt in range(4):
                op("act", lambda e, tt=tt: e.activation(out=E[:, 1, tt, :], in_=tmpm[:, tt, :], func=AF.Exp, bias=rkb[:, 6, tt:tt + 1]),
                   reads=[key("tmpm"), key("rk6")], writes=[key("E1")])
            pqk, qkk = bF(1)
            for tt in range(4):
                sl = slice(tt * 128, (tt + 1) * 128)
                op("pe", lambda e, sl=sl: e.matmul(out=pqk[:, sl], lhsT=kTa[:, sl], rhs=qTa[:, sl], start=True, stop=True), reads=[kTk, qTk], writes=[qkk])
            op("dve", lambda e: e.tensor_tensor(out=qkT[:], in0=v4(pqk[:, :]), in1=E[:, 1], op=ALU.mult), reads=[qkk, key("E1")], writes=[key("qkT")])
            op("pool", lambda e: e.tensor_tensor(out=qdT[:], in0=v4(qTa[:, :]), in1=Eq[:], op=ALU.mult), reads=[qTk, key("Eq")], writes=[key("qdT")])
        yield "FRONT_DONE"
        HB = cx["HB"]
        bR = lambda i: (bankF[i], "ps%d" % i)
        hk = lambda n, hh: key(n) + "h%d" % hh
        px, pxk = bB(2)
        for tt in range(4):
            op("pe", lambda e, tt=tt: e.transpose(out=px[:, tt * 128:(tt + 1) * 128], in_=YP[0][:, tt, 0:128], identity=identb[:]),
               reads=[key("YP0a"), "identb"], writes=[pxk])
        for hh in range(2):
            tsl = slice(2 * hh, 2 * hh + 2)
            op("act", lambda e, tsl=tsl, hh=hh: e.copy(out=XX[0][:, tsl, :], in_=v4(px[:, 0:512])[:, tsl, :]), reads=[pxk], writes=[hk("XX0", hh)])
            op("pool", lambda e, tsl=tsl: e.tensor_tensor(out=YP[1][:, tsl, 128:256], in0=YP[0][:, tsl, 0:128], in1=identb[:].unsqueeze(1).broadcast_to([128, 2, 128]), op=ALU.add),
               reads=[key("YP0a"), "identb"], writes=[hk("YP1b", hh)])
        yield
        for hh in range(2):
            yb_, xb2, xo = HB[hh]
            py, pyk = bR(yb_); pz, pzk = bR(xb2)
            for tt in (2 * hh, 2 * hh + 1):
                o0 = (tt % 2) * 128
                op("pe", lambda e, tt=tt, o0=o0, py=py: e.matmul(out=py[:, o0:o0 + 128], lhsT=XX[0][:, tt, :], rhs=YP[0][:, tt, 0:128], start=True, stop=True),
                   reads=[hk("XX0", hh), key("YP0a")], writes=[pyk])
                op("pe", lambda e, tt=tt, o0=o0, pz=pz, xo=xo: e.matmul(out=pz[:, xo + o0:xo + o0 + 128], lhsT=YP[0][:, tt, 0:128], rhs=XX[0][:, tt, :], start=True, stop=True),
                   reads=[hk("XX0", hh), key("YP0a")], writes=[pzk])
        for hh in range(2):
            yb_, xb2, xo = HB[hh]
            py, pyk = bR(yb_); pz, pzk = bR(xb2)
            tsl = slice(2 * hh, 2 * hh + 2)
            op("dve", lambda e, tsl=tsl, py=py: e.tensor_copy(out=YP[1][:, tsl, 0:128], in_=py[:, 0:256].rearrange("p (a b) -> p a b", a=2)),
               reads=[pyk], writes=[hk("YP1a", hh)])
            op("act", lambda e, tsl=tsl, pz=pz, xo=xo: e.copy(out=XX[1][:, tsl, :], in_=pz[:, xo:xo + 256].rearrange("p (a b) -> p a b", a=2)),
               reads=[pzk], writes=[hk("XX1", hh)])
        yield
        cur = 1
        for lvl in range(2, 6):
            c_, n_ = cur, 1 - cur
            last = (lvl == 5)
            for hh in range(2):
                yb_, xb2, xo = HB[hh]
                pv_, pvk = bR(yb_); pxx, pxxk = bR(xb2)
                for tt in (2 * hh, 2 * hh + 1):
                    o0 = (tt % 2) * 256
                    if not last:
                        op("pe", lambda e, tt=tt, c_=c_, pv_=pv_, o0=o0: e.matmul(out=pv_[:, o0:o0 + 256], lhsT=XX[c_][:, tt, :], rhs=YP[c_][:, tt, :], start=True, stop=True),
                           reads=[hk("XX%d" % c_, hh), hk("YP%da" % c_, hh), hk("YP%db" % c_, hh)], writes=[pvk])
                    else:
                        op("pe", lambda e, tt=tt, c_=c_, pv_=pv_, o0=o0: e.matmul(out=pv_[:, o0 + 128:o0 + 256], lhsT=XX[c_][:, tt, :], rhs=YP[c_][:, tt, 128:256], start=True, stop=True),
                           reads=[hk("XX%d" % c_, hh), hk("YP%db" % c_, hh)], writes=[pvk])
                for tt in (2 * hh, 2 * hh + 1):
                    o1 = (tt % 2) * 128
                    op("pe", lambda e, tt=tt, c_=c_, pxx=pxx, o1=o1, xo=xo: e.matmul(out=pxx[:, xo + o1:xo + o1 + 128], lhsT=YP[c_][:, tt, 0:128], rhs=XX[c_][:, tt, :], start=True, stop=True),
                       reads=[hk("XX%d" % c_, hh), hk("YP%da" % c_, hh)], writes=[pxxk])
            for hh in range(2):
                yb_, xb2, xo = HB[hh]
                pv_, pvk = bR(yb_); pxx, pxxk = bR(xb2)
                pv3 = pv_[:, :].rearrange("p (a b) -> p a b", a=2)
                tsl = slice(2 * hh, 2 * hh + 2)
                if not last:
                    op("act", lambda e, n_=n_, pv3=pv3, tsl=tsl: e.copy(out=YP[n_][:, tsl, 0:128], in_=pv3[:, :, 0:128]), reads=[pvk], writes=[hk("YP%da" % n_, hh)] + ([key("YP0a")] if n_ == 0 else []))
                op("dve", lambda e, c_=c_, n_=n_, pv3=pv3, tsl=tsl: e.tensor_tensor(out=YP[n_][:, tsl, 128:256], in0=YP[c_][:, tsl, 128:256], in1=pv3[:, :, 128:256], op=ALU.add),
                   reads=[pvk, hk("YP%db" % c_, hh)], writes=[hk("YP%db" % n_, hh)])
                op("act", lambda e, n_=n_, pxx=pxx, tsl=tsl, xo=xo: e.copy(out=XX[n_][:, tsl, :], in_=pxx[:, xo:xo + 256].rearrange("p (a b) -> p a b", a=2)),
                   reads=[pxxk], writes=[hk("XX%d" % n_, hh)])
            cur = n_
            yield
        c_ = cur
        for hh in range(2):
            yb_, xb2, xo = HB[hh]
            py, pyk = bR(yb_)
            tsl = slice(2 * hh, 2 * hh + 2)
            for tt in (2 * hh, 2 * hh + 1):
                o1 = (tt % 2) * 128
                op("pe", lambda e, tt=tt, py=py, o1=o1: e.matmul(out=py[:, o1:o1 + 128], lhsT=XX[c_][:, tt, :], rhs=YP[c_][:, tt, 128:256], start=True, stop=True),
                   reads=[hk("XX%d" % c_, hh), hk("YP%db" % c_, hh)], writes=[pyk])
            op("dve", lambda e, py=py, tsl=tsl: e.tensor_tensor(out=tmpm[:, tsl, :], in0=YP[c_][:, tsl, 128:256], in1=py[:, 0:256].rearrange("p (a b) -> p a b", a=2), op=ALU.add),
               reads=[pyk, hk("YP%db" % c_, hh), key("tmpm")], writes=[key("tmpm") + "h%d" % hh])
        tmk = [key("tmpm") + "h0", key("tmpm") + "h1"]
        op("act", lambda e: e.copy(out=TTb[:], in_=tmpm[:]), reads=tmk, writes=[key("TTb")])
        op("pool", lambda e: e.tensor_tensor(out=TbT[:], in0=tmpm[:], in1=bc4(hc("cB")), op=ALU.mult), reads=tmk + [ck("cB")], writes=[key("TbT")])
        yield
        pu, puk = bF(1)
        pw_, pwk_ = bF(2)
        for tt in range(4):
            sl = slice(tt * 128, (tt + 1) * 128)
            op("pe", lambda e, tt=tt, sl=sl: e.matmul(out=pu[:, sl], lhsT=TbT[:, tt, :], rhs=vtok[:, tt, :], start=True, stop=True),
               reads=[key("TbT"), key("vtok")], writes=[puk])
        for tt in range(4):
            sl = slice(tt * 128, (tt + 1) * 128)
            op("pe", lambda e, tt=tt, sl=sl: e.matmul(out=pw_[:, sl], lhsT=kg[:, tt, :], rhs=TTb[:, tt, :], start=True, stop=True),
               reads=[key("kg"), key("TTb")], writes=[pwk_])
        op("act", lambda e: e.copy(out=ub[:], in_=v4(pu[:, :])), reads=[puk], writes=[key("ub")])
        op("dve", lambda e: e.tensor_copy(out=wT[:], in_=v4(pw_[:, :])), reads=[pwk_], writes=[key("wT")])
        yield

    def gdn_chain(g, h, cx, cs):
        gi = g % 2; full = g >= G_FULL
        B = cs
        key = lambda n: "%s_%s" % (n, cs["id"])
        cbk = cx["cbanks"]
        C = {n: CT[n][:, gi] for n in CN}
        ck = lambda n: "%s_%d" % (n, gi)
        bF = lambda i: (bankF[i], "ps%d" % i)
        kd, ub, wT, ubf = B["kd"], B["ub"], B["wT"], B["ubf"]
        qkT, qdT, osb, og, sg = B.get("qkT"), B.get("qdT"), B.get("osb"), B.get("og"), B.get("sg")
        Sk, Sbk = "S%d" % h, "Sb%d" % h
        flip = 0
        for tt in range(4):
            t = 4 * g + tt
            store = full and t >= T0
            for hf in range(2):
                rows = slice(hf * 64, hf * 64 + 64)
                cdn = "cCD0" if hf == 0 else "cCD1"
                cd = C[cdn][:, tt, h:h + 1]
                pw, pwk = bF(cbk[flip % len(cbk)]); ps_, psk = bF(cbk[(flip + 1) % len(cbk)]); flip = 1 - flip
                op("pe", lambda e: e.matmul(out=pw[:, 0:128], lhsT=wT[:, tt, :], rhs=Sb[h][:], start=True, stop=True), reads=[key("wT"), Sbk], writes=[pwk])
                op("dve", lambda e: e.tensor_tensor(out=ubf[rows, :], in0=ub[rows, tt, :], in1=pw[rows, 0:128], op=ALU.subtract),
                   reads=[key("ub"), pwk], writes=[key("ubf")])
                if store:
                    op("pe", lambda e: e.matmul(out=pw[:, 128:256], lhsT=qdT[:, tt, :], rhs=Sb[h][:], start=True, stop=False), reads=[key("qdT"), Sbk, pwk], writes=[pwk])
                    op("pe", lambda e: e.matmul(out=pw[:, 128:256], lhsT=qkT[:, tt, :], rhs=ubf[:], start=False, stop=True),
                       reads=[key("qkT"), key("ubf"), pwk], writes=[pwk])
                    op("act", lambda e: e.copy(out=osb[rows, :], in_=pw[rows, 128:256]), reads=[pwk], writes=[key("osb")])
                yield
                op("pe", lambda e: e.matmul(out=ps_[:, 384:512], lhsT=kd[rows, tt, :], rhs=ubf[rows, :], start=True, stop=True), reads=[key("kd"), key("ubf")], writes=[psk])
                op("dve", lambda e: e.scalar_tensor_tensor(out=Sb[h][:], in0=Sst[h][:], scalar=cd, in1=ps_[:, 384:512], op0=ALU.mult, op1=ALU.add),
                   reads=[Sk, ck(cdn), psk], writes=[Sbk])
                op("dve", lambda e: e.scalar_tensor_tensor(out=Sst[h][:], in0=Sst[h][:], scalar=cd, in1=ps_[:, 384:512], op0=ALU.mult, op1=ALU.add),
                   reads=[Sk, ck(cdn), psk], writes=[Sk])
                yield
            if store:
                s_ = t - T0
                pg, pgk = bF(cbk[flip % len(cbk)])
                for k in range(8):
                    op("pe", lambda e, k=k: e.matmul(out=pg[:, 256:384], lhsT=nTg[gi][:, k, tt * 128:(tt + 1) * 128],
                                                    rhs=PH["Wgg"][:, k, h * 128:(h + 1) * 128], start=(k == 0), stop=(k == 7)),
                       reads=["nTg%d" % gi, "Wgg"], writes=[pgk])
                op("act", lambda e: e.activation(out=sg[:], in_=pg[:, 256:384], func=AF.Silu), reads=[pgk], writes=[key("sg")])
                sc = st[:, 4 + h:5 + h]; sck = "st%d" % (4 + h)
                rms_rstd(osb[:], [key("osb")], HD, sc, sck)
                op("dve", lambda e: e.scalar_tensor_tensor(out=og[:], in0=osb[:], scalar=sc, in1=GNW, op0=ALU.mult, op1=ALU.mult),
                   reads=[key("osb"), sck, "gsmbc"], writes=[key("og")])
                op("pool", lambda e: e.tensor_tensor(out=mix[:, s_, h * 128:(h + 1) * 128], in0=og[:], in1=sg[:], op=ALU.mult),
                   reads=[key("og"), key("sg")], writes=["mix%d" % h])
                yield

    def gdn_task(g, heads, cx, prefetch):
        gi = g % 2; full = g >= G_FULL
        prev = None
        if cx.get("prefetched") != g:
            for _ in gdn_proj(gi, heads[0], full, cx):
                yield
        for i, h in enumerate(heads):
            cs = cx["cs"][i % len(cx["cs"])]
            subs = [gdn_local(g, h, cx, cs)] + ([prev] if prev is not None else [])
            while subs:
                for sg_ in list(subs):
                    try:
                        r = next(sg_)
                        if r == "FRONT_DONE" and i + 1 < len(heads):
                            subs.append(gdn_proj(gi, heads[i + 1], full, cx))
                    except StopIteration:
                        subs.remove(sg_)
                yield
            prev = gdn_chain(g, h, cx, cs)
        subs = [prev]
        if prefetch:
            subs.append(gdn_proj((g + 1) % 2, heads[0], full, cx))
            cx["prefetched"] = g + 1
        while subs:
            for sg_ in list(subs):
                try:
                    next(sg_)
                except StopIteration:
                    subs.remove(sg_)
            yield

    def ret_tile(t, tt, gi, h, full, ci):
        nk = "nTg%d" % gi
        Rr = PH["RS"]; rbk = PH["rbank"]; rbk2 = PH["rbank2"]
        cs_t, sn_t = PH["cs_t"], PH["sn_t"]
        rk_ = lambda n: "%s_r" % n
        rsb, rt1, rt2, krot, kz, rvt = Rr["rsb"], Rr["rt1"], Rr["rt2"], Rr["krot"], Rr["kz"], Rr["rvt"]
        qrot, rkT, rqT, rqx, rPT, sgr, osb, og = [Rr.get(n) for n in ("qrot", "rkT", "rqT", "rqx", "rPT", "sgr", "osb", "og")]
        csk, snk = "cs_t%d" % ci, "sn_t%d" % ci
        store = full and t >= T0
        pf, fk = bankF[rbk], "ps%d" % rbk
        for k in range(8):
            op("pe", lambda e, k=k: e.matmul(out=pf[:, 0:256], lhsT=nTg[gi][:, k, tt * 128:(tt + 1) * 128], rhs=Wrkv[h][:, k, :],
                                            start=(k == 0), stop=(k == 7)), reads=[nk, "Wrkv%d" % h], writes=[fk])
        if store:
            for k in range(8):
                op("pe", lambda e, k=k: e.matmul(out=pf[:, 256:512], lhsT=nTg[gi][:, k, tt * 128:(tt + 1) * 128], rhs=PH["Wrqg"][h][:, k, :],
                                                start=(k == 0), stop=(k == 7)), reads=[nk, "Wrqg%d" % h, fk], writes=[fk])
        op("act", lambda e: e.copy(out=rsb[:, 0:128], in_=pf[:, 0:128]), reads=[fk], writes=[rk_("rsb")])
        op("act", lambda e: e.copy(out=rvt[:], in_=pf[:, 128:256]), reads=[fk], writes=[rk_("rvt")])
        if store:
            op("act", lambda e: e.copy(out=rsb[:, 128:256], in_=pf[:, 256:384]), reads=[fk, rk_("rsb")], writes=[rk_("rsb")])
        if store:
            op("act", lambda e: e.activation(out=sgr[:], in_=pf[:, 384:512], func=AF.Silu), reads=[fk], writes=[rk_("sgr")])
        yield

        def rotary(src, dst, dstk):
            op("dve", lambda e: e.tensor_tensor(out=rt1[:], in0=src, in1=cs_t[ci][:], op=ALU.mult), reads=[rk_("rsb"), csk], writes=[rk_("rt1")])
            op("dve", lambda e: e.tensor_tensor(out=rt2[:, 0:128:2], in0=src[:, 1:128:2], in1=sn_t[ci][:, 0:128:2], op=ALU.mult),
               reads=[rk_("rsb"), snk], writes=[rk_("rt2e")])
            op("dve", lambda e: e.tensor_tensor(out=rt2[:, 1:128:2], in0=src[:, 0:128:2], in1=sn_t[ci][:, 1:128:2], op=ALU.mult),
               reads=[rk_("rsb"), snk], writes=[rk_("rt2o")])
            op("pool", lambda e: e.tensor_tensor(out=dst[:], in0=rt1[:], in1=rt2[:], op=ALU.add),
               reads=[rk_("rt1"), rk_("rt2e"), rk_("rt2o")], writes=[dstk])
        rotary(rsb[:, 0:128], krot, rk_("krot"))
        op("act", lambda e: e.activation(out=kz[:], in_=krot[:], func=AF.Copy, scale=rv[:, h:h + 1]), reads=[rk_("krot"), "rv"], writes=[rk_("kz")])
        Rk, Rbk = "R%d" % h, "Rb%d" % h
        yield
        if store:
            rotary(rsb[:, 128:256], qrot, rk_("qrot"))
            pb, pk = bank[rbk2], "ps%d" % rbk2
            op("pe", lambda e: e.transpose(out=pb[:, 0:128], in_=krot[:], identity=identb[:]), reads=[rk_("krot"), "identb"], writes=[pk])
            op("pe", lambda e: e.transpose(out=pb[:, 128:256], in_=qrot[:], identity=identb[:]), reads=[rk_("qrot"), "identb"], writes=[pk])
            op("act", lambda e: e.copy(out=rkT[:], in_=pb[:, 0:128]), reads=[pk], writes=[rk_("rkT")])
            op("act", lambda e: e.copy(out=rqT[:], in_=pb[:, 128:256]), reads=[pk], writes=[rk_("rqT")])
            op("pool", lambda e: e.tensor_tensor(out=rqx[:], in0=rqT[:], in1=xibc[:, h, :], op=ALU.mult), reads=[rk_("rqT"), "xibc"], writes=[rk_("rqx")])
            yield
            psc, sck = bankF[rbk2], "ps%d" % rbk2
            op("pe", lambda e: e.matmul(out=psc[:, 0:128], lhsT=rkT[:], rhs=rqT[:], start=True, stop=True), reads=[rk_("rkT"), rk_("rqT")], writes=[sck])
            op("dve", lambda e: e.tensor_tensor(out=rPT[:], in0=psc[:, 0:128], in1=rm[:, h, :], op=ALU.mult), reads=[sck, "rm"], writes=[rk_("rPT")])
            po, pok = bankF[rbk], "ps%d" % rbk
            op("pe", lambda e: e.matmul(out=po[:, 0:128], lhsT=rPT[:], rhs=rvt[:], start=True, stop=False), reads=[rk_("rPT"), rk_("rvt")], writes=[pok])
            op("pe", lambda e: e.matmul(out=po[:, 0:128], lhsT=rqx[:], rhs=Rb[h][:], start=False, stop=True), reads=[rk_("rqx"), Rbk, pok], writes=[pok])
            s_ = t - T0
            op("act", lambda e: e.copy(out=osb[:], in_=po[:, 0:128]), reads=[pok], writes=[rk_("osb")])
            yield
            sc = st[:, 8 + h:9 + h]; sck2 = "st%d" % (8 + h)
            rms_rstd(osb[:], [rk_("osb")], HD, sc, sck2)
            op("dve", lambda e: e.tensor_scalar(out=og[:], in0=osb[:], scalar1=sc, scalar2=None, op0=ALU.mult), reads=[rk_("osb"), sck2], writes=[rk_("og")])
            op("pool", lambda e: e.tensor_tensor(out=mix[:, s_, 512 + h * 128:512 + (h + 1) * 128], in0=og[:], in1=sgr[:], op=ALU.mult),
               reads=[rk_("og"), rk_("sgr")], writes=["mix%d" % (4 + h)])
        if not store:
            yield
        pr, prk = bankF[rbk], "ps%d" % rbk
        op("pe", lambda e: e.matmul(out=pr[:, 0:128], lhsT=kz[:], rhs=rvt[:], start=True, stop=True), reads=[rk_("kz"), rk_("rvt")], writes=[prk])
        g128 = float(np.float64(1.0 - 2.0 ** (-5 - h)) ** 128)
        op("dve", lambda e: e.scalar_tensor_tensor(out=Rst[h][:], in0=Rst[h][:], scalar=g128, in1=pr[:, 0:128], op0=ALU.mult, op1=ALU.add),
           reads=[Rk, prk], writes=[Rk])
        op("act", lambda e: e.copy(out=Rb[h][:], in_=Rst[h][:]), reads=[Rk], writes=[Rbk])
        yield

    def prep_task(g):
        gi = g % 2
        for r in range(6):
            if 0 <= r - 2 < 4:
                norm_b(4 * g + r - 2, gi, r - 2)
            if r < 4:
                norm_a(4 * g + r)
            yield
        yield
        if "g" in parts:
            yield from gdn_common(gi, g >= G_FULL)

    def ret_task(g):
        gi = g % 2; full = g >= G_FULL
        for tt in range(4):
            t = 4 * g + tt
            ci = t % 2
            dma("sp", PH["cs_t"][ci][:], cosF[t * 128:(t + 1) * 128, :], writes=["cs_t%d" % ci])
            dma("sp", PH["sn_t"][ci][:], sinS[t * 128:(t + 1) * 128, :], writes=["sn_t%d" % ci])
            for h in range(4):
                yield from ret_tile(t, tt, gi, h, full, ci)


    def run_tasks(tasks, reps=None):
        tasks = list(tasks)
        reps = dict(reps or {})
        while tasks:
            for tk_ in list(tasks):
                for _ in range(reps.get(id(tk_), 1)):
                    try:
                        next(tk_)
                    except StopIteration:
                        tasks.remove(tk_)
                        break

    g0 = NG - ng_run
    g1 = NG
    if dev_g0 is not None:
        g0 = dev_g0; g1 = g0 + ng_run
    run_tasks([prep_task(g0)])
    cur_phase = [None]
    for g in range(g0, g1):
        full = g >= G_FULL
        if cur_phase[0] != full:
            if cur_phase[0] is not None:
                p.barrier()
                es_ph[0].close()
                es_ph[0] = ExitStack()
            alloc_phase(full)
            cur_phase[0] = full
        tasks = []
        if "g" in parts:
            pf_ok = (g + 1 < g1) and ((g + 1 >= G_FULL) == full)
            if full:
                tasks.append(gdn_task(g, [0, 1, 2, 3], PH["streams"][0], pf_ok))
            else:
                tasks.append(gdn_task(g, [0, 2], PH["streams"][0], pf_ok))
                tasks.append(gdn_task(g, [1, 3], PH["streams"][1], pf_ok))
        reps = {}
        if "r" in parts:
            rt_ = ret_task(g)
            tasks.append(rt_)
            reps[id(rt_)] = 1 if full else 2
        if g + 1 < g1:
            tasks.append(prep_task(g + 1))
        run_tasks(tasks, reps)

    def early(src):
        p.barrier()
        dma("sp", out[0:128, :], src, writes=["out0"])
        p.finish()
        return nc, p
    if phase == "pass":
        return early(xt[0][:])
    p.barrier()
    es_ph[0].close()
    es_keep.close()
    PHTAG[0] = "_X"
    hres = R("hres", [128, NSLOT, D], F32)
    n2T = R("n2T", [128, 8, 2 + OWN], BF16)
    es_b = ExitStack()
    Wout = L("Wout", [128, 8, D], BF16, es_b)
    mixT = [L("mixT%d" % i, [128, 8, 128], BF16, es_b) for i in range(2)]
    xt2 = [L("xt2%d" % i, [128, D], F32, es_b) for i in range(2)]
    junk = L("junk2", [128, D], F32, es_b)
    nb2 = [L("nb2%d" % i, [128, D], BF16, es_b) for i in range(2)]
    st = L("st2", [128, 4], F32, es_b)
    dma("pool", Wout[:], w_out.rearrange("(k p) c -> p k c", p=128), writes=["Wout"])
    nw2 = L("nw2", [128, D], F32, es_b)
    dma("sp", nw2[:], vecs[1:2, :].partition_broadcast(128), writes=["nw2"])

    def b3_a(s_):
        t = T0 + s_
        dma("sp", xt2[s_ % 2][:], xpad[t * 128:(t + 1) * 128, :], writes=["xt2%d" % (s_ % 2)])
        pb, pk = nB()
        for k in range(8):
            op("pe", lambda e, k=k: e.transpose(out=pb[:, k * 128:(k + 1) * 128], in_=mix[:, s_, k * 128:(k + 1) * 128], identity=identb[:]),
               reads=["mix%d" % k, "identb"], writes=[pk])
        op("act", lambda e: e.copy(out=mixT[s_ % 2][:], in_=pb[:, :].rearrange("p (k n) -> p k n", k=8)), reads=[pk], writes=["mixT%d" % (s_ % 2)])

    def b3_b(s_):
        xb_ = xt2[s_ % 2]; xk = "xt2%d" % (s_ % 2); mT = mixT[s_ % 2]; mk = "mixT%d" % (s_ % 2)
        for half in range(2):
            pf, fk = nF()
            for k in range(8):
                op("pe", lambda e, k=k: e.matmul(out=pf[:, :], lhsT=mT[:, k, :], rhs=Wout[:, k, half * 512:(half + 1) * 512],
                                                start=(k == 0), stop=(k == 7)), reads=[mk, "Wout"], writes=[fk])
            op("dve", lambda e: e.tensor_tensor(out=hres[:, s_, half * 512:(half + 1) * 512], in0=xb_[:, half * 512:(half + 1) * 512],
                                                in1=pf[:, :], op=ALU.add), reads=[xk, fk], writes=["h%d_%d" % (s_, half)])

    def b3_c(s_):
        hks = ["h%d_0" % s_, "h%d_1" % s_]
        sc = st[:, (s_ % 2):(s_ % 2) + 1]; sck = "stb%d" % (s_ % 2)
        rms_rstd(hres[:, s_, :], hks, D, sc, sck)
        op("dve", lambda e: e.scalar_tensor_tensor(out=nb2[s_ % 2][:], in0=hres[:, s_, :], scalar=sc, in1=nw2[:],
                                                   op0=ALU.mult, op1=ALU.mult), reads=hks + [sck, "nw2"], writes=["nb2%d" % (s_ % 2)])

    def b3_d(s_):
        nb_ = nb2[s_ % 2]; nbk = "nb2%d" % (s_ % 2)
        pb, pk = nB()
        for k in range(8):
            op("pe", lambda e, k=k: e.transpose(out=pb[:, k * 128:(k + 1) * 128], in_=nb_[:, k * 128:(k + 1) * 128], identity=identb[:]),
               reads=[nbk, "identb"], writes=[pk])
        pv = pb[:, :].rearrange("p (k n) -> p k n", k=8)
        if s_ == 0:
            op("act", lambda e: e.copy(out=n2T[:, :, 0:2], in_=pv[:, :, 126:128]), reads=[pk], writes=["n2T_h"])
        else:
            op("act", lambda e: e.copy(out=n2T[:, :, 2 + (s_ - 1) * 128:2 + s_ * 128], in_=pv), reads=[pk], writes=["n2T_%d" % ((s_ - 1) // 4)])

    for r in range(NSLOT + 3):
        if 0 <= r - 3 < NSLOT:
            b3_d(r - 3)
        if 0 <= r - 2 < NSLOT:
            b3_c(r - 2)
        if 0 <= r - 1 < NSLOT:
            b3_b(r - 1)
        if r < NSLOT:
            b3_a(r)

    if phase == "b3":
        return early(hres[:, 1, :])
    p.barrier()
    es_b.close()
    es_mix.close()
    es_c = ExitStack()
    NGRP = DFF // 256
    Wgu = [L("Wgu%d" % i, [128, 2, 8, 256], BF16, es_c) for i in range(2)]
    Wd = [L("Wd%d" % i, [128, 2, D], BF16, es_c) for i in range(2)]
    mcw = L("mcw", [128, 44, 3], F32, es_c)
    hb = [L("hb%d" % i, [128, 2 + OWN], F32, es_c) for i in range(2)]
    yb = [L("yb%d" % i, [128, OWN], F32, es_c) for i in range(2)]
    actT = [L("actT%d" % i, [128, 2, OWN], BF16, es_c) for i in range(2)]
    st = L("st3", [128, 4], F32, es_c)
    ob = [L("ob%d" % i, [128, D], F32, es_c) for i in range(2)]
    junk = L("junk3", [128, D], BF16, es_c)
    dma("sp", mcw[:], mcwT.rearrange("(c p) i -> p c i", p=128), writes=["mcw"])
    nw3 = L("nw3", [128, D], F32, es_c)
    dma("sp", nw3[:], vecs[2:3, :].partition_broadcast(128), writes=["nw3"])
    w_up_v = w_up.rearrange("(k p) c -> p k c", p=128)
    w_dn_v = w_down.rearrange("(c p) d -> p c d", p=128)

    def load_up(gi_):
        b = gi_ % 2
        dma("pool", Wgu[b][:, 0, :, :], w_up_v[:, :, gi_ * 256:(gi_ + 1) * 256], writes=["Wgu%d" % b])
        dma("pool", Wgu[b][:, 1, :, :], w_up_v[:, :, DFF + gi_ * 256:DFF + (gi_ + 1) * 256], writes=["Wgu%d" % b])

    def load_dn(gi_):
        b = gi_ % 2
        dma("pool", Wd[b][:], w_dn_v[:, gi_ * 2:gi_ * 2 + 2, :], writes=["Wd%d" % b])

    def ffn_up(gi_):
        b = gi_ % 2
        ak = "actT%d" % b
        for fc in range(2):
            for which in range(2):
                hb_ = hb[which]; hbk = "hb%d" % which
                cidx = which * 22 + gi_ * 2 + fc
                lw = lambda k: Wgu[b][:, which, k, fc * 128:(fc + 1) * 128]
                pf, fk = nF()
                for k in range(8):
                    op("pe", lambda e, k=k: e.matmul(out=pf[:, 0:2], lhsT=lw(k), rhs=n2T[:, k, 0:2], start=(k == 0), stop=(k == 7)),
                       reads=["Wgu%d" % b, "n2T_h"], writes=[fk])
                op("act", lambda e: e.copy(out=hb_[:, 0:2], in_=pf[:, 0:2]), reads=[fk], writes=[hbk + "h"])
                for tb in range(4):
                    pf, fk = nF()
                    for k in range(8):
                        op("pe", lambda e, k=k: e.matmul(out=pf[:, :], lhsT=lw(k), rhs=n2T[:, k, 2 + tb * 512:2 + (tb + 1) * 512],
                                                        start=(k == 0), stop=(k == 7)), reads=["Wgu%d" % b, "n2T_%d" % tb], writes=[fk])
                    op("act", lambda e, tb=tb: e.copy(out=hb_[:, 2 + tb * 512:2 + (tb + 1) * 512], in_=pf[:, :]), reads=[fk], writes=[hbk + "_%d" % tb])
                hbks = [hbk + "h"] + [hbk + "_%d" % i for i in range(4)]
                ybk = "yb%d" % which
                op("act", lambda e: e.activation(out=yb[which][:], in_=hb_[:, 2:2 + OWN], func=AF.Copy, scale=mcw[:, cidx, 2:3]),
                   reads=hbks + ["mcw"], writes=[ybk])
                for i in (1, 0):
                    op("dve", lambda e, i=i: e.scalar_tensor_tensor(out=yb[which][:], in0=hb_[:, i:i + OWN], scalar=mcw[:, cidx, i:i + 1],
                                                                   in1=yb[which][:], op0=ALU.mult, op1=ALU.add), reads=hbks + ["mcw", ybk], writes=[ybk])
            op("act", lambda e: e.activation(out=yb[0][:], in_=yb[0][:], func=AF.Silu), reads=["yb0"], writes=["yb0"])
            op("pool", lambda e: e.tensor_tensor(out=actT[b][:, fc, :], in0=yb[0][:], in1=yb[1][:], op=ALU.mult),
               reads=["yb0", "yb1"], writes=[ak + "_%d" % fc])

    def final_tile(tt):
        hks = ["h%d_0" % (tt + 1), "h%d_1" % (tt + 1)]
        sc = st[:, (tt % 2):(tt % 2) + 1]; sck = "stf%d" % (tt % 2)
        rms_rstd(hres[:, tt + 1, :], hks, D, sc, sck)
        o_ = ob[tt % 2]; ok = "ob%d" % (tt % 2)
        op("dve", lambda e: e.scalar_tensor_tensor(out=o_[:], in0=hres[:, tt + 1, :], scalar=sc, in1=nw3[:],
                                                   op0=ALU.mult, op1=ALU.mult), reads=hks + [sck, "nw3"], writes=[ok])
        dma("sp", out[tt * 128:(tt + 1) * 128, :], o_[:], reads=[ok], writes=["out%d" % tt])

    def ffn_down(gi_):
        b = gi_ % 2
        ak = "actT%d" % b
        for tt in range(16):
            for half in range(2):
                pf, fk = nF()
                for fc in range(2):
                    op("pe", lambda e, fc=fc: e.matmul(out=pf[:, :], lhsT=actT[b][:, fc, tt * 128:(tt + 1) * 128],
                                                      rhs=Wd[b][:, fc, half * 512:(half + 1) * 512], start=(fc == 0), stop=(fc == 1)),
                       reads=[ak + "_0", ak + "_1", "Wd%d" % b], writes=[fk])
                hk = "h%d_%d" % (tt + 1, half)
                op("dve", lambda e: e.tensor_tensor(out=hres[:, tt + 1, half * 512:(half + 1) * 512],
                                                    in0=hres[:, tt + 1, half * 512:(half + 1) * 512], in1=pf[:, :], op=ALU.add),
                   reads=[fk, hk], writes=[hk])
            if gi_ == NGRP - 1 and tt >= 1:
                final_tile(tt - 1)
        if gi_ == NGRP - 1:
            final_tile(15)

    load_up(0); load_dn(0)
    for gi_ in range(NGRP + 1):
        if gi_ + 1 < NGRP:
            load_up(gi_ + 1)
        if gi_ < NGRP:
            ffn_up(gi_)
        if gi_ >= 1:
            ffn_down(gi_ - 1)
        if gi_ + 1 < NGRP:
            load_dn(gi_ + 1)

    p.finish()
    es_c.close()
    es_r.close()
    return nc, p


def _consts():
    idx = np.arange(128)
    same = (idx[:, None] // 64) == (idx[None, :] // 64)
    cm = np.zeros((10, 128, 128), np.float32)
    cm[0] = np.eye(128)
    cm[1] = 1.0
    cm[2] = (same & (idx[:, None] < idx[None, :]))
    cm[3] = (same & (idx[:, None] <= idx[None, :]))
    cm[4] = cm[3]
    cm[5] = (idx[:, None] < 64) * np.ones((1, 128))
    cm[6] = (idx[:, None] >= 64) * np.ones((1, 128))
    cm[7] = (1.0 - cm[2]) * -30000.0
    cm[8] = (1.0 - cm[3]) * -30000.0
    hh = np.arange(4, dtype=np.float64)
    gam = 1.0 - 2.0 ** (-5.0 - hh)
    lg = np.log(gam)
    rel = (idx[None, :] - idx[:, None]).astype(np.float64)
    rmat = np.where(rel[None] >= 0, np.exp(rel[None] * lg[:, None, None]), 0.0) * HD ** -0.5
    rvec = np.zeros((128, 8), np.float64)
    rvec[:, 0:4] = np.exp((127.0 - idx[:, None]) * lg[None, :]) * HD ** -0.5
    rxi = np.exp((idx[None, :] + 1.0) * lg[:, None])
    return cm, rmat.astype(np.float32), rvec.astype(np.float32), rxi.astype(np.float32)


_CACHE = {}


def kernel(x, attn_norm_w, w_in, gdn_conv_w, gdn_a_log, gdn_dt_bias, gdn_norm_w, w_out, mlp_norm_w,
           w_up, mlp_conv_w, w_down, final_norm_w):
    f = lambda a: np.ascontiguousarray(np.asarray(a, dtype=np.float32))
    x2 = f(x).reshape(S, D)
    if "nc" not in _CACHE:
        _CACHE["nc"] = build_program()[0]
    nc = _CACHE["nc"]
    cm, rmat, rvec, rxi = _consts()
    vecs = np.stack([f(attn_norm_w)[0], f(mlp_norm_w)[0], f(final_norm_w)], 0)
    gsm = np.concatenate([f(gdn_a_log)[0], f(gdn_dt_bias)[0], f(gdn_norm_w)[0]])[None, :]
    angle = (1.0 / (10000.0 ** np.linspace(0.0, 1.0, 64, dtype=np.float32))).astype(np.float32)
    angle = np.repeat(angle, 2)
    sign = np.tile(np.array([-1.0, 1.0], np.float32), 64)
    common = {
        "w_in": f(w_in)[0], "w_out": f(w_out)[0], "w_up": f(w_up)[0], "w_down": f(w_down)[0],
        "gcwT": np.ascontiguousarray(f(gdn_conv_w)[0].T), "mcwT": np.ascontiguousarray(f(mlp_conv_w)[0].T),
        "vecs": np.ascontiguousarray(vecs), "gsm": np.ascontiguousarray(gsm),
        "cmat": cm, "rmat": rmat, "rvec": rvec, "rxi": rxi,
    }
    in_maps = []
    for c in range(NCORES):
        n_real = OWN * (c + 1)
        xp = np.zeros((S, D), np.float32)
        xp[S - n_real:] = x2[:n_real]
        pos = (np.arange(S, dtype=np.int64) - (S - n_real)).astype(np.float32)
        phase = pos[:, None] * angle[None, :]
        m = dict(common)
        m["xpad"] = xp
        m["cosF"] = np.cos(phase).astype(np.float32)
        m["sinS"] = (np.sin(phase) * sign[None, :]).astype(np.float32)
        in_maps.append(m)
    res = run_bass_kernel_spmd(nc, in_maps, core_ids=list(range(NCORES)))
    outs = [np.asarray(res.results[c]["out"], dtype=np.float32) for c in range(NCORES)]
    return np.concatenate(outs, 0).reshape(1, S, D)
```

```python
import math
from contextlib import ExitStack
import numpy as np
import concourse.bass as bass
import concourse.mybir as mybir
from concourse.bass_utils import run_bass_kernel_spmd

F32 = mybir.dt.float32
BF16 = mybir.dt.bfloat16
ALU = mybir.AluOpType
AF = mybir.ActivationFunctionType

NCORES = 8
S = 16384
D = 1024
NT = S // 128
NG = NT // 4
OWN = 2048
T0 = NT - OWN // 128 - 1
NSLOT = NT - T0
G_FULL = T0 // 4
DFF = 2816
EPS = 1e-6
HD = 128
IN_COLS = 4104


class Prog:
    ENG = ("pe", "act", "dve", "pool", "sp")

    def __init__(self, nc, n_dma_sems=8):
        self.nc = nc
        self.eng = {"pe": nc.tensor, "act": nc.scalar, "dve": nc.vector,
                    "pool": nc.gpsimd, "sp": nc.sync}
        self.sem = {e: nc.alloc_semaphore("c_" + e) for e in self.ENG}
        self.cnt = {e: 0 for e in self.ENG}
        self.dsem = {e: [nc.alloc_semaphore("d_%s%d" % (e, i)) for i in range(n_dma_sems)]
                     for e in ("sp", "pool")}
        self.dval = {e: [0] * n_dma_sems for e in ("sp", "pool")}
        self.drr = {e: 0 for e in ("sp", "pool")}
        self.seen = {e: {} for e in self.ENG}
        self.lastw = {}
        self.readers = {}
        self.n_ins = 0
        self.n_wait = 0

    def _semof(self, tok):
        return self.sem[tok] if isinstance(tok, str) else self.dsem[tok[0]][tok[1]]

    def _wait(self, e, tok, val):
        if tok == e and e == "pe":
            return
        if self.seen[e].get(tok, 0) >= val:
            return
        self.eng[e].wait_ge(self._semof(tok), val)
        self.seen[e][tok] = val
        self.n_wait += 1

    def _deps(self, e, reads, writes):
        for k in reads:
            w = self.lastw.get(k)
            if w is not None:
                self._wait(e, *w)
        for k in writes:
            w = self.lastw.get(k)
            if w is not None:
                self._wait(e, *w)
            for r in self.readers.get(k, ()):
                self._wait(e, *r)

    def _commit(self, tokval, reads, writes):
        for k in writes:
            self.lastw[k] = tokval
            self.readers[k] = []
        for k in reads:
            lst = self.readers.setdefault(k, [])
            lst[:] = [r for r in lst if r[0] != tokval[0]]
            lst.append(tokval)

    def op(self, e, fn, reads=(), writes=()):
        ex = [k for k in reads if k.startswith("ps")]
        if ex:
            reads = [k for k in reads if not k.startswith("ps")]
            writes = list(writes) + [k for k in ex if k not in writes]
        self._deps(e, reads, writes)
        ins = fn(self.eng[e])
        self.cnt[e] += 1
        ins.then_inc(self.sem[e], 1)
        self._commit((e, self.cnt[e]), reads, writes)
        self.n_ins += 1
        return ins

    def dma(self, e, out, in_, reads=(), writes=(), **kw):
        i = self.drr[e]
        self.drr[e] = (i + 1) % len(self.dsem[e])
        tok = (e, i)
        if self.dval[e][i] > 0:
            self._wait(e, tok, self.dval[e][i])
        self._deps(e, reads, writes)
        ins = self.eng[e].dma_start(out=out, in_=in_, **kw)
        self.dval[e][i] += 16
        ins.then_inc(self.dsem[e][i], 16)
        self._commit((tok, self.dval[e][i]), reads, writes)
        self.n_ins += 1
        return ins

    def barrier(self):
        for e in self.ENG:
            for e2 in self.ENG:
                if e2 != e and self.cnt[e2] > 0:
                    self._wait(e, e2, self.cnt[e2])
            for q in self.dsem:
                for i, v in enumerate(self.dval[q]):
                    if v > 0:
                        self._wait(e, (q, i), v)
        self.lastw.clear()
        self.readers.clear()

    def finish(self):
        for q in self.dsem:
            for i, v in enumerate(self.dval[q]):
                if v > 0:
                    self._wait("sp", (q, i), v)


def build_program(ng_run=NG, phase="all", parts="gr", dev_g0=None):
    nc = bass.Bass("TRN2", target_bir_lowering=False)
    dt_in = lambda name, shape: nc.dram_tensor(name, list(shape), F32, kind="ExternalInput").ap()
    xpad = dt_in("xpad", [S, D])
    w_in = dt_in("w_in", [D, IN_COLS])
    w_out = dt_in("w_out", [D, D])
    w_up = dt_in("w_up", [D, 2 * DFF])
    w_down = dt_in("w_down", [DFF, D])
    gcwT = dt_in("gcwT", [1536, 4])
    mcwT = dt_in("mcwT", [2 * DFF, 3])
    vecs = dt_in("vecs", [3, D])
    gsm = dt_in("gsm", [1, 8 + 128])
    cmat = dt_in("cmat", [10, 128, 128])
    rmat = dt_in("rmat", [4, 128, 128])
    rvec = dt_in("rvec", [128, 8])
    rxi = dt_in("rxi", [4, 128])
    cosF = dt_in("cosF", [S, 128])
    sinS = dt_in("sinS", [S, 128])
    out = nc.dram_tensor("out", [OWN, D], F32, kind="ExternalOutput").ap()

    p = Prog(nc)
    SKIP = ""
    op = p.op
    def dma(q, o_, i_, reads=(), writes=(), tag="", **kw):
        if tag and tag in SKIP:
            return p.op("pool", lambda e: e.memset(o_, 1.0), reads=reads, writes=writes)
        return p.dma(q, o_, i_, reads=reads, writes=writes, **kw)

    bank = [nc.alloc_psum_tensor("bank%d" % i, [128, 1024], BF16) for i in range(8)]
    bankF = [bk_[:, :].bitcast(F32) for bk_ in bank]
    rr = {"B": 0, "F": 0}
    POOLS = {"B": [6, 7], "F": [0, 1, 2, 3, 4, 5]}

    def nB():
        lst = POOLS["B"]; i = lst[rr["B"] % len(lst)]; rr["B"] += 1
        return bank[i], "ps%d" % i

    def nF():
        lst = POOLS["F"]; i = lst[rr["F"] % len(lst)]; rr["F"] += 1
        return bankF[i], "ps%d" % i

    es_r = ExitStack()
    def R(name, shape, dt):
        return es_r.enter_context(nc.sbuf_tensor(name, list(shape), dt, side="right"))
    cm = R("cm", [128, 10, 128], F32)
    identb = R("identb", [128, 128], BF16)
    rm = R("rm", [128, 4, 128], F32)
    rv = R("rv", [128, 8], F32)
    xibc = R("xibc", [128, 4, 128], F32)
    nwbc = R("nwbc", [128, 1, D], F32)
    gsmbc = R("gsmbc", [128, 136], F32)
    gconst = R("gconst", [128, 16], F32)
    IDENT, ONES, MSU, MU, TRI, SC0, SC1 = [cm[:, i, :] for i in range(7)]
    dma("sp", cm[:], cmat.rearrange("m p n -> p m n"), writes=["cm"], tag="m")
    dma("sp", rm[:], rmat.rearrange("m p n -> p m n"), writes=["rm"], tag="m")
    dma("sp", rv[:], rvec, writes=["rv"])
    for h in range(4):
        dma("sp", xibc[:, h, :], rxi[h:h + 1, :].partition_broadcast(128), writes=["xibc"], tag="b")
    dma("sp", nwbc[:, 0, :], vecs[0:1, :].partition_broadcast(128), writes=["nwbc"], tag="b")
    dma("sp", gsmbc[:], gsm.partition_broadcast(128), writes=["gsmbc"], tag="b")
    op("dve", lambda e: e.tensor_copy(out=identb[:], in_=IDENT), reads=["cm"], writes=["identb"])
    op("act", lambda e: e.activation(out=gconst[:, 0:4], in_=gsmbc[:, 0:4], func=AF.Exp), reads=["gsmbc"], writes=["gconst"])
    op("dve", lambda e: e.tensor_scalar(out=gconst[:, 0:4], in0=gconst[:, 0:4], scalar1=-1.0, scalar2=None, op0=ALU.mult),
       reads=["gconst"], writes=["gconst"])
    op("dve", lambda e: e.tensor_copy(out=gconst[:, 4:8], in_=gsmbc[:, 4:8]), reads=["gsmbc", "gconst"], writes=["gconst"])
    GNW = gsmbc[:, 8:136]

    es_mix = ExitStack()
    mix = es_mix.enter_context(nc.sbuf_tensor("mix", [128, NSLOT, D], BF16, side="left"))
    es_keep = ExitStack()
    es_ph = [ExitStack()]
    PHTAG = [""]
    def L(name, shape, dt, es=None):
        if es is None:
            name = name + PHTAG[0]
        return (es or es_ph[0]).enter_context(nc.sbuf_tensor(name, list(shape), dt, side="left"))
    K = lambda name, shape, dt: L(name, shape, dt, es_keep)

    w_in_v = w_in.rearrange("(k p) c -> p k c", p=128)
    Wg = {}
    for h in range(4):
        for X in (1, 2):
            Wg[(h, X)] = K("Wg%d%d" % (h, X), [128, 8, 128], BF16)
            c0 = X * 512 + h * 128
            dma("pool", Wg[(h, X)][:], w_in_v[:, :, c0:c0 + 128], writes=["Wg%d%d" % (h, X)], tag="w")
    Wab = K("Wab", [128, 8, 8], BF16)
    dma("pool", Wab[:], w_in_v[:, :, 2048:2056], writes=["Wab"], tag="w")
    Wrkv = [K("Wrkv%d" % h, [128, 8, 256], BF16) for h in range(4)]
    for h in range(4):
        for j, X in enumerate((1, 2)):
            c0 = 2056 + X * 512 + h * 128
            dma("pool", Wrkv[h][:, :, j * 128:(j + 1) * 128], w_in_v[:, :, c0:c0 + 128], writes=["Wrkv%d" % h], tag="w")
    gcw = K("gcw", [128, 12, 4], F32)
    dma("sp", gcw[:], gcwT.rearrange("(c p) i -> p c i", p=128), writes=["gcw"], tag="c")
    xt = [K("xt%d" % i, [128, D], F32) for i in range(2)]
    junk = K("junk", [128, D], BF16)
    nb = [K("nb%d" % i, [128, D], BF16) for i in range(2)]
    nTg = [K("nTg%d" % i, [128, 8, 512], BF16) for i in range(2)]
    st = K("st", [128, 16], F32)
    ab = K("ab", [128, 2, 4, 8], F32)
    CN = ["cG", "cLB", "cB", "cGAM", "cGL0", "cGL1", "cEG", "cEKD", "cCD0", "cCD1", "cGLB", "cGC", "cBG", "cTMP"]
    CT = {n: K(n, [128, 2, 4, 4], F32) for n in CN}
    halo = [[K("halo%d%d" % (h, X), [128, 4], F32) for X in range(3)] for h in range(4)]
    Sst = [K("S%d" % h, [128, 128], F32) for h in range(4)]
    Sb = [K("Sb%d" % h, [128, 128], BF16) for h in range(4)]
    Rst = [K("R%d" % h, [128, 128], F32) for h in range(4)]
    Rb = [K("Rb%d" % h, [128, 128], BF16) for h in range(4)]
    for h in range(4):
        op("pool", lambda e, h=h: e.memset(Sst[h][:], 0.0), writes=["S%d" % h])
        op("pool", lambda e, h=h: e.memset(Sb[h][:], 0.0), writes=["Sb%d" % h])
        op("pool", lambda e, h=h: e.memset(Rst[h][:], 0.0), writes=["R%d" % h])
        op("pool", lambda e, h=h: e.memset(Rb[h][:], 0.0), writes=["Rb%d" % h])
        for X in range(3):
            op("pool", lambda e, h=h, X=X: e.memset(halo[h][X][:], 0.0), writes=["halo%d%d" % (h, X)])

    CHN = ("kd", "ub", "wT", "ubf", "qkT", "qdT", "osb", "og", "sg")
    PH = {}

    def alloc_stream(sid, full, nchain, lb, HB, cbanks):
        nE = 2 if full else 1
        cx = {"id": sid, "lb": lb, "HB": HB, "cbanks": cbanks}
        Ld = {}
        for n, shp, dt in (("lr", [128, 4, 2], F32), ("rkb", [128, 8, 4], F32), ("Dab", [128, nE, 4, 128], F32),
                           ("E", [128, nE, 4, 128], BF16), ("tmpm", [128, 4, 128], F32),
                           ("YP0", [128, 4, 256], BF16), ("YP1", [128, 4, 256], BF16),
                           ("XX0", [128, 4, 128], BF16), ("XX1", [128, 4, 128], BF16), ("kg", [128, 4, 128], BF16),
                           ("vtok", [128, 4, 128], BF16), ("TTb", [128, 4, 128], BF16), ("TbT", [128, 4, 128], BF16)):
            Ld[n] = L("%s_%s" % (n, sid), shp, dt)
        if full:
            Ld["Eq"] = L("Eq_%s" % sid, [128, 4, 128], BF16)
        cx["L"] = Ld
        cx["XT"] = [(L("XT%s%d" % (sid, X), [128, 512], BF16) if (full or X != 0) else None) for X in range(3)]
        cx["ybuf"] = [L("ybuf%s%d" % (sid, i), [128, 512], F32) for i in range(2)]
        cx["xw"] = [L("xw%s%d" % (sid, i), [128, 515], F32) for i in range(2)]
        cx["sq"] = L("sq%s" % sid, [128, 512], F32)
        cx["ssc"] = L("ssc%s" % sid, [128, 4, 2], F32)
        cx["cs"] = []
        for i in range(nchain):
            d = {"id": "%s%d" % (sid, i)}
            lst = [("kd", [128, 4, 128], BF16), ("ub", [128, 4, 128], F32), ("wT", [128, 4, 128], BF16), ("ubf", [128, 128], BF16)]
            if full:
                lst += [("qkT", [128, 4, 128], BF16), ("qdT", [128, 4, 128], BF16),
                        ("osb", [128, 128], F32), ("og", [128, 128], F32), ("sg", [128, 128], BF16)]
            for n, shp, dt in lst:
                d[n] = L("%s_%s" % (n, d["id"]), shp, dt)
            op("pool", lambda e, d=d: e.memset(d["ubf"][:], 0.0), writes=["ubf_%s" % d["id"]])
            cx["cs"].append(d)
        return cx

    def alloc_phase(full):
        PH.clear()
        PHTAG[0] = "_F" if full else "_S"
        Rr = {}
        lst = [("rsb", [128, 256], F32), ("rt1", [128, 128], F32), ("rt2", [128, 128], F32), ("krot", [128, 128], BF16),
               ("kz", [128, 128], BF16), ("rvt", [128, 128], BF16)]
        if full:
            lst += [("qrot", [128, 128], BF16), ("rkT", [128, 128], BF16), ("rqT", [128, 128], BF16), ("rqx", [128, 128], BF16),
                    ("rPT", [128, 128], BF16), ("sgr", [128, 128], BF16), ("osb", [128, 128], F32), ("og", [128, 128], F32)]
        for n, shp, dt in lst:
            Rr[n] = L("%s_r" % n, shp, dt)
        PH["RS"] = Rr
        PH["cs_t"] = [L("cs_t%d" % i, [128, 128], F32) for i in range(2)]
        PH["sn_t"] = [L("sn_t%d" % i, [128, 128], F32) for i in range(2)]
        if full:
            for h in range(4):
                Wg[(h, 0)] = L("Wg%d0" % h, [128, 8, 128], BF16)
                dma("pool", Wg[(h, 0)][:], w_in_v[:, :, h * 128:(h + 1) * 128], writes=["Wg%d0" % h], tag="w")
            PH["Wgg"] = L("Wgg", [128, 8, 512], BF16)
            dma("pool", PH["Wgg"][:], w_in_v[:, :, 1536:2048], writes=["Wgg"], tag="w")
            PH["Wrqg"] = [L("Wrqg%d" % h, [128, 8, 256], BF16) for h in range(4)]
            for h in range(4):
                for j, X in enumerate((0, 3)):
                    c0 = 2056 + X * 512 + h * 128
                    dma("pool", PH["Wrqg"][h][:, :, j * 128:(j + 1) * 128], w_in_v[:, :, c0:c0 + 128], writes=["Wrqg%d" % h], tag="w")
            PH["streams"] = [alloc_stream("A", True, 2, [0, 1, 2], [(0, 2, 0), (1, 2, 256)], [3, 4])]
            PH["rbank"] = 5; PH["rbank2"] = 6
        else:
            PH["streams"] = [alloc_stream("A", False, 2, [0, 1, 0], [(0, 1, 0), (0, 1, 256)], [4]),
                             alloc_stream("B", False, 2, [2, 3, 2], [(2, 3, 0), (2, 3, 256)], [5])]
            PH["rbank"] = 6; PH["rbank2"] = 6

    LN_DS = math.log(HD ** -0.5)

    def rms_rstd(src_ap, rkeys, n, col, ckey, jout=None, jkey="junk"):
        jo = junk[:, 0:n] if jout is None else jout
        op("act", lambda e: e.activation(out=jo, in_=src_ap, func=AF.Square, accum_out=col),
           reads=rkeys, writes=[jkey, ckey])
        op("act", lambda e: e.activation(out=col, in_=col, func=AF.Ln, bias=EPS, scale=1.0 / n),
           reads=[ckey], writes=[ckey])
        op("act", lambda e: e.activation(out=col, in_=col, func=AF.Exp, scale=-0.5), reads=[ckey], writes=[ckey])

    def norm_a(t):
        xb_ = xt[t % 2]; xk = "xt%d" % (t % 2); nb_ = nb[t % 2]; nbk = "nb%d" % (t % 2)
        dma("sp", xb_[:], xpad[t * 128:(t + 1) * 128, :], writes=[xk])
        rms_rstd(xb_[:], [xk], D, st[:, 0:1], "st0", jout=nb_[:], jkey=nbk)
        op("dve", lambda e: e.scalar_tensor_tensor(out=nb_[:], in0=xb_[:], scalar=st[:, 0:1], in1=nwbc[:, 0, :],
                                                   op0=ALU.mult, op1=ALU.mult), reads=[xk, "st0", "nwbc"], writes=[nbk])

    def norm_b(t, gi, tt):
        nb_ = nb[t % 2]; nbk = "nb%d" % (t % 2)
        pb, pk = bank[7], "ps7"
        for k in range(8):
            op("pe", lambda e, k=k: e.transpose(out=pb[:, k * 128:(k + 1) * 128], in_=nb_[:, k * 128:(k + 1) * 128],
                                               identity=identb[:]), reads=[nbk, "identb"], writes=[pk])
        nk = "nTg%d" % gi
        op("act", lambda e: e.copy(out=nTg[gi][:, :, tt * 128:(tt + 1) * 128],
                                   in_=pb[:, :].rearrange("p (k n) -> p k n", k=8)), reads=[pk], writes=[nk])

    def gdn_common(gi, full):
        nk = "nTg%d" % gi
        C = {n: CT[n][:, gi] for n in CN}
        ck = lambda n: "%s_%d" % (n, gi)
        abk = "ab%d" % gi
        for tt in range(4):
            pf, fk = bankF[7], "ps7"
            for k in range(8):
                op("pe", lambda e, k=k: e.matmul(out=pf[:, 0:8], lhsT=nTg[gi][:, k, tt * 128:(tt + 1) * 128],
                                                rhs=Wab[:, k, :], start=(k == 0), stop=(k == 7)),
                   reads=[nk, "Wab"], writes=[fk])
            op("dve", lambda e: e.tensor_copy(out=ab[:, gi, tt, :], in_=pf[:, 0:8]), reads=[fk], writes=[abk])
            yield
        for h in range(4):
            op("dve", lambda e, h=h: e.tensor_scalar(out=C["cTMP"][:, :, h], in0=ab[:, gi, :, h], scalar1=gconst[:, 4 + h:5 + h],
                                                    scalar2=None, op0=ALU.add), reads=[abk, "gconst"], writes=[ck("cTMP")])
        op("act", lambda e: e.activation(out=C["cTMP"], in_=C["cTMP"], func=AF.Exp), reads=[ck("cTMP")], writes=[ck("cTMP")])
        op("act", lambda e: e.activation(out=C["cTMP"], in_=C["cTMP"], func=AF.Ln, bias=1.0), reads=[ck("cTMP")], writes=[ck("cTMP")])
        for h in range(4):
            op("dve", lambda e, h=h: e.tensor_scalar(out=C["cG"][:, :, h], in0=C["cTMP"][:, :, h], scalar1=gconst[:, h:h + 1],
                                                    scalar2=None, op0=ALU.mult), reads=[ck("cTMP"), "gconst"], writes=[ck("cG")])
        op("act", lambda e: e.activation(out=C["cLB"], in_=ab[:, gi, :, 4:8], func=AF.Exp, scale=-1.0), reads=[abk], writes=[ck("cLB")])
        op("act", lambda e: e.activation(out=C["cLB"], in_=C["cLB"], func=AF.Ln, bias=1.0), reads=[ck("cLB")], writes=[ck("cLB")])
        op("act", lambda e: e.activation(out=C["cB"], in_=C["cLB"], func=AF.Exp, scale=-1.0), reads=[ck("cLB")], writes=[ck("cB")])
        op("dve", lambda e: e.tensor_scalar(out=C["cLB"], in0=C["cLB"], scalar1=-1.0, scalar2=None, op0=ALU.mult), reads=[ck("cLB"), ck("cB")], writes=[ck("cLB")])
        yield
        gflat = C["cG"].rearrange("p t h -> p (t h)")
        for lhs, dn in ((TRI, "cGAM"), (SC0, "cGL0"), (SC1, "cGL1")):
            pf, fk = bankF[7], "ps7"
            op("pe", lambda e, lhs=lhs: e.matmul(out=pf[:, 0:16], lhsT=lhs, rhs=gflat, start=True, stop=True),
               reads=["cm", ck("cG")], writes=[fk])
            op("dve", lambda e, dn=dn: e.tensor_copy(out=C[dn].rearrange("p t h -> p (t h)"), in_=pf[:, 0:16]),
               reads=[fk], writes=[ck(dn)])
        yield
        op("act", lambda e: e.activation(out=C["cEG"], in_=C["cGAM"], func=AF.Exp), reads=[ck("cGAM")], writes=[ck("cEG")])
        op("act", lambda e: e.activation(out=C["cCD0"], in_=C["cGL0"], func=AF.Exp), reads=[ck("cGL0")], writes=[ck("cCD0")])
        op("act", lambda e: e.activation(out=C["cCD1"], in_=C["cGL1"], func=AF.Exp), reads=[ck("cGL1")], writes=[ck("cCD1")])
        op("dve", lambda e: e.tensor_tensor(out=C["cEKD"][0:64], in0=C["cGL0"][0:64], in1=C["cGAM"][0:64], op=ALU.subtract),
           reads=[ck("cGL0"), ck("cGAM")], writes=[ck("cEKD")])
        op("dve", lambda e: e.tensor_tensor(out=C["cEKD"][64:128], in0=C["cGL1"][64:128], in1=C["cGAM"][64:128], op=ALU.subtract),
           reads=[ck("cGL1"), ck("cGAM"), ck("cEKD")], writes=[ck("cEKD")])
        op("act", lambda e: e.activation(out=C["cEKD"], in_=C["cEKD"], func=AF.Exp), reads=[ck("cEKD")], writes=[ck("cEKD")])
        op("dve", lambda e: e.tensor_tensor(out=C["cGLB"], in0=C["cGAM"], in1=C["cLB"], op=ALU.add), reads=[ck("cGAM"), ck("cLB")], writes=[ck("cGLB")])
        op("dve", lambda e: e.tensor_scalar(out=C["cGC"], in0=C["cGAM"], scalar1=LN_DS, scalar2=None, op0=ALU.add),
           reads=[ck("cGAM")], writes=[ck("cGC")])
        op("dve", lambda e: e.tensor_tensor(out=C["cBG"], in0=C["cB"], in1=C["cEG"], op=ALU.mult), reads=[ck("cB"), ck("cEG")], writes=[ck("cBG")])
        yield

    def gdn_proj(gi, h, full, cx):
        nk = "nTg%d" % gi
        sid = cx["id"]; lb = cx["lb"]
        sq_ = cx["sq"]; sqk = "sq%s" % sid
        Xs = (1, 2, 0) if full else (1, 2)

        def s1(i, X):
            xb_ = cx["xw"][i % 2]; xk = "xw%s%d" % (sid, i % 2)
            wk = "Wg%d%d" % (h, X); hk_ = "halo%d%d" % (h, X)
            op("pool", lambda e: e.tensor_copy(out=xb_[:, 0:3], in_=halo[h][X][:, 0:3]), reads=[hk_, xk], writes=[xk])
            pf, fk = bankF[lb[i % 3]], "ps%d" % lb[i % 3]
            for k in range(8):
                op("pe", lambda e, k=k: e.matmul(out=pf[:, :], lhsT=Wg[(h, X)][:, k, :], rhs=nTg[gi][:, k, :],
                                                start=(k == 0), stop=(k == 7)), reads=[wk, nk], writes=[fk])
            op("act", lambda e: e.copy(out=xb_[:, 3:515], in_=pf[:, :]), reads=[fk, xk], writes=[xk])

        def s2(i, X):
            xb_ = cx["xw"][i % 2]; xk = "xw%s%d" % (sid, i % 2)
            yb_ = cx["ybuf"][i % 2]; yk = "ybuf%s%d" % (sid, i % 2)
            tk = "XT%s%d" % (sid, X); hk_ = "halo%d%d" % (h, X)
            cw = lambda j: gcw[:, X * 4 + h, j:j + 1]
            op("dve", lambda e: e.tensor_scalar(out=yb_[:], in0=xb_[:, 3:515], scalar1=cw(3), scalar2=None, op0=ALU.mult),
               reads=[xk, "gcw"], writes=[yk])
            for j in (2, 1, 0):
                op("dve", lambda e, j=j: e.scalar_tensor_tensor(out=yb_[:], in0=xb_[:, j:j + 512], scalar=cw(j), in1=yb_[:],
                                                               op0=ALU.mult, op1=ALU.add), reads=[xk, "gcw", yk], writes=[yk])
            op("pool", lambda e: e.tensor_copy(out=halo[h][X][:, 0:3], in_=xb_[:, 512:515]), reads=[xk], writes=[hk_])
            op("act", lambda e: e.activation(out=cx["XT"][X][:], in_=yb_[:], func=AF.Silu), reads=[yk], writes=[tk])
            if X != 2:
                op("act", lambda e: e.activation(out=yb_[:], in_=yb_[:], func=AF.Silu), reads=[yk], writes=[yk])
                op("act", lambda e: e.activation(out=sq_[:], in_=yb_[:], func=AF.Square), reads=[yk], writes=[sqk])

        def s3(i, X):
            if X == 2:
                return
            j = 0 if X == 1 else 1
            pf2, fk2 = bankF[lb[(i + 1) % 3]], "ps%d" % lb[(i + 1) % 3]
            for tt in range(4):
                op("pe", lambda e, tt=tt: e.matmul(out=pf2[:, tt:tt + 1], lhsT=sq_[:, tt * 128:(tt + 1) * 128], rhs=cm[:, 1, 0:1], start=True, stop=True),
                   reads=[sqk, "cm"], writes=[fk2])
            op("dve", lambda e: e.tensor_copy(out=cx["ssc"][:, :, j], in_=pf2[:, 0:4]), reads=[fk2], writes=["ssc%s" % sid])

        n = len(Xs)
        for r in range(n + 3):
            if 0 <= r - 3 < n:
                s3(r - 3, Xs[r - 3])
            if 0 <= r - 1 < n:
                s2(r - 1, Xs[r - 1])
            if r < n:
                s1(r, Xs[r])
            yield

    def gdn_local(g, h, cx, cs):
        gi = g % 2; full = g >= G_FULL
        B = dict(cx["L"]); B.update(cs)
        sid = cx["id"]; lb = cx["lb"]
        key = lambda n: ("%s_%s" % (n, cs["id"])) if n in CHN else ("%s_%s" % (n, sid))
        C = {n: CT[n][:, gi] for n in CN}
        ck = lambda n: "%s_%d" % (n, gi)
        bF = lambda i: (bankF[lb[i]], "ps%d" % lb[i])
        bB = lambda i: (bank[lb[i]], "ps%d" % lb[i])
        kTa, vTa, qTa = cx["XT"][1], cx["XT"][2], cx["XT"][0]
        kTk, vTk, qTk = "XT%s1" % sid, "XT%s2" % sid, "XT%s0" % sid
        lr, rkb, Dab, E, tmpm = B["lr"], B["rkb"], B["Dab"], B["E"], B["tmpm"]
        YP = [B["YP0"], B["YP1"]]; XX = [B["XX0"], B["XX1"]]
        kg, kd, vtok, TTb, TbT, ub, wT = B["kg"], B["kd"], B["vtok"], B["TTb"], B["TbT"], B["ub"], B["wT"]
        Eq, qkT, qdT = B.get("Eq"), B.get("qkT"), B.get("qdT")
        hc = lambda n: C[n][:, :, h]
        bc4 = lambda ap2: ap2.unsqueeze(2).broadcast_to([128, 4, 128])
        rep4 = lambda ap2: ap2.unsqueeze(1).broadcast_to([128, 4, 128])
        v4 = lambda ap, w=128: ap.rearrange("p (a b) -> p a b", a=4)
        nc_ = 2 if full else 1
        sk = "ssc%s" % sid
        op("act", lambda e: e.activation(out=lr[:, :, 0:nc_], in_=cx["ssc"][:, :, 0:nc_], func=AF.Ln, bias=EPS), reads=[sk], writes=[key("lr")])
        op("act", lambda e: e.activation(out=rkb[:, 0, :], in_=lr[:, :, 0], func=AF.Exp, scale=-0.5), reads=[key("lr")], writes=[key("rk0")])
        op("dve", lambda e: e.scalar_tensor_tensor(out=rkb[:, 1, :], in0=lr[:, :, 0], scalar=-0.5, in1=hc("cGLB"), op0=ALU.mult, op1=ALU.add),
           reads=[key("lr"), ck("cGLB")], writes=[key("rk1")])
        op("dve", lambda e: e.tensor_tensor(out=rkb[:, 3, :], in0=rkb[:, 0, :], in1=hc("cBG"), op=ALU.mult), reads=[key("rk0"), ck("cBG")], writes=[key("rk3")])
        op("dve", lambda e: e.tensor_tensor(out=rkb[:, 4, :], in0=rkb[:, 0, :], in1=hc("cEKD"), op=ALU.mult), reads=[key("rk0"), ck("cEKD")], writes=[key("rk4")])
        op("dve", lambda e: e.scalar_tensor_tensor(out=rkb[:, 6, :], in0=lr[:, :, 0], scalar=-0.5, in1=hc("cGAM"), op0=ALU.mult, op1=ALU.subtract),
           reads=[key("lr"), ck("cGAM")], writes=[key("rk6")])
        if full:
            op("dve", lambda e: e.scalar_tensor_tensor(out=rkb[:, 2, :], in0=lr[:, :, 1], scalar=-0.5, in1=hc("cGC"), op0=ALU.mult, op1=ALU.add),
               reads=[key("lr"), ck("cGC")], writes=[key("rk2")])
        yield
        pb, pk = bB(0)
        for tt in range(4):
            op("pe", lambda e, tt=tt: e.transpose(out=pb[:, tt * 128:(tt + 1) * 128], in_=kTa[:, tt * 128:(tt + 1) * 128], identity=identb[:]),
               reads=[kTk, "identb"], writes=[pk])
        for tt in range(4):
            op("pe", lambda e, tt=tt: e.transpose(out=pb[:, 512 + tt * 128:512 + (tt + 1) * 128], in_=vTa[:, tt * 128:(tt + 1) * 128], identity=identb[:]),
               reads=[vTk, "identb"], writes=[pk])
        op("dve", lambda e: e.tensor_tensor(out=kg[:], in0=v4(pb[:, 0:512]), in1=bc4(rkb[:, 3, :]), op=ALU.mult), reads=[pk, key("rk3")], writes=[key("kg")])
        op("dve", lambda e: e.tensor_tensor(out=kd[:], in0=v4(pb[:, 0:512]), in1=bc4(rkb[:, 4, :]), op=ALU.mult), reads=[pk, key("rk4")], writes=[key("kd")])
        op("act", lambda e: e.copy(out=vtok[:], in_=v4(pb[:, 512:1024])), reads=[pk], writes=[key("vtok")])
        yield
        op("pool", lambda e: e.tensor_tensor(out=Dab[:, 0], in0=rep4(IDENT), in1=bc4(rkb[:, 1, :]), op=ALU.mult), reads=["cm", key("rk1")], writes=[key("Dab0")])
        pbc, bck = bF(1)
        op("pe", lambda e: e.matmul(out=pbc[:, :], lhsT=ONES, rhs=Dab[:, 0].rearrange("p a b -> p (a b)"), start=True, stop=True),
           reads=["cm", key("Dab0")], writes=[bck])
        if full:
            op("pool", lambda e: e.tensor_tensor(out=Dab[:, 1], in0=rep4(IDENT), in1=bc4(rkb[:, 2, :]), op=ALU.mult), reads=["cm", key("rk2")], writes=[key("Dab1")])
            pbc2, bck2 = bF(2)
            op("pe", lambda e: e.matmul(out=pbc2[:, :], lhsT=ONES, rhs=Dab[:, 1].rearrange("p a b -> p (a b)"), start=True, stop=True),
               reads=["cm", key("Dab1")], writes=[bck2])
        op("dve", lambda e: e.tensor_tensor(out=tmpm[:], in0=v4(pbc[:, :]), in1=rep4(cm[:, 7, :]), op=ALU.add), reads=[bck, "cm"],
           writes=[key("tmpm"), key("tmpm") + "h0", key("tmpm") + "h1"])
        for tt in range(4):
            op("act", lambda e, tt=tt: e.activation(out=E[:, 0, tt, :], in_=tmpm[:, tt, :], func=AF.Exp, bias=rkb[:, 6, tt:tt + 1]),
               reads=[key("tmpm"), key("rk6")], writes=[key("E0")])
        yield
        pkk, kkk = bF(0)
        for tt in range(4):
            sl = slice(tt * 128, (tt + 1) * 128)
            op("pe", lambda e, sl=sl: e.matmul(out=pkk[:, sl], lhsT=kTa[:, sl], rhs=kTa[:, sl], start=True, stop=True), reads=[kTk], writes=[kkk])
        Y0 = YP[0][:, :, 0:128]
        op("dve", lambda e: e.scalar_tensor_tensor(out=Y0, in0=v4(pkk[:, :]), scalar=-1.0, in1=E[:, 0], op0=ALU.mult, op1=ALU.mult),
           reads=[kkk, key("E0")], writes=[key("YP0a"), key("YP0a") + "h0", key("YP0a") + "h1"])
        if full:
            op("dve", lambda e: e.tensor_tensor(out=tmpm[:], in0=v4(pbc2[:, :]), in1=rep4(cm[:, 8, :]), op=ALU.add), reads=[bck2, "cm", key("tmpm")], writes=[key("tmpm")])
            op("act", lambda e: e.activation(out=Eq[:], in_=v4(pbc2[:, :]), func=AF.Exp), reads=[bck2], writes=[key("Eq")])
            for tt in range(4):
                op("act", lambda e, tt=tt: e.activation(out=E[:, 1, tt, :], in_=tmpm[:, tt, :], func=AF.Exp, bias=rkb[:, 6, tt:tt + 1]),
                   reads=[key("tmpm"), key("rk6")], writes=[key("E1")])
            pqk, qkk = bF(1)
            for tt in range(4):
                sl = slice(tt * 128, (tt + 1) * 128)
                op("pe", lambda e, sl=sl: e.matmul(out=pqk[:, sl], lhsT=kTa[:, sl], rhs=qTa[:, sl], start=True, stop=True), reads=[kTk, qTk], writes=[qkk])
            op("dve", lambda e: e.tensor_tensor(out=qkT[:], in0=v4(pqk[:, :]), in1=E[:, 1], op=ALU.mult), reads=[qkk, key("E1")], writes=[key("qkT")])
            op("pool", lambda e: e.tensor_tensor(out=qdT[:], in0=v4(qTa[:, :]), in1=Eq[:], op=ALU.mult), reads=[qTk, key("Eq")], writes=[key("qdT")])
        yield "FRONT_DONE"
        HB = cx["HB"]
        bR = lambda i: (bankF[i], "ps%d" % i)
        hk = lambda n, hh: key(n) + "h%d" % hh
        px, pxk = bB(2)
        for tt in range(4):
            op("pe", lambda e, tt=tt: e.transpose(out=px[:, tt * 128:(tt + 1) * 128], in_=YP[0][:, tt, 0:128], identity=identb[:]),
               reads=[key("YP0a"), "identb"], writes=[pxk])
        for hh in range(2):
            tsl = slice(2 * hh, 2 * hh + 2)
            op("act", lambda e, tsl=tsl, hh=hh: e.copy(out=XX[0][:, tsl, :], in_=v4(px[:, 0:512])[:, tsl, :]), reads=[pxk], writes=[hk("XX0", hh)])
            op("pool", lambda e, tsl=tsl: e.tensor_tensor(out=YP[1][:, tsl, 128:256], in0=YP[0][:, tsl, 0:128], in1=identb[:].unsqueeze(1).broadcast_to([128, 2, 128]), op=ALU.add),
               reads=[key("YP0a"), "identb"], writes=[hk("YP1b", hh)])
        yield
        for hh in range(2):
            yb_, xb2, xo = HB[hh]
            py, pyk = bR(yb_); pz, pzk = bR(xb2)
            for tt in (2 * hh, 2 * hh + 1):
                o0 = (tt % 2) * 128
                op("pe", lambda e, tt=tt, o0=o0, py=py: e.matmul(out=py[:, o0:o0 + 128], lhsT=XX[0][:, tt, :], rhs=YP[0][:, tt, 0:128], start=True, stop=True),
                   reads=[hk("XX0", hh), key("YP0a")], writes=[pyk])
                op("pe", lambda e, tt=tt, o0=o0, pz=pz, xo=xo: e.matmul(out=pz[:, xo + o0:xo + o0 + 128], lhsT=YP[0][:, tt, 0:128], rhs=XX[0][:, tt, :], start=True, stop=True),
                   reads=[hk("XX0", hh), key("YP0a")], writes=[pzk])
        for hh in range(2):
            yb_, xb2, xo = HB[hh]
            py, pyk = bR(yb_); pz, pzk = bR(xb2)
            tsl = slice(2 * hh, 2 * hh + 2)
            op("dve", lambda e, tsl=tsl, py=py: e.tensor_copy(out=YP[1][:, tsl, 0:128], in_=py[:, 0:256].rearrange("p (a b) -> p a b", a=2)),
               reads=[pyk], writes=[hk("YP1a", hh)])
            op("act", lambda e, tsl=tsl, pz=pz, xo=xo: e.copy(out=XX[1][:, tsl, :], in_=pz[:, xo:xo + 256].rearrange("p (a b) -> p a b", a=2)),
               reads=[pzk], writes=[hk("XX1", hh)])
        yield
        cur = 1
        for lvl in range(2, 6):
            c_, n_ = cur, 1 - cur
            last = (lvl == 5)
            for hh in range(2):
                yb_, xb2, xo = HB[hh]
                pv_, pvk = bR(yb_); pxx, pxxk = bR(xb2)
                for tt in (2 * hh, 2 * hh + 1):
                    o0 = (tt % 2) * 256
                    if not last:
                        op("pe", lambda e, tt=tt, c_=c_, pv_=pv_, o0=o0: e.matmul(out=pv_[:, o0:o0 + 256], lhsT=XX[c_][:, tt, :], rhs=YP[c_][:, tt, :], start=True, stop=True),
                           reads=[hk("XX%d" % c_, hh), hk("YP%da" % c_, hh), hk("YP%db" % c_, hh)], writes=[pvk])
                    else:
                        op("pe", lambda e, tt=tt, c_=c_, pv_=pv_, o0=o0: e.matmul(out=pv_[:, o0 + 128:o0 + 256], lhsT=XX[c_][:, tt, :], rhs=YP[c_][:, tt, 128:256], start=True, stop=True),
                           reads=[hk("XX%d" % c_, hh), hk("YP%db" % c_, hh)], writes=[pvk])
                for tt in (2 * hh, 2 * hh + 1):
                    o1 = (tt % 2) * 128
                    op("pe", lambda e, tt=tt, c_=c_, pxx=pxx, o1=o1, xo=xo: e.matmul(out=pxx[:, xo + o1:xo + o1 + 128], lhsT=YP[c_][:, tt, 0:128], rhs=XX[c_][:, tt, :], start=True, stop=True),
                       reads=[hk("XX%d" % c_, hh), hk("YP%da" % c_, hh)], writes=[pxxk])
            for hh in range(2):
                yb_, xb2, xo = HB[hh]
                pv_, pvk = bR(yb_); pxx, pxxk = bR(xb2)
                pv3 = pv_[:, :].rearrange("p (a b) -> p a b", a=2)
                tsl = slice(2 * hh, 2 * hh + 2)
                if not last:
                    op("act", lambda e, n_=n_, pv3=pv3, tsl=tsl: e.copy(out=YP[n_][:, tsl, 0:128], in_=pv3[:, :, 0:128]), reads=[pvk], writes=[hk("YP%da" % n_, hh)] + ([key("YP0a")] if n_ == 0 else []))
                op("dve", lambda e, c_=c_, n_=n_, pv3=pv3, tsl=tsl: e.tensor_tensor(out=YP[n_][:, tsl, 128:256], in0=YP[c_][:, tsl, 128:256], in1=pv3[:, :, 128:256], op=ALU.add),
                   reads=[pvk, hk("YP%db" % c_, hh)], writes=[hk("YP%db" % n_, hh)])
                op("act", lambda e, n_=n_, pxx=pxx, tsl=tsl, xo=xo: e.copy(out=XX[n_][:, tsl, :], in_=pxx[:, xo:xo + 256].rearrange("p (a b) -> p a b", a=2)),
                   reads=[pxxk], writes=[hk("XX%d" % n_, hh)])
            cur = n_
            yield
        c_ = cur
        for hh in range(2):
            yb_, xb2, xo = HB[hh]
            py, pyk = bR(yb_)
            tsl = slice(2 * hh, 2 * hh + 2)
            for tt in (2 * hh, 2 * hh + 1):
                o1 = (tt % 2) * 128
                op("pe", lambda e, tt=tt, py=py, o1=o1: e.matmul(out=py[:, o1:o1 + 128], lhsT=XX[c_][:, tt, :], rhs=YP[c_][:, tt, 128:256], start=True, stop=True),
                   reads=[hk("XX%d" % c_, hh), hk("YP%db" % c_, hh)], writes=[pyk])
            op("dve", lambda e, py=py, tsl=tsl: e.tensor_tensor(out=tmpm[:, tsl, :], in0=YP[c_][:, tsl, 128:256], in1=py[:, 0:256].rearrange("p (a b) -> p a b", a=2), op=ALU.add),
               reads=[pyk, hk("YP%db" % c_, hh), key("tmpm")], writes=[key("tmpm") + "h%d" % hh])
        tmk = [key("tmpm") + "h0", key("tmpm") + "h1"]
        op("act", lambda e: e.copy(out=TTb[:], in_=tmpm[:]), reads=tmk, writes=[key("TTb")])
        op("pool", lambda e: e.tensor_tensor(out=TbT[:], in0=tmpm[:], in1=bc4(hc("cB")), op=ALU.mult), reads=tmk + [ck("cB")], writes=[key("TbT")])
        yield
        pu, puk = bF(1)
        pw_, pwk_ = bF(2)
        for tt in range(4):
            sl = slice(tt * 128, (tt + 1) * 128)
            op("pe", lambda e, tt=tt, sl=sl: e.matmul(out=pu[:, sl], lhsT=TbT[:, tt, :], rhs=vtok[:, tt, :], start=True, stop=True),
               reads=[key("TbT"), key("vtok")], writes=[puk])
        for tt in range(4):
            sl = slice(tt * 128, (tt + 1) * 128)
            op("pe", lambda e, tt=tt, sl=sl: e.matmul(out=pw_[:, sl], lhsT=kg[:, tt, :], rhs=TTb[:, tt, :], start=True, stop=True),
               reads=[key("kg"), key("TTb")], writes=[pwk_])
        op("act", lambda e: e.copy(out=ub[:], in_=v4(pu[:, :])), reads=[puk], writes=[key("ub")])
        op("dve", lambda e: e.tensor_copy(out=wT[:], in_=v4(pw_[:, :])), reads=[pwk_], writes=[key("wT")])
        yield

    def gdn_chain(g, h, cx, cs):
        gi = g % 2; full = g >= G_FULL
        B = cs
        key = lambda n: "%s_%s" % (n, cs["id"])
        cbk = cx["cbanks"]
        C = {n: CT[n][:, gi] for n in CN}
        ck = lambda n: "%s_%d" % (n, gi)
        bF = lambda i: (bankF[i], "ps%d" % i)
        kd, ub, wT, ubf = B["kd"], B["ub"], B["wT"], B["ubf"]
        qkT, qdT, osb, og, sg = B.get("qkT"), B.get("qdT"), B.get("osb"), B.get("og"), B.get("sg")
        Sk, Sbk = "S%d" % h, "Sb%d" % h
        flip = 0
        for tt in range(4):
            t = 4 * g + tt
            store = full and t >= T0
            for hf in range(2):
                rows = slice(hf * 64, hf * 64 + 64)
                cdn = "cCD0" if hf == 0 else "cCD1"
                cd = C[cdn][:, tt, h:h + 1]
                pw, pwk = bF(cbk[flip % len(cbk)]); ps_, psk = bF(cbk[(flip + 1) % len(cbk)]); flip = 1 - flip
                op("pe", lambda e: e.matmul(out=pw[:, 0:128], lhsT=wT[:, tt, :], rhs=Sb[h][:], start=True, stop=True), reads=[key("wT"), Sbk], writes=[pwk])
                op("dve", lambda e: e.tensor_tensor(out=ubf[rows, :], in0=ub[rows, tt, :], in1=pw[rows, 0:128], op=ALU.subtract),
                   reads=[key("ub"), pwk], writes=[key("ubf")])
                if store:
                    op("pe", lambda e: e.matmul(out=pw[:, 128:256], lhsT=qdT[:, tt, :], rhs=Sb[h][:], start=True, stop=False), reads=[key("qdT"), Sbk, pwk], writes=[pwk])
                    op("pe", lambda e: e.matmul(out=pw[:, 128:256], lhsT=qkT[:, tt, :], rhs=ubf[:], start=False, stop=True),
                       reads=[key("qkT"), key("ubf"), pwk], writes=[pwk])
                    op("act", lambda e: e.copy(out=osb[rows, :], in_=pw[rows, 128:256]), reads=[pwk], writes=[key("osb")])
                yield
                op("pe", lambda e: e.matmul(out=ps_[:, 384:512], lhsT=kd[rows, tt, :], rhs=ubf[rows, :], start=True, stop=True), reads=[key("kd"), key("ubf")], writes=[psk])
                op("dve", lambda e: e.scalar_tensor_tensor(out=Sb[h][:], in0=Sst[h][:], scalar=cd, in1=ps_[:, 384:512], op0=ALU.mult, op1=ALU.add),
                   reads=[Sk, ck(cdn), psk], writes=[Sbk])
                op("dve", lambda e: e.scalar_tensor_tensor(out=Sst[h][:], in0=Sst[h][:], scalar=cd, in1=ps_[:, 384:512], op0=ALU.mult, op1=ALU.add),
                   reads=[Sk, ck(cdn), psk], writes=[Sk])
                yield
            if store:
                s_ = t - T0
                pg, pgk = bF(cbk[flip % len(cbk)])
                for k in range(8):
                    op("pe", lambda e, k=k: e.matmul(out=pg[:, 256:384], lhsT=nTg[gi][:, k, tt * 128:(tt + 1) * 128],
                                                    rhs=PH["Wgg"][:, k, h * 128:(h + 1) * 128], start=(k == 0), stop=(k == 7)),
                       reads=["nTg%d" % gi, "Wgg"], writes=[pgk])
                op("act", lambda e: e.activation(out=sg[:], in_=pg[:, 256:384], func=AF.Silu), reads=[pgk], writes=[key("sg")])
                sc = st[:, 4 + h:5 + h]; sck = "st%d" % (4 + h)
                rms_rstd(osb[:], [key("osb")], HD, sc, sck)
                op("dve", lambda e: e.scalar_tensor_tensor(out=og[:], in0=osb[:], scalar=sc, in1=GNW, op0=ALU.mult, op1=ALU.mult),
                   reads=[key("osb"), sck, "gsmbc"], writes=[key("og")])
                op("pool", lambda e: e.tensor_tensor(out=mix[:, s_, h * 128:(h + 1) * 128], in0=og[:], in1=sg[:], op=ALU.mult),
                   reads=[key("og"), key("sg")], writes=["mix%d" % h])
                yield

    def gdn_task(g, heads, cx, prefetch):
        gi = g % 2; full = g >= G_FULL
        prev = None
        if cx.get("prefetched") != g:
            for _ in gdn_proj(gi, heads[0], full, cx):
                yield
        for i, h in enumerate(heads):
            cs = cx["cs"][i % len(cx["cs"])]
            subs = [gdn_local(g, h, cx, cs)] + ([prev] if prev is not None else [])
            while subs:
                for sg_ in list(subs):
                    try:
                        r = next(sg_)
                        if r == "FRONT_DONE" and i + 1 < len(heads):
                            subs.append(gdn_proj(gi, heads[i + 1], full, cx))
                    except StopIteration:
                        subs.remove(sg_)
                yield
            prev = gdn_chain(g, h, cx, cs)
        subs = [prev]
        if prefetch:
            subs.append(gdn_proj((g + 1) % 2, heads[0], full, cx))
            cx["prefetched"] = g + 1
        while subs:
            for sg_ in list(subs):
                try:
                    next(sg_)
                except StopIteration:
                    subs.remove(sg_)
            yield

    def ret_tile(t, tt, gi, h, full, ci):
        nk = "nTg%d" % gi
        Rr = PH["RS"]; rbk = PH["rbank"]; rbk2 = PH["rbank2"]
        cs_t, sn_t = PH["cs_t"], PH["sn_t"]
        rk_ = lambda n: "%s_r" % n
        rsb, rt1, rt2, krot, kz, rvt = Rr["rsb"], Rr["rt1"], Rr["rt2"], Rr["krot"], Rr["kz"], Rr["rvt"]
        qrot, rkT, rqT, rqx, rPT, sgr, osb, og = [Rr.get(n) for n in ("qrot", "rkT", "rqT", "rqx", "rPT", "sgr", "osb", "og")]
        csk, snk = "cs_t%d" % ci, "sn_t%d" % ci
        store = full and t >= T0
        pf, fk = bankF[rbk], "ps%d" % rbk
        for k in range(8):
            op("pe", lambda e, k=k: e.matmul(out=pf[:, 0:256], lhsT=nTg[gi][:, k, tt * 128:(tt + 1) * 128], rhs=Wrkv[h][:, k, :],
                                            start=(k == 0), stop=(k == 7)), reads=[nk, "Wrkv%d" % h], writes=[fk])
        if store:
            for k in range(8):
                op("pe", lambda e, k=k: e.matmul(out=pf[:, 256:512], lhsT=nTg[gi][:, k, tt * 128:(tt + 1) * 128], rhs=PH["Wrqg"][h][:, k, :],
                                                start=(k == 0), stop=(k == 7)), reads=[nk, "Wrqg%d" % h, fk], writes=[fk])
        op("act", lambda e: e.copy(out=rsb[:, 0:128], in_=pf[:, 0:128]), reads=[fk], writes=[rk_("rsb")])
        op("act", lambda e: e.copy(out=rvt[:], in_=pf[:, 128:256]), reads=[fk], writes=[rk_("rvt")])
        if store:
            op("act", lambda e: e.copy(out=rsb[:, 128:256], in_=pf[:, 256:384]), reads=[fk, rk_("rsb")], writes=[rk_("rsb")])
        if store:
            op("act", lambda e: e.activation(out=sgr[:], in_=pf[:, 384:512], func=AF.Silu), reads=[fk], writes=[rk_("sgr")])
        yield

        def rotary(src, dst, dstk):
            op("dve", lambda e: e.tensor_tensor(out=rt1[:], in0=src, in1=cs_t[ci][:], op=ALU.mult), reads=[rk_("rsb"), csk], writes=[rk_("rt1")])
            op("dve", lambda e: e.tensor_tensor(out=rt2[:, 0:128:2], in0=src[:, 1:128:2], in1=sn_t[ci][:, 0:128:2], op=ALU.mult),
               reads=[rk_("rsb"), snk], writes=[rk_("rt2e")])
            op("dve", lambda e: e.tensor_tensor(out=rt2[:, 1:128:2], in0=src[:, 0:128:2], in1=sn_t[ci][:, 1:128:2], op=ALU.mult),
               reads=[rk_("rsb"), snk], writes=[rk_("rt2o")])
            op("pool", lambda e: e.tensor_tensor(out=dst[:], in0=rt1[:], in1=rt2[:], op=ALU.add),
               reads=[rk_("rt1"), rk_("rt2e"), rk_("rt2o")], writes=[dstk])
        rotary(rsb[:, 0:128], krot, rk_("krot"))
        op("act", lambda e: e.activation(out=kz[:], in_=krot[:], func=AF.Copy, scale=rv[:, h:h + 1]), reads=[rk_("krot"), "rv"], writes=[rk_("kz")])
        Rk, Rbk = "R%d" % h, "Rb%d" % h
        yield
        if store:
            rotary(rsb[:, 128:256], qrot, rk_("qrot"))
            yield
            pb, pk = bank[rbk2], "ps%d" % rbk2
            op("pe", lambda e: e.transpose(out=pb[:, 0:128], in_=krot[:], identity=identb[:]), reads=[rk_("krot"), "identb"], writes=[pk])
            op("pe", lambda e: e.transpose(out=pb[:, 128:256], in_=qrot[:], identity=identb[:]), reads=[rk_("qrot"), "identb"], writes=[pk])
            op("act", lambda e: e.copy(out=rkT[:], in_=pb[:, 0:128]), reads=[pk], writes=[rk_("rkT")])
            op("act", lambda e: e.copy(out=rqT[:], in_=pb[:, 128:256]), reads=[pk], writes=[rk_("rqT")])
            op("pool", lambda e: e.tensor_tensor(out=rqx[:], in0=rqT[:], in1=xibc[:, h, :], op=ALU.mult), reads=[rk_("rqT"), "xibc"], writes=[rk_("rqx")])
            yield
            psc, sck = bankF[rbk2], "ps%d" % rbk2
            op("pe", lambda e: e.matmul(out=psc[:, 0:128], lhsT=rkT[:], rhs=rqT[:], start=True, stop=True), reads=[rk_("rkT"), rk_("rqT")], writes=[sck])
            op("dve", lambda e: e.tensor_tensor(out=rPT[:], in0=psc[:, 0:128], in1=rm[:, h, :], op=ALU.mult), reads=[sck, "rm"], writes=[rk_("rPT")])
            yield
            po, pok = bankF[rbk], "ps%d" % rbk
            op("pe", lambda e: e.matmul(out=po[:, 0:128], lhsT=rPT[:], rhs=rvt[:], start=True, stop=False), reads=[rk_("rPT"), rk_("rvt")], writes=[pok])
            op("pe", lambda e: e.matmul(out=po[:, 0:128], lhsT=rqx[:], rhs=Rb[h][:], start=False, stop=True), reads=[rk_("rqx"), Rbk, pok], writes=[pok])
            s_ = t - T0
            op("act", lambda e: e.copy(out=osb[:], in_=po[:, 0:128]), reads=[pok], writes=[rk_("osb")])
            yield
            sc = st[:, 8 + h:9 + h]; sck2 = "st%d" % (8 + h)
            rms_rstd(osb[:], [rk_("osb")], HD, sc, sck2)
            op("dve", lambda e: e.tensor_scalar(out=og[:], in0=osb[:], scalar1=sc, scalar2=None, op0=ALU.mult), reads=[rk_("osb"), sck2], writes=[rk_("og")])
            op("pool", lambda e: e.tensor_tensor(out=mix[:, s_, 512 + h * 128:512 + (h + 1) * 128], in0=og[:], in1=sgr[:], op=ALU.mult),
               reads=[rk_("og"), rk_("sgr")], writes=["mix%d" % (4 + h)])
        if not store:
            yield
        pr, prk = bankF[rbk], "ps%d" % rbk
        op("pe", lambda e: e.matmul(out=pr[:, 0:128], lhsT=kz[:], rhs=rvt[:], start=True, stop=True), reads=[rk_("kz"), rk_("rvt")], writes=[prk])
        g128 = float(np.float64(1.0 - 2.0 ** (-5 - h)) ** 128)
        op("dve", lambda e: e.scalar_tensor_tensor(out=Rst[h][:], in0=Rst[h][:], scalar=g128, in1=pr[:, 0:128], op0=ALU.mult, op1=ALU.add),
           reads=[Rk, prk], writes=[Rk])
        op("act", lambda e: e.copy(out=Rb[h][:], in_=Rst[h][:]), reads=[Rk], writes=[Rbk])
        yield

    def prep_task(g):
        gi = g % 2
        for r in range(6):
            if 0 <= r - 2 < 4:
                norm_b(4 * g + r - 2, gi, r - 2)
            if r < 4:
                norm_a(4 * g + r)
            yield
        yield
        if "g" in parts:
            yield from gdn_common(gi, g >= G_FULL)

    def ret_task(g):
        gi = g % 2; full = g >= G_FULL
        for tt in range(4):
            t = 4 * g + tt
            ci = t % 2
            dma("sp", PH["cs_t"][ci][:], cosF[t * 128:(t + 1) * 128, :], writes=["cs_t%d" % ci])
            dma("sp", PH["sn_t"][ci][:], sinS[t * 128:(t + 1) * 128, :], writes=["sn_t%d" % ci])
            for h in range(4):
                yield from ret_tile(t, tt, gi, h, full, ci)


    def run_tasks(tasks, reps=None):
        tasks = list(tasks)
        reps = dict(reps or {})
        while tasks:
            for tk_ in list(tasks):
                for _ in range(reps.get(id(tk_), 1)):
                    try:
                        next(tk_)
                    except StopIteration:
                        tasks.remove(tk_)
                        break

    g0 = NG - ng_run
    g1 = NG
    if dev_g0 is not None:
        g0 = dev_g0; g1 = g0 + ng_run
    run_tasks([prep_task(g0)])
    cur_phase = [None]
    for g in range(g0, g1):
        full = g >= G_FULL
        if cur_phase[0] != full:
            if cur_phase[0] is not None:
                p.barrier()
                es_ph[0].close()
                es_ph[0] = ExitStack()
            alloc_phase(full)
            cur_phase[0] = full
        tasks = []
        if "g" in parts:
            pf_ok = (g + 1 < g1) and ((g + 1 >= G_FULL) == full)
            if full:
                tasks.append(gdn_task(g, [0, 1, 2, 3], PH["streams"][0], pf_ok))
            else:
                tasks.append(gdn_task(g, [0, 2], PH["streams"][0], pf_ok))
                tasks.append(gdn_task(g, [1, 3], PH["streams"][1], pf_ok))
        reps = {}
        if "r" in parts:
            rt_ = ret_task(g)
            tasks.append(rt_)
            reps[id(rt_)] = 1 if full else 2
        if g + 1 < g1:
            tasks.append(prep_task(g + 1))
        run_tasks(tasks, reps)

    def early(src):
        p.barrier()
        dma("sp", out[0:128, :], src, writes=["out0"])
        p.finish()
        return nc, p
    if phase == "pass":
        return early(xt[0][:])
    p.barrier()
    es_ph[0].close()
    es_keep.close()
    PHTAG[0] = "_X"
    hres = R("hres", [128, NSLOT, D], F32)
    n2T = R("n2T", [128, 8, 2 + OWN], BF16)
    es_b = ExitStack()
    Wout = L("Wout", [128, 8, D], BF16, es_b)
    mixT = [L("mixT%d" % i, [128, 8, 128], BF16, es_b) for i in range(2)]
    xt2 = [L("xt2%d" % i, [128, D], F32, es_b) for i in range(2)]
    junk = L("junk2", [128, D], F32, es_b)
    nb2 = [L("nb2%d" % i, [128, D], BF16, es_b) for i in range(2)]
    st = L("st2", [128, 4], F32, es_b)
    dma("pool", Wout[:], w_out.rearrange("(k p) c -> p k c", p=128), writes=["Wout"])
    nw2 = L("nw2", [128, D], F32, es_b)
    dma("sp", nw2[:], vecs[1:2, :].partition_broadcast(128), writes=["nw2"])

    def b3_a(s_):
        t = T0 + s_
        dma("sp", xt2[s_ % 2][:], xpad[t * 128:(t + 1) * 128, :], writes=["xt2%d" % (s_ % 2)])
        pb, pk = nB()
        for k in range(8):
            op("pe", lambda e, k=k: e.transpose(out=pb[:, k * 128:(k + 1) * 128], in_=mix[:, s_, k * 128:(k + 1) * 128], identity=identb[:]),
               reads=["mix%d" % k, "identb"], writes=[pk])
        op("act", lambda e: e.copy(out=mixT[s_ % 2][:], in_=pb[:, :].rearrange("p (k n) -> p k n", k=8)), reads=[pk], writes=["mixT%d" % (s_ % 2)])

    def b3_b(s_):
        xb_ = xt2[s_ % 2]; xk = "xt2%d" % (s_ % 2); mT = mixT[s_ % 2]; mk = "mixT%d" % (s_ % 2)
        for half in range(2):
            pf, fk = nF()
            for k in range(8):
                op("pe", lambda e, k=k: e.matmul(out=pf[:, :], lhsT=mT[:, k, :], rhs=Wout[:, k, half * 512:(half + 1) * 512],
                                                start=(k == 0), stop=(k == 7)), reads=[mk, "Wout"], writes=[fk])
            op("dve", lambda e: e.tensor_tensor(out=hres[:, s_, half * 512:(half + 1) * 512], in0=xb_[:, half * 512:(half + 1) * 512],
                                                in1=pf[:, :], op=ALU.add), reads=[xk, fk], writes=["h%d_%d" % (s_, half)])

    def b3_c(s_):
        hks = ["h%d_0" % s_, "h%d_1" % s_]
        sc = st[:, (s_ % 2):(s_ % 2) + 1]; sck = "stb%d" % (s_ % 2)
        rms_rstd(hres[:, s_, :], hks, D, sc, sck)
        op("dve", lambda e: e.scalar_tensor_tensor(out=nb2[s_ % 2][:], in0=hres[:, s_, :], scalar=sc, in1=nw2[:],
                                                   op0=ALU.mult, op1=ALU.mult), reads=hks + [sck, "nw2"], writes=["nb2%d" % (s_ % 2)])

    def b3_d(s_):
        nb_ = nb2[s_ % 2]; nbk = "nb2%d" % (s_ % 2)
        pb, pk = nB()
        for k in range(8):
            op("pe", lambda e, k=k: e.transpose(out=pb[:, k * 128:(k + 1) * 128], in_=nb_[:, k * 128:(k + 1) * 128], identity=identb[:]),
               reads=[nbk, "identb"], writes=[pk])
        pv = pb[:, :].rearrange("p (k n) -> p k n", k=8)
        if s_ == 0:
            op("act", lambda e: e.copy(out=n2T[:, :, 0:2], in_=pv[:, :, 126:128]), reads=[pk], writes=["n2T_h"])
        else:
            op("act", lambda e: e.copy(out=n2T[:, :, 2 + (s_ - 1) * 128:2 + s_ * 128], in_=pv), reads=[pk], writes=["n2T_%d" % ((s_ - 1) // 4)])

    for r in range(NSLOT + 3):
        if 0 <= r - 3 < NSLOT:
            b3_d(r - 3)
        if 0 <= r - 2 < NSLOT:
            b3_c(r - 2)
        if 0 <= r - 1 < NSLOT:
            b3_b(r - 1)
        if r < NSLOT:
            b3_a(r)

    if phase == "b3":
        return early(hres[:, 1, :])
    p.barrier()
    es_b.close()
    es_mix.close()
    es_c = ExitStack()
    NGRP = DFF // 256
    Wgu = [L("Wgu%d" % i, [128, 2, 8, 256], BF16, es_c) for i in range(2)]
    Wd = [L("Wd%d" % i, [128, 2, D], BF16, es_c) for i in range(2)]
    mcw = L("mcw", [128, 44, 3], F32, es_c)
    hb = [L("hb%d" % i, [128, 2 + OWN], F32, es_c) for i in range(2)]
    yb = [L("yb%d" % i, [128, OWN], F32, es_c) for i in range(2)]
    actT = [L("actT%d" % i, [128, 2, OWN], BF16, es_c) for i in range(2)]
    st = L("st3", [128, 4], F32, es_c)
    ob = [L("ob%d" % i, [128, D], F32, es_c) for i in range(2)]
    junk = L("junk3", [128, D], BF16, es_c)
    dma("sp", mcw[:], mcwT.rearrange("(c p) i -> p c i", p=128), writes=["mcw"])
    nw3 = L("nw3", [128, D], F32, es_c)
    dma("sp", nw3[:], vecs[2:3, :].partition_broadcast(128), writes=["nw3"])
    w_up_v = w_up.rearrange("(k p) c -> p k c", p=128)
    w_dn_v = w_down.rearrange("(c p) d -> p c d", p=128)

    def load_up(gi_):
        b = gi_ % 2
        dma("pool", Wgu[b][:, 0, :, :], w_up_v[:, :, gi_ * 256:(gi_ + 1) * 256], writes=["Wgu%d" % b])
        dma("pool", Wgu[b][:, 1, :, :], w_up_v[:, :, DFF + gi_ * 256:DFF + (gi_ + 1) * 256], writes=["Wgu%d" % b])

    def load_dn(gi_):
        b = gi_ % 2
        dma("pool", Wd[b][:], w_dn_v[:, gi_ * 2:gi_ * 2 + 2, :], writes=["Wd%d" % b])

    def ffn_up(gi_):
        b = gi_ % 2
        ak = "actT%d" % b
        for fc in range(2):
            for which in range(2):
                hb_ = hb[which]; hbk = "hb%d" % which
                cidx = which * 22 + gi_ * 2 + fc
                lw = lambda k: Wgu[b][:, which, k, fc * 128:(fc + 1) * 128]
                pf, fk = nF()
                for k in range(8):
                    op("pe", lambda e, k=k: e.matmul(out=pf[:, 0:2], lhsT=lw(k), rhs=n2T[:, k, 0:2], start=(k == 0), stop=(k == 7)),
                       reads=["Wgu%d" % b, "n2T_h"], writes=[fk])
                op("act", lambda e: e.copy(out=hb_[:, 0:2], in_=pf[:, 0:2]), reads=[fk], writes=[hbk + "h"])
                for tb in range(4):
                    pf, fk = nF()
                    for k in range(8):
                        op("pe", lambda e, k=k: e.matmul(out=pf[:, :], lhsT=lw(k), rhs=n2T[:, k, 2 + tb * 512:2 + (tb + 1) * 512],
                                                        start=(k == 0), stop=(k == 7)), reads=["Wgu%d" % b, "n2T_%d" % tb], writes=[fk])
                    op("act", lambda e, tb=tb: e.copy(out=hb_[:, 2 + tb * 512:2 + (tb + 1) * 512], in_=pf[:, :]), reads=[fk], writes=[hbk + "_%d" % tb])
                hbks = [hbk + "h"] + [hbk + "_%d" % i for i in range(4)]
                ybk = "yb%d" % which
                op("act", lambda e: e.activation(out=yb[which][:], in_=hb_[:, 2:2 + OWN], func=AF.Copy, scale=mcw[:, cidx, 2:3]),
                   reads=hbks + ["mcw"], writes=[ybk])
                for i in (1, 0):
                    op("dve", lambda e, i=i: e.scalar_tensor_tensor(out=yb[which][:], in0=hb_[:, i:i + OWN], scalar=mcw[:, cidx, i:i + 1],
                                                                   in1=yb[which][:], op0=ALU.mult, op1=ALU.add), reads=hbks + ["mcw", ybk], writes=[ybk])
            op("act", lambda e: e.activation(out=yb[0][:], in_=yb[0][:], func=AF.Silu), reads=["yb0"], writes=["yb0"])
            op("pool", lambda e: e.tensor_tensor(out=actT[b][:, fc, :], in0=yb[0][:], in1=yb[1][:], op=ALU.mult),
               reads=["yb0", "yb1"], writes=[ak + "_%d" % fc])

    def final_tile(tt):
        hks = ["h%d_0" % (tt + 1), "h%d_1" % (tt + 1)]
        sc = st[:, (tt % 2):(tt % 2) + 1]; sck = "stf%d" % (tt % 2)
        rms_rstd(hres[:, tt + 1, :], hks, D, sc, sck)
        o_ = ob[tt % 2]; ok = "ob%d" % (tt % 2)
        op("dve", lambda e: e.scalar_tensor_tensor(out=o_[:], in0=hres[:, tt + 1, :], scalar=sc, in1=nw3[:],
                                                   op0=ALU.mult, op1=ALU.mult), reads=hks + [sck, "nw3"], writes=[ok])
        dma("sp", out[tt * 128:(tt + 1) * 128, :], o_[:], reads=[ok], writes=["out%d" % tt])

    def ffn_down(gi_):
        b = gi_ % 2
        ak = "actT%d" % b
        for tt in range(16):
            for half in range(2):
                pf, fk = nF()
                for fc in range(2):
                    op("pe", lambda e, fc=fc: e.matmul(out=pf[:, :], lhsT=actT[b][:, fc, tt * 128:(tt + 1) * 128],
                                                      rhs=Wd[b][:, fc, half * 512:(half + 1) * 512], start=(fc == 0), stop=(fc == 1)),
                       reads=[ak + "_0", ak + "_1", "Wd%d" % b], writes=[fk])
                hk = "h%d_%d" % (tt + 1, half)
                op("dve", lambda e: e.tensor_tensor(out=hres[:, tt + 1, half * 512:(half + 1) * 512],
                                                    in0=hres[:, tt + 1, half * 512:(half + 1) * 512], in1=pf[:, :], op=ALU.add),
                   reads=[fk, hk], writes=[hk])
            if gi_ == NGRP - 1 and tt >= 1:
                final_tile(tt - 1)
        if gi_ == NGRP - 1:
            final_tile(15)

    load_up(0); load_dn(0)
    for gi_ in range(NGRP + 1):
        if gi_ + 1 < NGRP:
            load_up(gi_ + 1)
        if gi_ < NGRP:
            ffn_up(gi_)
        if gi_ >= 1:
            ffn_down(gi_ - 1)
        if gi_ + 1 < NGRP:
            load_dn(gi_ + 1)

    p.finish()
    es_c.close()
    es_r.close()
    return nc, p


def _consts():
    idx = np.arange(128)
    same = (idx[:, None] // 64) == (idx[None, :] // 64)
    cm = np.zeros((10, 128, 128), np.float32)
    cm[0] = np.eye(128)
    cm[1] = 1.0
    cm[2] = (same & (idx[:, None] < idx[None, :]))
    cm[3] = (same & (idx[:, None] <= idx[None, :]))
    cm[4] = cm[3]
    cm[5] = (idx[:, None] < 64) * np.ones((1, 128))
    cm[6] = (idx[:, None] >= 64) * np.ones((1, 128))
    cm[7] = (1.0 - cm[2]) * -30000.0
    cm[8] = (1.0 - cm[3]) * -30000.0
    hh = np.arange(4, dtype=np.float64)
    gam = 1.0 - 2.0 ** (-5.0 - hh)
    lg = np.log(gam)
    rel = (idx[None, :] - idx[:, None]).astype(np.float64)
    rmat = np.where(rel[None] >= 0, np.exp(rel[None] * lg[:, None, None]), 0.0) * HD ** -0.5
    rvec = np.zeros((128, 8), np.float64)
    rvec[:, 0:4] = np.exp((127.0 - idx[:, None]) * lg[None, :]) * HD ** -0.5
    rxi = np.exp((idx[None, :] + 1.0) * lg[:, None])
    return cm, rmat.astype(np.float32), rvec.astype(np.float32), rxi.astype(np.float32)


_CACHE = {}


def kernel(x, attn_norm_w, w_in, gdn_conv_w, gdn_a_log, gdn_dt_bias, gdn_norm_w, w_out, mlp_norm_w,
           w_up, mlp_conv_w, w_down, final_norm_w):
    f = lambda a: np.ascontiguousarray(np.asarray(a, dtype=np.float32))
    x2 = f(x).reshape(S, D)
    if "nc" not in _CACHE:
        _CACHE["nc"] = build_program()[0]
    nc = _CACHE["nc"]
    cm, rmat, rvec, rxi = _consts()
    vecs = np.stack([f(attn_norm_w)[0], f(mlp_norm_w)[0], f(final_norm_w)], 0)
    gsm = np.concatenate([f(gdn_a_log)[0], f(gdn_dt_bias)[0], f(gdn_norm_w)[0]])[None, :]
    angle = (1.0 / (10000.0 ** np.linspace(0.0, 1.0, 64, dtype=np.float32))).astype(np.float32)
    angle = np.repeat(angle, 2)
    sign = np.tile(np.array([-1.0, 1.0], np.float32), 64)
    common = {
        "w_in": f(w_in)[0], "w_out": f(w_out)[0], "w_up": f(w_up)[0], "w_down": f(w_down)[0],
        "gcwT": np.ascontiguousarray(f(gdn_conv_w)[0].T), "mcwT": np.ascontiguousarray(f(mlp_conv_w)[0].T),
        "vecs": np.ascontiguousarray(vecs), "gsm": np.ascontiguousarray(gsm),
        "cmat": cm, "rmat": rmat, "rvec": rvec, "rxi": rxi,
    }
    in_maps = []
    for c in range(NCORES):
        n_real = OWN * (c + 1)
        xp = np.zeros((S, D), np.float32)
        xp[S - n_real:] = x2[:n_real]
        pos = (np.arange(S, dtype=np.int64) - (S - n_real)).astype(np.float32)
        phase = pos[:, None] * angle[None, :]
        m = dict(common)
        m["xpad"] = xp
        m["cosF"] = np.cos(phase).astype(np.float32)
        m["sinS"] = (np.sin(phase) * sign[None, :]).astype(np.float32)
        in_maps.append(m)
    res = run_bass_kernel_spmd(nc, in_maps, core_ids=list(range(NCORES)))
    outs = [np.asarray(res.results[c]["out"], dtype=np.float32) for c in range(NCORES)]
    return np.concatenate(outs, 0).reshape(1, S, D)
```

```python
import math
from contextlib import ExitStack
import numpy as np
import concourse.bass as bass
import concourse.mybir as mybir
from concourse.bass_utils import run_bass_kernel_spmd

F32 = mybir.dt.float32
BF16 = mybir.dt.bfloat16
ALU = mybir.AluOpType
AF = mybir.ActivationFunctionType

NCORES = 8
S = 16384
D = 1024
NT = S // 128
NG = NT // 4
OWN = 2048
T0 = NT - OWN // 128 - 1
NSLOT = NT - T0
G_FULL = T0 // 4
DFF = 2816
EPS = 1e-6
HD = 128
IN_COLS = 4104


class Prog:
    ENG = ("pe", "act", "dve", "pool", "sp")

    def __init__(self, nc, n_dma_sems=8):
        self.nc = nc
        self.eng = {"pe": nc.tensor, "act": nc.scalar, "dve": nc.vector,
                    "pool": nc.gpsimd, "sp": nc.sync}
        self.sem = {e: nc.alloc_semaphore("c_" + e) for e in self.ENG}
        self.cnt = {e: 0 for e in self.ENG}
        self.dsem = {e: [nc.alloc_semaphore("d_%s%d" % (e, i)) for i in range(n_dma_sems)]
                     for e in ("sp", "pool")}
        self.dval = {e: [0] * n_dma_sems for e in ("sp", "pool")}
        self.drr = {e: 0 for e in ("sp", "pool")}
        self.seen = {e: {} for e in self.ENG}
        self.lastw = {}
        self.readers = {}
        self.n_ins = 0
        self.n_wait = 0

    def _semof(self, tok):
        return self.sem[tok] if isinstance(tok, str) else self.dsem[tok[0]][tok[1]]

    def _wait(self, e, tok, val):
        if tok == e and e == "pe":
            return
        if self.seen[e].get(tok, 0) >= val:
            return
        self.eng[e].wait_ge(self._semof(tok), val)
        self.seen[e][tok] = val
        self.n_wait += 1

    def _deps(self, e, reads, writes):
        for k in reads:
            w = self.lastw.get(k)
            if w is not None:
                self._wait(e, *w)
        for k in writes:
            w = self.lastw.get(k)
            if w is not None:
                self._wait(e, *w)
            for r in self.readers.get(k, ()):
                self._wait(e, *r)

    def _commit(self, tokval, reads, writes):
        for k in writes:
            self.lastw[k] = tokval
            self.readers[k] = []
        for k in reads:
            lst = self.readers.setdefault(k, [])
            lst[:] = [r for r in lst if r[0] != tokval[0]]
            lst.append(tokval)

    def op(self, e, fn, reads=(), writes=()):
        ex = [k for k in reads if k.startswith("ps")]
        if ex:
            reads = [k for k in reads if not k.startswith("ps")]
            writes = list(writes) + [k for k in ex if k not in writes]
        self._deps(e, reads, writes)
        ins = fn(self.eng[e])
        self.cnt[e] += 1
        ins.then_inc(self.sem[e], 1)
        self._commit((e, self.cnt[e]), reads, writes)
        self.n_ins += 1
        return ins

    def dma(self, e, out, in_, reads=(), writes=(), **kw):
        i = self.drr[e]
        self.drr[e] = (i + 1) % len(self.dsem[e])
        tok = (e, i)
        if self.dval[e][i] > 0:
            self._wait(e, tok, self.dval[e][i])
        self._deps(e, reads, writes)
        ins = self.eng[e].dma_start(out=out, in_=in_, **kw)
        self.dval[e][i] += 16
        ins.then_inc(self.dsem[e][i], 16)
        self._commit((tok, self.dval[e][i]), reads, writes)
        self.n_ins += 1
        return ins

    def barrier(self):
        for e in self.ENG:
            for e2 in self.ENG:
                if e2 != e and self.cnt[e2] > 0:
                    self._wait(e, e2, self.cnt[e2])
            for q in self.dsem:
                for i, v in enumerate(self.dval[q]):
                    if v > 0:
                        self._wait(e, (q, i), v)
        self.lastw.clear()
        self.readers.clear()

    def finish(self):
        for q in self.dsem:
            for i, v in enumerate(self.dval[q]):
                if v > 0:
                    self._wait("sp", (q, i), v)


def build_program(ng_run=NG, phase="all", parts="gr", dev_g0=None):
    nc = bass.Bass("TRN2", target_bir_lowering=False)
    dt_in = lambda name, shape: nc.dram_tensor(name, list(shape), F32, kind="ExternalInput").ap()
    xpad = dt_in("xpad", [S, D])
    w_in = dt_in("w_in", [D, IN_COLS])
    w_out = dt_in("w_out", [D, D])
    w_up = dt_in("w_up", [D, 2 * DFF])
    w_down = dt_in("w_down", [DFF, D])
    gcwT = dt_in("gcwT", [1536, 4])
    mcwT = dt_in("mcwT", [2 * DFF, 3])
    vecs = dt_in("vecs", [3, D])
    gsm = dt_in("gsm", [1, 8 + 128])
    cmat = dt_in("cmat", [10, 128, 128])
    rmat = dt_in("rmat", [4, 128, 128])
    rvec = dt_in("rvec", [128, 8])
    rxi = dt_in("rxi", [4, 128])
    cosF = dt_in("cosF", [S, 128])
    sinS = dt_in("sinS", [S, 128])
    out = nc.dram_tensor("out", [OWN, D], F32, kind="ExternalOutput").ap()

    p = Prog(nc)
    SKIP = ""
    op = p.op
    def dma(q, o_, i_, reads=(), writes=(), tag="", **kw):
        if tag and tag in SKIP:
            return p.op("pool", lambda e: e.memset(o_, 1.0), reads=reads, writes=writes)
        return p.dma(q, o_, i_, reads=reads, writes=writes, **kw)

    bank = [nc.alloc_psum_tensor("bank%d" % i, [128, 1024], BF16) for i in range(8)]
    bankF = [bk_[:, :].bitcast(F32) for bk_ in bank]
    rr = {"B": 0, "F": 0}
    POOLS = {"B": [6, 7], "F": [0, 1, 2, 3, 4, 5]}

    def nB():
        lst = POOLS["B"]; i = lst[rr["B"] % len(lst)]; rr["B"] += 1
        return bank[i], "ps%d" % i

    def nF():
        lst = POOLS["F"]; i = lst[rr["F"] % len(lst)]; rr["F"] += 1
        return bankF[i], "ps%d" % i

    es_r = ExitStack()
    def R(name, shape, dt):
        return es_r.enter_context(nc.sbuf_tensor(name, list(shape), dt, side="right"))
    cm = R("cm", [128, 10, 128], F32)
    identb = R("identb", [128, 128], BF16)
    rm = R("rm", [128, 4, 128], F32)
    rv = R("rv", [128, 8], F32)
    xibc = R("xibc", [128, 4, 128], F32)
    nwbc = R("nwbc", [128, 1, D], F32)
    gsmbc = R("gsmbc", [128, 136], F32)
    gconst = R("gconst", [128, 16], F32)
    IDENT, ONES, MSU, MU, TRI, SC0, SC1 = [cm[:, i, :] for i in range(7)]
    dma("sp", cm[:], cmat.rearrange("m p n -> p m n"), writes=["cm"], tag="m")
    dma("sp", rm[:], rmat.rearrange("m p n -> p m n"), writes=["rm"], tag="m")
    dma("sp", rv[:], rvec, writes=["rv"])
    for h in range(4):
        dma("sp", xibc[:, h, :], rxi[h:h + 1, :].partition_broadcast(128), writes=["xibc"], tag="b")
    dma("sp", nwbc[:, 0, :], vecs[0:1, :].partition_broadcast(128), writes=["nwbc"], tag="b")
    dma("sp", gsmbc[:], gsm.partition_broadcast(128), writes=["gsmbc"], tag="b")
    op("dve", lambda e: e.tensor_copy(out=identb[:], in_=IDENT), reads=["cm"], writes=["identb"])
    op("act", lambda e: e.activation(out=gconst[:, 0:4], in_=gsmbc[:, 0:4], func=AF.Exp), reads=["gsmbc"], writes=["gconst"])
    op("dve", lambda e: e.tensor_scalar(out=gconst[:, 0:4], in0=gconst[:, 0:4], scalar1=-1.0, scalar2=None, op0=ALU.mult),
       reads=["gconst"], writes=["gconst"])
    op("dve", lambda e: e.tensor_copy(out=gconst[:, 4:8], in_=gsmbc[:, 4:8]), reads=["gsmbc", "gconst"], writes=["gconst"])
    GNW = gsmbc[:, 8:136]

    es_mix = ExitStack()
    mix = es_mix.enter_context(nc.sbuf_tensor("mix", [128, NSLOT, D], BF16, side="left"))
    es_keep = ExitStack()
    es_ph = [ExitStack()]
    PHTAG = [""]
    def L(name, shape, dt, es=None):
        if es is None:
            name = name + PHTAG[0]
        return (es or es_ph[0]).enter_context(nc.sbuf_tensor(name, list(shape), dt, side="left"))
    K = lambda name, shape, dt: L(name, shape, dt, es_keep)

    w_in_v = w_in.rearrange("(k p) c -> p k c", p=128)
    Wg = {}
    for h in range(4):
        for X in (1, 2):
            Wg[(h, X)] = K("Wg%d%d" % (h, X), [128, 8, 128], BF16)
            c0 = X * 512 + h * 128
            dma("pool", Wg[(h, X)][:], w_in_v[:, :, c0:c0 + 128], writes=["Wg%d%d" % (h, X)], tag="w")
    Wab = K("Wab", [128, 8, 8], BF16)
    dma("pool", Wab[:], w_in_v[:, :, 2048:2056], writes=["Wab"], tag="w")
    Wrkv = [K("Wrkv%d" % h, [128, 8, 256], BF16) for h in range(4)]
    for h in range(4):
        for j, X in enumerate((1, 2)):
            c0 = 2056 + X * 512 + h * 128
            dma("pool", Wrkv[h][:, :, j * 128:(j + 1) * 128], w_in_v[:, :, c0:c0 + 128], writes=["Wrkv%d" % h], tag="w")
    gcw = K("gcw", [128, 12, 4], F32)
    dma("sp", gcw[:], gcwT.rearrange("(c p) i -> p c i", p=128), writes=["gcw"], tag="c")
    xt = [K("xt%d" % i, [128, D], F32) for i in range(2)]
    junk = K("junk", [128, D], BF16)
    nb = [K("nb%d" % i, [128, D], BF16) for i in range(2)]
    nTg = [K("nTg%d" % i, [128, 8, 512], BF16) for i in range(2)]
    st = K("st", [128, 16], F32)
    ab = K("ab", [128, 2, 4, 8], F32)
    CN = ["cG", "cLB", "cB", "cGAM", "cGL0", "cGL1", "cEG", "cEKD", "cCD0", "cCD1", "cGLB", "cGC", "cBG", "cTMP"]
    CT = {n: K(n, [128, 2, 4, 4], F32) for n in CN}
    halo = [[K("halo%d%d" % (h, X), [128, 4], F32) for X in range(3)] for h in range(4)]
    Sst = [K("S%d" % h, [128, 128], F32) for h in range(4)]
    Sb = [K("Sb%d" % h, [128, 128], BF16) for h in range(4)]
    Rst = [K("R%d" % h, [128, 128], F32) for h in range(4)]
    Rb = [K("Rb%d" % h, [128, 128], BF16) for h in range(4)]
    for h in range(4):
        op("pool", lambda e, h=h: e.memset(Sst[h][:], 0.0), writes=["S%d" % h])
        op("pool", lambda e, h=h: e.memset(Sb[h][:], 0.0), writes=["Sb%d" % h])
        op("pool", lambda e, h=h: e.memset(Rst[h][:], 0.0), writes=["R%d" % h])
        op("pool", lambda e, h=h: e.memset(Rb[h][:], 0.0), writes=["Rb%d" % h])
        for X in range(3):
            op("pool", lambda e, h=h, X=X: e.memset(halo[h][X][:], 0.0), writes=["halo%d%d" % (h, X)])

    CHN = ("kd", "ub", "wT", "ubf", "qkT", "qdT", "osb", "og", "sg")
    PH = {}

    def alloc_stream(sid, full, nchain, lb, HB, cbanks):
        nE = 2 if full else 1
        cx = {"id": sid, "lb": lb, "HB": HB, "cbanks": cbanks}
        Ld = {}
        for n, shp, dt in (("lr", [128, 4, 2], F32), ("rkb", [128, 8, 4], F32), ("Dab", [128, nE, 4, 128], F32),
                           ("E", [128, nE, 4, 128], BF16), ("tmpm", [128, 4, 128], F32),
                           ("YP0", [128, 4, 256], BF16), ("YP1", [128, 4, 256], BF16),
                           ("XX0", [128, 4, 128], BF16), ("XX1", [128, 4, 128], BF16), ("kg", [128, 4, 128], BF16),
                           ("vtok", [128, 4, 128], BF16), ("TTb", [128, 4, 128], BF16), ("TbT", [128, 4, 128], BF16)):
            Ld[n] = L("%s_%s" % (n, sid), shp, dt)
        if full:
            Ld["Eq"] = L("Eq_%s" % sid, [128, 4, 128], BF16)
        cx["L"] = Ld
        cx["XT"] = [(L("XT%s%d" % (sid, X), [128, 512], BF16) if (full or X != 0) else None) for X in range(3)]
        cx["ybuf"] = [L("ybuf%s%d" % (sid, i), [128, 512], F32) for i in range(2)]
        cx["xw"] = [L("xw%s%d" % (sid, i), [128, 515], F32) for i in range(2)]
        cx["sq"] = L("sq%s" % sid, [128, 512], F32)
        cx["ssc"] = L("ssc%s" % sid, [128, 4, 2], F32)
        cx["cs"] = []
        for i in range(nchain):
            d = {"id": "%s%d" % (sid, i)}
            lst = [("kd", [128, 4, 128], BF16), ("ub", [128, 4, 128], F32), ("wT", [128, 4, 128], BF16), ("ubf", [128, 128], BF16)]
            if full:
                lst += [("qkT", [128, 4, 128], BF16), ("qdT", [128, 4, 128], BF16),
                        ("osb", [128, 128], F32), ("og", [128, 128], F32), ("sg", [128, 128], BF16)]
            for n, shp, dt in lst:
                d[n] = L("%s_%s" % (n, d["id"]), shp, dt)
            op("pool", lambda e, d=d: e.memset(d["ubf"][:], 0.0), writes=["ubf_%s" % d["id"]])
            cx["cs"].append(d)
        return cx

    def alloc_phase(full):
        PH.clear()
        PHTAG[0] = "_F" if full else "_S"
        Rr = {}
        lst = [("rsb", [128, 256], F32), ("rt1", [128, 128], F32), ("rt2", [128, 128], F32), ("krot", [128, 128], BF16),
               ("kz", [128, 128], BF16), ("rvt", [128, 128], BF16)]
        if full:
            lst += [("qrot", [128, 128], BF16), ("rkT", [128, 128], BF16), ("rqT", [128, 128], BF16), ("rqx", [128, 128], BF16),
                    ("rPT", [128, 128], BF16), ("sgr", [128, 128], BF16), ("osb", [128, 128], F32), ("og", [128, 128], F32)]
        for n, shp, dt in lst:
            Rr[n] = L("%s_r" % n, shp, dt)
        PH["RS"] = Rr
        PH["cs_t"] = [L("cs_t%d" % i, [128, 128], F32) for i in range(2)]
        PH["sn_t"] = [L("sn_t%d" % i, [128, 128], F32) for i in range(2)]
        if full:
            for h in range(4):
                Wg[(h, 0)] = L("Wg%d0" % h, [128, 8, 128], BF16)
                dma("pool", Wg[(h, 0)][:], w_in_v[:, :, h * 128:(h + 1) * 128], writes=["Wg%d0" % h], tag="w")
            PH["Wgg"] = L("Wgg", [128, 8, 512], BF16)
            dma("pool", PH["Wgg"][:], w_in_v[:, :, 1536:2048], writes=["Wgg"], tag="w")
            PH["Wrqg"] = [L("Wrqg%d" % h, [128, 8, 256], BF16) for h in range(4)]
            for h in range(4):
                for j, X in enumerate((0, 3)):
                    c0 = 2056 + X * 512 + h * 128
                    dma("pool", PH["Wrqg"][h][:, :, j * 128:(j + 1) * 128], w_in_v[:, :, c0:c0 + 128], writes=["Wrqg%d" % h], tag="w")
            PH["streams"] = [alloc_stream("A", True, 2, [0, 1, 2], [(0, 2, 0), (1, 2, 256)], [3, 4])]
            PH["rbank"] = 5; PH["rbank2"] = 6
        else:
            PH["streams"] = [alloc_stream("A", False, 2, [0, 1, 0], [(0, 1, 0), (0, 1, 256)], [4]),
                             alloc_stream("B", False, 2, [2, 3, 2], [(2, 3, 0), (2, 3, 256)], [5])]
            PH["rbank"] = 6; PH["rbank2"] = 6

    LN_DS = math.log(HD ** -0.5)

    def rms_rstd(src_ap, rkeys, n, col, ckey, jout=None, jkey="junk"):
        jo = junk[:, 0:n] if jout is None else jout
        op("act", lambda e: e.activation(out=jo, in_=src_ap, func=AF.Square, accum_out=col),
           reads=rkeys, writes=[jkey, ckey])
        op("act", lambda e: e.activation(out=col, in_=col, func=AF.Ln, bias=EPS, scale=1.0 / n),
           reads=[ckey], writes=[ckey])
        op("act", lambda e: e.activation(out=col, in_=col, func=AF.Exp, scale=-0.5), reads=[ckey], writes=[ckey])

    def norm_a(t):
        xb_ = xt[t % 2]; xk = "xt%d" % (t % 2); nb_ = nb[t % 2]; nbk = "nb%d" % (t % 2)
        dma("sp", xb_[:], xpad[t * 128:(t + 1) * 128, :], writes=[xk])
        rms_rstd(xb_[:], [xk], D, st[:, 0:1], "st0", jout=nb_[:], jkey=nbk)
        op("dve", lambda e: e.scalar_tensor_tensor(out=nb_[:], in0=xb_[:], scalar=st[:, 0:1], in1=nwbc[:, 0, :],
                                                   op0=ALU.mult, op1=ALU.mult), reads=[xk, "st0", "nwbc"], writes=[nbk])

    def norm_b(t, gi, tt):
        nb_ = nb[t % 2]; nbk = "nb%d" % (t % 2)
        pb, pk = bank[7], "ps7"
        for k in range(8):
            op("pe", lambda e, k=k: e.transpose(out=pb[:, k * 128:(k + 1) * 128], in_=nb_[:, k * 128:(k + 1) * 128],
                                               identity=identb[:]), reads=[nbk, "identb"], writes=[pk])
        nk = "nTg%d" % gi
        op("act", lambda e: e.copy(out=nTg[gi][:, :, tt * 128:(tt + 1) * 128],
                                   in_=pb[:, :].rearrange("p (k n) -> p k n", k=8)), reads=[pk], writes=[nk])

    def gdn_common(gi, full):
        nk = "nTg%d" % gi
        C = {n: CT[n][:, gi] for n in CN}
        ck = lambda n: "%s_%d" % (n, gi)
        abk = "ab%d" % gi
        for tt in range(4):
            pf, fk = bankF[7], "ps7"
            for k in range(8):
                op("pe", lambda e, k=k: e.matmul(out=pf[:, 0:8], lhsT=nTg[gi][:, k, tt * 128:(tt + 1) * 128],
                                                rhs=Wab[:, k, :], start=(k == 0), stop=(k == 7)),
                   reads=[nk, "Wab"], writes=[fk])
            op("dve", lambda e: e.tensor_copy(out=ab[:, gi, tt, :], in_=pf[:, 0:8]), reads=[fk], writes=[abk])
            yield
        for h in range(4):
            op("dve", lambda e, h=h: e.tensor_scalar(out=C["cTMP"][:, :, h], in0=ab[:, gi, :, h], scalar1=gconst[:, 4 + h:5 + h],
                                                    scalar2=None, op0=ALU.add), reads=[abk, "gconst"], writes=[ck("cTMP")])
        op("act", lambda e: e.activation(out=C["cTMP"], in_=C["cTMP"], func=AF.Exp), reads=[ck("cTMP")], writes=[ck("cTMP")])
        op("act", lambda e: e.activation(out=C["cTMP"], in_=C["cTMP"], func=AF.Ln, bias=1.0), reads=[ck("cTMP")], writes=[ck("cTMP")])
        for h in range(4):
            op("dve", lambda e, h=h: e.tensor_scalar(out=C["cG"][:, :, h], in0=C["cTMP"][:, :, h], scalar1=gconst[:, h:h + 1],
                                                    scalar2=None, op0=ALU.mult), reads=[ck("cTMP"), "gconst"], writes=[ck("cG")])
        op("act", lambda e: e.activation(out=C["cLB"], in_=ab[:, gi, :, 4:8], func=AF.Exp, scale=-1.0), reads=[abk], writes=[ck("cLB")])
        op("act", lambda e: e.activation(out=C["cLB"], in_=C["cLB"], func=AF.Ln, bias=1.0), reads=[ck("cLB")], writes=[ck("cLB")])
        op("act", lambda e: e.activation(out=C["cB"], in_=C["cLB"], func=AF.Exp, scale=-1.0), reads=[ck("cLB")], writes=[ck("cB")])
        op("dve", lambda e: e.tensor_scalar(out=C["cLB"], in0=C["cLB"], scalar1=-1.0, scalar2=None, op0=ALU.mult), reads=[ck("cLB"), ck("cB")], writes=[ck("cLB")])
        yield
        gflat = C["cG"].rearrange("p t h -> p (t h)")
        for lhs, dn in ((TRI, "cGAM"), (SC0, "cGL0"), (SC1, "cGL1")):
            pf, fk = bankF[7], "ps7"
            op("pe", lambda e, lhs=lhs: e.matmul(out=pf[:, 0:16], lhsT=lhs, rhs=gflat, start=True, stop=True),
               reads=["cm", ck("cG")], writes=[fk])
            op("dve", lambda e, dn=dn: e.tensor_copy(out=C[dn].rearrange("p t h -> p (t h)"), in_=pf[:, 0:16]),
               reads=[fk], writes=[ck(dn)])
        yield
        op("act", lambda e: e.activation(out=C["cEG"], in_=C["cGAM"], func=AF.Exp), reads=[ck("cGAM")], writes=[ck("cEG")])
        op("act", lambda e: e.activation(out=C["cCD0"], in_=C["cGL0"], func=AF.Exp), reads=[ck("cGL0")], writes=[ck("cCD0")])
        op("act", lambda e: e.activation(out=C["cCD1"], in_=C["cGL1"], func=AF.Exp), reads=[ck("cGL1")], writes=[ck("cCD1")])
        op("dve", lambda e: e.tensor_tensor(out=C["cEKD"][0:64], in0=C["cGL0"][0:64], in1=C["cGAM"][0:64], op=ALU.subtract),
           reads=[ck("cGL0"), ck("cGAM")], writes=[ck("cEKD")])
        op("dve", lambda e: e.tensor_tensor(out=C["cEKD"][64:128], in0=C["cGL1"][64:128], in1=C["cGAM"][64:128], op=ALU.subtract),
           reads=[ck("cGL1"), ck("cGAM"), ck("cEKD")], writes=[ck("cEKD")])
        op("act", lambda e: e.activation(out=C["cEKD"], in_=C["cEKD"], func=AF.Exp), reads=[ck("cEKD")], writes=[ck("cEKD")])
        op("dve", lambda e: e.tensor_tensor(out=C["cGLB"], in0=C["cGAM"], in1=C["cLB"], op=ALU.add), reads=[ck("cGAM"), ck("cLB")], writes=[ck("cGLB")])
        op("dve", lambda e: e.tensor_scalar(out=C["cGC"], in0=C["cGAM"], scalar1=LN_DS, scalar2=None, op0=ALU.add),
           reads=[ck("cGAM")], writes=[ck("cGC")])
        op("dve", lambda e: e.tensor_tensor(out=C["cBG"], in0=C["cB"], in1=C["cEG"], op=ALU.mult), reads=[ck("cB"), ck("cEG")], writes=[ck("cBG")])
        yield

    def gdn_proj(gi, h, full, cx):
        nk = "nTg%d" % gi
        sid = cx["id"]; lb = cx["lb"]
        sq_ = cx["sq"]; sqk = "sq%s" % sid
        Xs = (1, 2, 0) if full else (1, 2)

        def s1(i, X):
            xb_ = cx["xw"][i % 2]; xk = "xw%s%d" % (sid, i % 2)
            wk = "Wg%d%d" % (h, X); hk_ = "halo%d%d" % (h, X)
            op("pool", lambda e: e.tensor_copy(out=xb_[:, 0:3], in_=halo[h][X][:, 0:3]), reads=[hk_, xk], writes=[xk])
            pf, fk = bankF[lb[i % 3]], "ps%d" % lb[i % 3]
            for k in range(8):
                op("pe", lambda e, k=k: e.matmul(out=pf[:, :], lhsT=Wg[(h, X)][:, k, :], rhs=nTg[gi][:, k, :],
                                                start=(k == 0), stop=(k == 7)), reads=[wk, nk], writes=[fk])
            op("act", lambda e: e.copy(out=xb_[:, 3:515], in_=pf[:, :]), reads=[fk, xk], writes=[xk])

        def s2(i, X):
            xb_ = cx["xw"][i % 2]; xk = "xw%s%d" % (sid, i % 2)
            yb_ = cx["ybuf"][i % 2]; yk = "ybuf%s%d" % (sid, i % 2)
            tk = "XT%s%d" % (sid, X); hk_ = "halo%d%d" % (h, X)
            cw = lambda j: gcw[:, X * 4 + h, j:j + 1]
            op("dve", lambda e: e.tensor_scalar(out=yb_[:], in0=xb_[:, 3:515], scalar1=cw(3), scalar2=None, op0=ALU.mult),
               reads=[xk, "gcw"], writes=[yk])
            for j in (2, 1, 0):
                op("dve", lambda e, j=j: e.scalar_tensor_tensor(out=yb_[:], in0=xb_[:, j:j + 512], scalar=cw(j), in1=yb_[:],
                                                               op0=ALU.mult, op1=ALU.add), reads=[xk, "gcw", yk], writes=[yk])
            op("pool", lambda e: e.tensor_copy(out=halo[h][X][:, 0:3], in_=xb_[:, 512:515]), reads=[xk], writes=[hk_])
            op("act", lambda e: e.activation(out=cx["XT"][X][:], in_=yb_[:], func=AF.Silu), reads=[yk], writes=[tk])
            if X != 2:
                op("act", lambda e: e.activation(out=yb_[:], in_=yb_[:], func=AF.Silu), reads=[yk], writes=[yk])
                op("act", lambda e: e.activation(out=sq_[:], in_=yb_[:], func=AF.Square), reads=[yk], writes=[sqk])

        def s3(i, X):
            if X == 2:
                return
            j = 0 if X == 1 else 1
            pf2, fk2 = bankF[lb[(i + 1) % 3]], "ps%d" % lb[(i + 1) % 3]
            for tt in range(4):
                op("pe", lambda e, tt=tt: e.matmul(out=pf2[:, tt:tt + 1], lhsT=sq_[:, tt * 128:(tt + 1) * 128], rhs=cm[:, 1, 0:1], start=True, stop=True),
                   reads=[sqk, "cm"], writes=[fk2])
            op("dve", lambda e: e.tensor_copy(out=cx["ssc"][:, :, j], in_=pf2[:, 0:4]), reads=[fk2], writes=["ssc%s" % sid])

        n = len(Xs)
        for r in range(n + 3):
            if 0 <= r - 3 < n:
                s3(r - 3, Xs[r - 3])
            if 0 <= r - 1 < n:
                s2(r - 1, Xs[r - 1])
            if r < n:
                s1(r, Xs[r])
            yield

    def gdn_local(g, h, cx, cs):
        gi = g % 2; full = g >= G_FULL
        B = dict(cx["L"]); B.update(cs)
        sid = cx["id"]; lb = cx["lb"]
        key = lambda n: ("%s_%s" % (n, cs["id"])) if n in CHN else ("%s_%s" % (n, sid))
        C = {n: CT[n][:, gi] for n in CN}
        ck = lambda n: "%s_%d" % (n, gi)
        bF = lambda i: (bankF[lb[i]], "ps%d" % lb[i])
        bB = lambda i: (bank[lb[i]], "ps%d" % lb[i])
        kTa, vTa, qTa = cx["XT"][1], cx["XT"][2], cx["XT"][0]
        kTk, vTk, qTk = "XT%s1" % sid, "XT%s2" % sid, "XT%s0" % sid
        lr, rkb, Dab, E, tmpm = B["lr"], B["rkb"], B["Dab"], B["E"], B["tmpm"]
        YP = [B["YP0"], B["YP1"]]; XX = [B["XX0"], B["XX1"]]
        kg, kd, vtok, TTb, TbT, ub, wT = B["kg"], B["kd"], B["vtok"], B["TTb"], B["TbT"], B["ub"], B["wT"]
        Eq, qkT, qdT = B.get("Eq"), B.get("qkT"), B.get("qdT")
        hc = lambda n: C[n][:, :, h]
        bc4 = lambda ap2: ap2.unsqueeze(2).broadcast_to([128, 4, 128])
        rep4 = lambda ap2: ap2.unsqueeze(1).broadcast_to([128, 4, 128])
        v4 = lambda ap, w=128: ap.rearrange("p (a b) -> p a b", a=4)
        nc_ = 2 if full else 1
        sk = "ssc%s" % sid
        op("act", lambda e: e.activation(out=lr[:, :, 0:nc_], in_=cx["ssc"][:, :, 0:nc_], func=AF.Ln, bias=EPS), reads=[sk], writes=[key("lr")])
        op("act", lambda e: e.activation(out=rkb[:, 0, :], in_=lr[:, :, 0], func=AF.Exp, scale=-0.5), reads=[key("lr")], writes=[key("rk0")])
        op("dve", lambda e: e.scalar_tensor_tensor(out=rkb[:, 1, :], in0=lr[:, :, 0], scalar=-0.5, in1=hc("cGLB"), op0=ALU.mult, op1=ALU.add),
           reads=[key("lr"), ck("cGLB")], writes=[key("rk1")])
        op("dve", lambda e: e.tensor_tensor(out=rkb[:, 3, :], in0=rkb[:, 0, :], in1=hc("cBG"), op=ALU.mult), reads=[key("rk0"), ck("cBG")], writes=[key("rk3")])
        op("dve", lambda e: e.tensor_tensor(out=rkb[:, 4, :], in0=rkb[:, 0, :], in1=hc("cEKD"), op=ALU.mult), reads=[key("rk0"), ck("cEKD")], writes=[key("rk4")])
        op("dve", lambda e: e.scalar_tensor_tensor(out=rkb[:, 6, :], in0=lr[:, :, 0], scalar=-0.5, in1=hc("cGAM"), op0=ALU.mult, op1=ALU.subtract),
           reads=[key("lr"), ck("cGAM")], writes=[key("rk6")])
        if full:
            op("dve", lambda e: e.scalar_tensor_tensor(out=rkb[:, 2, :], in0=lr[:, :, 1], scalar=-0.5, in1=hc("cGC"), op0=ALU.mult, op1=ALU.add),
               reads=[key("lr"), ck("cGC")], writes=[key("rk2")])
        yield
        pb, pk = bB(0)
        for tt in range(4):
            op("pe", lambda e, tt=tt: e.transpose(out=pb[:, tt * 128:(tt + 1) * 128], in_=kTa[:, tt * 128:(tt + 1) * 128], identity=identb[:]),
               reads=[kTk, "identb"], writes=[pk])
        for tt in range(4):
            op("pe", lambda e, tt=tt: e.transpose(out=pb[:, 512 + tt * 128:512 + (tt + 1) * 128], in_=vTa[:, tt * 128:(tt + 1) * 128], identity=identb[:]),
               reads=[vTk, "identb"], writes=[pk])
        op("dve", lambda e: e.tensor_tensor(out=kg[:], in0=v4(pb[:, 0:512]), in1=bc4(rkb[:, 3, :]), op=ALU.mult), reads=[pk, key("rk3")], writes=[key("kg")])
        op("dve", lambda e: e.tensor_tensor(out=kd[:], in0=v4(pb[:, 0:512]), in1=bc4(rkb[:, 4, :]), op=ALU.mult), reads=[pk, key("rk4")], writes=[key("kd")])
        op("act", lambda e: e.copy(out=vtok[:], in_=v4(pb[:, 512:1024])), reads=[pk], writes=[key("vtok")])
        yield
        op("pool", lambda e: e.tensor_tensor(out=Dab[:, 0], in0=rep4(IDENT), in1=bc4(rkb[:, 1, :]), op=ALU.mult), reads=["cm", key("rk1")], writes=[key("Dab0")])
        pbc, bck = bF(1)
        op("pe", lambda e: e.matmul(out=pbc[:, :], lhsT=ONES, rhs=Dab[:, 0].rearrange("p a b -> p (a b)"), start=True, stop=True),
           reads=["cm", key("Dab0")], writes=[bck])
        if full:
            op("pool", lambda e: e.tensor_tensor(out=Dab[:, 1], in0=rep4(IDENT), in1=bc4(rkb[:, 2, :]), op=ALU.mult), reads=["cm", key("rk2")], writes=[key("Dab1")])
            pbc2, bck2 = bF(2)
            op("pe", lambda e: e.matmul(out=pbc2[:, :], lhsT=ONES, rhs=Dab[:, 1].rearrange("p a b -> p (a b)"), start=True, stop=True),
               reads=["cm", key("Dab1")], writes=[bck2])
        op("dve", lambda e: e.tensor_tensor(out=tmpm[:], in0=v4(pbc[:, :]), in1=rep4(cm[:, 7, :]), op=ALU.add), reads=[bck, "cm"],
           writes=[key("tmpm"), key("tmpm") + "h0", key("tmpm") + "h1"])
        for tt in range(4):
            op("act", lambda e, tt=tt: e.activation(out=E[:, 0, tt, :], in_=tmpm[:, tt, :], func=AF.Exp, bias=rkb[:, 6, tt:tt + 1]),
               reads=[key("tmpm"), key("rk6")], writes=[key("E0")])
        yield
        pkk, kkk = bF(0)
        for tt in range(4):
            sl = slice(tt * 128, (tt + 1) * 128)
            op("pe", lambda e, sl=sl: e.matmul(out=pkk[:, sl], lhsT=kTa[:, sl], rhs=kTa[:, sl], start=True, stop=True), reads=[kTk], writes=[kkk])
        Y0 = YP[0][:, :, 0:128]
        op("dve", lambda e: e.scalar_tensor_tensor(out=Y0, in0=v4(pkk[:, :]), scalar=-1.0, in1=E[:, 0], op0=ALU.mult, op1=ALU.mult),
           reads=[kkk, key("E0")], writes=[key("YP0a"), key("YP0a") + "h0", key("YP0a") + "h1"])
        if full:
            op("dve", lambda e: e.tensor_tensor(out=tmpm[:], in0=v4(pbc2[:, :]), in1=rep4(cm[:, 8, :]), op=ALU.add), reads=[bck2, "cm", key("tmpm")], writes=[key("tmpm")])
            op("act", lambda e: e.activation(out=Eq[:], in_=v4(pbc2[:, :]), func=AF.Exp), reads=[bck2], writes=[key("Eq")])
            for tt in range(4):
                op("act", lambda e, tt=tt: e.activation(out=E[:, 1, tt, :], in_=tmpm[:, tt, :], func=AF.Exp, bias=rkb[:, 6, tt:tt + 1]),
                   reads=[key("tmpm"), key("rk6")], writes=[key("E1")])
            pqk, qkk = bF(1)
            for tt in range(4):
                sl = slice(tt * 128, (tt + 1) * 128)
                op("pe", lambda e, sl=sl: e.matmul(out=pqk[:, sl], lhsT=kTa[:, sl], rhs=qTa[:, sl], start=True, stop=True), reads=[kTk, qTk], writes=[qkk])
            op("dve", lambda e: e.tensor_tensor(out=qkT[:], in0=v4(pqk[:, :]), in1=E[:, 1], op=ALU.mult), reads=[qkk, key("E1")], writes=[key("qkT")])
            op("pool", lambda e: e.tensor_tensor(out=qdT[:], in0=v4(qTa[:, :]), in1=Eq[:], op=ALU.mult), reads=[qTk, key("Eq")], writes=[key("qdT")])
        yield "FRONT_DONE"
        HB = cx["HB"]
        bR = lambda i: (bankF[i], "ps%d" % i)
        hk = lambda n, hh: key(n) + "h%d" % hh
        px, pxk = bB(2)
        for tt in range(4):
            op("pe", lambda e, tt=tt: e.transpose(out=px[:, tt * 128:(tt + 1) * 128], in_=YP[0][:, tt, 0:128], identity=identb[:]),
               reads=[key("YP0a"), "identb"], writes=[pxk])
        for hh in range(2):
            tsl = slice(2 * hh, 2 * hh + 2)
            op("act", lambda e, tsl=tsl, hh=hh: e.copy(out=XX[0][:, tsl, :], in_=v4(px[:, 0:512])[:, tsl, :]), reads=[pxk], writes=[hk("XX0", hh)])
            op("pool", lambda e, tsl=tsl: e.tensor_tensor(out=YP[1][:, tsl, 128:256], in0=YP[0][:, tsl, 0:128], in1=identb[:].unsqueeze(1).broadcast_to([128, 2, 128]), op=ALU.add),
               reads=[key("YP0a"), "identb"], writes=[hk("YP1b", hh)])
        yield
        for hh in range(2):
            yb_, xb2, xo = HB[hh]
            py, pyk = bR(yb_); pz, pzk = bR(xb2)
            for tt in (2 * hh, 2 * hh + 1):
                o0 = (tt % 2) * 128
                op("pe", lambda e, tt=tt, o0=o0, py=py: e.matmul(out=py[:, o0:o0 + 128], lhsT=XX[0][:, tt, :], rhs=YP[0][:, tt, 0:128], start=True, stop=True),
                   reads=[hk("XX0", hh), key("YP0a")], writes=[pyk])
                op("pe", lambda e, tt=tt, o0=o0, pz=pz, xo=xo: e.matmul(out=pz[:, xo + o0:xo + o0 + 128], lhsT=YP[0][:, tt, 0:128], rhs=XX[0][:, tt, :], start=True, stop=True),
                   reads=[hk("XX0", hh), key("YP0a")], writes=[pzk])
        for hh in range(2):
            yb_, xb2, xo = HB[hh]
            py, pyk = bR(yb_); pz, pzk = bR(xb2)
            tsl = slice(2 * hh, 2 * hh + 2)
            op("dve", lambda e, tsl=tsl, py=py: e.tensor_copy(out=YP[1][:, tsl, 0:128], in_=py[:, 0:256].rearrange("p (a b) -> p a b", a=2)),
               reads=[pyk], writes=[hk("YP1a", hh)])
            op("act", lambda e, tsl=tsl, pz=pz, xo=xo: e.copy(out=XX[1][:, tsl, :], in_=pz[:, xo:xo + 256].rearrange("p (a b) -> p a b", a=2)),
               reads=[pzk], writes=[hk("XX1", hh)])
        yield
        cur = 1
        for lvl in range(2, 6):
            c_, n_ = cur, 1 - cur
            last = (lvl == 5)
            for hh in range(2):
                yb_, xb2, xo = HB[hh]
                pv_, pvk = bR(yb_); pxx, pxxk = bR(xb2)
                for tt in (2 * hh, 2 * hh + 1):
                    o0 = (tt % 2) * 256
                    if not last:
                        op("pe", lambda e, tt=tt, c_=c_, pv_=pv_, o0=o0: e.matmul(out=pv_[:, o0:o0 + 256], lhsT=XX[c_][:, tt, :], rhs=YP[c_][:, tt, :], start=True, stop=True),
                           reads=[hk("XX%d" % c_, hh), hk("YP%da" % c_, hh), hk("YP%db" % c_, hh)], writes=[pvk])
                    else:
                        op("pe", lambda e, tt=tt, c_=c_, pv_=pv_, o0=o0: e.matmul(out=pv_[:, o0 + 128:o0 + 256], lhsT=XX[c_][:, tt, :], rhs=YP[c_][:, tt, 128:256], start=True, stop=True),
                           reads=[hk("XX%d" % c_, hh), hk("YP%db" % c_, hh)], writes=[pvk])
                for tt in (2 * hh, 2 * hh + 1):
                    o1 = (tt % 2) * 128
                    op("pe", lambda e, tt=tt, c_=c_, pxx=pxx, o1=o1, xo=xo: e.matmul(out=pxx[:, xo + o1:xo + o1 + 128], lhsT=YP[c_][:, tt, 0:128], rhs=XX[c_][:, tt, :], start=True, stop=True),
                       reads=[hk("XX%d" % c_, hh), hk("YP%da" % c_, hh)], writes=[pxxk])
            for hh in range(2):
                yb_, xb2, xo = HB[hh]
                pv_, pvk = bR(yb_); pxx, pxxk = bR(xb2)
                pv3 = pv_[:, :].rearrange("p (a b) -> p a b", a=2)
                tsl = slice(2 * hh, 2 * hh + 2)
                if not last:
                    op("act", lambda e, n_=n_, pv3=pv3, tsl=tsl: e.copy(out=YP[n_][:, tsl, 0:128], in_=pv3[:, :, 0:128]), reads=[pvk], writes=[hk("YP%da" % n_, hh)] + ([key("YP0a")] if n_ == 0 else []))
                op("dve", lambda e, c_=c_, n_=n_, pv3=pv3, tsl=tsl: e.tensor_tensor(out=YP[n_][:, tsl, 128:256], in0=YP[c_][:, tsl, 128:256], in1=pv3[:, :, 128:256], op=ALU.add),
                   reads=[pvk, hk("YP%db" % c_, hh)], writes=[hk("YP%db" % n_, hh)])
                op("act", lambda e, n_=n_, pxx=pxx, tsl=tsl, xo=xo: e.copy(out=XX[n_][:, tsl, :], in_=pxx[:, xo:xo + 256].rearrange("p (a b) -> p a b", a=2)),
                   reads=[pxxk], writes=[hk("XX%d" % n_, hh)])
            cur = n_
            yield
        c_ = cur
        for hh in range(2):
            yb_, xb2, xo = HB[hh]
            py, pyk = bR(yb_)
            tsl = slice(2 * hh, 2 * hh + 2)
            for tt in (2 * hh, 2 * hh + 1):
                o1 = (tt % 2) * 128
                op("pe", lambda e, tt=tt, py=py, o1=o1: e.matmul(out=py[:, o1:o1 + 128], lhsT=XX[c_][:, tt, :], rhs=YP[c_][:, tt, 128:256], start=True, stop=True),
                   reads=[hk("XX%d" % c_, hh), hk("YP%db" % c_, hh)], writes=[pyk])
            op("dve", lambda e, py=py, tsl=tsl: e.tensor_tensor(out=tmpm[:, tsl, :], in0=YP[c_][:, tsl, 128:256], in1=py[:, 0:256].rearrange("p (a b) -> p a b", a=2), op=ALU.add),
               reads=[pyk, hk("YP%db" % c_, hh), key("tmpm")], writes=[key("tmpm") + "h%d" % hh])
        tmk = [key("tmpm") + "h0", key("tmpm") + "h1"]
        op("act", lambda e: e.copy(out=TTb[:], in_=tmpm[:]), reads=tmk, writes=[key("TTb")])
        op("pool", lambda e: e.tensor_tensor(out=TbT[:], in0=tmpm[:], in1=bc4(hc("cB")), op=ALU.mult), reads=tmk + [ck("cB")], writes=[key("TbT")])
        yield
        pu, puk = bF(1)
        pw_, pwk_ = bF(2)
        for tt in range(4):
            sl = slice(tt * 128, (tt + 1) * 128)
            op("pe", lambda e, tt=tt, sl=sl: e.matmul(out=pu[:, sl], lhsT=TbT[:, tt, :], rhs=vtok[:, tt, :], start=True, stop=True),
               reads=[key("TbT"), key("vtok")], writes=[puk])
        for tt in range(4):
            sl = slice(tt * 128, (tt + 1) * 128)
            op("pe", lambda e, tt=tt, sl=sl: e.matmul(out=pw_[:, sl], lhsT=kg[:, tt, :], rhs=TTb[:, tt, :], start=True, stop=True),
               reads=[key("kg"), key("TTb")], writes=[pwk_])
        op("act", lambda e: e.copy(out=ub[:], in_=v4(pu[:, :])), reads=[puk], writes=[key("ub")])
        op("dve", lambda e: e.tensor_copy(out=wT[:], in_=v4(pw_[:, :])), reads=[pwk_], writes=[key("wT")])
        yield

    def gdn_chain(g, h, cx, cs):
        gi = g % 2; full = g >= G_FULL
        B = cs
        key = lambda n: "%s_%s" % (n, cs["id"])
        cbk = cx["cbanks"]
        C = {n: CT[n][:, gi] for n in CN}
        ck = lambda n: "%s_%d" % (n, gi)
        bF = lambda i: (bankF[i], "ps%d" % i)
        kd, ub, wT, ubf = B["kd"], B["ub"], B["wT"], B["ubf"]
        qkT, qdT, osb, og, sg = B.get("qkT"), B.get("qdT"), B.get("osb"), B.get("og"), B.get("sg")
        Sk, Sbk = "S%d" % h, "Sb%d" % h
        flip = 0
        for tt in range(4):
            t = 4 * g + tt
            store = full and t >= T0
            for hf in range(2):
                rows = slice(hf * 64, hf * 64 + 64)
                cdn = "cCD0" if hf == 0 else "cCD1"
                cd = C[cdn][:, tt, h:h + 1]
                pw, pwk = bF(cbk[flip % len(cbk)]); ps_, psk = bF(cbk[(flip + 1) % len(cbk)]); flip = 1 - flip
                op("pe", lambda e: e.matmul(out=pw[:, 0:128], lhsT=wT[:, tt, :], rhs=Sb[h][:], start=True, stop=True), reads=[key("wT"), Sbk], writes=[pwk])
                op("dve", lambda e: e.tensor_tensor(out=ubf[rows, :], in0=ub[rows, tt, :], in1=pw[rows, 0:128], op=ALU.subtract),
                   reads=[key("ub"), pwk], writes=[key("ubf")])
                yield
                if store:
                    op("pe", lambda e: e.matmul(out=ps_[:, 0:128], lhsT=qdT[:, tt, :], rhs=Sb[h][:], start=True, stop=False), reads=[key("qdT"), Sbk, psk], writes=[psk])
                    op("pe", lambda e: e.matmul(out=ps_[:, 0:128], lhsT=qkT[:, tt, :], rhs=ubf[:], start=False, stop=True),
                       reads=[key("qkT"), key("ubf"), psk], writes=[psk])
                op("pe", lambda e: e.matmul(out=ps_[:, 384:512], lhsT=kd[rows, tt, :], rhs=ubf[rows, :], start=True, stop=True), reads=[key("kd"), key("ubf")], writes=[psk])
                op("dve", lambda e: e.scalar_tensor_tensor(out=Sb[h][:], in0=Sst[h][:], scalar=cd, in1=ps_[:, 384:512], op0=ALU.mult, op1=ALU.add),
                   reads=[Sk, ck(cdn), psk], writes=[Sbk])
                op("dve", lambda e: e.scalar_tensor_tensor(out=Sst[h][:], in0=Sst[h][:], scalar=cd, in1=ps_[:, 384:512], op0=ALU.mult, op1=ALU.add),
                   reads=[Sk, ck(cdn), psk], writes=[Sk])
                if store:
                    op("act", lambda e: e.copy(out=osb[rows, :], in_=ps_[rows, 0:128]), reads=[psk], writes=[key("osb")])
                yield
            if store:
                s_ = t - T0
                pg, pgk = bF(cbk[flip % len(cbk)])
                for k in range(8):
                    op("pe", lambda e, k=k: e.matmul(out=pg[:, 256:384], lhsT=nTg[gi][:, k, tt * 128:(tt + 1) * 128],
                                                    rhs=PH["Wgg"][:, k, h * 128:(h + 1) * 128], start=(k == 0), stop=(k == 7)),
                       reads=["nTg%d" % gi, "Wgg"], writes=[pgk])
                op("act", lambda e: e.activation(out=sg[:], in_=pg[:, 256:384], func=AF.Silu), reads=[pgk], writes=[key("sg")])
                sc = st[:, 4 + h:5 + h]; sck = "st%d" % (4 + h)
                rms_rstd(osb[:], [key("osb")], HD, sc, sck)
                op("dve", lambda e: e.scalar_tensor_tensor(out=og[:], in0=osb[:], scalar=sc, in1=GNW, op0=ALU.mult, op1=ALU.mult),
                   reads=[key("osb"), sck, "gsmbc"], writes=[key("og")])
                op("pool", lambda e: e.tensor_tensor(out=mix[:, s_, h * 128:(h + 1) * 128], in0=og[:], in1=sg[:], op=ALU.mult),
                   reads=[key("og"), key("sg")], writes=["mix%d" % h])
                yield

    def gdn_task(g, heads, cx, prefetch):
        gi = g % 2; full = g >= G_FULL
        prev = None
        if cx.get("prefetched") != g:
            for _ in gdn_proj(gi, heads[0], full, cx):
                yield
        for i, h in enumerate(heads):
            cs = cx["cs"][i % len(cx["cs"])]
            subs = [gdn_local(g, h, cx, cs)] + ([prev] if prev is not None else [])
            while subs:
                for sg_ in list(subs):
                    try:
                        r = next(sg_)
                        if r == "FRONT_DONE" and i + 1 < len(heads):
                            subs.append(gdn_proj(gi, heads[i + 1], full, cx))
                    except StopIteration:
                        subs.remove(sg_)
                yield
            prev = gdn_chain(g, h, cx, cs)
        subs = [prev]
        if prefetch:
            subs.append(gdn_proj((g + 1) % 2, heads[0], full, cx))
            cx["prefetched"] = g + 1
        while subs:
            for sg_ in list(subs):
                try:
                    next(sg_)
                except StopIteration:
                    subs.remove(sg_)
            yield

    def ret_tile(t, tt, gi, h, full, ci):
        nk = "nTg%d" % gi
        Rr = PH["RS"]; rbk = PH["rbank"]; rbk2 = PH["rbank2"]
        cs_t, sn_t = PH["cs_t"], PH["sn_t"]
        rk_ = lambda n: "%s_r" % n
        rsb, rt1, rt2, krot, kz, rvt = Rr["rsb"], Rr["rt1"], Rr["rt2"], Rr["krot"], Rr["kz"], Rr["rvt"]
        qrot, rkT, rqT, rqx, rPT, sgr, osb, og = [Rr.get(n) for n in ("qrot", "rkT", "rqT", "rqx", "rPT", "sgr", "osb", "og")]
        csk, snk = "cs_t%d" % ci, "sn_t%d" % ci
        store = full and t >= T0
        pf, fk = bankF[rbk], "ps%d" % rbk
        for k in range(8):
            op("pe", lambda e, k=k: e.matmul(out=pf[:, 0:256], lhsT=nTg[gi][:, k, tt * 128:(tt + 1) * 128], rhs=Wrkv[h][:, k, :],
                                            start=(k == 0), stop=(k == 7)), reads=[nk, "Wrkv%d" % h], writes=[fk])
        if store:
            for k in range(8):
                op("pe", lambda e, k=k: e.matmul(out=pf[:, 256:512], lhsT=nTg[gi][:, k, tt * 128:(tt + 1) * 128], rhs=PH["Wrqg"][h][:, k, :],
                                                start=(k == 0), stop=(k == 7)), reads=[nk, "Wrqg%d" % h, fk], writes=[fk])
        op("act", lambda e: e.copy(out=rsb[:, 0:128], in_=pf[:, 0:128]), reads=[fk], writes=[rk_("rsb")])
        op("act", lambda e: e.copy(out=rvt[:], in_=pf[:, 128:256]), reads=[fk], writes=[rk_("rvt")])
        if store:
            op("act", lambda e: e.copy(out=rsb[:, 128:256], in_=pf[:, 256:384]), reads=[fk, rk_("rsb")], writes=[rk_("rsb")])
        if store:
            op("act", lambda e: e.activation(out=sgr[:], in_=pf[:, 384:512], func=AF.Silu), reads=[fk], writes=[rk_("sgr")])
        yield

        def rotary(src, dst, dstk):
            op("dve", lambda e: e.tensor_tensor(out=rt1[:], in0=src, in1=cs_t[ci][:], op=ALU.mult), reads=[rk_("rsb"), csk], writes=[rk_("rt1")])
            op("dve", lambda e: e.tensor_tensor(out=rt2[:, 0:128:2], in0=src[:, 1:128:2], in1=sn_t[ci][:, 0:128:2], op=ALU.mult),
               reads=[rk_("rsb"), snk], writes=[rk_("rt2e")])
            op("dve", lambda e: e.tensor_tensor(out=rt2[:, 1:128:2], in0=src[:, 0:128:2], in1=sn_t[ci][:, 1:128:2], op=ALU.mult),
               reads=[rk_("rsb"), snk], writes=[rk_("rt2o")])
            op("pool", lambda e: e.tensor_tensor(out=dst[:], in0=rt1[:], in1=rt2[:], op=ALU.add),
               reads=[rk_("rt1"), rk_("rt2e"), rk_("rt2o")], writes=[dstk])
        rotary(rsb[:, 0:128], krot, rk_("krot"))
        op("act", lambda e: e.activation(out=kz[:], in_=krot[:], func=AF.Copy, scale=rv[:, h:h + 1]), reads=[rk_("krot"), "rv"], writes=[rk_("kz")])
        Rk, Rbk = "R%d" % h, "Rb%d" % h
        yield
        if store:
            rotary(rsb[:, 128:256], qrot, rk_("qrot"))
            yield
            pb, pk = bank[rbk2], "ps%d" % rbk2
            op("pe", lambda e: e.transpose(out=pb[:, 0:128], in_=krot[:], identity=identb[:]), reads=[rk_("krot"), "identb"], writes=[pk])
            op("pe", lambda e: e.transpose(out=pb[:, 128:256], in_=qrot[:], identity=identb[:]), reads=[rk_("qrot"), "identb"], writes=[pk])
            op("act", lambda e: e.copy(out=rkT[:], in_=pb[:, 0:128]), reads=[pk], writes=[rk_("rkT")])
            op("act", lambda e: e.copy(out=rqT[:], in_=pb[:, 128:256]), reads=[pk], writes=[rk_("rqT")])
            op("pool", lambda e: e.tensor_tensor(out=rqx[:], in0=rqT[:], in1=xibc[:, h, :], op=ALU.mult), reads=[rk_("rqT"), "xibc"], writes=[rk_("rqx")])
            yield
            psc, sck = bankF[rbk2], "ps%d" % rbk2
            op("pe", lambda e: e.matmul(out=psc[:, 0:128], lhsT=rkT[:], rhs=rqT[:], start=True, stop=True), reads=[rk_("rkT"), rk_("rqT")], writes=[sck])
            op("dve", lambda e: e.tensor_tensor(out=rPT[:], in0=psc[:, 0:128], in1=rm[:, h, :], op=ALU.mult), reads=[sck, "rm"], writes=[rk_("rPT")])
            yield
            po, pok = bankF[rbk], "ps%d" % rbk
            op("pe", lambda e: e.matmul(out=po[:, 0:128], lhsT=rPT[:], rhs=rvt[:], start=True, stop=False), reads=[rk_("rPT"), rk_("rvt")], writes=[pok])
            op("pe", lambda e: e.matmul(out=po[:, 0:128], lhsT=rqx[:], rhs=Rb[h][:], start=False, stop=True), reads=[rk_("rqx"), Rbk, pok], writes=[pok])
            s_ = t - T0
            op("act", lambda e: e.copy(out=osb[:], in_=po[:, 0:128]), reads=[pok], writes=[rk_("osb")])
            yield
            sc = st[:, 8 + h:9 + h]; sck2 = "st%d" % (8 + h)
            rms_rstd(osb[:], [rk_("osb")], HD, sc, sck2)
            op("dve", lambda e: e.tensor_scalar(out=og[:], in0=osb[:], scalar1=sc, scalar2=None, op0=ALU.mult), reads=[rk_("osb"), sck2], writes=[rk_("og")])
            op("pool", lambda e: e.tensor_tensor(out=mix[:, s_, 512 + h * 128:512 + (h + 1) * 128], in0=og[:], in1=sgr[:], op=ALU.mult),
               reads=[rk_("og"), rk_("sgr")], writes=["mix%d" % (4 + h)])
        if not store:
            yield
        pr, prk = bankF[rbk], "ps%d" % rbk
        op("pe", lambda e: e.matmul(out=pr[:, 0:128], lhsT=kz[:], rhs=rvt[:], start=True, stop=True), reads=[rk_("kz"), rk_("rvt")], writes=[prk])
        g128 = float(np.float64(1.0 - 2.0 ** (-5 - h)) ** 128)
        op("dve", lambda e: e.scalar_tensor_tensor(out=Rst[h][:], in0=Rst[h][:], scalar=g128, in1=pr[:, 0:128], op0=ALU.mult, op1=ALU.add),
           reads=[Rk, prk], writes=[Rk])
        op("act", lambda e: e.copy(out=Rb[h][:], in_=Rst[h][:]), reads=[Rk], writes=[Rbk])
        yield

    def prep_task(g):
        gi = g % 2
        for r in range(6):
            if 0 <= r - 2 < 4:
                norm_b(4 * g + r - 2, gi, r - 2)
            if r < 4:
                norm_a(4 * g + r)
            yield
        yield
        if "g" in parts:
            yield from gdn_common(gi, g >= G_FULL)

    def ret_task(g):
        gi = g % 2; full = g >= G_FULL
        for tt in range(4):
            t = 4 * g + tt
            ci = t % 2
            dma("sp", PH["cs_t"][ci][:], cosF[t * 128:(t + 1) * 128, :], writes=["cs_t%d" % ci])
            dma("sp", PH["sn_t"][ci][:], sinS[t * 128:(t + 1) * 128, :], writes=["sn_t%d" % ci])
            for h in range(4):
                yield from ret_tile(t, tt, gi, h, full, ci)


    def run_tasks(tasks, reps=None):
        tasks = list(tasks)
        reps = dict(reps or {})
        while tasks:
            for tk_ in list(tasks):
                for _ in range(reps.get(id(tk_), 1)):
                    try:
                        next(tk_)
                    except StopIteration:
                        tasks.remove(tk_)
                        break

    g0 = NG - ng_run
    g1 = NG
    if dev_g0 is not None:
        g0 = dev_g0; g1 = g0 + ng_run
    run_tasks([prep_task(g0)])
    cur_phase = [None]
    for g in range(g0, g1):
        full = g >= G_FULL
        if cur_phase[0] != full:
            if cur_phase[0] is not None:
                p.barrier()
                es_ph[0].close()
                es_ph[0] = ExitStack()
            alloc_phase(full)
            cur_phase[0] = full
        tasks = []
        if "g" in parts:
            pf_ok = (g + 1 < g1) and ((g + 1 >= G_FULL) == full)
            if full:
                tasks.append(gdn_task(g, [0, 1, 2, 3], PH["streams"][0], pf_ok))
            else:
                tasks.append(gdn_task(g, [0, 2], PH["streams"][0], pf_ok))
                tasks.append(gdn_task(g, [1, 3], PH["streams"][1], pf_ok))
        reps = {}
        if "r" in parts:
            rt_ = ret_task(g)
            tasks.append(rt_)
            reps[id(rt_)] = 1 if full else 2
        if g + 1 < g1:
            tasks.append(prep_task(g + 1))
        run_tasks(tasks, reps)

    def early(src):
        p.barrier()
        dma("sp", out[0:128, :], src, writes=["out0"])
        p.finish()
        return nc, p
    if phase == "pass":
        return early(xt[0][:])
    p.barrier()
    es_ph[0].close()
    es_keep.close()
    PHTAG[0] = "_X"
    hres = R("hres", [128, NSLOT, D], F32)
    n2T = R("n2T", [128, 8, 2 + OWN], BF16)
    es_b = ExitStack()
    Wout = L("Wout", [128, 8, D], BF16, es_b)
    mixT = [L("mixT%d" % i, [128, 8, 128], BF16, es_b) for i in range(2)]
    xt2 = [L("xt2%d" % i, [128, D], F32, es_b) for i in range(2)]
    junk = L("junk2", [128, D], F32, es_b)
    nb2 = [L("nb2%d" % i, [128, D], BF16, es_b) for i in range(2)]
    st = L("st2", [128, 4], F32, es_b)
    dma("pool", Wout[:], w_out.rearrange("(k p) c -> p k c", p=128), writes=["Wout"])
    nw2 = L("nw2", [128, D], F32, es_b)
    dma("sp", nw2[:], vecs[1:2, :].partition_broadcast(128), writes=["nw2"])

    def b3_a(s_):
        t = T0 + s_
        dma("sp", xt2[s_ % 2][:], xpad[t * 128:(t + 1) * 128, :], writes=["xt2%d" % (s_ % 2)])
        pb, pk = nB()
        for k in range(8):
            op("pe", lambda e, k=k: e.transpose(out=pb[:, k * 128:(k + 1) * 128], in_=mix[:, s_, k * 128:(k + 1) * 128], identity=identb[:]),
               reads=["mix%d" % k, "identb"], writes=[pk])
        op("act", lambda e: e.copy(out=mixT[s_ % 2][:], in_=pb[:, :].rearrange("p (k n) -> p k n", k=8)), reads=[pk], writes=["mixT%d" % (s_ % 2)])

    def b3_b(s_):
        xb_ = xt2[s_ % 2]; xk = "xt2%d" % (s_ % 2); mT = mixT[s_ % 2]; mk = "mixT%d" % (s_ % 2)
        for half in range(2):
            pf, fk = nF()
            for k in range(8):
                op("pe", lambda e, k=k: e.matmul(out=pf[:, :], lhsT=mT[:, k, :], rhs=Wout[:, k, half * 512:(half + 1) * 512],
                                                start=(k == 0), stop=(k == 7)), reads=[mk, "Wout"], writes=[fk])
            op("dve", lambda e: e.tensor_tensor(out=hres[:, s_, half * 512:(half + 1) * 512], in0=xb_[:, half * 512:(half + 1) * 512],
                                                in1=pf[:, :], op=ALU.add), reads=[xk, fk], writes=["h%d_%d" % (s_, half)])

    def b3_c(s_):
        hks = ["h%d_0" % s_, "h%d_1" % s_]
        sc = st[:, (s_ % 2):(s_ % 2) + 1]; sck = "stb%d" % (s_ % 2)
        rms_rstd(hres[:, s_, :], hks, D, sc, sck)
        op("dve", lambda e: e.scalar_tensor_tensor(out=nb2[s_ % 2][:], in0=hres[:, s_, :], scalar=sc, in1=nw2[:],
                                                   op0=ALU.mult, op1=ALU.mult), reads=hks + [sck, "nw2"], writes=["nb2%d" % (s_ % 2)])

    def b3_d(s_):
        nb_ = nb2[s_ % 2]; nbk = "nb2%d" % (s_ % 2)
        pb, pk = nB()
        for k in range(8):
            op("pe", lambda e, k=k: e.transpose(out=pb[:, k * 128:(k + 1) * 128], in_=nb_[:, k * 128:(k + 1) * 128], identity=identb[:]),
               reads=[nbk, "identb"], writes=[pk])
        pv = pb[:, :].rearrange("p (k n) -> p k n", k=8)
        if s_ == 0:
            op("act", lambda e: e.copy(out=n2T[:, :, 0:2], in_=pv[:, :, 126:128]), reads=[pk], writes=["n2T_h"])
        else:
            op("act", lambda e: e.copy(out=n2T[:, :, 2 + (s_ - 1) * 128:2 + s_ * 128], in_=pv), reads=[pk], writes=["n2T_%d" % ((s_ - 1) // 4)])

    for r in range(NSLOT + 3):
        if 0 <= r - 3 < NSLOT:
            b3_d(r - 3)
        if 0 <= r - 2 < NSLOT:
            b3_c(r - 2)
        if 0 <= r - 1 < NSLOT:
            b3_b(r - 1)
        if r < NSLOT:
            b3_a(r)

    if phase == "b3":
        return early(hres[:, 1, :])
    p.barrier()
    es_b.close()
    es_mix.close()
    es_c = ExitStack()
    NGRP = DFF // 256
    Wgu = [L("Wgu%d" % i, [128, 2, 8, 256], BF16, es_c) for i in range(2)]
    Wd = [L("Wd%d" % i, [128, 2, D], BF16, es_c) for i in range(2)]
    mcw = L("mcw", [128, 44, 3], F32, es_c)
    hb = [L("hb%d" % i, [128, 2 + OWN], F32, es_c) for i in range(2)]
    yb = [L("yb%d" % i, [128, OWN], F32, es_c) for i in range(2)]
    actT = [L("actT%d" % i, [128, 2, OWN], BF16, es_c) for i in range(2)]
    st = L("st3", [128, 4], F32, es_c)
    ob = [L("ob%d" % i, [128, D], F32, es_c) for i in range(2)]
    junk = L("junk3", [128, D], BF16, es_c)
    dma("sp", mcw[:], mcwT.rearrange("(c p) i -> p c i", p=128), writes=["mcw"])
    nw3 = L("nw3", [128, D], F32, es_c)
    dma("sp", nw3[:], vecs[2:3, :].partition_broadcast(128), writes=["nw3"])
    w_up_v = w_up.rearrange("(k p) c -> p k c", p=128)
    w_dn_v = w_down.rearrange("(c p) d -> p c d", p=128)

    def load_up(gi_):
        b = gi_ % 2
        dma("pool", Wgu[b][:, 0, :, :], w_up_v[:, :, gi_ * 256:(gi_ + 1) * 256], writes=["Wgu%d" % b])
        dma("pool", Wgu[b][:, 1, :, :], w_up_v[:, :, DFF + gi_ * 256:DFF + (gi_ + 1) * 256], writes=["Wgu%d" % b])

    def load_dn(gi_):
        b = gi_ % 2
        dma("pool", Wd[b][:], w_dn_v[:, gi_ * 2:gi_ * 2 + 2, :], writes=["Wd%d" % b])

    def ffn_up(gi_):
        b = gi_ % 2
        ak = "actT%d" % b
        for fc in range(2):
            for which in range(2):
                hb_ = hb[which]; hbk = "hb%d" % which
                cidx = which * 22 + gi_ * 2 + fc
                lw = lambda k: Wgu[b][:, which, k, fc * 128:(fc + 1) * 128]
                pf, fk = nF()
                for k in range(8):
                    op("pe", lambda e, k=k: e.matmul(out=pf[:, 0:2], lhsT=lw(k), rhs=n2T[:, k, 0:2], start=(k == 0), stop=(k == 7)),
                       reads=["Wgu%d" % b, "n2T_h"], writes=[fk])
                op("act", lambda e: e.copy(out=hb_[:, 0:2], in_=pf[:, 0:2]), reads=[fk], writes=[hbk + "h"])
                for tb in range(4):
                    pf, fk = nF()
                    for k in range(8):
                        op("pe", lambda e, k=k: e.matmul(out=pf[:, :], lhsT=lw(k), rhs=n2T[:, k, 2 + tb * 512:2 + (tb + 1) * 512],
                                                        start=(k == 0), stop=(k == 7)), reads=["Wgu%d" % b, "n2T_%d" % tb], writes=[fk])
                    op("act", lambda e, tb=tb: e.copy(out=hb_[:, 2 + tb * 512:2 + (tb + 1) * 512], in_=pf[:, :]), reads=[fk], writes=[hbk + "_%d" % tb])
                hbks = [hbk + "h"] + [hbk + "_%d" % i for i in range(4)]
                ybk = "yb%d" % which
                op("act", lambda e: e.activation(out=yb[which][:], in_=hb_[:, 2:2 + OWN], func=AF.Copy, scale=mcw[:, cidx, 2:3]),
                   reads=hbks + ["mcw"], writes=[ybk])
                for i in (1, 0):
                    op("dve", lambda e, i=i: e.scalar_tensor_tensor(out=yb[which][:], in0=hb_[:, i:i + OWN], scalar=mcw[:, cidx, i:i + 1],
                                                                   in1=yb[which][:], op0=ALU.mult, op1=ALU.add), reads=hbks + ["mcw", ybk], writes=[ybk])
            op("act", lambda e: e.activation(out=yb[0][:], in_=yb[0][:], func=AF.Silu), reads=["yb0"], writes=["yb0"])
            op("pool", lambda e: e.tensor_tensor(out=actT[b][:, fc, :], in0=yb[0][:], in1=yb[1][:], op=ALU.mult),
               reads=["yb0", "yb1"], writes=[ak + "_%d" % fc])

    def final_tile(tt):
        hks = ["h%d_0" % (tt + 1), "h%d_1" % (tt + 1)]
        sc = st[:, (tt % 2):(tt % 2) + 1]; sck = "stf%d" % (tt % 2)
        rms_rstd(hres[:, tt + 1, :], hks, D, sc, sck)
        o_ = ob[tt % 2]; ok = "ob%d" % (tt % 2)
        op("dve", lambda e: e.scalar_tensor_tensor(out=o_[:], in0=hres[:, tt + 1, :], scalar=sc, in1=nw3[:],
                                                   op0=ALU.mult, op1=ALU.mult), reads=hks + [sck, "nw3"], writes=[ok])
        dma("sp", out[tt * 128:(tt + 1) * 128, :], o_[:], reads=[ok], writes=["out%d" % tt])

    def ffn_down(gi_):
        b = gi_ % 2
        ak = "actT%d" % b
        for tt in range(16):
            for half in range(2):
                pf, fk = nF()
                for fc in range(2):
                    op("pe", lambda e, fc=fc: e.matmul(out=pf[:, :], lhsT=actT[b][:, fc, tt * 128:(tt + 1) * 128],
                                                      rhs=Wd[b][:, fc, half * 512:(half + 1) * 512], start=(fc == 0), stop=(fc == 1)),
                       reads=[ak + "_0", ak + "_1", "Wd%d" % b], writes=[fk])
                hk = "h%d_%d" % (tt + 1, half)
                op("dve", lambda e: e.tensor_tensor(out=hres[:, tt + 1, half * 512:(half + 1) * 512],
                                                    in0=hres[:, tt + 1, half * 512:(half + 1) * 512], in1=pf[:, :], op=ALU.add),
                   reads=[fk, hk], writes=[hk])
            if gi_ == NGRP - 1 and tt >= 1:
                final_tile(tt - 1)
        if gi_ == NGRP - 1:
            final_tile(15)

    load_up(0); load_dn(0)
    for gi_ in range(NGRP + 1):
        if gi_ + 1 < NGRP:
            load_up(gi_ + 1)
        if gi_ < NGRP:
            ffn_up(gi_)
        if gi_ >= 1:
            ffn_down(gi_ - 1)
        if gi_ + 1 < NGRP:
            load_dn(gi_ + 1)

    p.finish()
    es_c.close()
    es_r.close()
    return nc, p


def _consts():
    idx = np.arange(128)
    same = (idx[:, None] // 64) == (idx[None, :] // 64)
    cm = np.zeros((10, 128, 128), np.float32)
    cm[0] = np.eye(128)
    cm[1] = 1.0
    cm[2] = (same & (idx[:, None] < idx[None, :]))
    cm[3] = (same & (idx[:, None] <= idx[None, :]))
    cm[4] = cm[3]
    cm[5] = (idx[:, None] < 64) * np.ones((1, 128))
    cm[6] = (idx[:, None] >= 64) * np.ones((1, 128))
    cm[7] = (1.0 - cm[2]) * -30000.0
    cm[8] = (1.0 - cm[3]) * -30000.0
    hh = np.arange(4, dtype=np.float64)
    gam = 1.0 - 2.0 ** (-5.0 - hh)
    lg = np.log(gam)
    rel = (idx[None, :] - idx[:, None]).astype(np.float64)
    rmat = np.where(rel[None] >= 0, np.exp(rel[None] * lg[:, None, None]), 0.0) * HD ** -0.5
    rvec = np.zeros((128, 8), np.float64)
    rvec[:, 0:4] = np.exp((127.0 - idx[:, None]) * lg[None, :]) * HD ** -0.5
    rxi = np.exp((idx[None, :] + 1.0) * lg[:, None])
    return cm, rmat.astype(np.float32), rvec.astype(np.float32), rxi.astype(np.float32)


_CACHE = {}


def kernel(x, attn_norm_w, w_in, gdn_conv_w, gdn_a_log, gdn_dt_bias, gdn_norm_w, w_out, mlp_norm_w,
           w_up, mlp_conv_w, w_down, final_norm_w):
    f = lambda a: np.ascontiguousarray(np.asarray(a, dtype=np.float32))
    x2 = f(x).reshape(S, D)
    if "nc" not in _CACHE:
        _CACHE["nc"] = build_program()[0]
    nc = _CACHE["nc"]
    cm, rmat, rvec, rxi = _consts()
    vecs = np.stack([f(attn_norm_w)[0], f(mlp_norm_w)[0], f(final_norm_w)], 0)
    gsm = np.concatenate([f(gdn_a_log)[0], f(gdn_dt_bias)[0], f(gdn_norm_w)[0]])[None, :]
    angle = (1.0 / (10000.0 ** np.linspace(0.0, 1.0, 64, dtype=np.float32))).astype(np.float32)
    angle = np.repeat(angle, 2)
    sign = np.tile(np.array([-1.0, 1.0], np.float32), 64)
    common = {
        "w_in": f(w_in)[0], "w_out": f(w_out)[0], "w_up": f(w_up)[0], "w_down": f(w_down)[0],
        "gcwT": np.ascontiguousarray(f(gdn_conv_w)[0].T), "mcwT": np.ascontiguousarray(f(mlp_conv_w)[0].T),
        "vecs": np.ascontiguousarray(vecs), "gsm": np.ascontiguousarray(gsm),
        "cmat": cm, "rmat": rmat, "rvec": rvec, "rxi": rxi,
    }
    in_maps = []
    for c in range(NCORES):
        n_real = OWN * (c + 1)
        xp = np.zeros((S, D), np.float32)
        xp[S - n_real:] = x2[:n_real]
        pos = (np.arange(S, dtype=np.int64) - (S - n_real)).astype(np.float32)
        phase = pos[:, None] * angle[None, :]
        m = dict(common)
        m["xpad"] = xp
        m["cosF"] = np.cos(phase).astype(np.float32)
        m["sinS"] = (np.sin(phase) * sign[None, :]).astype(np.float32)
        in_maps.append(m)
    res = run_bass_kernel_spmd(nc, in_maps, core_ids=list(range(NCORES)))
    outs = [np.asarray(res.results[c]["out"], dtype=np.float32) for c in range(NCORES)]
    return np.concatenate(outs, 0).reshape(1, S, D)
```

```python
import math
from contextlib import ExitStack
import numpy as np
import concourse.bass as bass
import concourse.mybir as mybir
from concourse.bass_utils import run_bass_kernel_spmd

F32 = mybir.dt.float32
BF16 = mybir.dt.bfloat16
ALU = mybir.AluOpType
AF = mybir.ActivationFunctionType

NCORES = 8
S = 16384
D = 1024
NT = S // 128
NG = NT // 4
OWN = 2048
T0 = NT - OWN // 128 - 1
NSLOT = NT - T0
G_FULL = T0 // 4
DFF = 2816
EPS = 1e-6
HD = 128
IN_COLS = 4104


class Prog:
    ENG = ("pe", "act", "dve", "pool", "sp")

    def __init__(self, nc, n_dma_sems=8):
        self.nc = nc
        self.eng = {"pe": nc.tensor, "act": nc.scalar, "dve": nc.vector,
                    "pool": nc.gpsimd, "sp": nc.sync}
        self.sem = {e: nc.alloc_semaphore("c_" + e) for e in self.ENG}
        self.cnt = {e: 0 for e in self.ENG}
        self.dsem = {e: [nc.alloc_semaphore("d_%s%d" % (e, i)) for i in range(n_dma_sems)]
                     for e in ("sp", "pool")}
        self.dval = {e: [0] * n_dma_sems for e in ("sp", "pool")}
        self.drr = {e: 0 for e in ("sp", "pool")}
        self.seen = {e: {} for e in self.ENG}
        self.lastw = {}
        self.readers = {}
        self.n_ins = 0
        self.n_wait = 0

    def _semof(self, tok):
        return self.sem[tok] if isinstance(tok, str) else self.dsem[tok[0]][tok[1]]

    def _wait(self, e, tok, val):
        if tok == e and e == "pe":
            return
        if self.seen[e].get(tok, 0) >= val:
            return
        self.eng[e].wait_ge(self._semof(tok), val)
        self.seen[e][tok] = val
        self.n_wait += 1

    def _deps(self, e, reads, writes):
        for k in reads:
            w = self.lastw.get(k)
            if w is not None:
                self._wait(e, *w)
        for k in writes:
            w = self.lastw.get(k)
            if w is not None:
                self._wait(e, *w)
            for r in self.readers.get(k, ()):
                self._wait(e, *r)

    def _commit(self, tokval, reads, writes):
        for k in writes:
            self.lastw[k] = tokval
            self.readers[k] = []
        for k in reads:
            lst = self.readers.setdefault(k, [])
            lst[:] = [r for r in lst if r[0] != tokval[0]]
            lst.append(tokval)

    def op(self, e, fn, reads=(), writes=()):
        ex = [k for k in reads if k.startswith("ps")]
        if ex:
            reads = [k for k in reads if not k.startswith("ps")]
            writes = list(writes) + [k for k in ex if k not in writes]
        self._deps(e, reads, writes)
        ins = fn(self.eng[e])
        self.cnt[e] += 1
        ins.then_inc(self.sem[e], 1)
        self._commit((e, self.cnt[e]), reads, writes)
        self.n_ins += 1
        return ins

    def dma(self, e, out, in_, reads=(), writes=(), **kw):
        i = self.drr[e]
        self.drr[e] = (i + 1) % len(self.dsem[e])
        tok = (e, i)
        if self.dval[e][i] > 0:
            self._wait(e, tok, self.dval[e][i])
        self._deps(e, reads, writes)
        ins = self.eng[e].dma_start(out=out, in_=in_, **kw)
        self.dval[e][i] += 16
        ins.then_inc(self.dsem[e][i], 16)
        self._commit((tok, self.dval[e][i]), reads, writes)
        self.n_ins += 1
        return ins

    def barrier(self):
        for e in self.ENG:
            for e2 in self.ENG:
                if e2 != e and self.cnt[e2] > 0:
                    self._wait(e, e2, self.cnt[e2])
            for q in self.dsem:
                for i, v in enumerate(self.dval[q]):
                    if v > 0:
                        self._wait(e, (q, i), v)
        self.lastw.clear()
        self.readers.clear()

    def finish(self):
        for q in self.dsem:
            for i, v in enumerate(self.dval[q]):
                if v > 0:
                    self._wait("sp", (q, i), v)


def build_program(ng_run=NG, phase="all", parts="gr", dev_g0=None):
    nc = bass.Bass("TRN2", target_bir_lowering=False)
    dt_in = lambda name, shape: nc.dram_tensor(name, list(shape), F32, kind="ExternalInput").ap()
    xpad = dt_in("xpad", [S, D])
    w_in = dt_in("w_in", [D, IN_COLS])
    w_out = dt_in("w_out", [D, D])
    w_up = dt_in("w_up", [D, 2 * DFF])
    w_down = dt_in("w_down", [DFF, D])
    gcwT = dt_in("gcwT", [1536, 4])
    mcwT = dt_in("mcwT", [2 * DFF, 3])
    vecs = dt_in("vecs", [3, D])
    gsm = dt_in("gsm", [1, 8 + 128])
    cmat = dt_in("cmat", [10, 128, 128])
    rmat = dt_in("rmat", [4, 128, 128])
    rvec = dt_in("rvec", [128, 8])
    rxi = dt_in("rxi", [4, 128])
    cosF = dt_in("cosF", [S, 128])
    sinS = dt_in("sinS", [S, 128])
    out = nc.dram_tensor("out", [OWN, D], F32, kind="ExternalOutput").ap()

    p = Prog(nc)
    SKIP = ""
    op = p.op
    def dma(q, o_, i_, reads=(), writes=(), tag="", **kw):
        if tag and tag in SKIP:
            return p.op("pool", lambda e: e.memset(o_, 1.0), reads=reads, writes=writes)
        return p.dma(q, o_, i_, reads=reads, writes=writes, **kw)

    bank = [nc.alloc_psum_tensor("bank%d" % i, [128, 1024], BF16) for i in range(8)]
    bankF = [bk_[:, :].bitcast(F32) for bk_ in bank]
    rr = {"B": 0, "F": 0}
    POOLS = {"B": [6, 7], "F": [0, 1, 2, 3, 4, 5]}

    def nB():
        lst = POOLS["B"]; i = lst[rr["B"] % len(lst)]; rr["B"] += 1
        return bank[i], "ps%d" % i

    def nF():
        lst = POOLS["F"]; i = lst[rr["F"] % len(lst)]; rr["F"] += 1
        return bankF[i], "ps%d" % i

    es_r = ExitStack()
    def R(name, shape, dt):
        return es_r.enter_context(nc.sbuf_tensor(name, list(shape), dt, side="right"))
    cm = R("cm", [128, 10, 128], F32)
    identb = R("identb", [128, 128], BF16)
    rm = R("rm", [128, 4, 128], F32)
    rv = R("rv", [128, 8], F32)
    xibc = R("xibc", [128, 4, 128], F32)
    nwbc = R("nwbc", [128, 1, D], F32)
    gsmbc = R("gsmbc", [128, 136], F32)
    gconst = R("gconst", [128, 16], F32)
    IDENT, ONES, MSU, MU, TRI, SC0, SC1 = [cm[:, i, :] for i in range(7)]
    dma("sp", cm[:], cmat.rearrange("m p n -> p m n"), writes=["cm"], tag="m")
    dma("sp", rm[:], rmat.rearrange("m p n -> p m n"), writes=["rm"], tag="m")
    dma("sp", rv[:], rvec, writes=["rv"])
    for h in range(4):
        dma("sp", xibc[:, h, :], rxi[h:h + 1, :].partition_broadcast(128), writes=["xibc"], tag="b")
    dma("sp", nwbc[:, 0, :], vecs[0:1, :].partition_broadcast(128), writes=["nwbc"], tag="b")
    dma("sp", gsmbc[:], gsm.partition_broadcast(128), writes=["gsmbc"], tag="b")
    op("dve", lambda e: e.tensor_copy(out=identb[:], in_=IDENT), reads=["cm"], writes=["identb"])
    op("act", lambda e: e.activation(out=gconst[:, 0:4], in_=gsmbc[:, 0:4], func=AF.Exp), reads=["gsmbc"], writes=["gconst"])
    op("dve", lambda e: e.tensor_scalar(out=gconst[:, 0:4], in0=gconst[:, 0:4], scalar1=-1.0, scalar2=None, op0=ALU.mult),
       reads=["gconst"], writes=["gconst"])
    op("dve", lambda e: e.tensor_copy(out=gconst[:, 4:8], in_=gsmbc[:, 4:8]), reads=["gsmbc", "gconst"], writes=["gconst"])
    GNW = gsmbc[:, 8:136]

    es_mix = ExitStack()
    mix = es_mix.enter_context(nc.sbuf_tensor("mix", [128, NSLOT, D], BF16, side="left"))
    es_keep = ExitStack()
    es_ph = [ExitStack()]
    PHTAG = [""]
    def L(name, shape, dt, es=None):
        if es is None:
            name = name + PHTAG[0]
        return (es or es_ph[0]).enter_context(nc.sbuf_tensor(name, list(shape), dt, side="left"))
    K = lambda name, shape, dt: L(name, shape, dt, es_keep)

    w_in_v = w_in.rearrange("(k p) c -> p k c", p=128)
    Wg = {}
    for h in range(4):
        for X in (1, 2):
            Wg[(h, X)] = K("Wg%d%d" % (h, X), [128, 8, 128], BF16)
            c0 = X * 512 + h * 128
            dma("pool", Wg[(h, X)][:], w_in_v[:, :, c0:c0 + 128], writes=["Wg%d%d" % (h, X)], tag="w")
    Wab = K("Wab", [128, 8, 8], BF16)
    dma("pool", Wab[:], w_in_v[:, :, 2048:2056], writes=["Wab"], tag="w")
    WrK = K("WrK", [128, 8, 512], BF16)
    WrV = K("WrV", [128, 8, 512], BF16)
    for h in range(4):
        c0 = 2056 + 1 * 512 + h * 128
        dma("pool", WrK[:, :, h * 128:(h + 1) * 128], w_in_v[:, :, c0:c0 + 128], writes=["WrK"], tag="w")
        c0 = 2056 + 2 * 512 + h * 128
        dma("pool", WrV[:, :, h * 128:(h + 1) * 128], w_in_v[:, :, c0:c0 + 128], writes=["WrV"], tag="w")
    gcw = K("gcw", [128, 12, 4], F32)
    dma("sp", gcw[:], gcwT.rearrange("(c p) i -> p c i", p=128), writes=["gcw"], tag="c")
    xt = [K("xt%d" % i, [128, D], F32) for i in range(2)]
    junk = K("junk", [128, D], BF16)
    nb = [K("nb%d" % i, [128, D], BF16) for i in range(2)]
    nTg = [K("nTg%d" % i, [128, 8, 512], BF16) for i in range(2)]
    st = K("st", [128, 16], F32)
    ab = K("ab", [128, 2, 4, 8], F32)
    CN = ["cG", "cLB", "cB", "cGAM", "cGL0", "cGL1", "cEG", "cEKD", "cCD0", "cCD1", "cGLB", "cGC", "cBG", "cTMP"]
    CT = {n: K(n, [128, 2, 4, 4], F32) for n in CN}
    halo = [[K("halo%d%d" % (h, X), [128, 4], F32) for X in range(3)] for h in range(4)]
    Sst = [K("S%d" % h, [128, 128], F32) for h in range(4)]
    Sb = [K("Sb%d" % h, [128, 128], BF16) for h in range(4)]
    Rst = [K("R%d" % h, [128, 128], F32) for h in range(4)]
    Rb = [K("Rb%d" % h, [128, 128], BF16) for h in range(4)]
    for h in range(4):
        op("pool", lambda e, h=h: e.memset(Sst[h][:], 0.0), writes=["S%d" % h])
        op("pool", lambda e, h=h: e.memset(Sb[h][:], 0.0), writes=["Sb%d" % h])
        op("pool", lambda e, h=h: e.memset(Rst[h][:], 0.0), writes=["R%d" % h])
        op("pool", lambda e, h=h: e.memset(Rb[h][:], 0.0), writes=["Rb%d" % h])
        for X in range(3):
            op("pool", lambda e, h=h, X=X: e.memset(halo[h][X][:], 0.0), writes=["halo%d%d" % (h, X)])

    CHN = ("kd", "ub", "wT", "ubf", "qkT", "qdT", "osb", "og", "sg")
    PH = {}

    def alloc_stream(sid, full, nchain, lb, HB, cbanks):
        nE = 2 if full else 1
        cx = {"id": sid, "lb": lb, "HB": HB, "cbanks": cbanks}
        Ld = {}
        for n, shp, dt in (("lr", [128, 4, 2], F32), ("rkb", [128, 8, 4], F32), ("Dab", [128, nE, 4, 128], F32),
                           ("E", [128, nE, 4, 128], BF16), ("tmpm", [128, 4, 128], F32),
                           ("YP0", [128, 4, 256], BF16), ("YP1", [128, 4, 256], BF16),
                           ("XX0", [128, 4, 128], BF16), ("XX1", [128, 4, 128], BF16), ("kg", [128, 4, 128], BF16),
                           ("vtok", [128, 4, 128], BF16), ("TTb", [128, 4, 128], BF16), ("TbT", [128, 4, 128], BF16)):
            Ld[n] = L("%s_%s" % (n, sid), shp, dt)
        if full:
            Ld["Eq"] = L("Eq_%s" % sid, [128, 4, 128], BF16)
        cx["L"] = Ld
        cx["XT"] = [(L("XT%s%d" % (sid, X), [128, 512], BF16) if (full or X != 0) else None) for X in range(3)]
        cx["ybuf"] = [L("ybuf%s%d" % (sid, i), [128, 512], F32) for i in range(2)]
        cx["xw"] = [L("xw%s%d" % (sid, i), [128, 515], F32) for i in range(2)]
        cx["sq"] = L("sq%s" % sid, [128, 512], F32)
        cx["ssc"] = L("ssc%s" % sid, [128, 4, 2], F32)
        cx["cs"] = []
        for i in range(nchain):
            d = {"id": "%s%d" % (sid, i)}
            lst = [("kd", [128, 4, 128], BF16), ("ub", [128, 4, 128], F32), ("wT", [128, 4, 128], BF16), ("ubf", [128, 128], BF16)]
            if full:
                lst += [("qkT", [128, 4, 128], BF16), ("qdT", [128, 4, 128], BF16),
                        ("osb", [128, 128], F32), ("og", [128, 128], F32), ("sg", [128, 128], BF16)]
            for n, shp, dt in lst:
                d[n] = L("%s_%s" % (n, d["id"]), shp, dt)
            op("pool", lambda e, d=d: e.memset(d["ubf"][:], 0.0), writes=["ubf_%s" % d["id"]])
            cx["cs"].append(d)
        return cx

    def alloc_phase(full):
        PH.clear()
        PHTAG[0] = "_F" if full else "_S"
        Rr = {}
        lst = [("rsb", [128, 256], F32), ("rt1", [128, 128], F32), ("rt2", [128, 128], F32), ("krot", [128, 128], BF16),
               ("kz", [128, 128], BF16), ("rvt", [128, 128], BF16)]
        if full:
            lst += [("qrot", [128, 128], BF16), ("rkT", [128, 128], BF16), ("rqT", [128, 128], BF16), ("rqx", [128, 128], BF16),
                    ("rPT", [128, 128], BF16), ("sgr", [128, 128], BF16), ("osb", [128, 128], F32), ("og", [128, 128], F32)]
        if full:
            for n, shp, dt in lst:
                Rr[n] = L("%s_r" % n, shp, dt)
        else:
            for n, shp, dt in (("rsb4", [128, 512], F32), ("rt14", [128, 512], F32), ("rt24", [128, 512], F32),
                               ("krot4", [128, 4, 128], BF16), ("kz4", [128, 4, 128], BF16), ("rvt4", [128, 4, 128], BF16)):
                Rr[n] = L("%s_r" % n, shp, dt)
        PH["RS"] = Rr
        PH["cs_t"] = [L("cs_t%d" % i, [128, 128], F32) for i in range(2)]
        PH["sn_t"] = [L("sn_t%d" % i, [128, 128], F32) for i in range(2)]
        if full:
            for h in range(4):
                Wg[(h, 0)] = L("Wg%d0" % h, [128, 8, 128], BF16)
                dma("pool", Wg[(h, 0)][:], w_in_v[:, :, h * 128:(h + 1) * 128], writes=["Wg%d0" % h], tag="w")
            PH["Wgg"] = L("Wgg", [128, 8, 512], BF16)
            dma("pool", PH["Wgg"][:], w_in_v[:, :, 1536:2048], writes=["Wgg"], tag="w")
            PH["Wrqg"] = [L("Wrqg%d" % h, [128, 8, 256], BF16) for h in range(4)]
            for h in range(4):
                for j, X in enumerate((0, 3)):
                    c0 = 2056 + X * 512 + h * 128
                    dma("pool", PH["Wrqg"][h][:, :, j * 128:(j + 1) * 128], w_in_v[:, :, c0:c0 + 128], writes=["Wrqg%d" % h], tag="w")
            PH["streams"] = [alloc_stream("A", True, 2, [0, 1, 2], [(0, 2, 0), (1, 2, 256)], [3, 4])]
            PH["rbank"] = 5; PH["rbank2"] = 6
        else:
            PH["streams"] = [alloc_stream("A", False, 2, [0, 1, 0], [(0, 1, 0), (0, 1, 256)], [4]),
                             alloc_stream("B", False, 2, [2, 3, 2], [(2, 3, 0), (2, 3, 256)], [5])]
            PH["rbank"] = 6; PH["rbank2"] = 6

    LN_DS = math.log(HD ** -0.5)

    def rms_rstd(src_ap, rkeys, n, col, ckey, jout=None, jkey="junk"):
        jo = junk[:, 0:n] if jout is None else jout
        op("act", lambda e: e.activation(out=jo, in_=src_ap, func=AF.Square, accum_out=col),
           reads=rkeys, writes=[jkey, ckey])
        op("act", lambda e: e.activation(out=col, in_=col, func=AF.Ln, bias=EPS, scale=1.0 / n),
           reads=[ckey], writes=[ckey])
        op("act", lambda e: e.activation(out=col, in_=col, func=AF.Exp, scale=-0.5), reads=[ckey], writes=[ckey])

    def norm_a(t):
        xb_ = xt[t % 2]; xk = "xt%d" % (t % 2); nb_ = nb[t % 2]; nbk = "nb%d" % (t % 2)
        dma("sp", xb_[:], xpad[t * 128:(t + 1) * 128, :], writes=[xk])
        rms_rstd(xb_[:], [xk], D, st[:, 0:1], "st0", jout=nb_[:], jkey=nbk)
        op("dve", lambda e: e.scalar_tensor_tensor(out=nb_[:], in0=xb_[:], scalar=st[:, 0:1], in1=nwbc[:, 0, :],
                                                   op0=ALU.mult, op1=ALU.mult), reads=[xk, "st0", "nwbc"], writes=[nbk])

    def norm_b(t, gi, tt):
        nb_ = nb[t % 2]; nbk = "nb%d" % (t % 2)
        pb, pk = bank[7], "ps7"
        for k in range(8):
            op("pe", lambda e, k=k: e.transpose(out=pb[:, k * 128:(k + 1) * 128], in_=nb_[:, k * 128:(k + 1) * 128],
                                               identity=identb[:]), reads=[nbk, "identb"], writes=[pk])
        nk = "nTg%d" % gi
        op("act", lambda e: e.copy(out=nTg[gi][:, :, tt * 128:(tt + 1) * 128],
                                   in_=pb[:, :].rearrange("p (k n) -> p k n", k=8)), reads=[pk], writes=[nk])

    def gdn_common(gi, full):
        nk = "nTg%d" % gi
        C = {n: CT[n][:, gi] for n in CN}
        ck = lambda n: "%s_%d" % (n, gi)
        abk = "ab%d" % gi
        for tt in range(4):
            pf, fk = bankF[7], "ps7"
            for k in range(8):
                op("pe", lambda e, k=k: e.matmul(out=pf[:, 0:8], lhsT=nTg[gi][:, k, tt * 128:(tt + 1) * 128],
                                                rhs=Wab[:, k, :], start=(k == 0), stop=(k == 7)),
                   reads=[nk, "Wab"], writes=[fk])
            op("dve", lambda e: e.tensor_copy(out=ab[:, gi, tt, :], in_=pf[:, 0:8]), reads=[fk], writes=[abk])
            yield
        for h in range(4):
            op("dve", lambda e, h=h: e.tensor_scalar(out=C["cTMP"][:, :, h], in0=ab[:, gi, :, h], scalar1=gconst[:, 4 + h:5 + h],
                                                    scalar2=None, op0=ALU.add), reads=[abk, "gconst"], writes=[ck("cTMP")])
        op("act", lambda e: e.activation(out=C["cTMP"], in_=C["cTMP"], func=AF.Exp), reads=[ck("cTMP")], writes=[ck("cTMP")])
        op("act", lambda e: e.activation(out=C["cTMP"], in_=C["cTMP"], func=AF.Ln, bias=1.0), reads=[ck("cTMP")], writes=[ck("cTMP")])
        for h in range(4):
            op("dve", lambda e, h=h: e.tensor_scalar(out=C["cG"][:, :, h], in0=C["cTMP"][:, :, h], scalar1=gconst[:, h:h + 1],
                                                    scalar2=None, op0=ALU.mult), reads=[ck("cTMP"), "gconst"], writes=[ck("cG")])
        op("act", lambda e: e.activation(out=C["cLB"], in_=ab[:, gi, :, 4:8], func=AF.Exp, scale=-1.0), reads=[abk], writes=[ck("cLB")])
        op("act", lambda e: e.activation(out=C["cLB"], in_=C["cLB"], func=AF.Ln, bias=1.0), reads=[ck("cLB")], writes=[ck("cLB")])
        op("act", lambda e: e.activation(out=C["cB"], in_=C["cLB"], func=AF.Exp, scale=-1.0), reads=[ck("cLB")], writes=[ck("cB")])
        op("dve", lambda e: e.tensor_scalar(out=C["cLB"], in0=C["cLB"], scalar1=-1.0, scalar2=None, op0=ALU.mult), reads=[ck("cLB"), ck("cB")], writes=[ck("cLB")])
        yield
        gflat = C["cG"].rearrange("p t h -> p (t h)")
        for lhs, dn in ((TRI, "cGAM"), (SC0, "cGL0"), (SC1, "cGL1")):
            pf, fk = bankF[7], "ps7"
            op("pe", lambda e, lhs=lhs: e.matmul(out=pf[:, 0:16], lhsT=lhs, rhs=gflat, start=True, stop=True),
               reads=["cm", ck("cG")], writes=[fk])
            op("dve", lambda e, dn=dn: e.tensor_copy(out=C[dn].rearrange("p t h -> p (t h)"), in_=pf[:, 0:16]),
               reads=[fk], writes=[ck(dn)])
        yield
        op("act", lambda e: e.activation(out=C["cEG"], in_=C["cGAM"], func=AF.Exp), reads=[ck("cGAM")], writes=[ck("cEG")])
        op("act", lambda e: e.activation(out=C["cCD0"], in_=C["cGL0"], func=AF.Exp), reads=[ck("cGL0")], writes=[ck("cCD0")])
        op("act", lambda e: e.activation(out=C["cCD1"], in_=C["cGL1"], func=AF.Exp), reads=[ck("cGL1")], writes=[ck("cCD1")])
        op("dve", lambda e: e.tensor_tensor(out=C["cEKD"][0:64], in0=C["cGL0"][0:64], in1=C["cGAM"][0:64], op=ALU.subtract),
           reads=[ck("cGL0"), ck("cGAM")], writes=[ck("cEKD")])
        op("dve", lambda e: e.tensor_tensor(out=C["cEKD"][64:128], in0=C["cGL1"][64:128], in1=C["cGAM"][64:128], op=ALU.subtract),
           reads=[ck("cGL1"), ck("cGAM"), ck("cEKD")], writes=[ck("cEKD")])
        op("act", lambda e: e.activation(out=C["cEKD"], in_=C["cEKD"], func=AF.Exp), reads=[ck("cEKD")], writes=[ck("cEKD")])
        op("dve", lambda e: e.tensor_tensor(out=C["cGLB"], in0=C["cGAM"], in1=C["cLB"], op=ALU.add), reads=[ck("cGAM"), ck("cLB")], writes=[ck("cGLB")])
        op("dve", lambda e: e.tensor_scalar(out=C["cGC"], in0=C["cGAM"], scalar1=LN_DS, scalar2=None, op0=ALU.add),
           reads=[ck("cGAM")], writes=[ck("cGC")])
        op("dve", lambda e: e.tensor_tensor(out=C["cBG"], in0=C["cB"], in1=C["cEG"], op=ALU.mult), reads=[ck("cB"), ck("cEG")], writes=[ck("cBG")])
        yield

    def gdn_proj(gi, h, full, cx):
        nk = "nTg%d" % gi
        sid = cx["id"]; lb = cx["lb"]
        sq_ = cx["sq"]; sqk = "sq%s" % sid
        Xs = (1, 2, 0) if full else (1, 2)

        def s1(i, X):
            xb_ = cx["xw"][i % 2]; xk = "xw%s%d" % (sid, i % 2)
            wk = "Wg%d%d" % (h, X); hk_ = "halo%d%d" % (h, X)
            op("pool", lambda e: e.tensor_copy(out=xb_[:, 0:3], in_=halo[h][X][:, 0:3]), reads=[hk_, xk], writes=[xk])
            pf, fk = bankF[lb[i % 3]], "ps%d" % lb[i % 3]
            for k in range(8):
                op("pe", lambda e, k=k: e.matmul(out=pf[:, :], lhsT=Wg[(h, X)][:, k, :], rhs=nTg[gi][:, k, :],
                                                start=(k == 0), stop=(k == 7)), reads=[wk, nk], writes=[fk])
            op("act", lambda e: e.copy(out=xb_[:, 3:515], in_=pf[:, :]), reads=[fk, xk], writes=[xk])

        def s2(i, X):
            xb_ = cx["xw"][i % 2]; xk = "xw%s%d" % (sid, i % 2)
            yb_ = cx["ybuf"][i % 2]; yk = "ybuf%s%d" % (sid, i % 2)
            tk = "XT%s%d" % (sid, X); hk_ = "halo%d%d" % (h, X)
            cw = lambda j: gcw[:, X * 4 + h, j:j + 1]
            op("dve", lambda e: e.tensor_scalar(out=yb_[:], in0=xb_[:, 3:515], scalar1=cw(3), scalar2=None, op0=ALU.mult),
               reads=[xk, "gcw"], writes=[yk])
            for j in (2, 1, 0):
                op("dve", lambda e, j=j: e.scalar_tensor_tensor(out=yb_[:], in0=xb_[:, j:j + 512], scalar=cw(j), in1=yb_[:],
                                                               op0=ALU.mult, op1=ALU.add), reads=[xk, "gcw", yk], writes=[yk])
            op("pool", lambda e: e.tensor_copy(out=halo[h][X][:, 0:3], in_=xb_[:, 512:515]), reads=[xk], writes=[hk_])
            op("act", lambda e: e.activation(out=cx["XT"][X][:], in_=yb_[:], func=AF.Silu), reads=[yk], writes=[tk])
            if X != 2:
                op("act", lambda e: e.activation(out=yb_[:], in_=yb_[:], func=AF.Silu), reads=[yk], writes=[yk])
                op("act", lambda e: e.activation(out=sq_[:], in_=yb_[:], func=AF.Square), reads=[yk], writes=[sqk])

        def s3(i, X):
            if X == 2:
                return
            j = 0 if X == 1 else 1
            pf2, fk2 = bankF[lb[(i + 1) % 3]], "ps%d" % lb[(i + 1) % 3]
            for tt in range(4):
                op("pe", lambda e, tt=tt: e.matmul(out=pf2[:, tt:tt + 1], lhsT=sq_[:, tt * 128:(tt + 1) * 128], rhs=cm[:, 1, 0:1], start=True, stop=True),
                   reads=[sqk, "cm"], writes=[fk2])
            op("dve", lambda e: e.tensor_copy(out=cx["ssc"][:, :, j], in_=pf2[:, 0:4]), reads=[fk2], writes=["ssc%s" % sid])

        n = len(Xs)
        for r in range(n + 3):
            if 0 <= r - 3 < n:
                s3(r - 3, Xs[r - 3])
            if 0 <= r - 1 < n:
                s2(r - 1, Xs[r - 1])
            if r < n:
                s1(r, Xs[r])
            yield

    def gdn_local(g, h, cx, cs):
        gi = g % 2; full = g >= G_FULL
        B = dict(cx["L"]); B.update(cs)
        sid = cx["id"]; lb = cx["lb"]
        key = lambda n: ("%s_%s" % (n, cs["id"])) if n in CHN else ("%s_%s" % (n, sid))
        C = {n: CT[n][:, gi] for n in CN}
        ck = lambda n: "%s_%d" % (n, gi)
        bF = lambda i: (bankF[lb[i]], "ps%d" % lb[i])
        bB = lambda i: (bank[lb[i]], "ps%d" % lb[i])
        kTa, vTa, qTa = cx["XT"][1], cx["XT"][2], cx["XT"][0]
        kTk, vTk, qTk = "XT%s1" % sid, "XT%s2" % sid, "XT%s0" % sid
        lr, rkb, Dab, E, tmpm = B["lr"], B["rkb"], B["Dab"], B["E"], B["tmpm"]
        YP = [B["YP0"], B["YP1"]]; XX = [B["XX0"], B["XX1"]]
        kg, kd, vtok, TTb, TbT, ub, wT = B["kg"], B["kd"], B["vtok"], B["TTb"], B["TbT"], B["ub"], B["wT"]
        Eq, qkT, qdT = B.get("Eq"), B.get("qkT"), B.get("qdT")
        hc = lambda n: C[n][:, :, h]
        bc4 = lambda ap2: ap2.unsqueeze(2).broadcast_to([128, 4, 128])
        rep4 = lambda ap2: ap2.unsqueeze(1).broadcast_to([128, 4, 128])
        v4 = lambda ap, w=128: ap.rearrange("p (a b) -> p a b", a=4)
        nc_ = 2 if full else 1
        sk = "ssc%s" % sid
        op("act", lambda e: e.activation(out=lr[:, :, 0:nc_], in_=cx["ssc"][:, :, 0:nc_], func=AF.Ln, bias=EPS), reads=[sk], writes=[key("lr")])
        op("act", lambda e: e.activation(out=rkb[:, 0, :], in_=lr[:, :, 0], func=AF.Exp, scale=-0.5), reads=[key("lr")], writes=[key("rk0")])
        op("dve", lambda e: e.scalar_tensor_tensor(out=rkb[:, 1, :], in0=lr[:, :, 0], scalar=-0.5, in1=hc("cGLB"), op0=ALU.mult, op1=ALU.add),
           reads=[key("lr"), ck("cGLB")], writes=[key("rk1")])
        op("dve", lambda e: e.tensor_tensor(out=rkb[:, 3, :], in0=rkb[:, 0, :], in1=hc("cBG"), op=ALU.mult), reads=[key("rk0"), ck("cBG")], writes=[key("rk3")])
        op("dve", lambda e: e.tensor_tensor(out=rkb[:, 4, :], in0=rkb[:, 0, :], in1=hc("cEKD"), op=ALU.mult), reads=[key("rk0"), ck("cEKD")], writes=[key("rk4")])
        op("dve", lambda e: e.scalar_tensor_tensor(out=rkb[:, 6, :], in0=lr[:, :, 0], scalar=-0.5, in1=hc("cGAM"), op0=ALU.mult, op1=ALU.subtract),
           reads=[key("lr"), ck("cGAM")], writes=[key("rk6")])
        if full:
            op("dve", lambda e: e.scalar_tensor_tensor(out=rkb[:, 2, :], in0=lr[:, :, 1], scalar=-0.5, in1=hc("cGC"), op0=ALU.mult, op1=ALU.add),
               reads=[key("lr"), ck("cGC")], writes=[key("rk2")])
        yield
        pb, pk = bB(0)
        for tt in range(4):
            op("pe", lambda e, tt=tt: e.transpose(out=pb[:, tt * 128:(tt + 1) * 128], in_=kTa[:, tt * 128:(tt + 1) * 128], identity=identb[:]),
               reads=[kTk, "identb"], writes=[pk])
        for tt in range(4):
            op("pe", lambda e, tt=tt: e.transpose(out=pb[:, 512 + tt * 128:512 + (tt + 1) * 128], in_=vTa[:, tt * 128:(tt + 1) * 128], identity=identb[:]),
               reads=[vTk, "identb"], writes=[pk])
        op("dve", lambda e: e.tensor_tensor(out=kg[:], in0=v4(pb[:, 0:512]), in1=bc4(rkb[:, 3, :]), op=ALU.mult), reads=[pk, key("rk3")], writes=[key("kg")])
        op("dve", lambda e: e.tensor_tensor(out=kd[:], in0=v4(pb[:, 0:512]), in1=bc4(rkb[:, 4, :]), op=ALU.mult), reads=[pk, key("rk4")], writes=[key("kd")])
        op("act", lambda e: e.copy(out=vtok[:], in_=v4(pb[:, 512:1024])), reads=[pk], writes=[key("vtok")])
        yield
        op("pool", lambda e: e.tensor_tensor(out=Dab[:, 0], in0=rep4(IDENT), in1=bc4(rkb[:, 1, :]), op=ALU.mult), reads=["cm", key("rk1")], writes=[key("Dab0")])
        pbc, bck = bF(1)
        op("pe", lambda e: e.matmul(out=pbc[:, :], lhsT=ONES, rhs=Dab[:, 0].rearrange("p a b -> p (a b)"), start=True, stop=True),
           reads=["cm", key("Dab0")], writes=[bck])
        if full:
            op("pool", lambda e: e.tensor_tensor(out=Dab[:, 1], in0=rep4(IDENT), in1=bc4(rkb[:, 2, :]), op=ALU.mult), reads=["cm", key("rk2")], writes=[key("Dab1")])
            pbc2, bck2 = bF(2)
            op("pe", lambda e: e.matmul(out=pbc2[:, :], lhsT=ONES, rhs=Dab[:, 1].rearrange("p a b -> p (a b)"), start=True, stop=True),
               reads=["cm", key("Dab1")], writes=[bck2])
        op("dve", lambda e: e.tensor_tensor(out=tmpm[:], in0=v4(pbc[:, :]), in1=rep4(cm[:, 7, :]), op=ALU.add), reads=[bck, "cm"],
           writes=[key("tmpm"), key("tmpm") + "h0", key("tmpm") + "h1"])
        for tt in range(4):
            op("act", lambda e, tt=tt: e.activation(out=E[:, 0, tt, :], in_=tmpm[:, tt, :], func=AF.Exp, bias=rkb[:, 6, tt:tt + 1]),
               reads=[key("tmpm"), key("rk6")], writes=[key("E0")])
        yield
        pkk, kkk = bF(0)
        for tt in range(4):
            sl = slice(tt * 128, (tt + 1) * 128)
            op("pe", lambda e, sl=sl: e.matmul(out=pkk[:, sl], lhsT=kTa[:, sl], rhs=kTa[:, sl], start=True, stop=True), reads=[kTk], writes=[kkk])
        Y0 = YP[0][:, :, 0:128]
        op("dve", lambda e: e.scalar_tensor_tensor(out=Y0, in0=v4(pkk[:, :]), scalar=-1.0, in1=E[:, 0], op0=ALU.mult, op1=ALU.mult),
           reads=[kkk, key("E0")], writes=[key("YP0a"), key("YP0a") + "h0", key("YP0a") + "h1"])
        if full:
            op("dve", lambda e: e.tensor_tensor(out=tmpm[:], in0=v4(pbc2[:, :]), in1=rep4(cm[:, 8, :]), op=ALU.add), reads=[bck2, "cm", key("tmpm")], writes=[key("tmpm")])
            op("act", lambda e: e.activation(out=Eq[:], in_=v4(pbc2[:, :]), func=AF.Exp), reads=[bck2], writes=[key("Eq")])
            for tt in range(4):
                op("act", lambda e, tt=tt: e.activation(out=E[:, 1, tt, :], in_=tmpm[:, tt, :], func=AF.Exp, bias=rkb[:, 6, tt:tt + 1]),
                   reads=[key("tmpm"), key("rk6")], writes=[key("E1")])
            pqk, qkk = bF(1)
            for tt in range(4):
                sl = slice(tt * 128, (tt + 1) * 128)
                op("pe", lambda e, sl=sl: e.matmul(out=pqk[:, sl], lhsT=kTa[:, sl], rhs=qTa[:, sl], start=True, stop=True), reads=[kTk, qTk], writes=[qkk])
            op("dve", lambda e: e.tensor_tensor(out=qkT[:], in0=v4(pqk[:, :]), in1=E[:, 1], op=ALU.mult), reads=[qkk, key("E1")], writes=[key("qkT")])
            op("pool", lambda e: e.tensor_tensor(out=qdT[:], in0=v4(qTa[:, :]), in1=Eq[:], op=ALU.mult), reads=[qTk, key("Eq")], writes=[key("qdT")])
        yield "FRONT_DONE"
        HB = cx["HB"]
        bR = lambda i: (bankF[i], "ps%d" % i)
        hk = lambda n, hh: key(n) + "h%d" % hh
        px, pxk = bB(2)
        for tt in range(4):
            op("pe", lambda e, tt=tt: e.transpose(out=px[:, tt * 128:(tt + 1) * 128], in_=YP[0][:, tt, 0:128], identity=identb[:]),
               reads=[key("YP0a"), "identb"], writes=[pxk])
        for hh in range(2):
            tsl = slice(2 * hh, 2 * hh + 2)
            op("act", lambda e, tsl=tsl, hh=hh: e.copy(out=XX[0][:, tsl, :], in_=v4(px[:, 0:512])[:, tsl, :]), reads=[pxk], writes=[hk("XX0", hh)])
            op("pool", lambda e, tsl=tsl: e.tensor_tensor(out=YP[1][:, tsl, 128:256], in0=YP[0][:, tsl, 0:128], in1=identb[:].unsqueeze(1).broadcast_to([128, 2, 128]), op=ALU.add),
               reads=[key("YP0a"), "identb"], writes=[hk("YP1b", hh)])
        yield
        for hh in range(2):
            yb_, xb2, xo = HB[hh]
            py, pyk = bR(yb_); pz, pzk = bR(xb2)
            for tt in (2 * hh, 2 * hh + 1):
                o0 = (tt % 2) * 128
                op("pe", lambda e, tt=tt, o0=o0, py=py: e.matmul(out=py[:, o0:o0 + 128], lhsT=XX[0][:, tt, :], rhs=YP[0][:, tt, 0:128], start=True, stop=True),
                   reads=[hk("XX0", hh), key("YP0a")], writes=[pyk])
                op("pe", lambda e, tt=tt, o0=o0, pz=pz, xo=xo: e.matmul(out=pz[:, xo + o0:xo + o0 + 128], lhsT=YP[0][:, tt, 0:128], rhs=XX[0][:, tt, :], start=True, stop=True),
                   reads=[hk("XX0", hh), key("YP0a")], writes=[pzk])
        for hh in range(2):
            yb_, xb2, xo = HB[hh]
            py, pyk = bR(yb_); pz, pzk = bR(xb2)
            tsl = slice(2 * hh, 2 * hh + 2)
            op("dve", lambda e, tsl=tsl, py=py: e.tensor_copy(out=YP[1][:, tsl, 0:128], in_=py[:, 0:256].rearrange("p (a b) -> p a b", a=2)),
               reads=[pyk], writes=[hk("YP1a", hh)])
            op("act", lambda e, tsl=tsl, pz=pz, xo=xo: e.copy(out=XX[1][:, tsl, :], in_=pz[:, xo:xo + 256].rearrange("p (a b) -> p a b", a=2)),
               reads=[pzk], writes=[hk("XX1", hh)])
        yield
        cur = 1
        for lvl in range(2, 6):
            c_, n_ = cur, 1 - cur
            last = (lvl == 5)
            for hh in range(2):
                yb_, xb2, xo = HB[hh]
                pv_, pvk = bR(yb_); pxx, pxxk = bR(xb2)
                for tt in (2 * hh, 2 * hh + 1):
                    o0 = (tt % 2) * 256
                    if not last:
                        op("pe", lambda e, tt=tt, c_=c_, pv_=pv_, o0=o0: e.matmul(out=pv_[:, o0:o0 + 256], lhsT=XX[c_][:, tt, :], rhs=YP[c_][:, tt, :], start=True, stop=True),
                           reads=[hk("XX%d" % c_, hh), hk("YP%da" % c_, hh), hk("YP%db" % c_, hh)], writes=[pvk])
                    else:
                        op("pe", lambda e, tt=tt, c_=c_, pv_=pv_, o0=o0: e.matmul(out=pv_[:, o0 + 128:o0 + 256], lhsT=XX[c_][:, tt, :], rhs=YP[c_][:, tt, 128:256], start=True, stop=True),
                           reads=[hk("XX%d" % c_, hh), hk("YP%db" % c_, hh)], writes=[pvk])
                for tt in (2 * hh, 2 * hh + 1):
                    o1 = (tt % 2) * 128
                    op("pe", lambda e, tt=tt, c_=c_, pxx=pxx, o1=o1, xo=xo: e.matmul(out=pxx[:, xo + o1:xo + o1 + 128], lhsT=YP[c_][:, tt, 0:128], rhs=XX[c_][:, tt, :], start=True, stop=True),
                       reads=[hk("XX%d" % c_, hh), hk("YP%da" % c_, hh)], writes=[pxxk])
            for hh in range(2):
                yb_, xb2, xo = HB[hh]
                pv_, pvk = bR(yb_); pxx, pxxk = bR(xb2)
                pv3 = pv_[:, :].rearrange("p (a b) -> p a b", a=2)
                tsl = slice(2 * hh, 2 * hh + 2)
                if not last:
                    op("act", lambda e, n_=n_, pv3=pv3, tsl=tsl: e.copy(out=YP[n_][:, tsl, 0:128], in_=pv3[:, :, 0:128]), reads=[pvk], writes=[hk("YP%da" % n_, hh)] + ([key("YP0a")] if n_ == 0 else []))
                op("dve", lambda e, c_=c_, n_=n_, pv3=pv3, tsl=tsl: e.tensor_tensor(out=YP[n_][:, tsl, 128:256], in0=YP[c_][:, tsl, 128:256], in1=pv3[:, :, 128:256], op=ALU.add),
                   reads=[pvk, hk("YP%db" % c_, hh)], writes=[hk("YP%db" % n_, hh)])
                op("act", lambda e, n_=n_, pxx=pxx, tsl=tsl, xo=xo: e.copy(out=XX[n_][:, tsl, :], in_=pxx[:, xo:xo + 256].rearrange("p (a b) -> p a b", a=2)),
                   reads=[pxxk], writes=[hk("XX%d" % n_, hh)])
            cur = n_
            yield
        c_ = cur
        for hh in range(2):
            yb_, xb2, xo = HB[hh]
            py, pyk = bR(yb_)
            tsl = slice(2 * hh, 2 * hh + 2)
            for tt in (2 * hh, 2 * hh + 1):
                o1 = (tt % 2) * 128
                op("pe", lambda e, tt=tt, py=py, o1=o1: e.matmul(out=py[:, o1:o1 + 128], lhsT=XX[c_][:, tt, :], rhs=YP[c_][:, tt, 128:256], start=True, stop=True),
                   reads=[hk("XX%d" % c_, hh), hk("YP%db" % c_, hh)], writes=[pyk])
            op("dve", lambda e, py=py, tsl=tsl: e.tensor_tensor(out=tmpm[:, tsl, :], in0=YP[c_][:, tsl, 128:256], in1=py[:, 0:256].rearrange("p (a b) -> p a b", a=2), op=ALU.add),
               reads=[pyk, hk("YP%db" % c_, hh), key("tmpm")], writes=[key("tmpm") + "h%d" % hh])
        tmk = [key("tmpm") + "h0", key("tmpm") + "h1"]
        op("act", lambda e: e.copy(out=TTb[:], in_=tmpm[:]), reads=tmk, writes=[key("TTb")])
        op("pool", lambda e: e.tensor_tensor(out=TbT[:], in0=tmpm[:], in1=bc4(hc("cB")), op=ALU.mult), reads=tmk + [ck("cB")], writes=[key("TbT")])
        yield
        pu, puk = bF(1)
        pw_, pwk_ = bF(2)
        for tt in range(4):
            sl = slice(tt * 128, (tt + 1) * 128)
            op("pe", lambda e, tt=tt, sl=sl: e.matmul(out=pu[:, sl], lhsT=TbT[:, tt, :], rhs=vtok[:, tt, :], start=True, stop=True),
               reads=[key("TbT"), key("vtok")], writes=[puk])
        for tt in range(4):
            sl = slice(tt * 128, (tt + 1) * 128)
            op("pe", lambda e, tt=tt, sl=sl: e.matmul(out=pw_[:, sl], lhsT=kg[:, tt, :], rhs=TTb[:, tt, :], start=True, stop=True),
               reads=[key("kg"), key("TTb")], writes=[pwk_])
        op("act", lambda e: e.copy(out=ub[:], in_=v4(pu[:, :])), reads=[puk], writes=[key("ub")])
        op("dve", lambda e: e.tensor_copy(out=wT[:], in_=v4(pw_[:, :])), reads=[pwk_], writes=[key("wT")])
        yield

    def gdn_chain(g, h, cx, cs):
        gi = g % 2; full = g >= G_FULL
        B = cs
        key = lambda n: "%s_%s" % (n, cs["id"])
        cbk = cx["cbanks"]
        C = {n: CT[n][:, gi] for n in CN}
        ck = lambda n: "%s_%d" % (n, gi)
        bF = lambda i: (bankF[i], "ps%d" % i)
        kd, ub, wT, ubf = B["kd"], B["ub"], B["wT"], B["ubf"]
        qkT, qdT, osb, og, sg = B.get("qkT"), B.get("qdT"), B.get("osb"), B.get("og"), B.get("sg")
        Sk, Sbk = "S%d" % h, "Sb%d" % h
        flip = 0
        for tt in range(4):
            t = 4 * g + tt
            store = full and t >= T0
            for hf in range(2):
                rows = slice(hf * 64, hf * 64 + 64)
                cdn = "cCD0" if hf == 0 else "cCD1"
                cd = C[cdn][:, tt, h:h + 1]
                pw, pwk = bF(cbk[flip % len(cbk)]); ps_, psk = bF(cbk[(flip + 1) % len(cbk)]); flip = 1 - flip
                op("pe", lambda e: e.matmul(out=pw[:, 0:128], lhsT=wT[:, tt, :], rhs=Sb[h][:], start=True, stop=True), reads=[key("wT"), Sbk], writes=[pwk])
                op("dve", lambda e: e.tensor_tensor(out=ubf[rows, :], in0=ub[rows, tt, :], in1=pw[rows, 0:128], op=ALU.subtract),
                   reads=[key("ub"), pwk], writes=[key("ubf")])
                yield
                if store:
                    op("pe", lambda e: e.matmul(out=ps_[:, 0:128], lhsT=qdT[:, tt, :], rhs=Sb[h][:], start=True, stop=False), reads=[key("qdT"), Sbk, psk], writes=[psk])
                    op("pe", lambda e: e.matmul(out=ps_[:, 0:128], lhsT=qkT[:, tt, :], rhs=ubf[:], start=False, stop=True),
                       reads=[key("qkT"), key("ubf"), psk], writes=[psk])
                op("pe", lambda e: e.matmul(out=ps_[:, 384:512], lhsT=kd[rows, tt, :], rhs=ubf[rows, :], start=True, stop=True), reads=[key("kd"), key("ubf")], writes=[psk])
                op("dve", lambda e: e.scalar_tensor_tensor(out=Sb[h][:], in0=Sst[h][:], scalar=cd, in1=ps_[:, 384:512], op0=ALU.mult, op1=ALU.add),
                   reads=[Sk, ck(cdn), psk], writes=[Sbk])
                op("dve", lambda e: e.scalar_tensor_tensor(out=Sst[h][:], in0=Sst[h][:], scalar=cd, in1=ps_[:, 384:512], op0=ALU.mult, op1=ALU.add),
                   reads=[Sk, ck(cdn), psk], writes=[Sk])
                if store:
                    op("act", lambda e: e.copy(out=osb[rows, :], in_=ps_[rows, 0:128]), reads=[psk], writes=[key("osb")])
                yield
            if store:
                s_ = t - T0
                pg, pgk = bF(cbk[flip % len(cbk)])
                for k in range(8):
                    op("pe", lambda e, k=k: e.matmul(out=pg[:, 256:384], lhsT=nTg[gi][:, k, tt * 128:(tt + 1) * 128],
                                                    rhs=PH["Wgg"][:, k, h * 128:(h + 1) * 128], start=(k == 0), stop=(k == 7)),
                       reads=["nTg%d" % gi, "Wgg"], writes=[pgk])
                op("act", lambda e: e.activation(out=sg[:], in_=pg[:, 256:384], func=AF.Silu), reads=[pgk], writes=[key("sg")])
                sc = st[:, 4 + h:5 + h]; sck = "st%d" % (4 + h)
                rms_rstd(osb[:], [key("osb")], HD, sc, sck)
                op("dve", lambda e: e.scalar_tensor_tensor(out=og[:], in0=osb[:], scalar=sc, in1=GNW, op0=ALU.mult, op1=ALU.mult),
                   reads=[key("osb"), sck, "gsmbc"], writes=[key("og")])
                op("pool", lambda e: e.tensor_tensor(out=mix[:, s_, h * 128:(h + 1) * 128], in0=og[:], in1=sg[:], op=ALU.mult),
                   reads=[key("og"), key("sg")], writes=["mix%d" % h])
                yield

    def gdn_task(g, heads, cx, prefetch):
        gi = g % 2; full = g >= G_FULL
        prev = None
        if cx.get("prefetched") != g:
            for _ in gdn_proj(gi, heads[0], full, cx):
                yield
        for i, h in enumerate(heads):
            cs = cx["cs"][i % len(cx["cs"])]
            subs = [gdn_local(g, h, cx, cs)] + ([prev] if prev is not None else [])
            while subs:
                for sg_ in list(subs):
                    try:
                        r = next(sg_)
                        if r == "FRONT_DONE" and i + 1 < len(heads):
                            subs.append(gdn_proj(gi, heads[i + 1], full, cx))
                    except StopIteration:
                        subs.remove(sg_)
                yield
            prev = gdn_chain(g, h, cx, cs)
        subs = [prev]
        if prefetch:
            subs.append(gdn_proj((g + 1) % 2, heads[0], full, cx))
            cx["prefetched"] = g + 1
        while subs:
            for sg_ in list(subs):
                try:
                    next(sg_)
                except StopIteration:
                    subs.remove(sg_)
            yield

    def ret_tile(t, tt, gi, h, full, ci):
        nk = "nTg%d" % gi
        Rr = PH["RS"]; rbk = PH["rbank"]; rbk2 = PH["rbank2"]
        cs_t, sn_t = PH["cs_t"], PH["sn_t"]
        rk_ = lambda n: "%s_r" % n
        rsb, rt1, rt2, krot, kz, rvt = Rr["rsb"], Rr["rt1"], Rr["rt2"], Rr["krot"], Rr["kz"], Rr["rvt"]
        qrot, rkT, rqT, rqx, rPT, sgr, osb, og = [Rr.get(n) for n in ("qrot", "rkT", "rqT", "rqx", "rPT", "sgr", "osb", "og")]
        csk, snk = "cs_t%d" % ci, "sn_t%d" % ci
        store = full and t >= T0
        pf, fk = bankF[rbk], "ps%d" % rbk
        for k in range(8):
            op("pe", lambda e, k=k: e.matmul(out=pf[:, 0:128], lhsT=nTg[gi][:, k, tt * 128:(tt + 1) * 128], rhs=WrK[:, k, h * 128:(h + 1) * 128],
                                            start=(k == 0), stop=(k == 7)), reads=[nk, "WrK"], writes=[fk])
        for k in range(8):
            op("pe", lambda e, k=k: e.matmul(out=pf[:, 128:256], lhsT=nTg[gi][:, k, tt * 128:(tt + 1) * 128], rhs=WrV[:, k, h * 128:(h + 1) * 128],
                                            start=(k == 0), stop=(k == 7)), reads=[nk, "WrV", fk], writes=[fk])
        if store:
            for k in range(8):
                op("pe", lambda e, k=k: e.matmul(out=pf[:, 256:512], lhsT=nTg[gi][:, k, tt * 128:(tt + 1) * 128], rhs=PH["Wrqg"][h][:, k, :],
                                                start=(k == 0), stop=(k == 7)), reads=[nk, "Wrqg%d" % h, fk], writes=[fk])
        op("act", lambda e: e.copy(out=rsb[:, 0:128], in_=pf[:, 0:128]), reads=[fk], writes=[rk_("rsb")])
        op("act", lambda e: e.copy(out=rvt[:], in_=pf[:, 128:256]), reads=[fk], writes=[rk_("rvt")])
        if store:
            op("act", lambda e: e.copy(out=rsb[:, 128:256], in_=pf[:, 256:384]), reads=[fk, rk_("rsb")], writes=[rk_("rsb")])
        if store:
            op("act", lambda e: e.activation(out=sgr[:], in_=pf[:, 384:512], func=AF.Silu), reads=[fk], writes=[rk_("sgr")])
        yield

        def rotary(src, dst, dstk):
            op("dve", lambda e: e.tensor_tensor(out=rt1[:], in0=src, in1=cs_t[ci][:], op=ALU.mult), reads=[rk_("rsb"), csk], writes=[rk_("rt1")])
            op("dve", lambda e: e.tensor_tensor(out=rt2[:, 0:128:2], in0=src[:, 1:128:2], in1=sn_t[ci][:, 0:128:2], op=ALU.mult),
               reads=[rk_("rsb"), snk], writes=[rk_("rt2e")])
            op("dve", lambda e: e.tensor_tensor(out=rt2[:, 1:128:2], in0=src[:, 0:128:2], in1=sn_t[ci][:, 1:128:2], op=ALU.mult),
               reads=[rk_("rsb"), snk], writes=[rk_("rt2o")])
            op("pool", lambda e: e.tensor_tensor(out=dst[:], in0=rt1[:], in1=rt2[:], op=ALU.add),
               reads=[rk_("rt1"), rk_("rt2e"), rk_("rt2o")], writes=[dstk])
        rotary(rsb[:, 0:128], krot, rk_("krot"))
        op("act", lambda e: e.activation(out=kz[:], in_=krot[:], func=AF.Copy, scale=rv[:, h:h + 1]), reads=[rk_("krot"), "rv"], writes=[rk_("kz")])
        Rk, Rbk = "R%d" % h, "Rb%d" % h
        yield
        if store:
            rotary(rsb[:, 128:256], qrot, rk_("qrot"))
            yield
            pb, pk = bank[rbk2], "ps%d" % rbk2
            op("pe", lambda e: e.transpose(out=pb[:, 0:128], in_=krot[:], identity=identb[:]), reads=[rk_("krot"), "identb"], writes=[pk])
            op("pe", lambda e: e.transpose(out=pb[:, 128:256], in_=qrot[:], identity=identb[:]), reads=[rk_("qrot"), "identb"], writes=[pk])
            op("act", lambda e: e.copy(out=rkT[:], in_=pb[:, 0:128]), reads=[pk], writes=[rk_("rkT")])
            op("act", lambda e: e.copy(out=rqT[:], in_=pb[:, 128:256]), reads=[pk], writes=[rk_("rqT")])
            op("pool", lambda e: e.tensor_tensor(out=rqx[:], in0=rqT[:], in1=xibc[:, h, :], op=ALU.mult), reads=[rk_("rqT"), "xibc"], writes=[rk_("rqx")])
            yield
            psc, sck = bankF[rbk2], "ps%d" % rbk2
            op("pe", lambda e: e.matmul(out=psc[:, 0:128], lhsT=rkT[:], rhs=rqT[:], start=True, stop=True), reads=[rk_("rkT"), rk_("rqT")], writes=[sck])
            op("dve", lambda e: e.tensor_tensor(out=rPT[:], in0=psc[:, 0:128], in1=rm[:, h, :], op=ALU.mult), reads=[sck, "rm"], writes=[rk_("rPT")])
            yield
            po, pok = bankF[rbk], "ps%d" % rbk
            op("pe", lambda e: e.matmul(out=po[:, 0:128], lhsT=rPT[:], rhs=rvt[:], start=True, stop=False), reads=[rk_("rPT"), rk_("rvt")], writes=[pok])
            op("pe", lambda e: e.matmul(out=po[:, 0:128], lhsT=rqx[:], rhs=Rb[h][:], start=False, stop=True), reads=[rk_("rqx"), Rbk, pok], writes=[pok])
            s_ = t - T0
            op("act", lambda e: e.copy(out=osb[:], in_=po[:, 0:128]), reads=[pok], writes=[rk_("osb")])
            yield
            sc = st[:, 8 + h:9 + h]; sck2 = "st%d" % (8 + h)
            rms_rstd(osb[:], [rk_("osb")], HD, sc, sck2)
            op("dve", lambda e: e.tensor_scalar(out=og[:], in0=osb[:], scalar1=sc, scalar2=None, op0=ALU.mult), reads=[rk_("osb"), sck2], writes=[rk_("og")])
            op("pool", lambda e: e.tensor_tensor(out=mix[:, s_, 512 + h * 128:512 + (h + 1) * 128], in0=og[:], in1=sgr[:], op=ALU.mult),
               reads=[rk_("og"), rk_("sgr")], writes=["mix%d" % (4 + h)])
        if not store:
            yield
        pr, prk = bankF[rbk], "ps%d" % rbk
        op("pe", lambda e: e.matmul(out=pr[:, 0:128], lhsT=kz[:], rhs=rvt[:], start=True, stop=True), reads=[rk_("kz"), rk_("rvt")], writes=[prk])
        g128 = float(np.float64(1.0 - 2.0 ** (-5 - h)) ** 128)
        op("dve", lambda e: e.scalar_tensor_tensor(out=Rst[h][:], in0=Rst[h][:], scalar=g128, in1=pr[:, 0:128], op0=ALU.mult, op1=ALU.add),
           reads=[Rk, prk], writes=[Rk])
        op("act", lambda e: e.copy(out=Rb[h][:], in_=Rst[h][:]), reads=[Rk], writes=[Rbk])
        yield

    def prep_task(g):
        gi = g % 2
        for r in range(6):
            if 0 <= r - 2 < 4:
                norm_b(4 * g + r - 2, gi, r - 2)
            if r < 4:
                norm_a(4 * g + r)
            yield
        yield
        if "g" in parts:
            yield from gdn_common(gi, g >= G_FULL)

    def ret_tile_state(t, tt, gi, ci):
        nk = "nTg%d" % gi
        Rr = PH["RS"]; rbk = PH["rbank"]
        cs_t, sn_t = PH["cs_t"], PH["sn_t"]
        rsb4, rt14, rt24, krot4, kz4, rvt4 = Rr["rsb4"], Rr["rt14"], Rr["rt24"], Rr["krot4"], Rr["kz4"], Rr["rvt4"]
        csk, snk = "cs_t%d" % ci, "sn_t%d" % ci
        pf, fk = bankF[rbk], "ps%d" % rbk
        tsl = slice(tt * 128, (tt + 1) * 128)
        for k in range(8):
            op("pe", lambda e, k=k: e.matmul(out=pf[:, :], lhsT=nTg[gi][:, k, tsl], rhs=WrK[:, k, :], start=(k == 0), stop=(k == 7)),
               reads=[nk, "WrK"], writes=[fk])
        yield
        op("act", lambda e: e.copy(out=rsb4[:], in_=pf[:, :]), reads=[fk], writes=["rsb4"])
        for k in range(8):
            op("pe", lambda e, k=k: e.matmul(out=pf[:, :], lhsT=nTg[gi][:, k, tsl], rhs=WrV[:, k, :], start=(k == 0), stop=(k == 7)),
               reads=[nk, "WrV"], writes=[fk])
        s4 = rsb4[:].rearrange("p (h m two) -> p h m two", h=4, two=2)
        r4 = rt24[:].rearrange("p (h m two) -> p h m two", h=4, two=2)
        bch = lambda ap2, n: ap2.unsqueeze(1).broadcast_to([128, 4, n])
        op("dve", lambda e: e.tensor_tensor(out=rt14[:].rearrange("p (h n) -> p h n", h=4), in0=rsb4[:].rearrange("p (h n) -> p h n", h=4),
                                            in1=bch(cs_t[ci][:], 128), op=ALU.mult), reads=["rsb4", csk], writes=["rt14"])
        op("dve", lambda e: e.tensor_tensor(out=r4[:, :, :, 0], in0=s4[:, :, :, 1], in1=bch(sn_t[ci][:, 0:128:2], 64), op=ALU.mult),
           reads=["rsb4", snk], writes=["rt24e"])
        op("dve", lambda e: e.tensor_tensor(out=r4[:, :, :, 1], in0=s4[:, :, :, 0], in1=bch(sn_t[ci][:, 1:128:2], 64), op=ALU.mult),
           reads=["rsb4", snk], writes=["rt24o"])
        op("pool", lambda e: e.tensor_tensor(out=krot4[:].rearrange("p h n -> p (h n)"), in0=rt14[:], in1=rt24[:], op=ALU.add),
           reads=["rt14", "rt24e", "rt24o"], writes=["krot4"])
        op("pool", lambda e: e.tensor_tensor(out=kz4[:], in0=krot4[:], in1=rv[:, 0:4].unsqueeze(2).broadcast_to([128, 4, 128]), op=ALU.mult),
           reads=["krot4", "rv"], writes=["kz4"])
        yield
        op("act", lambda e: e.copy(out=rvt4[:].rearrange("p h n -> p (h n)"), in_=pf[:, :]), reads=[fk], writes=["rvt4"])
        yield
        yield
        for h in range(4):
            op("pe", lambda e, h=h: e.matmul(out=pf[:, h * 128:(h + 1) * 128], lhsT=kz4[:, h, :], rhs=rvt4[:, h, :], start=True, stop=True),
               reads=["kz4", "rvt4"], writes=[fk])
        yield
        for h in range(4):
            g128 = float(np.float64(1.0 - 2.0 ** (-5 - h)) ** 128)
            op("dve", lambda e, h=h, g128=g128: e.scalar_tensor_tensor(out=Rst[h][:], in0=Rst[h][:], scalar=g128, in1=pf[:, h * 128:(h + 1) * 128],
                                                                      op0=ALU.mult, op1=ALU.add), reads=["R%d" % h, fk], writes=["R%d" % h])
            op("act", lambda e, h=h: e.copy(out=Rb[h][:], in_=Rst[h][:]), reads=["R%d" % h], writes=["Rb%d" % h])
        yield

    def ret_task(g):
        gi = g % 2; full = g >= G_FULL
        for tt in range(4):
            t = 4 * g + tt
            ci = t % 2
            dma("sp", PH["cs_t"][ci][:], cosF[t * 128:(t + 1) * 128, :], writes=["cs_t%d" % ci])
            dma("sp", PH["sn_t"][ci][:], sinS[t * 128:(t + 1) * 128, :], writes=["sn_t%d" % ci])
            if full:
                for h in range(4):
                    yield from ret_tile(t, tt, gi, h, full, ci)
            else:
                yield from ret_tile_state(t, tt, gi, ci)


    def run_tasks(tasks, reps=None):
        tasks = list(tasks)
        reps = dict(reps or {})
        while tasks:
            for tk_ in list(tasks):
                for _ in range(reps.get(id(tk_), 1)):
                    try:
                        next(tk_)
                    except StopIteration:
                        tasks.remove(tk_)
                        break

    g0 = NG - ng_run
    g1 = NG
    if dev_g0 is not None:
        g0 = dev_g0; g1 = g0 + ng_run
    run_tasks([prep_task(g0)])
    cur_phase = [None]
    for g in range(g0, g1):
        full = g >= G_FULL
        if cur_phase[0] != full:
            if cur_phase[0] is not None:
                p.barrier()
                es_ph[0].close()
                es_ph[0] = ExitStack()
            alloc_phase(full)
            cur_phase[0] = full
        tasks = []
        if "g" in parts:
            pf_ok = (g + 1 < g1) and ((g + 1 >= G_FULL) == full)
            if full:
                tasks.append(gdn_task(g, [0, 1, 2, 3], PH["streams"][0], pf_ok))
            else:
                tasks.append(gdn_task(g, [0, 2], PH["streams"][0], pf_ok))
                tasks.append(gdn_task(g, [1, 3], PH["streams"][1], pf_ok))
        reps = {}
        if "r" in parts:
            rt_ = ret_task(g)
            tasks.append(rt_)
            reps[id(rt_)] = 1
        if g + 1 < g1:
            tasks.append(prep_task(g + 1))
        run_tasks(tasks, reps)

    def early(src):
        p.barrier()
        dma("sp", out[0:128, :], src, writes=["out0"])
        p.finish()
        return nc, p
    if phase == "pass":
        return early(xt[0][:])
    p.barrier()
    es_ph[0].close()
    es_keep.close()
    PHTAG[0] = "_X"
    hres = R("hres", [128, NSLOT, D], F32)
    n2T = R("n2T", [128, 8, 2 + OWN], BF16)
    es_b = ExitStack()
    Wout = L("Wout", [128, 8, D], BF16, es_b)
    mixT = [L("mixT%d" % i, [128, 8, 128], BF16, es_b) for i in range(2)]
    xt2 = [L("xt2%d" % i, [128, D], F32, es_b) for i in range(2)]
    junk = L("junk2", [128, D], F32, es_b)
    nb2 = [L("nb2%d" % i, [128, D], BF16, es_b) for i in range(2)]
    st = L("st2", [128, 4], F32, es_b)
    dma("pool", Wout[:], w_out.rearrange("(k p) c -> p k c", p=128), writes=["Wout"])
    nw2 = L("nw2", [128, D], F32, es_b)
    dma("sp", nw2[:], vecs[1:2, :].partition_broadcast(128), writes=["nw2"])

    def b3_a(s_):
        t = T0 + s_
        dma("sp", xt2[s_ % 2][:], xpad[t * 128:(t + 1) * 128, :], writes=["xt2%d" % (s_ % 2)])
        pb, pk = nB()
        for k in range(8):
            op("pe", lambda e, k=k: e.transpose(out=pb[:, k * 128:(k + 1) * 128], in_=mix[:, s_, k * 128:(k + 1) * 128], identity=identb[:]),
               reads=["mix%d" % k, "identb"], writes=[pk])
        op("act", lambda e: e.copy(out=mixT[s_ % 2][:], in_=pb[:, :].rearrange("p (k n) -> p k n", k=8)), reads=[pk], writes=["mixT%d" % (s_ % 2)])

    def b3_b(s_):
        xb_ = xt2[s_ % 2]; xk = "xt2%d" % (s_ % 2); mT = mixT[s_ % 2]; mk = "mixT%d" % (s_ % 2)
        for half in range(2):
            pf, fk = nF()
            for k in range(8):
                op("pe", lambda e, k=k: e.matmul(out=pf[:, :], lhsT=mT[:, k, :], rhs=Wout[:, k, half * 512:(half + 1) * 512],
                                                start=(k == 0), stop=(k == 7)), reads=[mk, "Wout"], writes=[fk])
            op("dve", lambda e: e.tensor_tensor(out=hres[:, s_, half * 512:(half + 1) * 512], in0=xb_[:, half * 512:(half + 1) * 512],
                                                in1=pf[:, :], op=ALU.add), reads=[xk, fk], writes=["h%d_%d" % (s_, half)])

    def b3_c(s_):
        hks = ["h%d_0" % s_, "h%d_1" % s_]
        sc = st[:, (s_ % 2):(s_ % 2) + 1]; sck = "stb%d" % (s_ % 2)
        rms_rstd(hres[:, s_, :], hks, D, sc, sck)
        op("dve", lambda e: e.scalar_tensor_tensor(out=nb2[s_ % 2][:], in0=hres[:, s_, :], scalar=sc, in1=nw2[:],
                                                   op0=ALU.mult, op1=ALU.mult), reads=hks + [sck, "nw2"], writes=["nb2%d" % (s_ % 2)])

    def b3_d(s_):
        nb_ = nb2[s_ % 2]; nbk = "nb2%d" % (s_ % 2)
        pb, pk = nB()
        for k in range(8):
            op("pe", lambda e, k=k: e.transpose(out=pb[:, k * 128:(k + 1) * 128], in_=nb_[:, k * 128:(k + 1) * 128], identity=identb[:]),
               reads=[nbk, "identb"], writes=[pk])
        pv = pb[:, :].rearrange("p (k n) -> p k n", k=8)
        if s_ == 0:
            op("act", lambda e: e.copy(out=n2T[:, :, 0:2], in_=pv[:, :, 126:128]), reads=[pk], writes=["n2T_h"])
        else:
            op("act", lambda e: e.copy(out=n2T[:, :, 2 + (s_ - 1) * 128:2 + s_ * 128], in_=pv), reads=[pk], writes=["n2T_%d" % ((s_ - 1) // 4)])

    for r in range(NSLOT + 3):
        if 0 <= r - 3 < NSLOT:
            b3_d(r - 3)
        if 0 <= r - 2 < NSLOT:
            b3_c(r - 2)
        if 0 <= r - 1 < NSLOT:
            b3_b(r - 1)
        if r < NSLOT:
            b3_a(r)

    if phase == "b3":
        return early(hres[:, 1, :])
    p.barrier()
    es_b.close()
    es_mix.close()
    es_c = ExitStack()
    NGRP = DFF // 256
    Wgu = [L("Wgu%d" % i, [128, 2, 8, 256], BF16, es_c) for i in range(2)]
    Wd = [L("Wd%d" % i, [128, 2, D], BF16, es_c) for i in range(2)]
    mcw = L("mcw", [128, 44, 3], F32, es_c)
    hb = [L("hb%d" % i, [128, 2 + OWN], F32, es_c) for i in range(2)]
    yb = [L("yb%d" % i, [128, OWN], F32, es_c) for i in range(2)]
    actT = [L("actT%d" % i, [128, 2, OWN], BF16, es_c) for i in range(2)]
    st = L("st3", [128, 4], F32, es_c)
    ob = [L("ob%d" % i, [128, D], F32, es_c) for i in range(2)]
    junk = L("junk3", [128, D], BF16, es_c)
    dma("sp", mcw[:], mcwT.rearrange("(c p) i -> p c i", p=128), writes=["mcw"])
    nw3 = L("nw3", [128, D], F32, es_c)
    dma("sp", nw3[:], vecs[2:3, :].partition_broadcast(128), writes=["nw3"])
    w_up_v = w_up.rearrange("(k p) c -> p k c", p=128)
    w_dn_v = w_down.rearrange("(c p) d -> p c d", p=128)

    def load_up(gi_):
        b = gi_ % 2
        dma("pool", Wgu[b][:, 0, :, :], w_up_v[:, :, gi_ * 256:(gi_ + 1) * 256], writes=["Wgu%d" % b])
        dma("pool", Wgu[b][:, 1, :, :], w_up_v[:, :, DFF + gi_ * 256:DFF + (gi_ + 1) * 256], writes=["Wgu%d" % b])

    def load_dn(gi_):
        b = gi_ % 2
        dma("pool", Wd[b][:], w_dn_v[:, gi_ * 2:gi_ * 2 + 2, :], writes=["Wd%d" % b])

    def ffn_up(gi_):
        b = gi_ % 2
        ak = "actT%d" % b
        for fc in range(2):
            for which in range(2):
                hb_ = hb[which]; hbk = "hb%d" % which
                cidx = which * 22 + gi_ * 2 + fc
                lw = lambda k: Wgu[b][:, which, k, fc * 128:(fc + 1) * 128]
                pf, fk = nF()
                for k in range(8):
                    op("pe", lambda e, k=k: e.matmul(out=pf[:, 0:2], lhsT=lw(k), rhs=n2T[:, k, 0:2], start=(k == 0), stop=(k == 7)),
                       reads=["Wgu%d" % b, "n2T_h"], writes=[fk])
                op("act", lambda e: e.copy(out=hb_[:, 0:2], in_=pf[:, 0:2]), reads=[fk], writes=[hbk + "h"])
                for tb in range(4):
                    pf, fk = nF()
                    for k in range(8):
                        op("pe", lambda e, k=k: e.matmul(out=pf[:, :], lhsT=lw(k), rhs=n2T[:, k, 2 + tb * 512:2 + (tb + 1) * 512],
                                                        start=(k == 0), stop=(k == 7)), reads=["Wgu%d" % b, "n2T_%d" % tb], writes=[fk])
                    op("act", lambda e, tb=tb: e.copy(out=hb_[:, 2 + tb * 512:2 + (tb + 1) * 512], in_=pf[:, :]), reads=[fk], writes=[hbk + "_%d" % tb])
                hbks = [hbk + "h"] + [hbk + "_%d" % i for i in range(4)]
                ybk = "yb%d" % which
                op("act", lambda e: e.activation(out=yb[which][:], in_=hb_[:, 2:2 + OWN], func=AF.Copy, scale=mcw[:, cidx, 2:3]),
                   reads=hbks + ["mcw"], writes=[ybk])
                for i in (1, 0):
                    op("dve", lambda e, i=i: e.scalar_tensor_tensor(out=yb[which][:], in0=hb_[:, i:i + OWN], scalar=mcw[:, cidx, i:i + 1],
                                                                   in1=yb[which][:], op0=ALU.mult, op1=ALU.add), reads=hbks + ["mcw", ybk], writes=[ybk])
            op("act", lambda e: e.activation(out=yb[0][:], in_=yb[0][:], func=AF.Silu), reads=["yb0"], writes=["yb0"])
            op("pool", lambda e: e.tensor_tensor(out=actT[b][:, fc, :], in0=yb[0][:], in1=yb[1][:], op=ALU.mult),
               reads=["yb0", "yb1"], writes=[ak + "_%d" % fc])

    def final_tile(tt):
        hks = ["h%d_0" % (tt + 1), "h%d_1" % (tt + 1)]
        sc = st[:, (tt % 2):(tt % 2) + 1]; sck = "stf%d" % (tt % 2)
        rms_rstd(hres[:, tt + 1, :], hks, D, sc, sck)
        o_ = ob[tt % 2]; ok = "ob%d" % (tt % 2)
        op("dve", lambda e: e.scalar_tensor_tensor(out=o_[:], in0=hres[:, tt + 1, :], scalar=sc, in1=nw3[:],
                                                   op0=ALU.mult, op1=ALU.mult), reads=hks + [sck, "nw3"], writes=[ok])
        dma("sp", out[tt * 128:(tt + 1) * 128, :], o_[:], reads=[ok], writes=["out%d" % tt])

    def ffn_down(gi_):
        b = gi_ % 2
        ak = "actT%d" % b
        for tt in range(16):
            for half in range(2):
                pf, fk = nF()
                for fc in range(2):
                    op("pe", lambda e, fc=fc: e.matmul(out=pf[:, :], lhsT=actT[b][:, fc, tt * 128:(tt + 1) * 128],
                                                      rhs=Wd[b][:, fc, half * 512:(half + 1) * 512], start=(fc == 0), stop=(fc == 1)),
                       reads=[ak + "_0", ak + "_1", "Wd%d" % b], writes=[fk])
                hk = "h%d_%d" % (tt + 1, half)
                op("dve", lambda e: e.tensor_tensor(out=hres[:, tt + 1, half * 512:(half + 1) * 512],
                                                    in0=hres[:, tt + 1, half * 512:(half + 1) * 512], in1=pf[:, :], op=ALU.add),
                   reads=[fk, hk], writes=[hk])
            if gi_ == NGRP - 1 and tt >= 1:
                final_tile(tt - 1)
        if gi_ == NGRP - 1:
            final_tile(15)

    load_up(0); load_dn(0)
    for gi_ in range(NGRP + 1):
        if gi_ + 1 < NGRP:
            load_up(gi_ + 1)
        if gi_ < NGRP:
            ffn_up(gi_)
        if gi_ >= 1:
            ffn_down(gi_ - 1)
        if gi_ + 1 < NGRP:
            load_dn(gi_ + 1)

    p.finish()
    es_c.close()
    es_r.close()
    return nc, p


def _consts():
    idx = np.arange(128)
    same = (idx[:, None] // 64) == (idx[None, :] // 64)
    cm = np.zeros((10, 128, 128), np.float32)
    cm[0] = np.eye(128)
    cm[1] = 1.0
    cm[2] = (same & (idx[:, None] < idx[None, :]))
    cm[3] = (same & (idx[:, None] <= idx[None, :]))
    cm[4] = cm[3]
    cm[5] = (idx[:, None] < 64) * np.ones((1, 128))
    cm[6] = (idx[:, None] >= 64) * np.ones((1, 128))
    cm[7] = (1.0 - cm[2]) * -30000.0
    cm[8] = (1.0 - cm[3]) * -30000.0
    hh = np.arange(4, dtype=np.float64)
    gam = 1.0 - 2.0 ** (-5.0 - hh)
    lg = np.log(gam)
    rel = (idx[None, :] - idx[:, None]).astype(np.float64)
    rmat = np.where(rel[None] >= 0, np.exp(rel[None] * lg[:, None, None]), 0.0) * HD ** -0.5
    rvec = np.zeros((128, 8), np.float64)
    rvec[:, 0:4] = np.exp((127.0 - idx[:, None]) * lg[None, :]) * HD ** -0.5
    rxi = np.exp((idx[None, :] + 1.0) * lg[:, None])
    return cm, rmat.astype(np.float32), rvec.astype(np.float32), rxi.astype(np.float32)


_CACHE = {}


def kernel(x, attn_norm_w, w_in, gdn_conv_w, gdn_a_log, gdn_dt_bias, gdn_norm_w, w_out, mlp_norm_w,
           w_up, mlp_conv_w, w_down, final_norm_w):
    f = lambda a: np.ascontiguousarray(np.asarray(a, dtype=np.float32))
    x2 = f(x).reshape(S, D)
    if "nc" not in _CACHE:
        _CACHE["nc"] = build_program()[0]
    nc = _CACHE["nc"]
    cm, rmat, rvec, rxi = _consts()
    vecs = np.stack([f(attn_norm_w)[0], f(mlp_norm_w)[0], f(final_norm_w)], 0)
    gsm = np.concatenate([f(gdn_a_log)[0], f(gdn_dt_bias)[0], f(gdn_norm_w)[0]])[None, :]
    angle = (1.0 / (10000.0 ** np.linspace(0.0, 1.0, 64, dtype=np.float32))).astype(np.float32)
    angle = np.repeat(angle, 2)
    sign = np.tile(np.array([-1.0, 1.0], np.float32), 64)
    common = {
        "w_in": f(w_in)[0], "w_out": f(w_out)[0], "w_up": f(w_up)[0], "w_down": f(w_down)[0],
        "gcwT": np.ascontiguousarray(f(gdn_conv_w)[0].T), "mcwT": np.ascontiguousarray(f(mlp_conv_w)[0].T),
        "vecs": np.ascontiguousarray(vecs), "gsm": np.ascontiguousarray(gsm),
        "cmat": cm, "rmat": rmat, "rvec": rvec, "rxi": rxi,
    }
    in_maps = []
    for c in range(NCORES):
        n_real = OWN * (c + 1)
        xp = np.zeros((S, D), np.float32)
        xp[S - n_real:] = x2[:n_real]
        pos = (np.arange(S, dtype=np.int64) - (S - n_real)).astype(np.float32)
        phase = pos[:, None] * angle[None, :]
        m = dict(common)
        m["xpad"] = xp
        m["cosF"] = np.cos(phase).astype(np.float32)
        m["sinS"] = (np.sin(phase) * sign[None, :]).astype(np.float32)
        in_maps.append(m)
    res = run_bass_kernel_spmd(nc, in_maps, core_ids=list(range(NCORES)))
    outs = [np.asarray(res.results[c]["out"], dtype=np.float32) for c in range(NCORES)]
    return np.concatenate(outs, 0).reshape(1, S, D)
```
